# Optimizing a Trainium2 kernel written in Bass

```python
import math
import functools
import jax
import jax.numpy as jnp
from jax import lax
import numpy as np

D_MODEL = 1024
BATCH = 8
SEQ = 4096
DEPTH = 2

MEM_LEN = 256
N_EVEN = (DEPTH + 1) // 2
N_ODD = DEPTH // 2
N_DIR = 2
CONV_W = 4
CONV_PAD = (2, 1)
CHUNK = 64
EPS = 1e-6
DN_HEADS = 4
DN_HEAD_DIM = D_MODEL // 8
DN_WIDTH = DN_HEADS * DN_HEAD_DIM
ML_HEADS = 4
ML_HEAD_DIM = D_MODEL // 8
ML_WIDTH = ML_HEADS * ML_HEAD_DIM
D_MIX = DN_WIDTH + ML_WIDTH
AB_SIZES = (3 * DN_WIDTH, DN_WIDTH, N_DIR * DN_HEADS, N_DIR * DN_HEADS,
            2 * ML_WIDTH, ML_WIDTH, ML_WIDTH, N_DIR * ML_HEADS, N_DIR * ML_HEADS)
AB_IN_COLS = sum(AB_SIZES)
LRU_WIDTH = D_MODEL
LRU_BLOCKS = 4
LRU_BLOCK = LRU_WIDTH // LRU_BLOCKS
LRU_C = 8.0
XA_HEADS = 4
XA_HEAD_DIM = D_MODEL // XA_HEADS
D_FF = 4 * D_MODEL

kernel_name = 'bidir_hybrid_deltanet_mlstm_rglru_trunk'


def rmsnorm(x, w):
    xf = x.astype(jnp.float32)
    y = xf * lax.rsqrt(jnp.mean(xf * xf, axis=-1, keepdims=True) + EPS)
    return (y * w.astype(jnp.float32)).astype(x.dtype)


def l2norm(t):
    return t * lax.rsqrt(jnp.sum(t * t, axis=-1, keepdims=True) + 1e-6)


def dwconv(x, w):
    return lax.conv_general_dilated(x, w[:, None, :].astype(x.dtype), (1,), [CONV_PAD],
                                    dimension_numbers=('NWC', 'WIO', 'NWC'),
                                    feature_group_count=x.shape[-1])


def to_heads(t, n_heads):
    b, s, _ = t.shape
    return t.reshape(b, s, n_heads, -1).transpose(0, 2, 1, 3)


def from_heads(t):
    b, h, s, d = t.shape
    return t.transpose(0, 2, 1, 3).reshape(b, s, h * d)


def both_dirs(t_fwd, t_bwd):
    return jnp.concatenate([t_fwd, jnp.flip(t_bwd, axis=2)], axis=1)


def merge_dirs(o, n_heads):
    return o[:, :n_heads] + jnp.flip(o[:, n_heads:], axis=2)


def chunk_split(t):
    b, h, s = t.shape[:3]
    t = t.reshape(b, h, s // CHUNK, CHUNK, *t.shape[3:])
    return jnp.moveaxis(t, 2, 0)


def chunk_merge(t):
    n, b, h, c, d = t.shape
    return jnp.moveaxis(t, 0, 2).reshape(b, h, n * c, d)


def gated_delta_rule(q, k, v, g, beta):
    dk = q.shape[-1]
    q, k, v = chunk_split(q * dk ** -0.5), chunk_split(k), chunk_split(v)
    g = jnp.cumsum(chunk_split(g), axis=-1)
    beta = chunk_split(beta)
    incl = jnp.tril(jnp.ones((CHUNK, CHUNK), dtype=bool))
    strict = jnp.tril(jnp.ones((CHUNK, CHUNK), dtype=bool), -1)
    decay = jnp.exp(jnp.where(incl, g[..., :, None] - g[..., None, :], -jnp.inf))
    k_beta = k * beta[..., None]
    a_mat = jnp.where(strict, jnp.einsum('nbhid,nbhjd->nbhij', k_beta, k) * decay, 0.0)
    lhs = a_mat + jnp.eye(CHUNK, dtype=a_mat.dtype)
    solve = functools.partial(lax.linalg.triangular_solve, left_side=True, lower=True,
                              unit_diagonal=True)
    u = solve(lhs, v * beta[..., None])
    w = solve(lhs, k_beta * jnp.exp(g)[..., None])
    attn = jnp.einsum('nbhid,nbhjd->nbhij', q, k) * decay
    q_dec = q * jnp.exp(g)[..., None]
    k_dec = k * jnp.exp(g[..., -1:] - g)[..., None]
    chunk_decay = jnp.exp(g[..., -1])

    def step(state, xs):
        q_c, k_c, u_c, w_c, attn_c, dec_c = xs
        v_new = u_c - jnp.einsum('bhcd,bhde->bhce', w_c, state)
        o_c = (jnp.einsum('bhcd,bhde->bhce', q_c, state)
               + jnp.einsum('bhij,bhje->bhie', attn_c, v_new))
        state = state * dec_c[..., None, None] + jnp.einsum('bhcd,bhce->bhde', k_c, v_new)
        return state, o_c

    n, b, h = q.shape[:3]
    s0 = jnp.zeros((b, h, dk, v.shape[-1]), q.dtype)
    _, o = lax.scan(step, s0, (q_dec, k_dec, u, w, attn, chunk_decay))
    return chunk_merge(o)


def mlstm(q, k, v, log_i, log_f):
    dk = q.shape[-1]
    q, k, v = chunk_split(q), chunk_split(k * dk ** -0.5), chunk_split(v)
    log_i = chunk_split(log_i)
    bcum = jnp.cumsum(chunk_split(log_f), axis=-1)
    incl = jnp.tril(jnp.ones((CHUNK, CHUNK), dtype=bool))
    d_mat = jnp.where(incl, bcum[..., :, None] - bcum[..., None, :] + log_i[..., None, :],
                      -jnp.inf)
    m_intra = jnp.max(d_mat, axis=-1)
    w_src = bcum[..., -1:] - bcum + log_i
    m_src = jnp.max(w_src, axis=-1)
    qk = jnp.einsum('nbhid,nbhjd->nbhij', q, k)

    def step(carry, xs):
        c_st, n_st, m_st = carry
        q_c, k_c, v_c, b_c, d_c, mi_c, ws_c, ms_c, qk_c = xs
        m_t = jnp.maximum(b_c + m_st[..., None], mi_c)
        inter = jnp.exp(b_c + m_st[..., None] - m_t)
        p = jnp.exp(d_c - m_t[..., None]) * qk_c
        num = (inter[..., None] * jnp.einsum('bhcd,bhde->bhce', q_c, c_st)
               + jnp.einsum('bhij,bhje->bhie', p, v_c))
        den = inter * jnp.einsum('bhcd,bhd->bhc', q_c, n_st) + jnp.sum(p, axis=-1)
        h_c = num / jnp.maximum(jnp.abs(den), jnp.exp(-m_t))[..., None]
        m_new = jnp.maximum(b_c[..., -1] + m_st, ms_c)
        dec = jnp.exp(b_c[..., -1] + m_st - m_new)
        src = jnp.exp(ws_c - m_new[..., None])
        c_st = dec[..., None, None] * c_st + jnp.einsum('bhc,bhcd,bhce->bhde', src, k_c, v_c)
        n_st = dec[..., None] * n_st + jnp.einsum('bhc,bhcd->bhd', src, k_c)
        return (c_st, n_st, m_new), h_c

    n, b, h = q.shape[:3]
    init = (jnp.zeros((b, h, dk, v.shape[-1]), q.dtype), jnp.zeros((b, h, dk), q.dtype),
            jnp.zeros((b, h), q.dtype))
    _, hs = lax.scan(step, init, (q, k, v, bcum, d_mat, m_intra, w_src, m_src, qk))
    return chunk_merge(hs)


def mixer_ab(h, w_in, w_out, dn_conv_w, dn_a_log, dn_dt_bias, dn_out_norm,
             ml_conv_w, ml_i_bias, ml_f_bias):
    b, s, _ = h.shape
    f32 = jnp.float32
    offsets = np.cumsum(AB_SIZES)[:-1].tolist()
    (dn_qkv, dn_z, dn_a, dn_b, ml_qk, ml_v, ml_o, ml_i, ml_f) = jnp.split(h @ w_in, offsets, axis=-1)
    qkv = jax.nn.silu(dwconv(dn_qkv, dn_conv_w)).astype(f32)
    q, k, v = (to_heads(t, DN_HEADS) for t in jnp.split(qkv, 3, axis=-1))
    q, k = l2norm(q), l2norm(k)
    a_pre = dn_a.astype(f32).reshape(b, s, N_DIR, DN_HEADS)
    g = -jnp.exp(dn_a_log.astype(f32)) * jax.nn.softplus(a_pre + dn_dt_bias.astype(f32))
    beta = jax.nn.sigmoid(dn_b.astype(f32).reshape(b, s, N_DIR, DN_HEADS))
    g, beta = g.transpose(2, 0, 3, 1), beta.transpose(2, 0, 3, 1)
    o = gated_delta_rule(both_dirs(q, q), both_dirs(k, k), both_dirs(v, v),
                         both_dirs(g[0], g[1]), both_dirs(beta[0], beta[1]))
    o = merge_dirs(o, DN_HEADS)
    o = rmsnorm(o, dn_out_norm) * jax.nn.silu(to_heads(dn_z.astype(f32), DN_HEADS))
    dn_out = from_heads(o)
    mqk = jax.nn.silu(dwconv(ml_qk, ml_conv_w)).astype(f32)
    mq, mk = (to_heads(t, ML_HEADS) for t in jnp.split(mqk, 2, axis=-1))
    mv = to_heads(ml_v.astype(f32), ML_HEADS)
    log_i = (ml_i.astype(f32).reshape(b, s, N_DIR, ML_HEADS)
             + ml_i_bias.astype(f32)).transpose(2, 0, 3, 1)
    log_f = jax.nn.log_sigmoid(ml_f.astype(f32).reshape(b, s, N_DIR, ML_HEADS)
                               + ml_f_bias.astype(f32)).transpose(2, 0, 3, 1)
    hm = mlstm(both_dirs(mq, mq), both_dirs(mk, mk), both_dirs(mv, mv),
               both_dirs(log_i[0], log_i[1]), both_dirs(log_f[0], log_f[1]))
    hm = merge_dirs(hm, ML_HEADS) * jax.nn.sigmoid(to_heads(ml_o.astype(f32), ML_HEADS))
    ml_out = from_heads(hm)
    mixed = jnp.concatenate([dn_out, ml_out], axis=-1).astype(h.dtype)
    return mixed @ w_out


def linear_combine(left, right):
    return (left[0] * right[0], right[0] * left[1] + right[1])


def mixer_c(h, w_in, w_out, conv_w, conv_b, w_a, b_a, w_x, b_x, lam):
    b, s, _ = h.shape
    f32 = jnp.float32
    gate, xb = jnp.split(h @ w_in, 2, axis=-1)
    u = (dwconv(xb, conv_w) + conv_b).astype(f32)
    ub = u.reshape(b, s, LRU_BLOCKS, LRU_BLOCK)

    def blockdiag(w, bias):
        y = jnp.einsum('bsnk,dnkj->bsdnj', ub, w.astype(f32)).reshape(b, s, N_DIR, LRU_WIDTH)
        return y + bias.astype(f32)

    r = jax.nn.sigmoid(blockdiag(w_a, b_a))
    i = jax.nn.sigmoid(blockdiag(w_x, b_x))
    log_a = -LRU_C * r * jax.nn.softplus(-lam.astype(f32))
    a = jnp.exp(log_a)
    inp = jnp.sqrt(-jnp.expm1(2.0 * log_a)) * i * u[:, :, None, :]
    _, h_f = lax.associative_scan(linear_combine, (a[:, :, 0], inp[:, :, 0]), axis=1)
    _, h_b = lax.associative_scan(linear_combine, (a[:, :, 1], inp[:, :, 1]), reverse=True, axis=1)
    y = (h_f + h_b).astype(h.dtype) * jax.nn.gelu(gate)
    return y @ w_out


def cross_attend(h, mem_n, w_q, w_kv, w_o):
    b, s, _ = h.shape
    m = mem_n.shape[1]
    q = (h @ w_q).reshape(b, s, XA_HEADS, XA_HEAD_DIM)
    k, v = jnp.split(mem_n @ w_kv, 2, axis=-1)
    k = k.reshape(b, m, XA_HEADS, XA_HEAD_DIM)
    v = v.reshape(b, m, XA_HEADS, XA_HEAD_DIM)
    scores = jnp.einsum('bshd,bmhd->bhsm', q, k).astype(jnp.float32) * XA_HEAD_DIM ** -0.5
    p = jax.nn.softmax(scores, axis=-1).astype(h.dtype)
    o = jnp.einsum('bhsm,bmhd->bshd', p, v).reshape(b, s, XA_HEADS * XA_HEAD_DIM)
    return o @ w_o


def squared_relu_mlp(h, w_up, w_down):
    return jnp.square(jax.nn.relu(h @ w_up)) @ w_down


def setup_inputs(seed: int = 0):
    key = jax.random.key(seed)
    keys = iter(jax.random.split(key, 48))
    f32 = jnp.float32

    def nrm(shape, scale):
        return scale * jax.random.normal(next(keys), shape, f32)

    def gain(shape):
        return 1.0 + nrm(shape, 0.1)

    def unif(shape, lo, hi):
        return jax.random.uniform(next(keys), shape, f32, lo, hi)

    dt = jnp.exp(unif((N_EVEN, N_DIR, DN_HEADS), math.log(1e-3), math.log(1e-1)))
    lam_u = unif((N_ODD, N_DIR, LRU_WIDTH), 0.9, 0.999) ** (1.0 / LRU_C)
    return {
        'x': nrm((BATCH, SEQ, D_MODEL), 1.0),
        'mem': nrm((BATCH, MEM_LEN, D_MODEL), 1.0),
        'mix_norm_pre': gain((DEPTH, D_MODEL)),
        'mix_norm_post': gain((DEPTH, D_MODEL)),
        'ab_w_in': nrm((N_EVEN, D_MODEL, AB_IN_COLS), D_MODEL ** -0.5),
        'ab_w_out': nrm((N_EVEN, D_MIX, D_MODEL), D_MIX ** -0.5),
        'dn_conv_w': nrm((N_EVEN, CONV_W, 3 * DN_WIDTH), CONV_W ** -0.5),
        'dn_a_log': jnp.log(unif((N_EVEN, N_DIR, DN_HEADS), 1.0, 16.0)),
        'dn_dt_bias': dt + jnp.log(-jnp.expm1(-dt)),
        'dn_out_norm': gain((N_EVEN, DN_HEAD_DIM)),
        'ml_conv_w': nrm((N_EVEN, CONV_W, 2 * ML_WIDTH), CONV_W ** -0.5),
        'ml_i_bias': nrm((N_EVEN, N_DIR, ML_HEADS), 0.1),
        'ml_f_bias': jnp.linspace(3.0, 6.0, ML_HEADS, dtype=f32) + nrm((N_EVEN, N_DIR, ML_HEADS), 0.1),
        'c_w_in': nrm((N_ODD, D_MODEL, 2 * LRU_WIDTH), D_MODEL ** -0.5),
        'c_w_out': nrm((N_ODD, LRU_WIDTH, D_MODEL), LRU_WIDTH ** -0.5),
        'c_conv_w': nrm((N_ODD, CONV_W, LRU_WIDTH), CONV_W ** -0.5),
        'c_conv_b': nrm((N_ODD, LRU_WIDTH), 0.01),
        'c_w_a': nrm((N_ODD, N_DIR, LRU_BLOCKS, LRU_BLOCK, LRU_BLOCK), LRU_BLOCK ** -0.5),
        'c_b_a': nrm((N_ODD, N_DIR, LRU_WIDTH), 0.01),
        'c_w_x': nrm((N_ODD, N_DIR, LRU_BLOCKS, LRU_BLOCK, LRU_BLOCK), LRU_BLOCK ** -0.5),
        'c_b_x': nrm((N_ODD, N_DIR, LRU_WIDTH), 0.01),
        'c_lambda': jnp.log(lam_u) - jnp.log1p(-lam_u),
        'xa_norm_pre': gain((DEPTH, D_MODEL)),
        'xa_norm_post': gain((DEPTH, D_MODEL)),
        'xa_mem_norm': gain((DEPTH, D_MODEL)),
        'xa_w_q': nrm((DEPTH, D_MODEL, XA_HEADS * XA_HEAD_DIM), D_MODEL ** -0.5),
        'xa_w_kv': nrm((DEPTH, D_MODEL, 2 * XA_HEADS * XA_HEAD_DIM), D_MODEL ** -0.5),
        'xa_w_o': nrm((DEPTH, XA_HEADS * XA_HEAD_DIM, D_MODEL), D_MODEL ** -0.5),
        'ffn_norm_pre': gain((DEPTH, D_MODEL)),
        'ffn_norm_post': gain((DEPTH, D_MODEL)),
        'ffn_w_up': nrm((DEPTH, D_MODEL, D_FF), D_MODEL ** -0.5),
        'ffn_w_down': nrm((DEPTH, D_FF, D_MODEL), D_FF ** -0.5),
    }


def reference(x, mem, mix_norm_pre, mix_norm_post, ab_w_in, ab_w_out, dn_conv_w, dn_a_log,
              dn_dt_bias, dn_out_norm, ml_conv_w, ml_i_bias, ml_f_bias, c_w_in, c_w_out,
              c_conv_w, c_conv_b, c_w_a, c_b_a, c_w_x, c_b_x, c_lambda, xa_norm_pre,
              xa_norm_post, xa_mem_norm, xa_w_q, xa_w_kv, xa_w_o, ffn_norm_pre, ffn_norm_post,
              ffn_w_up, ffn_w_down):
    h = x
    for layer in range(DEPTH):
        j = layer // 2
        hn = rmsnorm(h, mix_norm_pre[layer])
        if layer % 2 == 0:
            y = mixer_ab(hn, ab_w_in[j], ab_w_out[j], dn_conv_w[j], dn_a_log[j], dn_dt_bias[j],
                         dn_out_norm[j], ml_conv_w[j], ml_i_bias[j], ml_f_bias[j])
        else:
            y = mixer_c(hn, c_w_in[j], c_w_out[j], c_conv_w[j], c_conv_b[j], c_w_a[j], c_b_a[j],
                        c_w_x[j], c_b_x[j], c_lambda[j])
        h = h + rmsnorm(y, mix_norm_post[layer])
        y = cross_attend(rmsnorm(h, xa_norm_pre[layer]), rmsnorm(mem, xa_mem_norm[layer]),
                         xa_w_q[layer], xa_w_kv[layer], xa_w_o[layer])
        h = h + rmsnorm(y, xa_norm_post[layer])
        y = squared_relu_mlp(rmsnorm(h, ffn_norm_pre[layer]), ffn_w_up[layer], ffn_w_down[layer])
        h = h + rmsnorm(y, ffn_norm_post[layer])
    return h
```

```python
import numpy as np
from contextlib import ExitStack
import concourse.bass as bass
import concourse.mybir as mybir
from concourse.bass_utils import run_bass_kernel_spmd

F32 = mybir.dt.float32
BF16 = mybir.dt.bfloat16
AF = mybir.ActivationFunctionType
ALU = mybir.AluOpType
AX = mybir.AxisListType
D = 1024
EPS = 1e-6
NEG = -30000.0


class _Eng:
    def __init__(self, ctx, name, be, is_dma, nslots=14):
        self.name, self.be, self.is_dma = name, be, is_dma
        self.waited = {}
        if is_dma:
            self.slots = [ctx.new_sem(f"{name}_d{i}") for i in range(nslots)]
            self.n = 0
        else:
            self.sem = ctx.new_sem(f"{name}_s")
            self.count = 0


class _Buf:
    __slots__ = ("w", "r", "rd")

    def __init__(self):
        self.w = None
        self.r = {}
        self.rd = []


class Ctx:
    def __init__(self, nc, es):
        self.nc, self.es = nc, es
        self.sems = []
        self.bufs = {}
        self.engs = {}
        for name, be, dma in (("pe", nc.tensor, False), ("act", nc.scalar, False),
                              ("dve", nc.vector, False), ("pool", nc.gpsimd, False),
                              ("sp", nc.sync, True), ("gq", nc.gpsimd, True)):
            self.engs[name] = _Eng(self, name, be, dma)
        self.nops = 0
        import os as _os
        self.limit = int(_os.environ["OP_LIMIT"]) if "OP_LIMIT" in _os.environ else None
        self.trace = tuple(int(v) for v in _os.environ["OP_TRACE"].split(",")) if "OP_TRACE" in _os.environ else None

    def new_sem(self, name):
        s = self.es.enter_context(self.nc.semaphore(name))
        self.sems.append(s)
        return len(self.sems) - 1

    def buf(self, k):
        b = self.bufs.get(k)
        if b is None:
            b = self.bufs[k] = _Buf()
        return b

    def op(self, eng, reads, writes, fn):
        E = self.engs[eng]
        need = {}
        if self.trace is not None and self.trace[0] <= self.nops < self.trace[1]:
            print("OP", self.nops, eng, "R", reads, "W", writes)
        if self.limit is not None and self.nops >= self.limit:
            self.nops += 1
            return None

        def add(ev, raw):
            de, si, val = ev
            if de is E and not E.is_dma:
                if E.name == "pe" or not raw:
                    return
            if need.get(si, 0) < val:
                need[si] = val

        for k in reads:
            b = self.buf(k)
            if b.w is not None:
                add(b.w, True)
        for k in writes:
            b = self.buf(k)
            if b.w is not None:
                add(b.w, True)
            for ev in b.r.values():
                add(ev, False)
            for ev in b.rd:
                add(ev, False)
        banks = set()
        for k in list(reads) + list(writes):
            bk = self.bank(k)
            if bk is not None:
                banks.add(bk)
        for bk in banks:
            b = self.buf(bk)
            if b.w is not None:
                add(b.w, False)
        if E.is_dma:
            slot = E.n % len(E.slots)
            gen = E.n // len(E.slots)
            si_own = E.slots[slot]
            if gen > 0 and need.get(si_own, 0) < 16 * gen:
                need[si_own] = 16 * gen
        for si, val in need.items():
            if E.waited.get(si, 0) >= val:
                continue
            E.be.wait_ge(self.sems[si], val)
            E.waited[si] = val
        ins = fn(E.be)
        if E.is_dma:
            ins.then_inc(self.sems[si_own], 16)
            ev = (E, si_own, 16 * (gen + 1))
            E.n += 1
        else:
            E.count += 1
            ins.then_inc(self.sems[E.sem], 1)
            ev = (E, E.sem, E.count)
        for k in reads:
            b = self.buf(k)
            if E.is_dma:
                b.rd.append(ev)
            else:
                b.r[E.name] = ev
        for k in writes:
            b = self.buf(k)
            b.w = ev
            b.r = {}
            b.rd = []
        for bk in banks:
            self.buf(bk).w = ev
        self.nops += 1
        return ev

    _BANKED = ("r_pb", "ab_pp", "x_pa", "f_pu", "c_pp", "pT", "py")

    def bank(self, k):
        if isinstance(k, tuple):
            if k[0] in self._BANKED:
                return ("BANK", k[0], k[1])
            if k[0] == "r_pTb":
                return ("BANK", "r_pTb")
        elif k == "x_pss":
            return ("BANK", "x_pss")
        return None

    def barrier(self):
        evs = []
        for E in self.engs.values():
            if E.is_dma:
                for i, si in enumerate(E.slots):
                    cnt = (E.n - i + len(E.slots) - 1) // len(E.slots) if E.n > i else 0
                    if cnt > 0:
                        evs.append((si, 16 * cnt))
            elif E.count > 0:
                evs.append((E.sem, E.count))
        for E in self.engs.values():
            for si, val in evs:
                if (not E.is_dma) and si == E.sem:
                    continue
                if E.waited.get(si, 0) >= val:
                    continue
                E.be.wait_ge(self.sems[si], val)
                E.waited[si] = val

    def finish(self):
        for E in self.engs.values():
            if E.is_dma:
                for i, si in enumerate(E.slots):
                    cnt = (E.n - i + len(E.slots) - 1) // len(E.slots) if E.n > i else 0
                    if cnt > 0 and E.waited.get(si, 0) < 16 * cnt:
                        E.be.wait_ge(self.sems[si], 16 * cnt)
                        E.waited[si] = 16 * cnt


class Prog:
    def __init__(self, S):
        self.S = S
        self.NT = S // 128
        self.nc = bass.Bass("TRN2", target_bir_lowering=False)
        self.dram = {}

    def din(self, name, shape, dt=F32):
        self.dram[name] = self.nc.dram_tensor(name, list(shape), dt, kind="ExternalInput").ap()
        return self.dram[name]

    def dout(self, name, shape, dt=F32):
        self.dram[name] = self.nc.dram_tensor(name, list(shape), dt, kind="ExternalOutput").ap()
        return self.dram[name]

    def dscr(self, name, shape, dt=F32):
        self.dram[name] = self.nc.dram_tensor(name, list(shape), dt, kind="Internal").ap()
        return self.dram[name]


_UID = [0]


def _sb(es, nc, name, shape, dt):
    _UID[0] += 1
    return es.enter_context(nc.sbuf_tensor(f"{name}_{_UID[0]}", list(shape), dt))


def _ps(es, nc, name, shape, dt):
    _UID[0] += 1
    return es.enter_context(nc.psum_tensor(f"{name}_{_UID[0]}", list(shape), dt))


class Builder:
    def __init__(self, S, Mm=256):
        self.S, self.NT, self.Mm = S, S // 128, Mm
        self.P = Prog(S)
        self.nc = self.P.nc

    def load_consts(self, c, es):
        nc = self.nc
        cm = self.P.dram["cmask"]
        self.cst = _sb(es, nc, "cst", [128, 8, 128], F32)
        c.op("sp", [], ["cst"], lambda e: e.dma_start(out=self.cst[:], in_=cm.rearrange("p (k f) -> p k f", k=8)))
        self.idb = _sb(es, nc, "idb", [128, 128], BF16)
        c.op("dve", ["cst"], ["idb"], lambda e: e.tensor_copy(out=self.idb[:], in_=self.cst[:, 0, :]))
        self.onesb = _sb(es, nc, "onesb", [128, 128], BF16)
        c.op("dve", ["cst"], ["onesb"], lambda e: e.tensor_copy(out=self.onesb[:], in_=self.cst[:, 5, :]))
        self.I = self.cst[:, 0, :]
        self.LE = self.cst[:, 1, :]
        self.GE = self.cst[:, 2, :]
        self.LT = self.cst[:, 3, :]
        self.GT = self.cst[:, 4, :]
        self.ONES = self.cst[:, 5, :]
        self.NLT = self.cst[:, 6, :]
        self.NGT = self.cst[:, 7, :]

    def load_bc(self, c, tile, key, src_row):
        c.op("sp", [], [key], lambda e: e.dma_start(out=tile, in_=src_row.partition_broadcast(128)))

    def rstd_from_ss(self, c, ss, key, n):
        c.op("dve", [key], [key], lambda e: e.tensor_scalar(out=ss, in0=ss, scalar1=1.0 / n, scalar2=EPS, op0=ALU.mult, op1=ALU.add))
        c.op("act", [key], [key], lambda e: e.activation(out=ss, in_=ss, func=AF.Sqrt))
        c.op("dve", [key], [key], lambda e: e.reciprocal(out=ss, in_=ss))

    def prenorm_tile(self, c, W, src_ap, wkey, wbc, dstT, dkey, slot, skey=None):
        h, sq, ss, xn, pT = W["h"][slot], W["sq"], W["ss"][slot], W["xn"][slot], W["pT"][slot]
        hk, ssk, xnk, pk = ("h", slot), ("ss", slot), ("xn", slot), ("pT", slot)
        c.op("sp", [skey] if skey else [], [hk], lambda e: e.dma_start(out=h[:], in_=src_ap))
        c.op("act", [hk], ["sq", ssk], lambda e: e.activation(out=sq[:], in_=h[:], func=AF.Square, accum_out=ss[:]))
        self.rstd_from_ss(c, ss[:], ssk, D)
        c.op("dve", [hk, ssk, wkey], [xnk], lambda e: e.scalar_tensor_tensor(out=xn[:], in0=h[:], scalar=ss[:], in1=wbc, op0=ALU.mult, op1=ALU.mult))

        def tr(e):
            for k in range(8):
                ins = e.transpose(out=pT[:, k * 128:(k + 1) * 128], in_=xn[:, k * 128:(k + 1) * 128], identity=self.idb[:])
            return ins
        c.op("pe", [xnk, "idb"], [pk], tr)
        c.op("act", [pk], [dkey], lambda e: e.copy(out=dstT, in_=pT[:].rearrange("p (k t) -> p k t", k=8)))

    def alloc_prenorm(self, es, tag="", sq=None):
        nc = self.nc
        W = {"h": [_sb(es, nc, f"pn_h{i}{tag}", [128, D], F32) for i in range(2)],
             "sq": sq if sq is not None else _sb(es, nc, f"pn_sq{tag}", [128, D], F32),
             "ss": [_sb(es, nc, f"pn_ss{i}{tag}", [128, 1], F32) for i in range(2)],
             "xn": [_sb(es, nc, f"pn_xn{i}{tag}", [128, D], BF16) for i in range(2)],
             "pT": [_ps(es, nc, f"pn_pT{i}{tag}", [128, D], BF16) for i in range(2)]}
        return W

    def outproj_tile(self, c, W, zT_fn, zkeys, KC, wo, wokey, wpost, wpkey, res_ap, out_ap, slot, rkey=None, okey=None):
        py = W["py"][slot % len(W["py"])]
        pk = ("py", slot % len(W["py"]))
        hr, ss2, t1 = W["hr"][slot], W["ss2"][slot], W["t1"][slot]
        hrk, s2k, t1k = ("hr", slot), ("ss2", slot), ("t1", slot)

        def mm(e):
            for nb in range(2):
                for kc in range(KC):
                    ins = e.matmul(py[nb][:], lhsT=zT_fn(kc), rhs=wo[:, kc, nb * 512:(nb + 1) * 512], start=(kc == 0), stop=(kc == KC - 1))
            return ins
        c.op("pe", list(zkeys) + [wokey], [pk], mm)
        c.op("sp", [rkey] if rkey else [], [hrk], lambda e: e.dma_start(out=hr[:], in_=res_ap))
        c.op("act", [pk], ["sq2", (s2k, 0)], lambda e: e.activation(out=W["sq2"][:, 0:512], in_=py[0][:], func=AF.Square, accum_out=ss2[:, 0:1]))
        c.op("act", [pk], ["sq2", (s2k, 1)], lambda e: e.activation(out=W["sq2"][:, 512:1024], in_=py[1][:], func=AF.Square, accum_out=ss2[:, 1:2]))
        c.op("dve", [(s2k, 0), (s2k, 1)], [s2k], lambda e: e.tensor_tensor(out=ss2[:, 2:3], in0=ss2[:, 0:1], in1=ss2[:, 1:2], op=ALU.add))
        self.rstd_from_ss(c, ss2[:, 2:3], s2k, D)
        for nb in range(2):
            c.op("dve", [pk, s2k, wpkey], [(t1k, nb)], lambda e, nb=nb: e.scalar_tensor_tensor(out=t1[:, nb * 512:(nb + 1) * 512], in0=py[nb][:], scalar=ss2[:, 2:3], in1=wpost[:, nb * 512:(nb + 1) * 512], op0=ALU.mult, op1=ALU.mult))
        c.op("pool", [(t1k, 0), (t1k, 1), hrk], [hrk], lambda e: e.tensor_tensor(out=hr[:], in0=hr[:], in1=t1[:], op=ALU.add))
        c.op("gq", [hrk], [okey] if okey else [], lambda e: e.dma_start(out=out_ap, in_=hr[:]))

    def alloc_outproj(self, es, tag="", npy=2):
        nc = self.nc
        return {"py": [[_ps(es, nc, f"op_py{i}{j}{tag}", [128, 512], F32) for j in range(2)] for i in range(npy)],
                "hr": [_sb(es, nc, f"op_hr{i}{tag}", [128, D], F32) for i in range(2)],
                "ss2": [_sb(es, nc, f"op_ss{i}{tag}", [128, 4], F32) for i in range(2)],
                "t1": [_sb(es, nc, f"op_t1{i}{tag}", [128, D], F32) for i in range(2)],
                "sq2": _sb(es, nc, f"op_sq2{tag}", [128, D], F32)}

    def load_w_bf16(self, c, dst, key, src3):
        KC = dst.shape[1]
        N = dst.shape[2]
        step = max(1, 2048 // N)
        step = min(KC, 4)
        for k0 in range(0, KC, step):
            k1 = min(KC, k0 + step)
            for n0 in range(0, N, 2048):
                n1 = min(N, n0 + 2048)
                c.op("gq", [], [key], lambda e, k0=k0, k1=k1, n0=n0, n1=n1: e.dma_start(out=dst[:, k0:k1, n0:n1], in_=src3[:, k0:k1, n0:n1]))

    def ffn(self, c, L, src, dst):
        nc, S, NT = self.nc, self.S, self.NT
        Dr = self.P.dram
        with ExitStack() as es:
            wup = _sb(es, nc, "f_wup", [128, 8, 4096], BF16)
            wdn = _sb(es, nc, "f_wdn", [128, 32, 1024], BF16)
            wpre = _sb(es, nc, "f_wpre", [128, D], F32)
            wpost = _sb(es, nc, "f_wpost", [128, D], F32)
            self.load_bc(c, wpre[:], "f_wpre", Dr["ffn_norm_pre"][L:L + 1, :])
            self.load_bc(c, wpost[:], "f_wpost", Dr["ffn_norm_post"][L:L + 1, :])
            self.load_w_bf16(c, wup, "f_wup", Dr["ffn_w_up"][L].rearrange("(k p) n -> p k n", p=128))
            self.load_w_bf16(c, wdn, "f_wdn", Dr["ffn_w_down"][L].rearrange("(k p) n -> p k n", p=128))
            OP = self.alloc_outproj(es, "f")
            PN = self.alloc_prenorm(es, "f", sq=OP["sq2"])
            TB = 256
            hnT = [_sb(es, nc, f"f_hnT{i}", [128, 8, TB], BF16) for i in range(2)]
            aT = _sb(es, nc, "f_aT", [128, 32, TB], BF16)
            rl = [_sb(es, nc, f"f_rl{i}", [128, TB], F32) for i in range(2)]
            pu = [_ps(es, nc, f"f_pu{i}", [128, 512], F32) for i in range(2)]
            nblk = S // TB
            tpb = TB // 128
            for b in range(nblk):
                hb = hnT[b % 2]
                for t in range(tpb):
                    tt = b * tpb + t
                    self.prenorm_tile(c, PN, src[tt * 128:(tt + 1) * 128, :], "f_wpre", wpre[:], hb[:, :, t * 128:(t + 1) * 128], ("f_hnT", b % 2, t), tt % 2, skey=(src.tensor.name, tt))
                hkeys = [("f_hnT", b % 2, t) for t in range(tpb)]
                for fc in range(32):
                    p = pu[fc % 2]

                    def mm(e, fc=fc, p=p):
                        for kc in range(8):
                            ins = e.matmul(p[:, 0:TB], lhsT=wup[:, kc, fc * 128:(fc + 1) * 128], rhs=hb[:, kc, :], start=(kc == 0), stop=(kc == 7))
                        return ins
                    c.op("pe", hkeys + ["f_wup"], [("f_pu", fc % 2)], mm)
                    r = rl[fc % 2]
                    c.op("act", [("f_pu", fc % 2)], [("f_rl", fc % 2)], lambda e, p=p, r=r: e.activation(out=r[:], in_=p[:, 0:TB], func=AF.Relu))
                    c.op("dve", [("f_rl", fc % 2)], [("f_aT", fc)], lambda e, r=r, fc=fc: e.tensor_tensor(out=aT[:, fc, :], in0=r[:], in1=r[:], op=ALU.mult))
                akeys = [("f_aT", fc) for fc in range(32)]
                for t in range(tpb):
                    tt = b * tpb + t
                    self.outproj_tile(c, OP, lambda kc, t=t: aT[:, kc, t * 128:(t + 1) * 128], akeys, 32, wdn, "f_wdn", wpost, "f_wpost",
                                      src[tt * 128:(tt + 1) * 128, :], dst[tt * 128:(tt + 1) * 128, :], tt % 2,
                                      rkey=(src.tensor.name, tt), okey=(dst.tensor.name, tt))
            c.barrier()

    def xa(self, c, L, src, dst):
        nc, S, NT, Mm = self.nc, self.S, self.NT, self.Mm
        Dr = self.P.dram
        MC = Mm // 128
        with ExitStack() as es:
            wq = _sb(es, nc, "x_wq", [128, 8, 1024], BF16)
            wo = _sb(es, nc, "x_wo", [128, 8, 1024], BF16)
            wpre = _sb(es, nc, "x_wpre", [128, D], F32)
            wpost = _sb(es, nc, "x_wpost", [128, D], F32)
            wmem = _sb(es, nc, "x_wmem", [128, D], F32)
            self.load_bc(c, wpre[:], "x_wpre", Dr["xa_norm_pre"][L:L + 1, :])
            self.load_bc(c, wpost[:], "x_wpost", Dr["xa_norm_post"][L:L + 1, :])
            self.load_bc(c, wmem[:], "x_wmem", Dr["xa_mem_norm"][L:L + 1, :])
            self.load_w_bf16(c, wq, "x_wq", Dr["xa_w_q"][L].rearrange("(k p) n -> p k n", p=128))
            self.load_w_bf16(c, wo, "x_wo", Dr["xa_w_o"][L].rearrange("(k p) n -> p k n", p=128))
            PN = self.alloc_prenorm(es, "x")
            OP = self.alloc_outproj(es, "x", npy=1)
            kT = _sb(es, nc, "x_kT", [128, 8, Mm], BF16)
            V = _sb(es, nc, "x_V", [128, MC, 1024], BF16)
            pa = [_ps(es, nc, f"x_pa{i}", [128, 512], F32) for i in range(2)]
            with ExitStack() as es2:
                wkv = _sb(es2, nc, "x_wkv", [128, 8, 2048], BF16)
                self.load_w_bf16(c, wkv, "x_wkv", Dr["xa_w_kv"][L].rearrange("(k p) n -> p k n", p=128))
                mnT = _sb(es2, nc, "x_mnT", [128, 8, Mm], BF16)
                for mc in range(MC):
                    self.prenorm_tile(c, PN, Dr["mem"][mc * 128:(mc + 1) * 128, :], "x_wmem", wmem[:], mnT[:, :, mc * 128:(mc + 1) * 128], ("x_mnT", mc), mc % 2)
                mkeys = [("x_mnT", mc) for mc in range(MC)]
                for ch in range(8):
                    p = pa[ch % 2]

                    def mm(e, ch=ch, p=p):
                        for kc in range(8):
                            ins = e.matmul(p[:, 0:Mm], lhsT=wkv[:, kc, ch * 128:(ch + 1) * 128], rhs=mnT[:, kc, :], start=(kc == 0), stop=(kc == 7))
                        return ins
                    c.op("pe", mkeys + ["x_wkv"], [("x_pa", ch % 2)], mm)
                    c.op("act", [("x_pa", ch % 2)], ["x_kT"], lambda e, ch=ch, p=p: e.copy(out=kT[:, ch, :], in_=p[:, 0:Mm]))
                for mc in range(MC):
                    for nb in range(2):
                        p = pa[nb]

                        def mm(e, mc=mc, nb=nb, p=p):
                            for kc in range(8):
                                ins = e.matmul(p[:], lhsT=mnT[:, kc, mc * 128:(mc + 1) * 128], rhs=wkv[:, kc, 1024 + nb * 512:1024 + (nb + 1) * 512], start=(kc == 0), stop=(kc == 7))
                            return ins
                        c.op("pe", mkeys + ["x_wkv"], [("x_pa", nb)], mm)
                        c.op("act", [("x_pa", nb)], ["x_V"], lambda e, mc=mc, nb=nb, p=p: e.copy(out=V[:, mc, nb * 512:(nb + 1) * 512], in_=p[:]))
                c.barrier()
            hnT = [_sb(es, nc, f"x_hnT{i}", [128, 8, 128], BF16) for i in range(2)]
            qT = [_sb(es, nc, f"x_qT{i}", [128, 8, 128], BF16) for i in range(2)]
            oT = [_sb(es, nc, f"x_oT{i}", [128, 8, 128], BF16) for i in range(2)]
            sc = [_sb(es, nc, f"x_sc{i}", [128, Mm], F32) for i in range(2)]
            pb = [_sb(es, nc, f"x_pb{i}", [128, Mm], BF16) for i in range(2)]
            pTs = [_sb(es, nc, f"x_pTs{i}", [128, MC, 128], BF16) for i in range(2)]
            st = [_sb(es, nc, f"x_st{i}", [128, 4], F32) for i in range(2)]
            ps_s = [_ps(es, nc, f"x_pss{i}", [128, 512], F32) for i in range(1)]
            scale = 256.0 ** -0.5
            it = 0
            for tt in range(NT):
                sl = tt % 2
                self.prenorm_tile(c, PN, src[tt * 128:(tt + 1) * 128, :], "x_wpre", wpre[:], hnT[sl][:], ("x_hnT", sl), sl, skey=(src.tensor.name, tt))
                for half in range(2):
                    p = pa[half]

                    def mm(e, half=half, p=p, sl=sl):
                        for cc in range(4):
                            ch = half * 4 + cc
                            for kc in range(8):
                                ins = e.matmul(p[:, cc * 128:(cc + 1) * 128], lhsT=wq[:, kc, ch * 128:(ch + 1) * 128], rhs=hnT[sl][:, kc, :], start=(kc == 0), stop=(kc == 7))
                        return ins
                    c.op("pe", [("x_hnT", sl), "x_wq"], [("x_pa", half)], mm)
                    c.op("act", [("x_pa", half)], [("x_qT", sl, half)], lambda e, half=half, p=p, sl=sl: e.copy(out=qT[sl][:, half * 4:(half + 1) * 4, :], in_=p[:].rearrange("p (k t) -> p k t", k=4)))
                for hd in range(4):
                    i2 = it % 2
                    it += 1
                    pss = ps_s[0]

                    def mm(e, hd=hd, sl=sl, pss=pss):
                        for k2 in range(2):
                            ins = e.matmul(pss[:, 0:Mm], lhsT=qT[sl][:, hd * 2 + k2, :], rhs=kT[:, hd * 2 + k2, :], start=(k2 == 0), stop=(k2 == 1))
                        return ins
                    c.op("pe", [("x_qT", sl, hd // 2), "x_kT"], ["x_pss"], mm)
                    s_, sc_, pb_, pT_ = st[i2], sc[i2], pb[i2], pTs[i2]
                    sk, sck, pbk, pTk = ("x_st", i2), ("x_sc", i2), ("x_pb", i2), ("x_pTs", i2)
                    c.op("dve", ["x_pss"], [sk], lambda e, s_=s_, pss=pss: e.tensor_reduce(out=s_[:, 0:1], in_=pss[:, 0:Mm], axis=AX.X, op=ALU.max))
                    c.op("dve", [sk], [sk], lambda e, s_=s_: e.tensor_scalar(out=s_[:, 1:2], in0=s_[:, 0:1], scalar1=-scale, scalar2=None, op0=ALU.mult))
                    c.op("act", ["x_pss", sk], [sck, (sk, "sum")], lambda e, s_=s_, sc_=sc_, pss=pss: e.activation(out=sc_[:], in_=pss[:, 0:Mm], func=AF.Exp, bias=s_[:, 1:2], scale=scale, accum_out=s_[:, 2:3]))
                    c.op("dve", [(sk, "sum")], [(sk, "sum")], lambda e, s_=s_: e.reciprocal(out=s_[:, 3:4], in_=s_[:, 2:3]))
                    c.op("dve", [sck, (sk, "sum")], [pbk], lambda e, s_=s_, sc_=sc_, pb_=pb_: e.tensor_scalar(out=pb_[:], in0=sc_[:], scalar1=s_[:, 3:4], scalar2=None, op0=ALU.mult))
                    ptp = PN["pT"][i2]

                    def tr(e, pb_=pb_, ptp=ptp):
                        for mc in range(MC):
                            ins = e.transpose(out=ptp[:, mc * 128:(mc + 1) * 128], in_=pb_[:, mc * 128:(mc + 1) * 128], identity=self.idb[:])
                        return ins
                    c.op("pe", [pbk, "idb"], [("pT", i2)], tr)
                    c.op("act", [("pT", i2)], [pTk], lambda e, pT_=pT_, ptp=ptp: e.copy(out=pT_[:], in_=ptp[:, 0:Mm].rearrange("p (k t) -> p k t", k=MC)))
                    po = pa[hd % 2]

                    def mm2(e, hd=hd, pT_=pT_, po=po):
                        for d2 in range(2):
                            for mc in range(MC):
                                ins = e.matmul(po[:, d2 * 128:(d2 + 1) * 128], lhsT=V[:, mc, hd * 256 + d2 * 128: hd * 256 + (d2 + 1) * 128], rhs=pT_[:, mc, :], start=(mc == 0), stop=(mc == MC - 1))
                        return ins
                    c.op("pe", [pTk, "x_V"], [("x_pa", hd % 2)], mm2)
                    c.op("act", [("x_pa", hd % 2)], [("x_oT", sl, hd)], lambda e, hd=hd, po=po, sl=sl: e.copy(out=oT[sl][:, hd * 2:hd * 2 + 2, :], in_=po[:, 0:256].rearrange("p (k t) -> p k t", k=2)))
                okeys = [("x_oT", sl, hd) for hd in range(4)]
                self.outproj_tile(c, OP, lambda kc, sl=sl: oT[sl][:, kc, :], okeys, 8, wo, "x_wo", wpost, "x_wpost",
                                  src[tt * 128:(tt + 1) * 128, :], dst[tt * 128:(tt + 1) * 128, :], sl,
                                  rkey=(src.tensor.name, tt), okey=(dst.tensor.name, tt))
            c.barrier()

    def load_cols(self, c, dst, key, vec, n):
        for k in range(n):
            c.op("sp", [], [key], lambda e, k=k: e.dma_start(out=dst[:, k:k + 1], in_=vec[k * 128:(k + 1) * 128].rearrange("(p o) -> p o", o=1)))

    def stage_a_full(self, c, es, hnT, hkey, wrow, src, tag):
        nc = self.nc
        with ExitStack() as es2:
            PN = self.alloc_prenorm(es2, tag)
            wpre = _sb(es2, nc, f"{tag}_wpre", [128, D], F32)
            self.load_bc(c, wpre[:], f"{tag}_wpre", wrow)
            for tt in range(self.NT):
                self.prenorm_tile(c, PN, src[tt * 128:(tt + 1) * 128, :], f"{tag}_wpre", wpre[:], hnT[:, :, tt * 128:(tt + 1) * 128], (hkey, tt), tt % 2, skey=(src.tensor.name, tt))
            c.barrier()

    def stage_c_scr(self, c, L, zscr, KC, wout_ap, wpost_row, src, dst, tag):
        nc = self.nc
        with ExitStack() as es:
            wo = _sb(es, nc, f"{tag}_wo", [128, KC, 1024], BF16)
            wpost = _sb(es, nc, f"{tag}_wpost", [128, D], F32)
            self.load_bc(c, wpost[:], f"{tag}_wpost", wpost_row)
            self.load_w_bf16(c, wo, f"{tag}_wo", wout_ap.rearrange("(k p) n -> p k n", p=128))
            OP = self.alloc_outproj(es, tag)
            zt = [_sb(es, nc, f"{tag}_zt{i}", [128, KC, 512], BF16) for i in range(2)]
            for b in range(self.S // 512):
                z = zt[b % 2]
                c.op("sp", [(zscr.tensor.name, b)], [(f"{tag}_zt", b % 2)], lambda e, z=z, b=b: e.dma_start(out=z[:], in_=zscr[:, :, b * 512:(b + 1) * 512].rearrange("k p t -> p k t")))
                for t in range(4):
                    tt = b * 4 + t
                    self.outproj_tile(c, OP, lambda kc, z=z, t=t: z[:, kc, t * 128:(t + 1) * 128], [(f"{tag}_zt", b % 2)], KC, wo, f"{tag}_wo", wpost, f"{tag}_wpost",
                                      src[tt * 128:(tt + 1) * 128, :], dst[tt * 128:(tt + 1) * 128, :], tt % 2,
                                      rkey=(src.tensor.name, tt), okey=(dst.tensor.name, tt))
            c.barrier()

    def mixer_c(self, c, L, src, dst):
        nc, S, NT = self.nc, self.S, self.NT
        Dr = self.P.dram
        j = L // 2
        NB = S // 512
        yscr = self.P.dram.get("c_yscr")
        if yscr is None:
            yscr = self.P.dscr("c_yscr", [8, 128, S], BF16)
        with ExitStack() as es:
            hnT = _sb(es, nc, "c_hnT", [128, 8, S], BF16)
            self.stage_a_full(c, es, hnT, "c_hnT", Dr["mix_norm_pre"][L:L + 1, :], src, "c")
            hkeys = [("c_hnT", tt) for tt in range(NT)]
            cw = _sb(es, nc, "c_cw", [128, 4, 8], F32)
            for tap in range(4):
                self.load_cols(c, cw[:, tap, :], "c_cw", Dr["c_conv_w"][j, tap], 8)
            cb = _sb(es, nc, "c_cb", [128, 8], F32)
            self.load_cols(c, cb, "c_cb", Dr["c_conv_b"][j], 8)
            ba = _sb(es, nc, "c_ba", [128, 2, 8], F32)
            bx = _sb(es, nc, "c_bx", [128, 2, 8], F32)
            c1 = _sb(es, nc, "c_c1", [128, 2, 8], F32)
            for d_ in range(2):
                self.load_cols(c, ba[:, d_, :], "c_ba", Dr["c_b_a"][j, d_], 8)
                self.load_cols(c, bx[:, d_, :], "c_bx", Dr["c_b_x"][j, d_], 8)
                self.load_cols(c, c1[:, d_, :], "c_c1", Dr["c_lambda"][j, d_], 8)
            c.op("act", ["c_c1"], ["c_c1"], lambda e: e.activation(out=c1[:], in_=c1[:], func=AF.Exp, scale=-1.0))
            c.op("act", ["c_c1"], ["c_c1"], lambda e: e.activation(out=c1[:], in_=c1[:], func=AF.Ln, bias=1.0))
            c.op("dve", ["c_c1"], ["c_c1"], lambda e: e.tensor_scalar(out=c1[:], in0=c1[:], scalar1=-8.0, scalar2=None, op0=ALU.mult))
            xbf = _sb(es, nc, "c_xbf", [128, S + 3], F32)
            u = [_sb(es, nc, f"c_u{i}", [128, S], F32) for i in range(2)]
            ub = [_sb(es, nc, f"c_ub{i}", [128, S], BF16) for i in range(2)]
            af = _sb(es, nc, "c_af", [128, S], F32)
            inp = _sb(es, nc, "c_inp", [128, S], F32)
            acc = _sb(es, nc, "c_acc", [128, S], F32)
            win = [_sb(es, nc, f"c_win{i}", [128, 8, 128], BF16) for i in range(2)]
            wa = [_sb(es, nc, f"c_wa{i}", [128, 2, 128], BF16) for i in range(2)]
            wx = [_sb(es, nc, f"c_wx{i}", [128, 2, 128], BF16) for i in range(2)]
            tmp = {n: [_sb(es, nc, f"c_{n}{i}", [128, 512], F32) for i in range(2)] for n in ("r", "i", "mu")}
            yst = [_sb(es, nc, f"c_yst{i}", [128, 512], BF16) for i in range(2)]
            pp = [_ps(es, nc, f"c_pp{i}", [128, 512], F32) for i in range(4)]
            win_n = 0
            wg_n = 0
            pn = 0

            def inproj(col0, dst_fn, dkeys_fn):
                nonlocal win_n, pn
                w = win[win_n % 2]
                wk = ("c_win", win_n % 2)
                win_n += 1
                src3 = Dr["c_w_in"][j].rearrange("(k p) n -> p k n", p=128)[:, :, col0:col0 + 128]
                self.load_w_bf16(c, w, wk, src3)
                for tb in range(NB):
                    p = pp[pn % 2]
                    pk = ("c_pp", pn % 2)
                    pn += 1

                    def mm(e, p=p, w=w, tb=tb):
                        for kc in range(8):
                            ins = e.matmul(p[:], lhsT=w[:, kc, :], rhs=hnT[:, kc, tb * 512:(tb + 1) * 512], start=(kc == 0), stop=(kc == 7))
                        return ins
                    c.op("pe", hkeys[tb * 4:(tb + 1) * 4] + [wk], [pk], mm)
                    dst_fn(tb, p, pk)

            for blk in range(4):
                for c2 in range(2):
                    ch = blk * 2 + c2
                    c.op("pool", [], ["c_xbf_h"], lambda e: e.memset(xbf[:, 0:2], 0.0))
                    c.op("pool", [], ["c_xbf_h"], lambda e: e.memset(xbf[:, S + 2:S + 3], 0.0))

                    def ev(tb, p, pk):
                        c.op("act", [pk], [("c_xbf", tb)], lambda e: e.copy(out=xbf[:, 2 + tb * 512:2 + (tb + 1) * 512], in_=p[:]))
                    c.op("pool", ["c_hs"], ["c_hs"] + [("c_xbf", tb) for tb in range(NB)], lambda e: e.memset(xbf[:, 0:1], 0.0))
                    inproj(1024 + ch * 128, ev, None)
                    xk = [("c_xbf", tb) for tb in range(NB)] + ["c_xbf_h"]
                    uu = u[c2]
                    uk = ("c_u", c2)
                    c.op("dve", xk + ["c_cw", "c_cb"], [uk], lambda e, uu=uu, ch=ch: e.tensor_scalar(out=uu[:], in0=xbf[:, 0:S], scalar1=cw[:, 0, ch:ch + 1], scalar2=cb[:, ch:ch + 1], op0=ALU.mult, op1=ALU.add))
                    for tap in range(1, 4):
                        c.op("dve", xk + [uk], [uk], lambda e, uu=uu, ch=ch, tap=tap: e.scalar_tensor_tensor(out=uu[:], in0=xbf[:, tap:tap + S], scalar=cw[:, tap, ch:ch + 1], in1=uu[:], op0=ALU.mult, op1=ALU.add))
                    c.op("pool", [uk], [("c_ub", c2)], lambda e, uu=uu, c2=c2: e.tensor_copy(out=ub[c2][:], in_=uu[:]))
                    c.op("pool", xk, ["c_hs"], lambda e: e.memset(xbf[:, 0:1], 0.0))
                ubk = [("c_ub", 0), ("c_ub", 1)]
                for jc in range(2):
                    ch = blk * 2 + jc
                    for dr in range(2):
                        wa_, wx_ = wa[wg_n % 2], wx[wg_n % 2]
                        wak, wxk = ("c_wa", wg_n % 2), ("c_wx", wg_n % 2)
                        wg_n += 1
                        self.load_w_bf16(c, wa_, wak, Dr["c_w_a"][j, dr, blk].rearrange("(k p) n -> p k n", p=128)[:, :, jc * 128:(jc + 1) * 128])
                        self.load_w_bf16(c, wx_, wxk, Dr["c_w_x"][j, dr, blk].rearrange("(k p) n -> p k n", p=128)[:, :, jc * 128:(jc + 1) * 128])
                        for tb in range(NB):
                            sl = tb % 2
                            pa_, px_ = pp[2], pp[3]
                            ts = slice(tb * 512, (tb + 1) * 512)

                            def mm(e, w_=wa_, p_=pa_, ts=ts):
                                for kc in range(2):
                                    ins = e.matmul(p_[:], lhsT=w_[:, kc, :], rhs=ub[kc][:, ts], start=(kc == 0), stop=(kc == 1))
                                return ins
                            c.op("pe", ubk + [wak], [("c_pp", 2)], mm)

                            def mm2(e, w_=wx_, p_=px_, ts=ts):
                                for kc in range(2):
                                    ins = e.matmul(p_[:], lhsT=w_[:, kc, :], rhs=ub[kc][:, ts], start=(kc == 0), stop=(kc == 1))
                                return ins
                            c.op("pe", ubk + [wxk], [("c_pp", 3)], mm2)
                            r_, i_, mu_ = tmp["r"][sl], tmp["i"][sl], tmp["mu"][sl]
                            c.op("act", [("c_pp", 2), "c_ba"], [("c_r", sl)], lambda e, r_=r_, pa_=pa_, dr=dr, ch=ch: e.activation(out=r_[:], in_=pa_[:], func=AF.Sigmoid, bias=ba[:, dr, ch:ch + 1]))
                            c.op("act", [("c_pp", 3), "c_bx"], [("c_i", sl)], lambda e, i_=i_, px_=px_, dr=dr, ch=ch: e.activation(out=i_[:], in_=px_[:], func=AF.Sigmoid, bias=bx[:, dr, ch:ch + 1]))
                            c.op("act", [("c_r", sl), "c_c1"], [("c_af", tb)], lambda e, r_=r_, ts=ts, dr=dr, ch=ch: e.activation(out=af[:, ts], in_=r_[:], func=AF.Exp, scale=c1[:, dr, ch:ch + 1]))
                            c.op("dve", [("c_af", tb)], [("c_mu", sl)], lambda e, mu_=mu_, ts=ts: e.tensor_tensor(out=mu_[:], in0=af[:, ts], in1=af[:, ts], op=ALU.mult))
                            c.op("act", [("c_mu", sl)], [("c_mu", sl)], lambda e, mu_=mu_: e.activation(out=mu_[:], in_=mu_[:], func=AF.Sqrt, scale=-1.0, bias=1.0))
                            c.op("dve", [("c_mu", sl), ("c_i", sl)], [("c_mu", sl)], lambda e, mu_=mu_, i_=i_: e.tensor_tensor(out=mu_[:], in0=mu_[:], in1=i_[:], op=ALU.mult))
                            c.op("dve", [("c_mu", sl), ("c_u", jc)], [("c_inp", tb)], lambda e, mu_=mu_, ts=ts, jc=jc: e.tensor_tensor(out=inp[:, ts], in0=mu_[:], in1=u[jc][:, ts], op=ALU.mult))
                        afk = [("c_af", tb) for tb in range(NB)]
                        ink = [("c_inp", tb) for tb in range(NB)]
                        if dr == 0:
                            c.op("dve", afk + ink, ["c_acc"], lambda e: e.tensor_tensor_scan(out=acc[:], data0=af[:], data1=inp[:], initial=0.0, op0=ALU.mult, op1=ALU.add))
                        else:
                            c.op("dve", afk + ink, ["c_hs"], lambda e: e.tensor_tensor_scan(out=xbf[:, 0:S][:, ::-1], data0=af[:, ::-1], data1=inp[:, ::-1], initial=0.0, op0=ALU.mult, op1=ALU.add))
                            c.op("pool", ["c_hs", "c_acc"], ["c_acc"], lambda e: e.tensor_tensor(out=acc[:], in0=acc[:], in1=xbf[:, 0:S], op=ALU.add))

                    def evg(tb, p, pk, ch=ch):
                        sl = tb % 2
                        g1, g2, ys = tmp["r"][sl], tmp["i"][sl], yst[sl]
                        ts = slice(tb * 512, (tb + 1) * 512)
                        c.op("act", [pk], [("c_r", sl)], lambda e: e.activation(out=g1[:], in_=p[:], func=AF.Square))
                        c.op("dve", [("c_r", sl)], [("c_r", sl)], lambda e: e.tensor_scalar(out=g1[:], in0=g1[:], scalar1=0.044715, scalar2=1.0, op0=ALU.mult, op1=ALU.add))
                        c.op("dve", [("c_r", sl), pk], [("c_r", sl)], lambda e: e.tensor_tensor(out=g1[:], in0=g1[:], in1=p[:], op=ALU.mult))
                        c.op("act", [("c_r", sl)], [("c_i", sl)], lambda e: e.activation(out=g2[:], in_=g1[:], func=AF.Sigmoid, scale=1.5957691216057308))
                        c.op("dve", [("c_i", sl), pk], [("c_i", sl)], lambda e: e.tensor_tensor(out=g2[:], in0=g2[:], in1=p[:], op=ALU.mult))
                        c.op("dve", [("c_i", sl), "c_acc"], [("c_yst", sl)], lambda e: e.tensor_tensor(out=ys[:], in0=g2[:], in1=acc[:, ts], op=ALU.mult))
                        c.op("gq", [("c_yst", sl)], [("c_yscr", tb)], lambda e: e.dma_start(out=yscr[ch, :, ts], in_=ys[:]))
                    inproj(ch * 128, evg, None)
            c.barrier()
        self.stage_c_scr(c, L, yscr, 8, Dr["c_w_out"][j], Dr["mix_norm_post"][L:L + 1, :], src, dst, "cc")

    def mixer_ab(self, c, L, src, dst):
        nc, S, NT = self.nc, self.S, self.NT
        Dr = self.P.dram
        j = L // 2
        NB = S // 512
        P = self.P
        FM = P.dram.get("ab_fm") or P.dscr("ab_fm", [16, 128, S], F32)
        TM = {n: (P.dram.get("ab_" + n) or P.dscr("ab_" + n, [S, 512], F32)) for n in ("dnk", "dnv", "dnz", "mlk", "mlv", "mlo")}
        mixscr = P.dram.get("ab_mix") or P.dscr("ab_mix", [8, 128, S], BF16)
        I, ONES = self.I, self.ONES
        dirs = [dict(INC=self.LE, AFT=self.GT, STRICT=self.GT, INCLji=self.LE, NEGM=self.NLT),
                dict(INC=self.GE, AFT=self.LT, STRICT=self.LT, INCLji=self.GE, NEGM=self.NGT)]
        with ExitStack() as eo:
            gates = _sb(eo, nc, "ab_gates", [128, NT, 32], F32)
            Gg = _sb(eo, nc, "ab_Gg", [128, NT, 8], F32)
            Bt = _sb(eo, nc, "ab_Bt", [128, NT, 8], F32)
            nBt = _sb(eo, nc, "ab_nBt", [128, NT, 8], F32)
            Li = _sb(eo, nc, "ab_Li", [128, NT, 8], F32)
            Lf = _sb(eo, nc, "ab_Lf", [128, NT, 8], F32)
            Edn = _sb(eo, nc, "ab_Edn", [128, NT, 24], F32)
            Mlt = _sb(eo, nc, "ab_Mlt", [128, NT, 24], F32)
            Bk = _sb(eo, nc, "ab_Bk", [128, NT, 8], F32)
            Ws = _sb(eo, nc, "ab_Ws", [128, NT, 8], F32)
            prm = _sb(eo, nc, "ab_prm", [128, 4, 8], F32)
            dnw = _sb(eo, nc, "ab_dnw", [128, 128], F32)
            for i_, nm in enumerate(("dn_a_log", "dn_dt_bias", "ml_i_bias", "ml_f_bias")):
                self.load_bc(c, prm[:, i_, :], "ab_prm", Dr[nm][j:j + 1].rearrange("o d h -> o (d h)"))
            self.load_bc(c, dnw[:], "ab_dnw", Dr["dn_out_norm"][j:j + 1, :])
            with ExitStack() as es:
                hnT = _sb(es, nc, "ab_hnT", [128, 8, S], BF16)
                self.stage_a_full(c, es, hnT, "ab_hnT", Dr["mix_norm_pre"][L:L + 1, :], src, "ab")
                hkeys = [("ab_hnT", tt) for tt in range(NT)]
                win3 = Dr["ab_w_in"][j].rearrange("(k p) n -> p k n", p=128)
                wg = _sb(es, nc, "ab_wg", [128, 8, 32], BF16)
                c.op("gq", [], ["ab_wg"], lambda e: e.dma_start(out=wg[:, :, 0:16], in_=win3[:, :, 2048:2064]))
                c.op("gq", [], ["ab_wg"], lambda e: e.dma_start(out=wg[:, :, 16:32], in_=win3[:, :, 4112:4128]))
                pp = [_ps(es, nc, f"ab_pp{i}", [128, 512], F32) for i in range(4)]
                for tt in range(NT):
                    p = pp[tt % 2]

                    def mm(e, p=p, tt=tt):
                        for kc in range(8):
                            ins = e.matmul(p[:, 0:32], lhsT=hnT[:, kc, tt * 128:(tt + 1) * 128], rhs=wg[:, kc, :], start=(kc == 0), stop=(kc == 7))
                        return ins
                    c.op("pe", [hkeys[tt], "ab_wg"], [("ab_pp", tt % 2)], mm)
                    c.op("act", [("ab_pp", tt % 2)], ["ab_gates"], lambda e, p=p, tt=tt: e.copy(out=gates[:, tt, :], in_=p[:, 0:32]))
                def bc(i_):
                    return prm[:, i_, :].unsqueeze(1).broadcast_to([128, NT, 8])
                c.op("dve", ["ab_gates", "ab_prm"], ["ab_Gg"], lambda e: e.tensor_tensor(out=Gg[:], in0=gates[:, :, 0:8], in1=bc(1), op=ALU.add))
                c.op("act", ["ab_Gg"], ["ab_Gg"], lambda e: e.activation(out=Gg[:], in_=Gg[:], func=AF.Exp))
                c.op("act", ["ab_Gg"], ["ab_Gg"], lambda e: e.activation(out=Gg[:], in_=Gg[:], func=AF.Ln, bias=1.0))
                c.op("act", ["ab_prm"], ["ab_prm0"], lambda e: e.activation(out=prm[:, 0, :], in_=prm[:, 0, :], func=AF.Exp))
                c.op("dve", ["ab_Gg", "ab_prm0"], ["ab_Gg"], lambda e: e.scalar_tensor_tensor(out=Gg[:], in0=Gg[:], scalar=-1.0, in1=bc(0), op0=ALU.mult, op1=ALU.mult))
                c.op("act", ["ab_gates"], ["ab_Bt"], lambda e: e.activation(out=Bt[:], in_=gates[:, :, 8:16], func=AF.Sigmoid))
                c.op("dve", ["ab_Bt"], ["ab_nBt"], lambda e: e.tensor_scalar(out=nBt[:], in0=Bt[:], scalar1=-1.0, scalar2=None, op0=ALU.mult))
                c.op("dve", ["ab_gates", "ab_prm"], ["ab_Li"], lambda e: e.tensor_tensor(out=Li[:], in0=gates[:, :, 16:24], in1=bc(2), op=ALU.add))
                c.op("dve", ["ab_gates", "ab_prm"], ["ab_Lf"], lambda e: e.tensor_tensor(out=Lf[:], in0=gates[:, :, 24:32], in1=bc(3), op=ALU.add))
                c.op("act", ["ab_Lf"], ["ab_Lf"], lambda e: e.activation(out=Lf[:], in_=Lf[:], func=AF.Exp, scale=-1.0))
                c.op("act", ["ab_Lf"], ["ab_Lf"], lambda e: e.activation(out=Lf[:], in_=Lf[:], func=AF.Ln, bias=1.0))
                c.op("dve", ["ab_Lf"], ["ab_Lf"], lambda e: e.tensor_scalar(out=Lf[:], in0=Lf[:], scalar1=-1.0, scalar2=None, op0=ALU.mult))
                LE, GE, LT, GT = self.LE, self.GE, self.LT, self.GT
                for tt in range(NT):
                    p = pp[2 + tt % 2]

                    def mm(e, p=p, tt=tt):
                        for o_, T_ in ((0, Gg), (24, Lf)):
                            e.matmul(p[:, o_ + 0:o_ + 4], lhsT=LE, rhs=T_[:, tt, 0:4], start=True, stop=True)
                            e.matmul(p[:, o_ + 4:o_ + 8], lhsT=GE, rhs=T_[:, tt, 4:8], start=True, stop=True)
                            e.matmul(p[:, o_ + 8:o_ + 12], lhsT=GT, rhs=T_[:, tt, 0:4], start=True, stop=True)
                            e.matmul(p[:, o_ + 12:o_ + 16], lhsT=LT, rhs=T_[:, tt, 4:8], start=True, stop=True)
                            ins = e.matmul(p[:, o_ + 16:o_ + 24], lhsT=ONES, rhs=T_[:, tt, 0:8], start=True, stop=True)
                        return ins
                    c.op("pe", ["ab_Gg", "ab_Lf", "cst"], [("ab_pp", 2 + tt % 2)], mm)
                    c.op("act", [("ab_pp", 2 + tt % 2)], ["ab_Edn"], lambda e, p=p, tt=tt: e.activation(out=Edn[:, tt, :], in_=p[:, 0:24], func=AF.Exp))
                    c.op("act", [("ab_pp", 2 + tt % 2)], ["ab_Mlt"], lambda e, p=p, tt=tt: e.copy(out=Mlt[:, tt, :], in_=p[:, 24:48]))
                c.op("dve", ["ab_Bt", "ab_Edn"], ["ab_Bk"], lambda e: e.tensor_tensor(out=Bk[:], in0=Bt[:], in1=Edn[:, :, 0:8], op=ALU.mult))
                c.op("dve", ["ab_Li", "ab_Mlt"], ["ab_Ws"], lambda e: e.tensor_tensor(out=Ws[:], in0=Li[:], in1=Mlt[:, :, 8:16], op=ALU.add))
                xbf = _sb(es, nc, "ab_xbf", [128, S + 3], F32)
                uu = _sb(es, nc, "ab_u", [128, S], F32)
                cwt = [_sb(es, nc, f"ab_cwt{i}", [128, 4], F32) for i in range(2)]
                win = [_sb(es, nc, f"ab_win{i}", [128, 8, 128], BF16) for i in range(2)]
                sqb = [_sb(es, nc, f"ab_sqb{i}", [128, 512], F32) for i in range(2)]
                rsb = [_sb(es, nc, f"ab_rsb{i}", [128, 512], F32) for i in range(2)]
                stg = [_sb(es, nc, f"ab_stg{i}", [128, 4, 128], F32) for i in range(2)]
                c.op("pool", [], ["ab_xbf_h"], lambda e: e.memset(xbf[:, 0:2], 0.0))
                c.op("pool", [], ["ab_xbf_h"], lambda e: e.memset(xbf[:, S + 2:S + 3], 0.0))
                specs = []
                for h in range(4):
                    specs.append(dict(col=h * 128, conv=("dn_conv_w", h * 128), act=AF.Silu, l2=True, scale=128.0 ** -0.5, fm=h, tm=None))
                for h in range(4):
                    specs.append(dict(col=512 + h * 128, conv=("dn_conv_w", 512 + h * 128), act=AF.Silu, l2=True, scale=1.0, fm=4 + h, tm=("dnk", h)))
                for h in range(4):
                    specs.append(dict(col=1024 + h * 128, conv=("dn_conv_w", 1024 + h * 128), act=AF.Silu, l2=False, scale=None, fm=None, tm=("dnv", h)))
                for h in range(4):
                    specs.append(dict(col=1536 + h * 128, conv=None, act=AF.Silu, l2=False, scale=None, fm=None, tm=("dnz", h)))
                for h in range(4):
                    specs.append(dict(col=2064 + h * 128, conv=("ml_conv_w", h * 128), act=AF.Silu, l2=False, scale=None, fm=8 + h, tm=None))
                for h in range(4):
                    specs.append(dict(col=2576 + h * 128, conv=("ml_conv_w", 512 + h * 128), act=AF.Silu, l2=False, scale=128.0 ** -0.5, fm=12 + h, tm=("mlk", h)))
                for h in range(4):
                    specs.append(dict(col=3088 + h * 128, conv=None, act=None, l2=False, scale=None, fm=None, tm=("mlv", h)))
                for h in range(4):
                    specs.append(dict(col=3600 + h * 128, conv=None, act=AF.Sigmoid, l2=False, scale=None, fm=None, tm=("mlo", h)))
                pn = 0
                for si, sp in enumerate(specs):
                    w = win[si % 2]
                    wk = ("ab_win", si % 2)
                    self.load_w_bf16(c, w, wk, win3[:, :, sp["col"]:sp["col"] + 128])
                    xk = [("ab_xbf", tb) for tb in range(NB)]
                    for tb in range(NB):
                        p = pp[pn % 2]
                        pk = ("ab_pp", pn % 2)
                        pn += 1

                        def mm(e, p=p, w=w, tb=tb):
                            for kc in range(8):
                                ins = e.matmul(p[:], lhsT=w[:, kc, :], rhs=hnT[:, kc, tb * 512:(tb + 1) * 512], start=(kc == 0), stop=(kc == 7))
                            return ins
                        c.op("pe", hkeys[tb * 4:(tb + 1) * 4] + [wk], [pk], mm)
                        c.op("act", [pk], [("ab_xbf", tb)], lambda e, p=p, tb=tb: e.copy(out=xbf[:, 2 + tb * 512:2 + (tb + 1) * 512], in_=p[:]))
                    if sp["conv"] is not None:
                        cw_ = cwt[si % 2]
                        cwk = ("ab_cwt", si % 2)
                        nm, c0 = sp["conv"]
                        for tap in range(4):
                            c.op("sp", [], [cwk], lambda e, cw_=cw_, tap=tap, nm=nm, c0=c0: e.dma_start(out=cw_[:, tap:tap + 1], in_=Dr[nm][j, tap, c0:c0 + 128].rearrange("(p o) -> p o", o=1)))
                        c.op("dve", xk + ["ab_xbf_h", cwk], ["ab_u"], lambda e, cw_=cw_: e.tensor_scalar(out=uu[:], in0=xbf[:, 0:S], scalar1=cw_[:, 0:1], scalar2=None, op0=ALU.mult))
                        for tap in range(1, 4):
                            c.op("dve", xk + ["ab_xbf_h", cwk, "ab_u"], ["ab_u"], lambda e, cw_=cw_, tap=tap: e.scalar_tensor_tensor(out=uu[:], in0=xbf[:, tap:tap + S], scalar=cw_[:, tap:tap + 1], in1=uu[:], op0=ALU.mult, op1=ALU.add))
                        if sp["act"] is not None:
                            c.op("act", ["ab_u"], ["ab_u"], lambda e, f=sp["act"]: e.activation(out=uu[:], in_=uu[:], func=f))
                    else:
                        if sp["act"] is not None:
                            c.op("act", xk, ["ab_u"], lambda e, f=sp["act"]: e.activation(out=uu[:], in_=xbf[:, 2:S + 2], func=f))
                        else:
                            c.op("pool", xk, ["ab_u"], lambda e: e.tensor_copy(out=uu[:], in_=xbf[:, 2:S + 2]))
                    if sp["l2"]:
                        for tb in range(NB):
                            sl = tb % 2
                            ts = slice(tb * 512, (tb + 1) * 512)
                            c.op("act", ["ab_u"], [("ab_sqb", sl)], lambda e, sl=sl, ts=ts: e.activation(out=sqb[sl][:], in_=uu[:, ts], func=AF.Square))
                            p = pp[2 + sl]
                            c.op("pe", [("ab_sqb", sl), "cst"], [("ab_pp", 2 + sl)], lambda e, p=p, sl=sl: e.matmul(p[:], lhsT=ONES, rhs=sqb[sl][:], start=True, stop=True))
                            c.op("act", [("ab_pp", 2 + sl)], [("ab_rsb", sl)], lambda e, p=p, sl=sl: e.activation(out=rsb[sl][:], in_=p[:], func=AF.Sqrt, bias=1e-6))
                            c.op("dve", [("ab_rsb", sl)], [("ab_rsb", sl)], lambda e, sl=sl: e.reciprocal(out=rsb[sl][:], in_=rsb[sl][:]))
                            c.op("dve", [("ab_rsb", sl), "ab_u"], ["ab_u"], lambda e, sl=sl, ts=ts, sc_=sp["scale"]: e.scalar_tensor_tensor(out=uu[:, ts], in0=uu[:, ts], scalar=sc_, in1=rsb[sl][:], op0=ALU.mult, op1=ALU.mult))
                    elif sp["scale"] is not None:
                        c.op("dve", ["ab_u"], ["ab_u"], lambda e, sc_=sp["scale"]: e.tensor_scalar(out=uu[:], in0=uu[:], scalar1=sc_, scalar2=None, op0=ALU.mult))
                    if sp["fm"] is not None:
                        c.op("sp", ["ab_u"], [("ab_fm", sp["fm"])], lambda e, f=sp["fm"]: e.dma_start(out=FM[f], in_=uu[:]))
                    if sp["tm"] is not None:
                        nm, h = sp["tm"]
                        for tb in range(NB):
                            sl = tb % 2
                            p = pp[2 + sl]

                            def tr(e, p=p, tb=tb):
                                for t4 in range(4):
                                    ins = e.transpose(out=p[:, t4 * 128:(t4 + 1) * 128], in_=uu[:, tb * 512 + t4 * 128: tb * 512 + (t4 + 1) * 128], identity=I)
                                return ins
                            c.op("pe", ["ab_u", "cst"], [("ab_pp", 2 + sl)], tr)
                            c.op("act", [("ab_pp", 2 + sl)], [("ab_stg", sl)], lambda e, p=p, sl=sl: e.copy(out=stg[sl][:], in_=p[:].rearrange("p (t d) -> p t d", t=4)))
                            c.op("gq", [("ab_stg", sl)], [("ab_tm", nm, h)], lambda e, sl=sl, tb=tb, nm=nm, h=h: e.dma_start(out=TM[nm][tb * 512:(tb + 1) * 512, h * 128:(h + 1) * 128].rearrange("(t p) d -> p t d", p=128), in_=stg[sl][:]))
                c.barrier()
            import os as _os
            _algs = tuple(a for a in _os.environ.get("AB_ALGS", "dn,ml").split(",") if a)
            WIN = int(_os.environ.get("AB_WIN", "4"))
            with ExitStack() as es:
                qT = _sb(es, nc, "r_qT", [128, S], F32)
                ktok = _sb(es, nc, "r_ktok", [128, NT, 128], F32)
                vtok = _sb(es, nc, "r_vtok", [128, NT, 129], F32)
                ost = _sb(es, nc, "r_ost", [128, NT, 128], F32)
                gtok = qT[:].rearrange("p (t d) -> p t d", d=128)
                dn_names = ["Gmat", "Gle", "eD", "eDT", "egb", "t1", "t2", "attnT", "qd", "Xv", "Xk", "kd", "u", "wT", "vn"] + \
                           [f"P{k}" for k in range(2)] + [f"PT{k}" for k in range(2)] + [f"R{k}" for k in range(2)]
                F32R = mybir.dt.float32r
                rset = set([f"P{k}" for k in range(7)] + [f"PT{k}" for k in range(6)] + ["R0", "R1", "Xv", "Xk"])
                bfn = set(["wT", "qd", "attnT", "kd", "vn"])
                wt = {n: [_sb(es, nc, f"r_{n}{i}", [128, 128], BF16 if n in bfn else F32) for i in range(WIN)] for n in dn_names}
                qTb = _sb(es, nc, "r_qTb", [128, S], BF16)
                kTb = _sb(es, nc, "r_kTb", [128, S], BF16)
                vtokb = _sb(es, nc, "r_vtokb", [128, NT, 129], BF16)
                Ssh = [[_sb(es, nc, f"r_Ssh{d_}{i}", [128, 129], BF16) for i in range(2)] for d_ in range(2)]
                pTm = [_sb(es, nc, f"r_pTm{i}", [128, 128], BF16) for i in range(WIN)]
                pm = [_sb(es, nc, f"r_pm{i}", [128, 128], BF16) for i in range(WIN)]
                ksm = [_sb(es, nc, f"r_ksm{i}", [128, 128], BF16) for i in range(WIN)]
                alias = {"X": "Gmat", "e": "eD", "p": "eDT", "pT": "egb", "ks": "t1"}
                dmall = _sb(es, nc, "r_dmall", [128, 2, NT, 128], F32)
                mlc = _sb(es, nc, "r_mlc", [128, 12, 2, NT], F32)
                zc = _sb(es, nc, "r_zc", [128, 1], F32)
                c.op("pool", [], ["r_zc"], lambda e: e.memset(zc[:], 0.0))
                nd = [_sb(es, nc, f"r_nd{i}", [128, 129], F32) for i in range(WIN)]
                dcol = [_sb(es, nc, f"r_dcol{i}", [128, 4], F32) for i in range(WIN)]
                Sst = [[_sb(es, nc, f"r_S{d_}{i}", [128, 129], F32) for i in range(2)] for d_ in range(2)]
                ob = [_sb(es, nc, f"r_ob{i}", [128, 128], BF16) for i in range(2)]
                ot = [_sb(es, nc, f"r_ot{i}", [128, 128], F32) for i in range(2)]
                oss = [_sb(es, nc, f"r_oss{i}", [128, 2], F32) for i in range(2)]
                mst4 = [_sb(es, nc, f"r_mx{i}", [128, 512], BF16) for i in range(2)]
                pb = [_ps(es, nc, f"r_pb{i}", [128, 512], F32) for i in range(7)]
                pTb = _ps(es, nc, "r_pTb", [128, 1024], BF16)

                def Q(b, q, n=128):
                    return pb[b][:, q * 128:q * 128 + n]

                def K_(b, q):
                    return ("r_pb", b, q)

                def run_units(gens):
                    active = []
                    it = iter(gens)
                    while True:
                        while len(active) < WIN:
                            g = next(it, None)
                            if g is None:
                                break
                            active.append(g)
                        if not active:
                            break
                        for g in list(active):
                            try:
                                next(g)
                            except StopIteration:
                                active.remove(g)

                free = list(range(WIN))
                turn = [0, 0]

                def dn_unit(h, dr, si, t):
                    M = dirs[dr]
                    col = dr * 4 + h
                    sl_ = free.pop()
                    W = {n: wt[n][sl_] for n in dn_names}
                    for kq in range(7):
                        W[f"P{kq}"] = wt[f"P{kq % 2}"][sl_]
                    for kq in range(6):
                        W[f"PT{kq}"] = wt[f"PT{kq % 2}"][sl_]
                    ts = slice(t * 128, (t + 1) * 128)
                    ab_, vb_, sb_ = 2 * (sl_ % 2), 2 * (sl_ % 2) + 1, 4 + dr

                    def Wr(n):
                        return W[n][:].bitcast(F32R)

                    def k(n):
                        if n[0] == "P" and n[-1].isdigit():
                            n = n[:-1] + str(int(n[-1]) % 2)
                        return ("r_" + n, sl_)
                    Sc, Sn = Sst[dr][si % 2], Sst[dr][(si + 1) % 2]
                    Sck, Snk = ("r_S", dr, si % 2), ("r_S", dr, (si + 1) % 2)
                    gcol = Gg[:, t, col:col + 1]
                    c.op("dve", ["ab_Gg"], [k("Gmat")], lambda e: e.tensor_scalar(out=W["Gmat"][:], in0=M["AFT"], scalar1=gcol, scalar2=None, op0=ALU.mult))
                    c.op("act", ["ab_Gg"], [k("Gle")], lambda e: e.activation(out=W["Gle"][:], in_=M["INC"], func=AF.Copy, scale=gcol))
                    yield
                    c.op("act", ["r_vtok", "ab_Bt"], [k("Xv")], lambda e: e.activation(out=Wr("Xv"), in_=vtok[:, t, 0:128], func=AF.Copy, scale=Bt[:, t, col:col + 1]))
                    c.op("act", ["r_ktok", "ab_Bk"], [k("Xk")], lambda e: e.activation(out=Wr("Xk"), in_=ktok[:, t, :], func=AF.Copy, scale=Bk[:, t, col:col + 1]))
                    c.op("act", ["r_ktok", "ab_Edn"], [k("kd")], lambda e: e.activation(out=W["kd"][:], in_=ktok[:, t, :], func=AF.Copy, scale=Edn[:, t, 8 + col:8 + col + 1]))
                    yield

                    def mmA(e):
                        e.matmul(Q(ab_, 0), lhsT=M["INC"], rhs=W["Gmat"][:], start=True, stop=True)
                        e.matmul(Q(ab_, 2), lhsT=ONES, rhs=W["Gle"][:], start=True, stop=True)
                        return e.matmul(Q(ab_, 1), lhsT=W["Gmat"][:], rhs=M["INC"], start=True, stop=True)
                    c.op("pe", [k("Gmat"), k("Gle")], [K_(ab_, 0), K_(ab_, 1), K_(ab_, 2)], mmA)
                    c.op("act", [K_(ab_, 0)], [k("eD")], lambda e: e.activation(out=W["eD"][:], in_=Q(ab_, 0), func=AF.Exp))
                    c.op("act", [K_(ab_, 1)], [k("eDT")], lambda e: e.activation(out=W["eDT"][:], in_=Q(ab_, 1), func=AF.Exp))
                    c.op("act", [K_(ab_, 2)], [k("egb")], lambda e: e.activation(out=W["egb"][:], in_=Q(ab_, 2), func=AF.Exp))
                    yield
                    c.op("pool", [k("eD")], [k("t1")], lambda e: e.tensor_tensor(out=W["t1"][:], in0=W["eD"][:], in1=M["STRICT"], op=ALU.mult))
                    c.op("pool", [k("eDT")], [k("t2")], lambda e: e.tensor_tensor(out=W["t2"][:], in0=W["eDT"][:], in1=M["INCLji"], op=ALU.mult))
                    yield
                    c.op("pool", [k("egb"), "r_qTb"], [k("qd")], lambda e: e.tensor_tensor(out=W["qd"][:], in0=qTb[:, ts], in1=W["egb"][:], op=ALU.mult))
                    yield

                    def mmB(e):
                        e.matmul(Q(vb_, 1), lhsT=kTb[:, ts], rhs=kTb[:, ts], start=True, stop=True)
                        return e.matmul(Q(vb_, 0), lhsT=kTb[:, ts], rhs=qTb[:, ts], start=True, stop=True)
                    c.op("pe", ["r_kTb", "r_qTb"], [K_(vb_, 1), K_(vb_, 0)], mmB)
                    c.op("dve", [K_(vb_, 1), k("t1"), "ab_nBt"], [k("P0")], lambda e: e.scalar_tensor_tensor(out=Wr("P0"), in0=Q(vb_, 1), scalar=nBt[:, t, col:col + 1], in1=W["t1"][:], op0=ALU.mult, op1=ALU.mult))
                    c.op("dve", [K_(vb_, 0), k("t2")], [k("attnT")], lambda e: e.tensor_tensor(out=W["attnT"][:], in0=Q(vb_, 0), in1=W["t2"][:], op=ALU.mult))
                    yield
                    if "P0" in bfn:
                        qv = Q(vb_, 2).bitcast(BF16)[:, 0:128]
                        c.op("pe", [k("P0"), "idb"], [K_(vb_, 2)], lambda e: e.transpose(out=qv, in_=W["P0"][:], identity=self.idb[:]))
                        c.op("dve", [K_(vb_, 2)], [k("PT0")], lambda e: e.tensor_copy(out=W["PT0"][:], in_=qv))
                    else:
                        c.op("pe", [k("P0")], [K_(vb_, 2)], lambda e: e.transpose(out=Q(vb_, 2), in_=W["P0"][:], identity=I))
                        c.op("dve", [K_(vb_, 2)], [k("PT0")], lambda e: e.tensor_copy(out=Wr("PT0"), in_=Q(vb_, 2)))
                    c.op("dve", [k("PT0")], [k("R0")], lambda e: e.tensor_tensor(out=Wr("R0"), in0=W["PT0"][:], in1=I, op=ALU.add))
                    yield
                    rc = "R0"
                    for kk in range(1, 7):
                        q2 = kk % 2
                        c.op("pe", [k(f"P{kk-1}"), k(f"PT{kk-1}")], [K_(ab_, q2)], lambda e, kk=kk, q2=q2: e.matmul(Q(ab_, q2), lhsT=Wr(f"PT{kk-1}"), rhs=Wr(f"P{kk-1}"), start=True, stop=True))
                        c.op("act", [K_(ab_, q2)], [k(f"P{kk}")], lambda e, kk=kk, q2=q2: e.copy(out=Wr(f"P{kk}"), in_=Q(ab_, q2)))
                        yield
                        if kk < 6:
                            c.op("pe", [k(f"P{kk-1}"), k(f"PT{kk-1}")], [K_(vb_, 2 + q2)], lambda e, kk=kk, q2=q2: e.matmul(Q(vb_, 2 + q2), lhsT=Wr(f"P{kk-1}"), rhs=Wr(f"PT{kk-1}"), start=True, stop=True))
                            c.op("dve", [K_(vb_, 2 + q2)], [k(f"PT{kk}")], lambda e, kk=kk, q2=q2: e.tensor_copy(out=Wr(f"PT{kk}"), in_=Q(vb_, 2 + q2)))
                            yield
                        rn = "R1" if rc == "R0" else "R0"
                        c.op("pe", [k(f"P{kk}"), k(rc)], [K_(vb_, q2)], lambda e, kk=kk, rc=rc, q2=q2: e.matmul(Q(vb_, q2), lhsT=Wr(f"P{kk}"), rhs=Wr(rc), start=True, stop=True))
                        c.op("dve", [K_(vb_, q2), k(rc)], [k(rn)], lambda e, rc=rc, rn=rn, q2=q2: e.tensor_tensor(out=Wr(rn), in0=W[rc][:], in1=Q(vb_, q2), op=ALU.add))
                        yield
                        rc = rn
                    c.op("pe", [k(rc), k("Xv")], [K_(ab_, 0)], lambda e: e.matmul(Q(ab_, 0), lhsT=Wr(rc), rhs=Wr("Xv"), start=True, stop=True))
                    c.op("act", [K_(ab_, 0)], [k("u")], lambda e: e.copy(out=W["u"][:], in_=Q(ab_, 0)))
                    yield
                    c.op("pe", [k(rc), k("Xk")], [K_(ab_, 1)], lambda e: e.matmul(Q(ab_, 1), lhsT=Wr("Xk"), rhs=Wr(rc), start=True, stop=True))
                    c.op("act", [K_(ab_, 1)], [k("wT")], lambda e: e.copy(out=W["wT"][:], in_=Q(ab_, 1)))
                    yield
                    while turn[dr] != si:
                        yield
                    Sbc, Sbn = Ssh[dr][si % 2], Ssh[dr][(si + 1) % 2]
                    Sbck, Sbnk = ("r_Ssh", dr, si % 2), ("r_Ssh", dr, (si + 1) % 2)
                    c.op("pe", [k("wT"), Sbck], [K_(sb_, 0)], lambda e: e.matmul(Q(sb_, 0), lhsT=W["wT"][:], rhs=Sbc[:, 0:128], start=True, stop=True))
                    c.op("dve", [K_(sb_, 0), k("u")], [k("vn")], lambda e: e.tensor_tensor(out=W["vn"][:], in0=W["u"][:], in1=Q(sb_, 0), op=ALU.subtract))

                    def mmo(e):
                        e.matmul(Q(sb_, 2), lhsT=W["qd"][:], rhs=Sbc[:, 0:128], start=True, stop=False)
                        e.matmul(Q(sb_, 2), lhsT=W["attnT"][:], rhs=W["vn"][:], start=False, stop=True)
                        return e.matmul(Q(sb_, 1), lhsT=W["kd"][:], rhs=W["vn"][:], start=True, stop=True)
                    c.op("pe", [k("qd"), k("attnT"), k("vn"), k("kd"), Sbck], [K_(sb_, 2), K_(sb_, 1)], mmo)
                    c.op("dve", [K_(sb_, 1), Sck, "ab_Edn"], [Snk], lambda e: e.scalar_tensor_tensor(out=Sn[:, 0:128], in0=Sc[:, 0:128], scalar=Edn[:, t, 16 + col:16 + col + 1], in1=Q(sb_, 1), op0=ALU.mult, op1=ALU.add))
                    c.op("act", [Snk], [Sbnk], lambda e: e.copy(out=Sbn[:, 0:128], in_=Sn[:, 0:128]))
                    c.op("dve", [K_(sb_, 2), ("r_ost", t)], [("r_ost", t)], lambda e: e.tensor_tensor(out=ost[:, t, :], in0=ost[:, t, :], in1=Q(sb_, 2), op=ALU.add))
                    turn[dr] += 1
                    free.append(sl_)

                MI, MS, B1, MNEW, A1, MT, NEGM, INTER, EMT, DEC, SRC, TMP = range(12)

                def ml_prep(h, dr, si, t):
                    M = dirs[dr]
                    col = dr * 4 + h
                    sl_ = free.pop()
                    X = wt[alias["X"]][sl_]
                    xk = ("r_" + alias["X"], sl_)
                    vb_ = 2 * (sl_ % 2) + 1
                    lf, li = Lf[:, t, col:col + 1], Li[:, t, col:col + 1]
                    c.op("dve", ["ab_Lf"], [xk], lambda e: e.tensor_scalar(out=X[:], in0=M["AFT"], scalar1=lf, scalar2=None, op0=ALU.mult))
                    c.op("dve", ["ab_Li", xk], [xk], lambda e: e.scalar_tensor_tensor(out=X[:], in0=I, scalar=li, in1=X[:], op0=ALU.mult, op1=ALU.add))
                    yield

                    def mmA(e):
                        e.matmul(Q(vb_, 0), lhsT=M["INC"], rhs=X[:], start=True, stop=True)
                        return e.matmul(Q(vb_, 1), lhsT=ONES, rhs=X[:], start=True, stop=True)
                    c.op("pe", [xk], [K_(vb_, 0), K_(vb_, 1)], mmA)
                    c.op("dve", [K_(vb_, 0)], [("r_dmall", dr, t)], lambda e: e.tensor_tensor(out=dmall[:, dr, t, :], in0=Q(vb_, 0), in1=M["NEGM"], op=ALU.add))
                    c.op("dve", [K_(vb_, 1)], [("r_mlc", MS, dr)], lambda e: e.tensor_reduce(out=mlc[:, MS, dr, t:t + 1], in_=Q(vb_, 1), axis=AX.X, op=ALU.max))
                    yield
                    c.op("dve", [("r_dmall", dr, t)], [("r_mlc", MI, dr)], lambda e: e.tensor_reduce(out=mlc[:, MI, dr, t:t + 1], in_=dmall[:, dr, t, :], axis=AX.X, op=ALU.max))
                    free.append(sl_)

                def ml_main(h, dr, si, t):
                    col = dr * 4 + h
                    sl_ = free.pop()
                    ts = slice(t * 128, (t + 1) * 128)
                    e_ = wt[alias["e"]][sl_]
                    ek = ("r_" + alias["e"], sl_)
                    p_, pT_, ks_ = pm[sl_], pTm[sl_], ksm[sl_]
                    pk_, pTk, ksk = ("r_pm", sl_), ("r_pTm", sl_), ("r_ksm", sl_)
                    Cbc, Cbn = Ssh[dr][si % 2], Ssh[dr][(si + 1) % 2]
                    Cbck, Cbnk = ("r_Ssh", dr, si % 2), ("r_Ssh", dr, (si + 1) % 2)
                    Cc, Cn = Sst[dr][si % 2], Sst[dr][(si + 1) % 2]
                    Cck, Cnk = ("r_S", dr, si % 2), ("r_S", dr, (si + 1) % 2)
                    ab_, vb_, sb_ = 2 * (sl_ % 2), 2 * (sl_ % 2) + 1, 4 + dr

                    def col_(i_):
                        return mlc[:, i_, dr, t:t + 1]
                    c.op("act", [("r_dmall", dr, t), ("r_mlc", NEGM, dr)], [ek], lambda e: e.activation(out=e_[:], in_=dmall[:, dr, t, :], func=AF.Exp, bias=col_(NEGM)))
                    c.op("act", ["r_ktok", ("r_mlc", SRC, dr)], [ksk], lambda e: e.activation(out=ks_[:], in_=ktok[:, t, :], func=AF.Copy, scale=col_(SRC)))
                    yield
                    c.op("pe", ["r_qTb", "r_kTb"], [K_(vb_, 2)], lambda e: e.matmul(Q(vb_, 2), lhsT=qTb[:, ts], rhs=kTb[:, ts], start=True, stop=True))
                    c.op("dve", [ek, K_(vb_, 2)], [pk_], lambda e: e.tensor_tensor(out=p_[:], in0=e_[:], in1=Q(vb_, 2), op=ALU.mult))
                    yield
                    qv = Q(ab_, 0).bitcast(BF16)[:, 0:128]
                    c.op("pe", [pk_, "idb"], [K_(ab_, 0)], lambda e: e.transpose(out=qv, in_=p_[:], identity=self.idb[:]))
                    c.op("act", [K_(ab_, 0)], [pTk], lambda e: e.copy(out=pT_[:], in_=qv))
                    yield
                    c.op("pe", [pTk, "r_vtokb"], [K_(ab_, 2)], lambda e: e.matmul(pb[ab_][:, 256:385], lhsT=pT_[:], rhs=vtokb[:, t, :], start=True, stop=True))
                    c.op("dve", [K_(ab_, 2)], [("r_nd", sl_)], lambda e: e.tensor_copy(out=nd[sl_][:], in_=pb[ab_][:, 256:385]))
                    yield
                    while turn[dr] != si:
                        yield

                    def mms(e):
                        e.matmul(pb[sb_][:, 0:129], lhsT=qTb[:, ts], rhs=Cbc[:, 0:129], start=True, stop=True)
                        return e.matmul(pb[sb_][:, 256:385], lhsT=ks_[:], rhs=vtokb[:, t, :], start=True, stop=True)
                    c.op("pe", ["r_qTb", Cbck, ksk, "r_vtokb"], [K_(sb_, 0), K_(sb_, 2)], mms)
                    c.op("dve", [K_(sb_, 2), Cck, ("r_mlc", DEC, dr)], [Cnk], lambda e: e.scalar_tensor_tensor(out=Cn[:], in0=Cc[:], scalar=col_(DEC), in1=pb[sb_][:, 256:385], op0=ALU.mult, op1=ALU.add))
                    c.op("act", [Cnk], [Cbnk], lambda e: e.copy(out=Cbn[:], in_=Cn[:]))
                    c.op("dve", [K_(sb_, 0), ("r_mlc", INTER, dr), ("r_nd", sl_)], [("r_nd", sl_)], lambda e: e.scalar_tensor_tensor(out=nd[sl_][:], in0=pb[sb_][:, 0:129], scalar=col_(INTER), in1=nd[sl_][:], op0=ALU.mult, op1=ALU.add))
                    turn[dr] += 1
                    yield
                    dc = dcol[sl_]
                    dk = ("r_dcol", sl_)
                    c.op("dve", [("r_nd", sl_)], [dk], lambda e: e.tensor_scalar(out=dc[:, 0:1], in0=nd[sl_][:, 128:129], scalar1=-1.0, scalar2=None, op0=ALU.mult))
                    yield
                    c.op("dve", [("r_nd", sl_), dk], [dk], lambda e: e.tensor_tensor(out=dc[:, 1:2], in0=nd[sl_][:, 128:129], in1=dc[:, 0:1], op=ALU.max))
                    yield
                    c.op("dve", [dk, ("r_mlc", EMT, dr)], [dk], lambda e: e.tensor_tensor(out=dc[:, 2:3], in0=dc[:, 1:2], in1=col_(EMT), op=ALU.max))
                    yield
                    c.op("dve", [dk], [dk], lambda e: e.reciprocal(out=dc[:, 3:4], in_=dc[:, 2:3]))
                    yield
                    c.op("dve", [("r_nd", sl_), dk, ("r_ost", t)], [("r_ost", t)], lambda e: e.scalar_tensor_tensor(out=ost[:, t, :], in0=nd[sl_][:, 0:128], scalar=dc[:, 3:4], in1=ost[:, t, :], op0=ALU.mult, op1=ALU.add))
                    free.append(sl_)

                for alg in _algs:
                    for h in range(int(_os.environ.get("AB_NH", "4"))):
                        if alg == "dn":
                            fq, fk, tk, tv, tg = h, 4 + h, "dnk", "dnv", "dnz"
                        else:
                            fq, fk, tk, tv, tg = 8 + h, 12 + h, "mlk", "mlv", "mlo"
                        c.op("sp", [("ab_fm", fk)], ["r_qT"], lambda e: e.dma_start(out=qT[:], in_=FM[fk]))
                        c.op("act", ["r_qT"], ["r_kTb"], lambda e: e.copy(out=kTb[:], in_=qT[:]))
                        c.op("sp", [("ab_fm", fq)], ["r_qT"], lambda e: e.dma_start(out=qT[:], in_=FM[fq]))
                        c.op("pool", ["r_qT"], ["r_qTb"], lambda e: e.tensor_copy(out=qTb[:], in_=qT[:]))
                        for t0 in range(0, NT, 4):
                            t1_ = min(NT, t0 + 4)
                            for (dst_, nm_, ky_) in ((ktok, tk, "r_ktok"), (vtok, tv, "r_vtok")):
                                c.op("sp", [("ab_tm", nm_, h)], [ky_], lambda e, dst_=dst_, nm_=nm_, t0=t0, t1_=t1_: e.dma_start(out=dst_[:, t0:t1_, 0:128], in_=TM[nm_][t0 * 128:t1_ * 128, h * 128:(h + 1) * 128].rearrange("(t p) d -> p t d", p=128)))
                        c.op("pool", ["r_vtok"], ["r_vtok1"], lambda e: e.memset(vtok[:, :, 128:129], 1.0))
                        c.op("dve", ["r_vtok", "r_vtok1"], ["r_vtokb"], lambda e: e.tensor_copy(out=vtokb[:], in_=vtok[:]))
                        c.op("pool", [("r_ost", t) for t in range(NT)], [("r_ost", t) for t in range(NT)], lambda e: e.memset(ost[:], 0.0))
                        for dr in range(2):
                            c.op("pool", [("r_S", dr, 0)], [("r_S", dr, 0)], lambda e, dr=dr: e.memset(Sst[dr][0][:], 0.0))
                            c.op("pool", [("r_Ssh", dr, 0)], [("r_Ssh", dr, 0)], lambda e, dr=dr: e.memset(Ssh[dr][0][:], 0.0))
                        orders = [list(range(NT)), list(range(NT - 1, -1, -1))]
                        if alg == "dn":
                            turn[0] = turn[1] = 0
                            gens = []
                            for si in range(NT):
                                for dr in range(2):
                                    gens.append(dn_unit(h, dr, si, orders[dr][si]))
                            run_units(gens)
                        else:
                            gens = []
                            for si in range(NT):
                                for dr in range(2):
                                    gens.append(ml_prep(h, dr, si, orders[dr][si]))
                            run_units(gens)
                            for si in range(NT):
                                for dr in range(2):
                                    t = orders[dr][si]
                                    col = dr * 4 + h
                                    prev = zc[:] if si == 0 else mlc[:, MNEW, dr, orders[dr][si - 1]:orders[dr][si - 1] + 1]
                                    c.op("dve", ["ab_Mlt", ("r_mlc", MNEW, dr), "r_zc"], [("r_mlc", B1, dr)], lambda e, t=t, dr=dr, col=col, prev=prev: e.tensor_tensor(out=mlc[:, B1, dr, t:t + 1], in0=Mlt[:, t, 16 + col:16 + col + 1], in1=prev, op=ALU.add))
                                    c.op("dve", [("r_mlc", B1, dr), ("r_mlc", MS, dr)], [("r_mlc", MNEW, dr)], lambda e, t=t, dr=dr: e.tensor_tensor(out=mlc[:, MNEW, dr, t:t + 1], in0=mlc[:, B1, dr, t:t + 1], in1=mlc[:, MS, dr, t:t + 1], op=ALU.max))
                            for dr in range(2):
                                col = dr * 4 + h

                                def A_(i_, dr=dr):
                                    return mlc[:, i_, dr, :]

                                def kk_(i_, dr=dr):
                                    return ("r_mlc", i_, dr)
                                bcum_, btot_, wsrc_ = Mlt[:, :, col], Mlt[:, :, 16 + col], Ws[:, :, col]
                                c.op("dve", ["ab_Mlt"], [kk_(A1)], lambda e, dr=dr: e.tensor_tensor(out=A_(A1), in0=bcum_, in1=btot_, op=ALU.subtract))
                                c.op("dve", [kk_(A1), kk_(B1)], [kk_(A1)], lambda e, dr=dr: e.tensor_tensor(out=A_(A1), in0=A_(A1), in1=A_(B1), op=ALU.add))
                                c.op("dve", [kk_(A1), kk_(MI)], [kk_(MT)], lambda e, dr=dr: e.tensor_tensor(out=A_(MT), in0=A_(A1), in1=A_(MI), op=ALU.max))
                                c.op("dve", [kk_(MT)], [kk_(NEGM)], lambda e, dr=dr: e.tensor_scalar(out=A_(NEGM), in0=A_(MT), scalar1=-1.0, scalar2=None, op0=ALU.mult))
                                c.op("dve", [kk_(A1), kk_(MT)], [kk_(TMP)], lambda e, dr=dr: e.tensor_tensor(out=A_(TMP), in0=A_(A1), in1=A_(MT), op=ALU.subtract))
                                c.op("act", [kk_(TMP)], [kk_(INTER)], lambda e, dr=dr: e.activation(out=A_(INTER), in_=A_(TMP), func=AF.Exp))
                                c.op("act", [kk_(NEGM)], [kk_(EMT)], lambda e, dr=dr: e.activation(out=A_(EMT), in_=A_(NEGM), func=AF.Exp))
                                c.op("dve", [kk_(B1), kk_(MNEW), kk_(INTER)], [kk_(TMP)], lambda e, dr=dr: e.tensor_tensor(out=A_(TMP), in0=A_(B1), in1=A_(MNEW), op=ALU.subtract))
                                c.op("act", [kk_(TMP)], [kk_(DEC)], lambda e, dr=dr: e.activation(out=A_(DEC), in_=A_(TMP), func=AF.Exp))
                                c.op("dve", ["ab_Ws", kk_(MNEW), kk_(DEC)], [kk_(TMP)], lambda e, dr=dr: e.tensor_tensor(out=A_(TMP), in0=wsrc_, in1=A_(MNEW), op=ALU.subtract))
                                c.op("act", [kk_(TMP)], [kk_(SRC)], lambda e, dr=dr: e.activation(out=A_(SRC), in_=A_(TMP), func=AF.Exp))
                            turn[0] = turn[1] = 0
                            gens = []
                            for si in range(NT):
                                for dr in range(2):
                                    gens.append(ml_main(h, dr, si, orders[dr][si]))
                            run_units(gens)
                        mchunk = h if alg == "dn" else 4 + h
                        for t0 in range(0, NT, 4):
                            t1_ = min(NT, t0 + 4)
                            c.op("sp", [("ab_tm", tg, h)], ["r_qT"], lambda e, t0=t0, t1_=t1_: e.dma_start(out=gtok[:, t0:t1_, :], in_=TM[tg][t0 * 128:t1_ * 128, h * 128:(h + 1) * 128].rearrange("(t p) d -> p t d", p=128)))
                        for t in range(NT):
                            u2 = t % 2
                            if alg == "dn":
                                c.op("act", [("r_ost", t)], [("r_ot", u2), ("r_oss", u2)], lambda e, t=t, u2=u2: e.activation(out=ot[u2][:], in_=ost[:, t, :], func=AF.Square, accum_out=oss[u2][:, 0:1]))
                                c.op("dve", [("r_oss", u2)], [("r_oss", u2)], lambda e, u2=u2: e.tensor_scalar(out=oss[u2][:, 0:1], in0=oss[u2][:, 0:1], scalar1=1.0 / 128, scalar2=EPS, op0=ALU.mult, op1=ALU.add))
                                c.op("act", [("r_oss", u2)], [("r_oss", u2)], lambda e, u2=u2: e.activation(out=oss[u2][:, 0:1], in_=oss[u2][:, 0:1], func=AF.Sqrt))
                                c.op("dve", [("r_oss", u2)], [("r_oss", u2)], lambda e, u2=u2: e.reciprocal(out=oss[u2][:, 0:1], in_=oss[u2][:, 0:1]))
                                c.op("dve", [("r_ost", t), ("r_oss", u2), "ab_dnw"], [("r_ot", u2)], lambda e, t=t, u2=u2: e.scalar_tensor_tensor(out=ot[u2][:], in0=ost[:, t, :], scalar=oss[u2][:, 0:1], in1=dnw[:], op0=ALU.mult, op1=ALU.mult))
                                c.op("dve", [("r_ot", u2), "r_qT"], [("r_ob", u2)], lambda e, t=t, u2=u2: e.tensor_tensor(out=ob[u2][:], in0=ot[u2][:], in1=gtok[:, t, :], op=ALU.mult))
                            else:
                                c.op("dve", [("r_ost", t), "r_qT"], [("r_ob", u2)], lambda e, t=t, u2=u2: e.tensor_tensor(out=ob[u2][:], in0=ost[:, t, :], in1=gtok[:, t, :], op=ALU.mult))
                            c.op("pe", [("r_ob", u2), "idb"], [("r_pTb", t % 4)], lambda e, t=t, u2=u2: e.transpose(out=pTb[:, (t % 4) * 128:(t % 4 + 1) * 128], in_=ob[u2][:], identity=self.idb[:]))
                            if t % 4 == 3:
                                tb = t // 4
                                mx = mst4[tb % 2]
                                c.op("act", [("r_pTb", q_) for q_ in range(4)], [("r_mx", tb % 2)], lambda e, mx=mx: e.copy(out=mx[:], in_=pTb[:, 0:512]))
                                c.op("gq", [("r_mx", tb % 2)], [("ab_mix", tb)], lambda e, mx=mx, tb=tb, mchunk=mchunk: e.dma_start(out=mixscr[mchunk, :, tb * 512:(tb + 1) * 512], in_=mx[:]))
                c.barrier()
        self.stage_c_scr(c, L, mixscr, 8, Dr["ab_w_out"][j], Dr["mix_norm_post"][L:L + 1, :], src, dst, "abc")


W_NAMES = ['mix_norm_pre', 'mix_norm_post', 'ab_w_in', 'ab_w_out', 'dn_conv_w', 'dn_a_log', 'dn_dt_bias',
           'dn_out_norm', 'ml_conv_w', 'ml_i_bias', 'ml_f_bias', 'c_w_in', 'c_w_out', 'c_conv_w', 'c_conv_b',
           'c_w_a', 'c_b_a', 'c_w_x', 'c_b_x', 'c_lambda', 'xa_norm_pre', 'xa_norm_post', 'xa_mem_norm',
           'xa_w_q', 'xa_w_kv', 'xa_w_o', 'ffn_norm_pre', 'ffn_norm_post', 'ffn_w_up', 'ffn_w_down']


def const_masks():
    p = np.arange(128)[:, None]
    f = np.arange(128)[None, :]
    ms = [p == f, p <= f, p >= f, p < f, p > f, np.ones((128, 128), bool)]
    arr = [m.astype(np.float32) for m in ms]
    arr.append(NEG * (p < f).astype(np.float32))
    arr.append(NEG * (p > f).astype(np.float32))
    return np.ascontiguousarray(np.concatenate(arr, axis=1)).astype(np.float32)


def build(S, shapes, stages, Mm=256):
    B = Builder(S, Mm)
    P = B.P
    nc = B.nc
    for n, shp in shapes.items():
        P.din(n, shp)
    P.din("cmask", [128, 8 * 128])
    out = P.dout("out", [S, D])
    with ExitStack() as es:
        c = Ctx(nc, es)
        B.load_consts(c, es)
        src = P.dram["x"]
        for (name, L) in stages:
            getattr(B, name)(c, L, src, out)
            src = out
        c.barrier()
        c.finish()
    B.nops = c.nops
    return B


STAGES = [("mixer_ab", 0), ("xa", 0), ("ffn", 0), ("mixer_c", 1), ("xa", 1), ("ffn", 1)]


def kernel(**inputs):
    x = np.ascontiguousarray(np.asarray(inputs["x"], dtype=np.float32))
    mem = np.ascontiguousarray(np.asarray(inputs["mem"], dtype=np.float32))
    nb, S, _ = x.shape
    shapes = {"x": (S, D), "mem": tuple(mem.shape[1:])}
    ws = {}
    for n in W_NAMES:
        ws[n] = np.ascontiguousarray(np.asarray(inputs[n], dtype=np.float32))
        shapes[n] = ws[n].shape
    B = build(S, shapes, STAGES, Mm=mem.shape[1])
    cm = const_masks()
    in_maps = []
    for b in range(nb):
        m = {"x": x[b], "mem": mem[b], "cmask": cm}
        m.update(ws)
        in_maps.append(m)
    res = run_bass_kernel_spmd(B.nc, in_maps, core_ids=list(range(nb)))
    return np.stack([np.asarray(r["out"], dtype=np.float32) for r in res.results], axis=0)
```

```python
import numpy as np
from contextlib import ExitStack
import concourse.bass as bass
import concourse.mybir as mybir
from concourse.bass_utils import run_bass_kernel_spmd

F32 = mybir.dt.float32
BF16 = mybir.dt.bfloat16
AF = mybir.ActivationFunctionType
ALU = mybir.AluOpType
AX = mybir.AxisListType
D = 1024
EPS = 1e-6
NEG = -30000.0


class _Eng:
    def __init__(self, ctx, name, be, is_dma, nslots=14):
        self.name, self.be, self.is_dma = name, be, is_dma
        self.waited = {}
        if is_dma:
            self.slots = [ctx.new_sem(f"{name}_d{i}") for i in range(nslots)]
            self.n = 0
        else:
            self.sem = ctx.new_sem(f"{name}_s")
            self.count = 0


class _Buf:
    __slots__ = ("w", "r", "rd")

    def __init__(self):
        self.w = None
        self.r = {}
        self.rd = []


class Ctx:
    def __init__(self, nc, es):
        self.nc, self.es = nc, es
        self.sems = []
        self.bufs = {}
        self.engs = {}
        for name, be, dma in (("pe", nc.tensor, False), ("act", nc.scalar, False),
                              ("dve", nc.vector, False), ("pool", nc.gpsimd, False),
                              ("sp", nc.sync, True), ("gq", nc.gpsimd, True)):
            self.engs[name] = _Eng(self, name, be, dma)
        self.nops = 0
        import os as _os
        self.limit = int(_os.environ["OP_LIMIT"]) if "OP_LIMIT" in _os.environ else None
        self.trace = tuple(int(v) for v in _os.environ["OP_TRACE"].split(",")) if "OP_TRACE" in _os.environ else None

    def new_sem(self, name):
        s = self.es.enter_context(self.nc.semaphore(name))
        self.sems.append(s)
        return len(self.sems) - 1

    def buf(self, k):
        b = self.bufs.get(k)
        if b is None:
            b = self.bufs[k] = _Buf()
        return b

    def op(self, eng, reads, writes, fn):
        E = self.engs[eng]
        need = {}
        if self.trace is not None and self.trace[0] <= self.nops < self.trace[1]:
            print("OP", self.nops, eng, "R", reads, "W", writes)
        if self.limit is not None and self.nops >= self.limit:
            self.nops += 1
            return None

        def add(ev, raw):
            de, si, val = ev
            if de is E and not E.is_dma:
                if E.name == "pe" or not raw:
                    return
            if need.get(si, 0) < val:
                need[si] = val

        for k in reads:
            b = self.buf(k)
            if b.w is not None:
                add(b.w, True)
        for k in writes:
            b = self.buf(k)
            if b.w is not None:
                add(b.w, True)
            for ev in b.r.values():
                add(ev, False)
            for ev in b.rd:
                add(ev, False)
        banks = set()
        for k in list(reads) + list(writes):
            bk = self.bank(k)
            if bk is not None:
                banks.add(bk)
        for bk in banks:
            b = self.buf(bk)
            if b.w is not None:
                add(b.w, False)
        if E.is_dma:
            slot = E.n % len(E.slots)
            gen = E.n // len(E.slots)
            si_own = E.slots[slot]
            if gen > 0 and need.get(si_own, 0) < 16 * gen:
                need[si_own] = 16 * gen
        for si, val in need.items():
            if E.waited.get(si, 0) >= val:
                continue
            E.be.wait_ge(self.sems[si], val)
            E.waited[si] = val
        ins = fn(E.be)
        if E.is_dma:
            ins.then_inc(self.sems[si_own], 16)
            ev = (E, si_own, 16 * (gen + 1))
            E.n += 1
        else:
            E.count += 1
            ins.then_inc(self.sems[E.sem], 1)
            ev = (E, E.sem, E.count)
        for k in reads:
            b = self.buf(k)
            if E.is_dma:
                b.rd.append(ev)
            else:
                b.r[E.name] = ev
        for k in writes:
            b = self.buf(k)
            b.w = ev
            b.r = {}
            b.rd = []
        for bk in banks:
            self.buf(bk).w = ev
        self.nops += 1
        return ev

    _BANKED = ("r_pb", "ab_pp", "x_pa", "f_pu", "c_pp", "pT", "py")

    def bank(self, k):
        if isinstance(k, tuple):
            if k[0] in self._BANKED:
                return ("BANK", k[0], k[1])
            if k[0] == "r_pTb":
                return ("BANK", "r_pTb")
        elif k == "x_pss":
            return ("BANK", "x_pss")
        return None

    def barrier(self):
        evs = []
        for E in self.engs.values():
            if E.is_dma:
                for i, si in enumerate(E.slots):
                    cnt = (E.n - i + len(E.slots) - 1) // len(E.slots) if E.n > i else 0
                    if cnt > 0:
                        evs.append((si, 16 * cnt))
            elif E.count > 0:
                evs.append((E.sem, E.count))
        for E in self.engs.values():
            for si, val in evs:
                if (not E.is_dma) and si == E.sem:
                    continue
                if E.waited.get(si, 0) >= val:
                    continue
                E.be.wait_ge(self.sems[si], val)
                E.waited[si] = val

    def finish(self):
        for E in self.engs.values():
            if E.is_dma:
                for i, si in enumerate(E.slots):
                    cnt = (E.n - i + len(E.slots) - 1) // len(E.slots) if E.n > i else 0
                    if cnt > 0 and E.waited.get(si, 0) < 16 * cnt:
                        E.be.wait_ge(self.sems[si], 16 * cnt)
                        E.waited[si] = 16 * cnt


class Prog:
    def __init__(self, S):
        self.S = S
        self.NT = S // 128
        self.nc = bass.Bass("TRN2", target_bir_lowering=False)
        self.dram = {}

    def din(self, name, shape, dt=F32):
        self.dram[name] = self.nc.dram_tensor(name, list(shape), dt, kind="ExternalInput").ap()
        return self.dram[name]

    def dout(self, name, shape, dt=F32):
        self.dram[name] = self.nc.dram_tensor(name, list(shape), dt, kind="ExternalOutput").ap()
        return self.dram[name]

    def dscr(self, name, shape, dt=F32):
        self.dram[name] = self.nc.dram_tensor(name, list(shape), dt, kind="Internal").ap()
        return self.dram[name]


_UID = [0]


def _sb(es, nc, name, shape, dt):
    _UID[0] += 1
    return es.enter_context(nc.sbuf_tensor(f"{name}_{_UID[0]}", list(shape), dt))


def _ps(es, nc, name, shape, dt):
    _UID[0] += 1
    return es.enter_context(nc.psum_tensor(f"{name}_{_UID[0]}", list(shape), dt))


class Builder:
    def __init__(self, S, Mm=256):
        self.S, self.NT, self.Mm = S, S // 128, Mm
        self.P = Prog(S)
        self.nc = self.P.nc

    def load_consts(self, c, es):
        nc = self.nc
        cm = self.P.dram["cmask"]
        self.cst = _sb(es, nc, "cst", [128, 8, 128], F32)
        c.op("sp", [], ["cst"], lambda e: e.dma_start(out=self.cst[:], in_=cm.rearrange("p (k f) -> p k f", k=8)))
        self.idb = _sb(es, nc, "idb", [128, 128], BF16)
        c.op("dve", ["cst"], ["idb"], lambda e: e.tensor_copy(out=self.idb[:], in_=self.cst[:, 0, :]))
        self.onesb = _sb(es, nc, "onesb", [128, 128], BF16)
        c.op("dve", ["cst"], ["onesb"], lambda e: e.tensor_copy(out=self.onesb[:], in_=self.cst[:, 5, :]))
        self.I = self.cst[:, 0, :]
        self.LE = self.cst[:, 1, :]
        self.GE = self.cst[:, 2, :]
        self.LT = self.cst[:, 3, :]
        self.GT = self.cst[:, 4, :]
        self.ONES = self.cst[:, 5, :]
        self.NLT = self.cst[:, 6, :]
        self.NGT = self.cst[:, 7, :]

    def load_bc(self, c, tile, key, src_row):
        c.op("sp", [], [key], lambda e: e.dma_start(out=tile, in_=src_row.partition_broadcast(128)))

    def rstd_from_ss(self, c, ss, key, n):
        c.op("dve", [key], [key], lambda e: e.tensor_scalar(out=ss, in0=ss, scalar1=1.0 / n, scalar2=EPS, op0=ALU.mult, op1=ALU.add))
        c.op("act", [key], [key], lambda e: e.activation(out=ss, in_=ss, func=AF.Sqrt))
        c.op("dve", [key], [key], lambda e: e.reciprocal(out=ss, in_=ss))

    def prenorm_tile(self, c, W, src_ap, wkey, wbc, dstT, dkey, slot, skey=None):
        h, sq, ss, xn, pT = W["h"][slot], W["sq"], W["ss"][slot], W["xn"][slot], W["pT"][slot]
        hk, ssk, xnk, pk = ("h", slot), ("ss", slot), ("xn", slot), ("pT", slot)
        c.op("sp", [skey] if skey else [], [hk], lambda e: e.dma_start(out=h[:], in_=src_ap))
        c.op("act", [hk], ["sq", ssk], lambda e: e.activation(out=sq[:], in_=h[:], func=AF.Square, accum_out=ss[:]))
        self.rstd_from_ss(c, ss[:], ssk, D)
        c.op("dve", [hk, ssk, wkey], [xnk], lambda e: e.scalar_tensor_tensor(out=xn[:], in0=h[:], scalar=ss[:], in1=wbc, op0=ALU.mult, op1=ALU.mult))

        def tr(e):
            for k in range(8):
                ins = e.transpose(out=pT[:, k * 128:(k + 1) * 128], in_=xn[:, k * 128:(k + 1) * 128], identity=self.idb[:])
            return ins
        c.op("pe", [xnk, "idb"], [pk], tr)
        c.op("act", [pk], [dkey], lambda e: e.copy(out=dstT, in_=pT[:].rearrange("p (k t) -> p k t", k=8)))

    def alloc_prenorm(self, es, tag="", sq=None):
        nc = self.nc
        W = {"h": [_sb(es, nc, f"pn_h{i}{tag}", [128, D], F32) for i in range(2)],
             "sq": sq if sq is not None else _sb(es, nc, f"pn_sq{tag}", [128, D], F32),
             "ss": [_sb(es, nc, f"pn_ss{i}{tag}", [128, 1], F32) for i in range(2)],
             "xn": [_sb(es, nc, f"pn_xn{i}{tag}", [128, D], BF16) for i in range(2)],
             "pT": [_ps(es, nc, f"pn_pT{i}{tag}", [128, D], BF16) for i in range(2)]}
        return W

    def outproj_tile(self, c, W, zT_fn, zkeys, KC, wo, wokey, wpost, wpkey, res_ap, out_ap, slot, rkey=None, okey=None):
        py = W["py"][slot % len(W["py"])]
        pk = ("py", slot % len(W["py"]))
        hr, ss2, t1 = W["hr"][slot], W["ss2"][slot], W["t1"][slot]
        hrk, s2k, t1k = ("hr", slot), ("ss2", slot), ("t1", slot)

        def mm(e):
            for nb in range(2):
                for kc in range(KC):
                    ins = e.matmul(py[nb][:], lhsT=zT_fn(kc), rhs=wo[:, kc, nb * 512:(nb + 1) * 512], start=(kc == 0), stop=(kc == KC - 1))
            return ins
        c.op("pe", list(zkeys) + [wokey], [pk], mm)
        c.op("sp", [rkey] if rkey else [], [hrk], lambda e: e.dma_start(out=hr[:], in_=res_ap))
        c.op("act", [pk], ["sq2", (s2k, 0)], lambda e: e.activation(out=W["sq2"][:, 0:512], in_=py[0][:], func=AF.Square, accum_out=ss2[:, 0:1]))
        c.op("act", [pk], ["sq2", (s2k, 1)], lambda e: e.activation(out=W["sq2"][:, 512:1024], in_=py[1][:], func=AF.Square, accum_out=ss2[:, 1:2]))
        c.op("dve", [(s2k, 0), (s2k, 1)], [s2k], lambda e: e.tensor_tensor(out=ss2[:, 2:3], in0=ss2[:, 0:1], in1=ss2[:, 1:2], op=ALU.add))
        self.rstd_from_ss(c, ss2[:, 2:3], s2k, D)
        for nb in range(2):
            c.op("dve", [pk, s2k, wpkey], [(t1k, nb)], lambda e, nb=nb: e.scalar_tensor_tensor(out=t1[:, nb * 512:(nb + 1) * 512], in0=py[nb][:], scalar=ss2[:, 2:3], in1=wpost[:, nb * 512:(nb + 1) * 512], op0=ALU.mult, op1=ALU.mult))
        c.op("pool", [(t1k, 0), (t1k, 1), hrk], [hrk], lambda e: e.tensor_tensor(out=hr[:], in0=hr[:], in1=t1[:], op=ALU.add))
        c.op("gq", [hrk], [okey] if okey else [], lambda e: e.dma_start(out=out_ap, in_=hr[:]))

    def alloc_outproj(self, es, tag="", npy=2):
        nc = self.nc
        return {"py": [[_ps(es, nc, f"op_py{i}{j}{tag}", [128, 512], F32) for j in range(2)] for i in range(npy)],
                "hr": [_sb(es, nc, f"op_hr{i}{tag}", [128, D], F32) for i in range(2)],
                "ss2": [_sb(es, nc, f"op_ss{i}{tag}", [128, 4], F32) for i in range(2)],
                "t1": [_sb(es, nc, f"op_t1{i}{tag}", [128, D], F32) for i in range(2)],
                "sq2": _sb(es, nc, f"op_sq2{tag}", [128, D], F32)}

    def load_w_bf16(self, c, dst, key, src3):
        KC = dst.shape[1]
        N = dst.shape[2]
        step = max(1, 2048 // N)
        step = min(KC, 4)
        for k0 in range(0, KC, step):
            k1 = min(KC, k0 + step)
            for n0 in range(0, N, 2048):
                n1 = min(N, n0 + 2048)
                c.op("gq", [], [key], lambda e, k0=k0, k1=k1, n0=n0, n1=n1: e.dma_start(out=dst[:, k0:k1, n0:n1], in_=src3[:, k0:k1, n0:n1]))

    def ffn(self, c, L, src, dst):
        nc, S, NT = self.nc, self.S, self.NT
        Dr = self.P.dram
        with ExitStack() as es:
            wup = _sb(es, nc, "f_wup", [128, 8, 4096], BF16)
            wdn = _sb(es, nc, "f_wdn", [128, 32, 1024], BF16)
            wpre = _sb(es, nc, "f_wpre", [128, D], F32)
            wpost = _sb(es, nc, "f_wpost", [128, D], F32)
            self.load_bc(c, wpre[:], "f_wpre", Dr["ffn_norm_pre"][L:L + 1, :])
            self.load_bc(c, wpost[:], "f_wpost", Dr["ffn_norm_post"][L:L + 1, :])
            self.load_w_bf16(c, wup, "f_wup", Dr["ffn_w_up"][L].rearrange("(k p) n -> p k n", p=128))
            self.load_w_bf16(c, wdn, "f_wdn", Dr["ffn_w_down"][L].rearrange("(k p) n -> p k n", p=128))
            OP = self.alloc_outproj(es, "f")
            PN = self.alloc_prenorm(es, "f", sq=OP["sq2"])
            TB = 256
            hnT = [_sb(es, nc, f"f_hnT{i}", [128, 8, TB], BF16) for i in range(2)]
            aT = _sb(es, nc, "f_aT", [128, 32, TB], BF16)
            rl = [_sb(es, nc, f"f_rl{i}", [128, TB], F32) for i in range(2)]
            pu = [_ps(es, nc, f"f_pu{i}", [128, 512], F32) for i in range(2)]
            nblk = S // TB
            tpb = TB // 128
            for b in range(nblk):
                hb = hnT[b % 2]
                for t in range(tpb):
                    tt = b * tpb + t
                    self.prenorm_tile(c, PN, src[tt * 128:(tt + 1) * 128, :], "f_wpre", wpre[:], hb[:, :, t * 128:(t + 1) * 128], ("f_hnT", b % 2, t), tt % 2, skey=(src.tensor.name, tt))
                hkeys = [("f_hnT", b % 2, t) for t in range(tpb)]
                for fc in range(32):
                    p = pu[fc % 2]

                    def mm(e, fc=fc, p=p):
                        for kc in range(8):
                            ins = e.matmul(p[:, 0:TB], lhsT=wup[:, kc, fc * 128:(fc + 1) * 128], rhs=hb[:, kc, :], start=(kc == 0), stop=(kc == 7))
                        return ins
                    c.op("pe", hkeys + ["f_wup"], [("f_pu", fc % 2)], mm)
                    r = rl[fc % 2]
                    c.op("act", [("f_pu", fc % 2)], [("f_rl", fc % 2)], lambda e, p=p, r=r: e.activation(out=r[:], in_=p[:, 0:TB], func=AF.Relu))
                    c.op("dve", [("f_rl", fc % 2)], [("f_aT", fc)], lambda e, r=r, fc=fc: e.tensor_tensor(out=aT[:, fc, :], in0=r[:], in1=r[:], op=ALU.mult))
                akeys = [("f_aT", fc) for fc in range(32)]
                for t in range(tpb):
                    tt = b * tpb + t
                    self.outproj_tile(c, OP, lambda kc, t=t: aT[:, kc, t * 128:(t + 1) * 128], akeys, 32, wdn, "f_wdn", wpost, "f_wpost",
                                      src[tt * 128:(tt + 1) * 128, :], dst[tt * 128:(tt + 1) * 128, :], tt % 2,
                                      rkey=(src.tensor.name, tt), okey=(dst.tensor.name, tt))
            c.barrier()

    def xa(self, c, L, src, dst):
        nc, S, NT, Mm = self.nc, self.S, self.NT, self.Mm
        Dr = self.P.dram
        MC = Mm // 128
        with ExitStack() as es:
            wq = _sb(es, nc, "x_wq", [128, 8, 1024], BF16)
            wo = _sb(es, nc, "x_wo", [128, 8, 1024], BF16)
            wpre = _sb(es, nc, "x_wpre", [128, D], F32)
            wpost = _sb(es, nc, "x_wpost", [128, D], F32)
            wmem = _sb(es, nc, "x_wmem", [128, D], F32)
            self.load_bc(c, wpre[:], "x_wpre", Dr["xa_norm_pre"][L:L + 1, :])
            self.load_bc(c, wpost[:], "x_wpost", Dr["xa_norm_post"][L:L + 1, :])
            self.load_bc(c, wmem[:], "x_wmem", Dr["xa_mem_norm"][L:L + 1, :])
            self.load_w_bf16(c, wq, "x_wq", Dr["xa_w_q"][L].rearrange("(k p) n -> p k n", p=128))
            self.load_w_bf16(c, wo, "x_wo", Dr["xa_w_o"][L].rearrange("(k p) n -> p k n", p=128))
            PN = self.alloc_prenorm(es, "x")
            OP = self.alloc_outproj(es, "x", npy=1)
            kT = _sb(es, nc, "x_kT", [128, 8, Mm], BF16)
            V = _sb(es, nc, "x_V", [128, MC, 1024], BF16)
            pa = [_ps(es, nc, f"x_pa{i}", [128, 512], F32) for i in range(2)]
            with ExitStack() as es2:
                wkv = _sb(es2, nc, "x_wkv", [128, 8, 2048], BF16)
                self.load_w_bf16(c, wkv, "x_wkv", Dr["xa_w_kv"][L].rearrange("(k p) n -> p k n", p=128))
                mnT = _sb(es2, nc, "x_mnT", [128, 8, Mm], BF16)
                for mc in range(MC):
                    self.prenorm_tile(c, PN, Dr["mem"][mc * 128:(mc + 1) * 128, :], "x_wmem", wmem[:], mnT[:, :, mc * 128:(mc + 1) * 128], ("x_mnT", mc), mc % 2)
                mkeys = [("x_mnT", mc) for mc in range(MC)]
                for ch in range(8):
                    p = pa[ch % 2]

                    def mm(e, ch=ch, p=p):
                        for kc in range(8):
                            ins = e.matmul(p[:, 0:Mm], lhsT=wkv[:, kc, ch * 128:(ch + 1) * 128], rhs=mnT[:, kc, :], start=(kc == 0), stop=(kc == 7))
                        return ins
                    c.op("pe", mkeys + ["x_wkv"], [("x_pa", ch % 2)], mm)
                    c.op("act", [("x_pa", ch % 2)], ["x_kT"], lambda e, ch=ch, p=p: e.copy(out=kT[:, ch, :], in_=p[:, 0:Mm]))
                for mc in range(MC):
                    for nb in range(2):
                        p = pa[nb]

                        def mm(e, mc=mc, nb=nb, p=p):
                            for kc in range(8):
                                ins = e.matmul(p[:], lhsT=mnT[:, kc, mc * 128:(mc + 1) * 128], rhs=wkv[:, kc, 1024 + nb * 512:1024 + (nb + 1) * 512], start=(kc == 0), stop=(kc == 7))
                            return ins
                        c.op("pe", mkeys + ["x_wkv"], [("x_pa", nb)], mm)
                        c.op("act", [("x_pa", nb)], ["x_V"], lambda e, mc=mc, nb=nb, p=p: e.copy(out=V[:, mc, nb * 512:(nb + 1) * 512], in_=p[:]))
                c.barrier()
            hnT = [_sb(es, nc, f"x_hnT{i}", [128, 8, 128], BF16) for i in range(2)]
            qT = [_sb(es, nc, f"x_qT{i}", [128, 8, 128], BF16) for i in range(2)]
            oT = [_sb(es, nc, f"x_oT{i}", [128, 8, 128], BF16) for i in range(2)]
            sc = [_sb(es, nc, f"x_sc{i}", [128, Mm], F32) for i in range(2)]
            pb = [_sb(es, nc, f"x_pb{i}", [128, Mm], BF16) for i in range(2)]
            pTs = [_sb(es, nc, f"x_pTs{i}", [128, MC, 128], BF16) for i in range(2)]
            st = [_sb(es, nc, f"x_st{i}", [128, 4], F32) for i in range(2)]
            ps_s = [_ps(es, nc, f"x_pss{i}", [128, 512], F32) for i in range(1)]
            scale = 256.0 ** -0.5
            it = 0
            for tt in range(NT):
                sl = tt % 2
                self.prenorm_tile(c, PN, src[tt * 128:(tt + 1) * 128, :], "x_wpre", wpre[:], hnT[sl][:], ("x_hnT", sl), sl, skey=(src.tensor.name, tt))
                for half in range(2):
                    p = pa[half]

                    def mm(e, half=half, p=p, sl=sl):
                        for cc in range(4):
                            ch = half * 4 + cc
                            for kc in range(8):
                                ins = e.matmul(p[:, cc * 128:(cc + 1) * 128], lhsT=wq[:, kc, ch * 128:(ch + 1) * 128], rhs=hnT[sl][:, kc, :], start=(kc == 0), stop=(kc == 7))
                        return ins
                    c.op("pe", [("x_hnT", sl), "x_wq"], [("x_pa", half)], mm)
                    c.op("act", [("x_pa", half)], [("x_qT", sl, half)], lambda e, half=half, p=p, sl=sl: e.copy(out=qT[sl][:, half * 4:(half + 1) * 4, :], in_=p[:].rearrange("p (k t) -> p k t", k=4)))
                for hd in range(4):
                    i2 = it % 2
                    it += 1
                    pss = ps_s[0]

                    def mm(e, hd=hd, sl=sl, pss=pss):
                        for k2 in range(2):
                            ins = e.matmul(pss[:, 0:Mm], lhsT=qT[sl][:, hd * 2 + k2, :], rhs=kT[:, hd * 2 + k2, :], start=(k2 == 0), stop=(k2 == 1))
                        return ins
                    c.op("pe", [("x_qT", sl, hd // 2), "x_kT"], ["x_pss"], mm)
                    s_, sc_, pb_, pT_ = st[i2], sc[i2], pb[i2], pTs[i2]
                    sk, sck, pbk, pTk = ("x_st", i2), ("x_sc", i2), ("x_pb", i2), ("x_pTs", i2)
                    c.op("dve", ["x_pss"], [sk], lambda e, s_=s_, pss=pss: e.tensor_reduce(out=s_[:, 0:1], in_=pss[:, 0:Mm], axis=AX.X, op=ALU.max))
                    c.op("dve", [sk], [sk], lambda e, s_=s_: e.tensor_scalar(out=s_[:, 1:2], in0=s_[:, 0:1], scalar1=-scale, scalar2=None, op0=ALU.mult))
                    c.op("act", ["x_pss", sk], [sck, (sk, "sum")], lambda e, s_=s_, sc_=sc_, pss=pss: e.activation(out=sc_[:], in_=pss[:, 0:Mm], func=AF.Exp, bias=s_[:, 1:2], scale=scale, accum_out=s_[:, 2:3]))
                    c.op("dve", [(sk, "sum")], [(sk, "sum")], lambda e, s_=s_: e.reciprocal(out=s_[:, 3:4], in_=s_[:, 2:3]))
                    c.op("dve", [sck, (sk, "sum")], [pbk], lambda e, s_=s_, sc_=sc_, pb_=pb_: e.tensor_scalar(out=pb_[:], in0=sc_[:], scalar1=s_[:, 3:4], scalar2=None, op0=ALU.mult))
                    ptp = PN["pT"][i2]

                    def tr(e, pb_=pb_, ptp=ptp):
                        for mc in range(MC):
                            ins = e.transpose(out=ptp[:, mc * 128:(mc + 1) * 128], in_=pb_[:, mc * 128:(mc + 1) * 128], identity=self.idb[:])
                        return ins
                    c.op("pe", [pbk, "idb"], [("pT", i2)], tr)
                    c.op("act", [("pT", i2)], [pTk], lambda e, pT_=pT_, ptp=ptp: e.copy(out=pT_[:], in_=ptp[:, 0:Mm].rearrange("p (k t) -> p k t", k=MC)))
                    po = pa[hd % 2]

                    def mm2(e, hd=hd, pT_=pT_, po=po):
                        for d2 in range(2):
                            for mc in range(MC):
                                ins = e.matmul(po[:, d2 * 128:(d2 + 1) * 128], lhsT=V[:, mc, hd * 256 + d2 * 128: hd * 256 + (d2 + 1) * 128], rhs=pT_[:, mc, :], start=(mc == 0), stop=(mc == MC - 1))
                        return ins
                    c.op("pe", [pTk, "x_V"], [("x_pa", hd % 2)], mm2)
                    c.op("act", [("x_pa", hd % 2)], [("x_oT", sl, hd)], lambda e, hd=hd, po=po, sl=sl: e.copy(out=oT[sl][:, hd * 2:hd * 2 + 2, :], in_=po[:, 0:256].rearrange("p (k t) -> p k t", k=2)))
                okeys = [("x_oT", sl, hd) for hd in range(4)]
                self.outproj_tile(c, OP, lambda kc, sl=sl: oT[sl][:, kc, :], okeys, 8, wo, "x_wo", wpost, "x_wpost",
                                  src[tt * 128:(tt + 1) * 128, :], dst[tt * 128:(tt + 1) * 128, :], sl,
                                  rkey=(src.tensor.name, tt), okey=(dst.tensor.name, tt))
            c.barrier()

    def load_cols(self, c, dst, key, vec, n):
        for k in range(n):
            c.op("sp", [], [key], lambda e, k=k: e.dma_start(out=dst[:, k:k + 1], in_=vec[k * 128:(k + 1) * 128].rearrange("(p o) -> p o", o=1)))

    def stage_a_full(self, c, es, hnT, hkey, wrow, src, tag):
        nc = self.nc
        with ExitStack() as es2:
            PN = self.alloc_prenorm(es2, tag)
            wpre = _sb(es2, nc, f"{tag}_wpre", [128, D], F32)
            self.load_bc(c, wpre[:], f"{tag}_wpre", wrow)
            for tt in range(self.NT):
                self.prenorm_tile(c, PN, src[tt * 128:(tt + 1) * 128, :], f"{tag}_wpre", wpre[:], hnT[:, :, tt * 128:(tt + 1) * 128], (hkey, tt), tt % 2, skey=(src.tensor.name, tt))
            c.barrier()

    def stage_c_scr(self, c, L, zscr, KC, wout_ap, wpost_row, src, dst, tag):
        nc = self.nc
        with ExitStack() as es:
            wo = _sb(es, nc, f"{tag}_wo", [128, KC, 1024], BF16)
            wpost = _sb(es, nc, f"{tag}_wpost", [128, D], F32)
            self.load_bc(c, wpost[:], f"{tag}_wpost", wpost_row)
            self.load_w_bf16(c, wo, f"{tag}_wo", wout_ap.rearrange("(k p) n -> p k n", p=128))
            OP = self.alloc_outproj(es, tag)
            zt = [_sb(es, nc, f"{tag}_zt{i}", [128, KC, 512], BF16) for i in range(2)]
            for b in range(self.S // 512):
                z = zt[b % 2]
                c.op("sp", [(zscr.tensor.name, b)], [(f"{tag}_zt", b % 2)], lambda e, z=z, b=b: e.dma_start(out=z[:], in_=zscr[:, :, b * 512:(b + 1) * 512].rearrange("k p t -> p k t")))
                for t in range(4):
                    tt = b * 4 + t
                    self.outproj_tile(c, OP, lambda kc, z=z, t=t: z[:, kc, t * 128:(t + 1) * 128], [(f"{tag}_zt", b % 2)], KC, wo, f"{tag}_wo", wpost, f"{tag}_wpost",
                                      src[tt * 128:(tt + 1) * 128, :], dst[tt * 128:(tt + 1) * 128, :], tt % 2,
                                      rkey=(src.tensor.name, tt), okey=(dst.tensor.name, tt))
            c.barrier()

    def mixer_c(self, c, L, src, dst):
        nc, S, NT = self.nc, self.S, self.NT
        Dr = self.P.dram
        j = L // 2
        NB = S // 512
        yscr = self.P.dram.get("c_yscr")
        if yscr is None:
            yscr = self.P.dscr("c_yscr", [8, 128, S], BF16)
        with ExitStack() as es:
            hnT = _sb(es, nc, "c_hnT", [128, 8, S], BF16)
            self.stage_a_full(c, es, hnT, "c_hnT", Dr["mix_norm_pre"][L:L + 1, :], src, "c")
            hkeys = [("c_hnT", tt) for tt in range(NT)]
            cw = _sb(es, nc, "c_cw", [128, 4, 8], F32)
            for tap in range(4):
                self.load_cols(c, cw[:, tap, :], "c_cw", Dr["c_conv_w"][j, tap], 8)
            cb = _sb(es, nc, "c_cb", [128, 8], F32)
            self.load_cols(c, cb, "c_cb", Dr["c_conv_b"][j], 8)
            ba = _sb(es, nc, "c_ba", [128, 2, 8], F32)
            bx = _sb(es, nc, "c_bx", [128, 2, 8], F32)
            c1 = _sb(es, nc, "c_c1", [128, 2, 8], F32)
            for d_ in range(2):
                self.load_cols(c, ba[:, d_, :], "c_ba", Dr["c_b_a"][j, d_], 8)
                self.load_cols(c, bx[:, d_, :], "c_bx", Dr["c_b_x"][j, d_], 8)
                self.load_cols(c, c1[:, d_, :], "c_c1", Dr["c_lambda"][j, d_], 8)
            c.op("act", ["c_c1"], ["c_c1"], lambda e: e.activation(out=c1[:], in_=c1[:], func=AF.Exp, scale=-1.0))
            c.op("act", ["c_c1"], ["c_c1"], lambda e: e.activation(out=c1[:], in_=c1[:], func=AF.Ln, bias=1.0))
            c.op("dve", ["c_c1"], ["c_c1"], lambda e: e.tensor_scalar(out=c1[:], in0=c1[:], scalar1=-8.0, scalar2=None, op0=ALU.mult))
            xbf = _sb(es, nc, "c_xbf", [128, S + 3], F32)
            u = [_sb(es, nc, f"c_u{i}", [128, S], F32) for i in range(2)]
            ub = [_sb(es, nc, f"c_ub{i}", [128, S], BF16) for i in range(2)]
            af = _sb(es, nc, "c_af", [128, S], F32)
            inp = _sb(es, nc, "c_inp", [128, S], F32)
            acc = _sb(es, nc, "c_acc", [128, S], F32)
            win = [_sb(es, nc, f"c_win{i}", [128, 8, 128], BF16) for i in range(2)]
            wa = [_sb(es, nc, f"c_wa{i}", [128, 2, 128], BF16) for i in range(2)]
            wx = [_sb(es, nc, f"c_wx{i}", [128, 2, 128], BF16) for i in range(2)]
            tmp = {n: [_sb(es, nc, f"c_{n}{i}", [128, 512], F32) for i in range(2)] for n in ("r", "i", "mu")}
            yst = [_sb(es, nc, f"c_yst{i}", [128, 512], BF16) for i in range(2)]
            pp = [_ps(es, nc, f"c_pp{i}", [128, 512], F32) for i in range(4)]
            win_n = 0
            wg_n = 0
            pn = 0

            def inproj(col0, dst_fn, dkeys_fn):
                nonlocal win_n, pn
                w = win[win_n % 2]
                wk = ("c_win", win_n % 2)
                win_n += 1
                src3 = Dr["c_w_in"][j].rearrange("(k p) n -> p k n", p=128)[:, :, col0:col0 + 128]
                self.load_w_bf16(c, w, wk, src3)
                for tb in range(NB):
                    p = pp[pn % 2]
                    pk = ("c_pp", pn % 2)
                    pn += 1

                    def mm(e, p=p, w=w, tb=tb):
                        for kc in range(8):
                            ins = e.matmul(p[:], lhsT=w[:, kc, :], rhs=hnT[:, kc, tb * 512:(tb + 1) * 512], start=(kc == 0), stop=(kc == 7))
                        return ins
                    c.op("pe", hkeys[tb * 4:(tb + 1) * 4] + [wk], [pk], mm)
                    dst_fn(tb, p, pk)

            for blk in range(4):
                for c2 in range(2):
                    ch = blk * 2 + c2
                    c.op("pool", [], ["c_xbf_h"], lambda e: e.memset(xbf[:, 0:2], 0.0))
                    c.op("pool", [], ["c_xbf_h"], lambda e: e.memset(xbf[:, S + 2:S + 3], 0.0))

                    def ev(tb, p, pk):
                        c.op("act", [pk], [("c_xbf", tb)], lambda e: e.copy(out=xbf[:, 2 + tb * 512:2 + (tb + 1) * 512], in_=p[:]))
                    c.op("pool", ["c_hs"], ["c_hs"] + [("c_xbf", tb) for tb in range(NB)], lambda e: e.memset(xbf[:, 0:1], 0.0))
                    inproj(1024 + ch * 128, ev, None)
                    xk = [("c_xbf", tb) for tb in range(NB)] + ["c_xbf_h"]
                    uu = u[c2]
                    uk = ("c_u", c2)
                    c.op("dve", xk + ["c_cw", "c_cb"], [uk], lambda e, uu=uu, ch=ch: e.tensor_scalar(out=uu[:], in0=xbf[:, 0:S], scalar1=cw[:, 0, ch:ch + 1], scalar2=cb[:, ch:ch + 1], op0=ALU.mult, op1=ALU.add))
                    for tap in range(1, 4):
                        c.op("dve", xk + [uk], [uk], lambda e, uu=uu, ch=ch, tap=tap: e.scalar_tensor_tensor(out=uu[:], in0=xbf[:, tap:tap + S], scalar=cw[:, tap, ch:ch + 1], in1=uu[:], op0=ALU.mult, op1=ALU.add))
                    c.op("pool", [uk], [("c_ub", c2)], lambda e, uu=uu, c2=c2: e.tensor_copy(out=ub[c2][:], in_=uu[:]))
                    c.op("pool", xk, ["c_hs"], lambda e: e.memset(xbf[:, 0:1], 0.0))
                ubk = [("c_ub", 0), ("c_ub", 1)]
                for jc in range(2):
                    ch = blk * 2 + jc
                    for dr in range(2):
                        wa_, wx_ = wa[wg_n % 2], wx[wg_n % 2]
                        wak, wxk = ("c_wa", wg_n % 2), ("c_wx", wg_n % 2)
                        wg_n += 1
                        self.load_w_bf16(c, wa_, wak, Dr["c_w_a"][j, dr, blk].rearrange("(k p) n -> p k n", p=128)[:, :, jc * 128:(jc + 1) * 128])
                        self.load_w_bf16(c, wx_, wxk, Dr["c_w_x"][j, dr, blk].rearrange("(k p) n -> p k n", p=128)[:, :, jc * 128:(jc + 1) * 128])
                        for tb in range(NB):
                            sl = tb % 2
                            pa_, px_ = pp[2], pp[3]
                            ts = slice(tb * 512, (tb + 1) * 512)

                            def mm(e, w_=wa_, p_=pa_, ts=ts):
                                for kc in range(2):
                                    ins = e.matmul(p_[:], lhsT=w_[:, kc, :], rhs=ub[kc][:, ts], start=(kc == 0), stop=(kc == 1))
                                return ins
                            c.op("pe", ubk + [wak], [("c_pp", 2)], mm)

                            def mm2(e, w_=wx_, p_=px_, ts=ts):
                                for kc in range(2):
                                    ins = e.matmul(p_[:], lhsT=w_[:, kc, :], rhs=ub[kc][:, ts], start=(kc == 0), stop=(kc == 1))
                                return ins
                            c.op("pe", ubk + [wxk], [("c_pp", 3)], mm2)
                            r_, i_, mu_ = tmp["r"][sl], tmp["i"][sl], tmp["mu"][sl]
                            c.op("act", [("c_pp", 2), "c_ba"], [("c_r", sl)], lambda e, r_=r_, pa_=pa_, dr=dr, ch=ch: e.activation(out=r_[:], in_=pa_[:], func=AF.Sigmoid, bias=ba[:, dr, ch:ch + 1]))
                            c.op("act", [("c_pp", 3), "c_bx"], [("c_i", sl)], lambda e, i_=i_, px_=px_, dr=dr, ch=ch: e.activation(out=i_[:], in_=px_[:], func=AF.Sigmoid, bias=bx[:, dr, ch:ch + 1]))
                            c.op("act", [("c_r", sl), "c_c1"], [("c_af", tb)], lambda e, r_=r_, ts=ts, dr=dr, ch=ch: e.activation(out=af[:, ts], in_=r_[:], func=AF.Exp, scale=c1[:, dr, ch:ch + 1]))
                            c.op("dve", [("c_af", tb)], [("c_mu", sl)], lambda e, mu_=mu_, ts=ts: e.tensor_tensor(out=mu_[:], in0=af[:, ts], in1=af[:, ts], op=ALU.mult))
                            c.op("act", [("c_mu", sl)], [("c_mu", sl)], lambda e, mu_=mu_: e.activation(out=mu_[:], in_=mu_[:], func=AF.Sqrt, scale=-1.0, bias=1.0))
                            c.op("dve", [("c_mu", sl), ("c_i", sl)], [("c_mu", sl)], lambda e, mu_=mu_, i_=i_: e.tensor_tensor(out=mu_[:], in0=mu_[:], in1=i_[:], op=ALU.mult))
                            c.op("dve", [("c_mu", sl), ("c_u", jc)], [("c_inp", tb)], lambda e, mu_=mu_, ts=ts, jc=jc: e.tensor_tensor(out=inp[:, ts], in0=mu_[:], in1=u[jc][:, ts], op=ALU.mult))
                        afk = [("c_af", tb) for tb in range(NB)]
                        ink = [("c_inp", tb) for tb in range(NB)]
                        if dr == 0:
                            c.op("dve", afk + ink, ["c_acc"], lambda e: e.tensor_tensor_scan(out=acc[:], data0=af[:], data1=inp[:], initial=0.0, op0=ALU.mult, op1=ALU.add))
                        else:
                            c.op("dve", afk + ink, ["c_hs"], lambda e: e.tensor_tensor_scan(out=xbf[:, 0:S][:, ::-1], data0=af[:, ::-1], data1=inp[:, ::-1], initial=0.0, op0=ALU.mult, op1=ALU.add))
                            c.op("pool", ["c_hs", "c_acc"], ["c_acc"], lambda e: e.tensor_tensor(out=acc[:], in0=acc[:], in1=xbf[:, 0:S], op=ALU.add))

                    def evg(tb, p, pk, ch=ch):
                        sl = tb % 2
                        g1, g2, ys = tmp["r"][sl], tmp["i"][sl], yst[sl]
                        ts = slice(tb * 512, (tb + 1) * 512)
                        c.op("act", [pk], [("c_r", sl)], lambda e: e.activation(out=g1[:], in_=p[:], func=AF.Square))
                        c.op("dve", [("c_r", sl)], [("c_r", sl)], lambda e: e.tensor_scalar(out=g1[:], in0=g1[:], scalar1=0.044715, scalar2=1.0, op0=ALU.mult, op1=ALU.add))
                        c.op("dve", [("c_r", sl), pk], [("c_r", sl)], lambda e: e.tensor_tensor(out=g1[:], in0=g1[:], in1=p[:], op=ALU.mult))
                        c.op("act", [("c_r", sl)], [("c_i", sl)], lambda e: e.activation(out=g2[:], in_=g1[:], func=AF.Sigmoid, scale=1.5957691216057308))
                        c.op("dve", [("c_i", sl), pk], [("c_i", sl)], lambda e: e.tensor_tensor(out=g2[:], in0=g2[:], in1=p[:], op=ALU.mult))
                        c.op("dve", [("c_i", sl), "c_acc"], [("c_yst", sl)], lambda e: e.tensor_tensor(out=ys[:], in0=g2[:], in1=acc[:, ts], op=ALU.mult))
                        c.op("gq", [("c_yst", sl)], [("c_yscr", tb)], lambda e: e.dma_start(out=yscr[ch, :, ts], in_=ys[:]))
                    inproj(ch * 128, evg, None)
            c.barrier()
        self.stage_c_scr(c, L, yscr, 8, Dr["c_w_out"][j], Dr["mix_norm_post"][L:L + 1, :], src, dst, "cc")

    def mixer_ab(self, c, L, src, dst):
        nc, S, NT = self.nc, self.S, self.NT
        Dr = self.P.dram
        j = L // 2
        NB = S // 512
        P = self.P
        FM = P.dram.get("ab_fm") or P.dscr("ab_fm", [16, 128, S], F32)
        TM = {n: (P.dram.get("ab_" + n) or P.dscr("ab_" + n, [S, 512], F32)) for n in ("dnk", "dnv", "dnz", "mlk", "mlv", "mlo")}
        mixscr = P.dram.get("ab_mix") or P.dscr("ab_mix", [8, 128, S], BF16)
        I, ONES = self.I, self.ONES
        dirs = [dict(INC=self.LE, AFT=self.GT, STRICT=self.GT, INCLji=self.LE, NEGM=self.NLT),
                dict(INC=self.GE, AFT=self.LT, STRICT=self.LT, INCLji=self.GE, NEGM=self.NGT)]
        with ExitStack() as eo:
            gates = _sb(eo, nc, "ab_gates", [128, NT, 32], F32)
            Gg = _sb(eo, nc, "ab_Gg", [128, NT, 8], F32)
            Bt = _sb(eo, nc, "ab_Bt", [128, NT, 8], F32)
            nBt = _sb(eo, nc, "ab_nBt", [128, NT, 8], F32)
            Li = _sb(eo, nc, "ab_Li", [128, NT, 8], F32)
            Lf = _sb(eo, nc, "ab_Lf", [128, NT, 8], F32)
            Edn = _sb(eo, nc, "ab_Edn", [128, NT, 24], F32)
            Mlt = _sb(eo, nc, "ab_Mlt", [128, NT, 24], F32)
            Bk = _sb(eo, nc, "ab_Bk", [128, NT, 8], F32)
            Ws = _sb(eo, nc, "ab_Ws", [128, NT, 8], F32)
            prm = _sb(eo, nc, "ab_prm", [128, 4, 8], F32)
            dnw = _sb(eo, nc, "ab_dnw", [128, 128], F32)
            for i_, nm in enumerate(("dn_a_log", "dn_dt_bias", "ml_i_bias", "ml_f_bias")):
                self.load_bc(c, prm[:, i_, :], "ab_prm", Dr[nm][j:j + 1].rearrange("o d h -> o (d h)"))
            self.load_bc(c, dnw[:], "ab_dnw", Dr["dn_out_norm"][j:j + 1, :])
            with ExitStack() as es:
                hnT = _sb(es, nc, "ab_hnT", [128, 8, S], BF16)
                self.stage_a_full(c, es, hnT, "ab_hnT", Dr["mix_norm_pre"][L:L + 1, :], src, "ab")
                hkeys = [("ab_hnT", tt) for tt in range(NT)]
                win3 = Dr["ab_w_in"][j].rearrange("(k p) n -> p k n", p=128)
                wg = _sb(es, nc, "ab_wg", [128, 8, 32], BF16)
                c.op("gq", [], ["ab_wg"], lambda e: e.dma_start(out=wg[:, :, 0:16], in_=win3[:, :, 2048:2064]))
                c.op("gq", [], ["ab_wg"], lambda e: e.dma_start(out=wg[:, :, 16:32], in_=win3[:, :, 4112:4128]))
                pp = [_ps(es, nc, f"ab_pp{i}", [128, 512], F32) for i in range(4)]
                for tt in range(NT):
                    p = pp[tt % 2]

                    def mm(e, p=p, tt=tt):
                        for kc in range(8):
                            ins = e.matmul(p[:, 0:32], lhsT=hnT[:, kc, tt * 128:(tt + 1) * 128], rhs=wg[:, kc, :], start=(kc == 0), stop=(kc == 7))
                        return ins
                    c.op("pe", [hkeys[tt], "ab_wg"], [("ab_pp", tt % 2)], mm)
                    c.op("act", [("ab_pp", tt % 2)], ["ab_gates"], lambda e, p=p, tt=tt: e.copy(out=gates[:, tt, :], in_=p[:, 0:32]))
                def bc(i_):
                    return prm[:, i_, :].unsqueeze(1).broadcast_to([128, NT, 8])
                c.op("dve", ["ab_gates", "ab_prm"], ["ab_Gg"], lambda e: e.tensor_tensor(out=Gg[:], in0=gates[:, :, 0:8], in1=bc(1), op=ALU.add))
                c.op("act", ["ab_Gg"], ["ab_Gg"], lambda e: e.activation(out=Gg[:], in_=Gg[:], func=AF.Exp))
                c.op("act", ["ab_Gg"], ["ab_Gg"], lambda e: e.activation(out=Gg[:], in_=Gg[:], func=AF.Ln, bias=1.0))
                c.op("act", ["ab_prm"], ["ab_prm0"], lambda e: e.activation(out=prm[:, 0, :], in_=prm[:, 0, :], func=AF.Exp))
                c.op("dve", ["ab_Gg", "ab_prm0"], ["ab_Gg"], lambda e: e.scalar_tensor_tensor(out=Gg[:], in0=Gg[:], scalar=-1.0, in1=bc(0), op0=ALU.mult, op1=ALU.mult))
                c.op("act", ["ab_gates"], ["ab_Bt"], lambda e: e.activation(out=Bt[:], in_=gates[:, :, 8:16], func=AF.Sigmoid))
                c.op("dve", ["ab_Bt"], ["ab_nBt"], lambda e: e.tensor_scalar(out=nBt[:], in0=Bt[:], scalar1=-1.0, scalar2=None, op0=ALU.mult))
                c.op("dve", ["ab_gates", "ab_prm"], ["ab_Li"], lambda e: e.tensor_tensor(out=Li[:], in0=gates[:, :, 16:24], in1=bc(2), op=ALU.add))
                c.op("dve", ["ab_gates", "ab_prm"], ["ab_Lf"], lambda e: e.tensor_tensor(out=Lf[:], in0=gates[:, :, 24:32], in1=bc(3), op=ALU.add))
                c.op("act", ["ab_Lf"], ["ab_Lf"], lambda e: e.activation(out=Lf[:], in_=Lf[:], func=AF.Exp, scale=-1.0))
                c.op("act", ["ab_Lf"], ["ab_Lf"], lambda e: e.activation(out=Lf[:], in_=Lf[:], func=AF.Ln, bias=1.0))
                c.op("dve", ["ab_Lf"], ["ab_Lf"], lambda e: e.tensor_scalar(out=Lf[:], in0=Lf[:], scalar1=-1.0, scalar2=None, op0=ALU.mult))
                LE, GE, LT, GT = self.LE, self.GE, self.LT, self.GT
                for tt in range(NT):
                    p = pp[2 + tt % 2]

                    def mm(e, p=p, tt=tt):
                        for o_, T_ in ((0, Gg), (24, Lf)):
                            e.matmul(p[:, o_ + 0:o_ + 4], lhsT=LE, rhs=T_[:, tt, 0:4], start=True, stop=True)
                            e.matmul(p[:, o_ + 4:o_ + 8], lhsT=GE, rhs=T_[:, tt, 4:8], start=True, stop=True)
                            e.matmul(p[:, o_ + 8:o_ + 12], lhsT=GT, rhs=T_[:, tt, 0:4], start=True, stop=True)
                            e.matmul(p[:, o_ + 12:o_ + 16], lhsT=LT, rhs=T_[:, tt, 4:8], start=True, stop=True)
                            ins = e.matmul(p[:, o_ + 16:o_ + 24], lhsT=ONES, rhs=T_[:, tt, 0:8], start=True, stop=True)
                        return ins
                    c.op("pe", ["ab_Gg", "ab_Lf", "cst"], [("ab_pp", 2 + tt % 2)], mm)
                    c.op("act", [("ab_pp", 2 + tt % 2)], ["ab_Edn"], lambda e, p=p, tt=tt: e.activation(out=Edn[:, tt, :], in_=p[:, 0:24], func=AF.Exp))
                    c.op("act", [("ab_pp", 2 + tt % 2)], ["ab_Mlt"], lambda e, p=p, tt=tt: e.copy(out=Mlt[:, tt, :], in_=p[:, 24:48]))
                c.op("dve", ["ab_Bt", "ab_Edn"], ["ab_Bk"], lambda e: e.tensor_tensor(out=Bk[:], in0=Bt[:], in1=Edn[:, :, 0:8], op=ALU.mult))
                c.op("dve", ["ab_Li", "ab_Mlt"], ["ab_Ws"], lambda e: e.tensor_tensor(out=Ws[:], in0=Li[:], in1=Mlt[:, :, 8:16], op=ALU.add))
                xbf = _sb(es, nc, "ab_xbf", [128, S + 3], F32)
                uu = _sb(es, nc, "ab_u", [128, S], F32)
                cwt = [_sb(es, nc, f"ab_cwt{i}", [128, 4], F32) for i in range(2)]
                win = [_sb(es, nc, f"ab_win{i}", [128, 8, 128], BF16) for i in range(2)]
                sqb = [_sb(es, nc, f"ab_sqb{i}", [128, 512], F32) for i in range(2)]
                rsb = [_sb(es, nc, f"ab_rsb{i}", [128, 512], F32) for i in range(2)]
                stg = [_sb(es, nc, f"ab_stg{i}", [128, 4, 128], F32) for i in range(2)]
                c.op("pool", [], ["ab_xbf_h"], lambda e: e.memset(xbf[:, 0:2], 0.0))
                c.op("pool", [], ["ab_xbf_h"], lambda e: e.memset(xbf[:, S + 2:S + 3], 0.0))
                specs = []
                for h in range(4):
                    specs.append(dict(col=h * 128, conv=("dn_conv_w", h * 128), act=AF.Silu, l2=True, scale=128.0 ** -0.5, fm=h, tm=None))
                for h in range(4):
                    specs.append(dict(col=512 + h * 128, conv=("dn_conv_w", 512 + h * 128), act=AF.Silu, l2=True, scale=1.0, fm=4 + h, tm=("dnk", h)))
                for h in range(4):
                    specs.append(dict(col=1024 + h * 128, conv=("dn_conv_w", 1024 + h * 128), act=AF.Silu, l2=False, scale=None, fm=None, tm=("dnv", h)))
                for h in range(4):
                    specs.append(dict(col=1536 + h * 128, conv=None, act=AF.Silu, l2=False, scale=None, fm=None, tm=("dnz", h)))
                for h in range(4):
                    specs.append(dict(col=2064 + h * 128, conv=("ml_conv_w", h * 128), act=AF.Silu, l2=False, scale=None, fm=8 + h, tm=None))
                for h in range(4):
                    specs.append(dict(col=2576 + h * 128, conv=("ml_conv_w", 512 + h * 128), act=AF.Silu, l2=False, scale=128.0 ** -0.5, fm=12 + h, tm=("mlk", h)))
                for h in range(4):
                    specs.append(dict(col=3088 + h * 128, conv=None, act=None, l2=False, scale=None, fm=None, tm=("mlv", h)))
                for h in range(4):
                    specs.append(dict(col=3600 + h * 128, conv=None, act=AF.Sigmoid, l2=False, scale=None, fm=None, tm=("mlo", h)))
                pn = 0
                for si, sp in enumerate(specs):
                    w = win[si % 2]
                    wk = ("ab_win", si % 2)
                    self.load_w_bf16(c, w, wk, win3[:, :, sp["col"]:sp["col"] + 128])
                    xk = [("ab_xbf", tb) for tb in range(NB)]
                    for tb in range(NB):
                        p = pp[pn % 2]
                        pk = ("ab_pp", pn % 2)
                        pn += 1

                        def mm(e, p=p, w=w, tb=tb):
                            for kc in range(8):
                                ins = e.matmul(p[:], lhsT=w[:, kc, :], rhs=hnT[:, kc, tb * 512:(tb + 1) * 512], start=(kc == 0), stop=(kc == 7))
                            return ins
                        c.op("pe", hkeys[tb * 4:(tb + 1) * 4] + [wk], [pk], mm)
                        c.op("act", [pk], [("ab_xbf", tb)], lambda e, p=p, tb=tb: e.copy(out=xbf[:, 2 + tb * 512:2 + (tb + 1) * 512], in_=p[:]))
                    if sp["conv"] is not None:
                        cw_ = cwt[si % 2]
                        cwk = ("ab_cwt", si % 2)
                        nm, c0 = sp["conv"]
                        for tap in range(4):
                            c.op("sp", [], [cwk], lambda e, cw_=cw_, tap=tap, nm=nm, c0=c0: e.dma_start(out=cw_[:, tap:tap + 1], in_=Dr[nm][j, tap, c0:c0 + 128].rearrange("(p o) -> p o", o=1)))
                        c.op("dve", xk + ["ab_xbf_h", cwk], ["ab_u"], lambda e, cw_=cw_: e.tensor_scalar(out=uu[:], in0=xbf[:, 0:S], scalar1=cw_[:, 0:1], scalar2=None, op0=ALU.mult))
                        for tap in range(1, 4):
                            c.op("dve", xk + ["ab_xbf_h", cwk, "ab_u"], ["ab_u"], lambda e, cw_=cw_, tap=tap: e.scalar_tensor_tensor(out=uu[:], in0=xbf[:, tap:tap + S], scalar=cw_[:, tap:tap + 1], in1=uu[:], op0=ALU.mult, op1=ALU.add))
                        if sp["act"] is not None:
                            c.op("act", ["ab_u"], ["ab_u"], lambda e, f=sp["act"]: e.activation(out=uu[:], in_=uu[:], func=f))
                    else:
                        if sp["act"] is not None:
                            c.op("act", xk, ["ab_u"], lambda e, f=sp["act"]: e.activation(out=uu[:], in_=xbf[:, 2:S + 2], func=f))
                        else:
                            c.op("pool", xk, ["ab_u"], lambda e: e.tensor_copy(out=uu[:], in_=xbf[:, 2:S + 2]))
                    if sp["l2"]:
                        for tb in range(NB):
                            sl = tb % 2
                            ts = slice(tb * 512, (tb + 1) * 512)
                            c.op("act", ["ab_u"], [("ab_sqb", sl)], lambda e, sl=sl, ts=ts: e.activation(out=sqb[sl][:], in_=uu[:, ts], func=AF.Square))
                            p = pp[2 + sl]
                            c.op("pe", [("ab_sqb", sl), "cst"], [("ab_pp", 2 + sl)], lambda e, p=p, sl=sl: e.matmul(p[:], lhsT=ONES, rhs=sqb[sl][:], start=True, stop=True))
                            c.op("act", [("ab_pp", 2 + sl)], [("ab_rsb", sl)], lambda e, p=p, sl=sl: e.activation(out=rsb[sl][:], in_=p[:], func=AF.Sqrt, bias=1e-6))
                            c.op("dve", [("ab_rsb", sl)], [("ab_rsb", sl)], lambda e, sl=sl: e.reciprocal(out=rsb[sl][:], in_=rsb[sl][:]))
                            c.op("dve", [("ab_rsb", sl), "ab_u"], ["ab_u"], lambda e, sl=sl, ts=ts, sc_=sp["scale"]: e.scalar_tensor_tensor(out=uu[:, ts], in0=uu[:, ts], scalar=sc_, in1=rsb[sl][:], op0=ALU.mult, op1=ALU.mult))
                    elif sp["scale"] is not None:
                        c.op("dve", ["ab_u"], ["ab_u"], lambda e, sc_=sp["scale"]: e.tensor_scalar(out=uu[:], in0=uu[:], scalar1=sc_, scalar2=None, op0=ALU.mult))
                    if sp["fm"] is not None:
                        c.op("sp", ["ab_u"], [("ab_fm", sp["fm"])], lambda e, f=sp["fm"]: e.dma_start(out=FM[f], in_=uu[:]))
                    if sp["tm"] is not None:
                        nm, h = sp["tm"]
                        for tb in range(NB):
                            sl = tb % 2
                            p = pp[2 + sl]

                            def tr(e, p=p, tb=tb):
                                for t4 in range(4):
                                    ins = e.transpose(out=p[:, t4 * 128:(t4 + 1) * 128], in_=uu[:, tb * 512 + t4 * 128: tb * 512 + (t4 + 1) * 128], identity=I)
                                return ins
                            c.op("pe", ["ab_u", "cst"], [("ab_pp", 2 + sl)], tr)
                            c.op("act", [("ab_pp", 2 + sl)], [("ab_stg", sl)], lambda e, p=p, sl=sl: e.copy(out=stg[sl][:], in_=p[:].rearrange("p (t d) -> p t d", t=4)))
                            c.op("gq", [("ab_stg", sl)], [("ab_tm", nm, h)], lambda e, sl=sl, tb=tb, nm=nm, h=h: e.dma_start(out=TM[nm][tb * 512:(tb + 1) * 512, h * 128:(h + 1) * 128].rearrange("(t p) d -> p t d", p=128), in_=stg[sl][:]))
                c.barrier()
            import os as _os
            _algs = tuple(a for a in _os.environ.get("AB_ALGS", "dn,ml").split(",") if a)
            WIN = int(_os.environ.get("AB_WIN", "4"))
            with ExitStack() as es:
                qT = _sb(es, nc, "r_qT", [128, S], F32)
                ktok = _sb(es, nc, "r_ktok", [128, NT, 128], F32)
                vtok = _sb(es, nc, "r_vtok", [128, NT, 129], F32)
                ost = _sb(es, nc, "r_ost", [128, NT, 128], F32)
                gtok = qT[:].rearrange("p (t d) -> p t d", d=128)
                dn_names = ["Gmat", "Gle", "eD", "eDT", "egb", "t1", "t2", "attnT", "qd", "Xv", "Xk", "kd", "u", "wT", "vn"] + \
                           [f"P{k}" for k in range(2)] + [f"PT{k}" for k in range(2)] + [f"R{k}" for k in range(2)]
                F32R = mybir.dt.float32r
                cstr = _sb(es, nc, "r_cstr", [128, 8, 128], F32)
                c.op("dve", ["cst"], ["r_cstr"], lambda e: e.tensor_copy(out=cstr[:].bitcast(F32R), in_=self.cst[:]))
                mr = {"LE": cstr[:, 1, :].bitcast(F32R), "GE": cstr[:, 2, :].bitcast(F32R), "LT": cstr[:, 3, :].bitcast(F32R), "GT": cstr[:, 4, :].bitcast(F32R)}
                ONESr = cstr[:, 5, :].bitcast(F32R)
                dirs_r = [dict(INC=mr["LE"], AFT=mr["GT"]), dict(INC=mr["GE"], AFT=mr["LT"])]
                rset = set([f"P{k}" for k in range(7)] + [f"PT{k}" for k in range(6)] + ["R0", "R1", "Xv", "Xk"])
                bfn = set(["wT", "qd", "attnT", "kd", "vn"])
                wt = {n: [_sb(es, nc, f"r_{n}{i}", [128, 128], BF16 if n in bfn else F32) for i in range(WIN)] for n in dn_names}
                qTb = _sb(es, nc, "r_qTb", [128, S], BF16)
                kTb = _sb(es, nc, "r_kTb", [128, S], BF16)
                vtokb = _sb(es, nc, "r_vtokb", [128, NT, 129], BF16)
                Ssh = [[_sb(es, nc, f"r_Ssh{d_}{i}", [128, 129], BF16) for i in range(2)] for d_ in range(2)]
                pTm = [_sb(es, nc, f"r_pTm{i}", [128, 128], BF16) for i in range(WIN)]
                pm = [_sb(es, nc, f"r_pm{i}", [128, 128], BF16) for i in range(WIN)]
                ksm = [_sb(es, nc, f"r_ksm{i}", [128, 128], BF16) for i in range(WIN)]
                alias = {"X": "Gmat", "e": "eD", "p": "eDT", "pT": "egb", "ks": "t1"}
                dmall = _sb(es, nc, "r_dmall", [128, 2, NT, 128], F32)
                mlc = _sb(es, nc, "r_mlc", [128, 12, 2, NT], F32)
                zc = _sb(es, nc, "r_zc", [128, 1], F32)
                c.op("pool", [], ["r_zc"], lambda e: e.memset(zc[:], 0.0))
                nd = [_sb(es, nc, f"r_nd{i}", [128, 129], F32) for i in range(WIN)]
                dcol = [_sb(es, nc, f"r_dcol{i}", [128, 4], F32) for i in range(WIN)]
                Sst = [[_sb(es, nc, f"r_S{d_}{i}", [128, 129], F32) for i in range(2)] for d_ in range(2)]
                ob = [_sb(es, nc, f"r_ob{i}", [128, 128], BF16) for i in range(2)]
                ot = [_sb(es, nc, f"r_ot{i}", [128, 128], F32) for i in range(2)]
                oss = [_sb(es, nc, f"r_oss{i}", [128, 2], F32) for i in range(2)]
                mst4 = [_sb(es, nc, f"r_mx{i}", [128, 512], BF16) for i in range(2)]
                pb = [_ps(es, nc, f"r_pb{i}", [128, 512], F32) for i in range(7)]
                pTb = _ps(es, nc, "r_pTb", [128, 1024], BF16)

                def Q(b, q, n=128):
                    return pb[b][:, q * 128:q * 128 + n]

                def K_(b, q):
                    return ("r_pb", b, q)

                STAG = int(_os.environ.get("AB_STAG", "0"))

                def run_units(gens, stag=None):
                    stag = STAG if stag is None else stag
                    active = []
                    it = iter(gens)
                    done = False
                    since = stag
                    while True:
                        if (not done) and len(active) < WIN and (since >= stag or not active):
                            g = next(it, None)
                            if g is None:
                                done = True
                            else:
                                active.append(g)
                                since = 0
                        if not active:
                            if done:
                                break
                            continue
                        since += 1
                        for g in list(active):
                            try:
                                next(g)
                            except StopIteration:
                                active.remove(g)

                free = list(range(WIN))
                turn = [0, 0]

                def dn_unit(h, dr, si, t):
                    M = dirs[dr]
                    col = dr * 4 + h
                    sl_ = free.pop()
                    W = {n: wt[n][sl_] for n in dn_names}
                    for kq in range(7):
                        W[f"P{kq}"] = wt[f"P{kq % 2}"][sl_]
                    for kq in range(6):
                        W[f"PT{kq}"] = wt[f"PT{kq % 2}"][sl_]
                    ts = slice(t * 128, (t + 1) * 128)
                    ab_, vb_, sb_ = 2 * (sl_ % 2), 2 * (sl_ % 2) + 1, 4 + dr

                    def Wr(n):
                        return W[n][:].bitcast(F32R)

                    def k(n):
                        if n[0] == "P" and n[-1].isdigit():
                            n = n[:-1] + str(int(n[-1]) % 2)
                        return ("r_" + n, sl_)
                    Sc, Sn = Sst[dr][si % 2], Sst[dr][(si + 1) % 2]
                    Sck, Snk = ("r_S", dr, si % 2), ("r_S", dr, (si + 1) % 2)
                    gcol = Gg[:, t, col:col + 1]
                    c.op("dve", ["ab_Gg"], [k("Gmat")], lambda e: e.tensor_scalar(out=Wr("Gmat"), in0=M["AFT"], scalar1=gcol, scalar2=None, op0=ALU.mult))
                    c.op("act", ["ab_Gg"], [k("Gle")], lambda e: e.activation(out=Wr("Gle"), in_=M["INC"], func=AF.Copy, scale=gcol))
                    yield
                    c.op("act", ["r_vtok", "ab_Bt"], [k("Xv")], lambda e: e.activation(out=Wr("Xv"), in_=vtok[:, t, 0:128], func=AF.Copy, scale=Bt[:, t, col:col + 1]))
                    c.op("act", ["r_ktok", "ab_Bk"], [k("Xk")], lambda e: e.activation(out=Wr("Xk"), in_=ktok[:, t, :], func=AF.Copy, scale=Bk[:, t, col:col + 1]))
                    c.op("act", ["r_ktok", "ab_Edn"], [k("kd")], lambda e: e.activation(out=W["kd"][:], in_=ktok[:, t, :], func=AF.Copy, scale=Edn[:, t, 8 + col:8 + col + 1]))
                    yield

                    def mmA(e):
                        Mr = dirs_r[dr]
                        e.matmul(Q(ab_, 0), lhsT=Mr["INC"], rhs=Wr("Gmat"), start=True, stop=True)
                        e.matmul(Q(ab_, 2), lhsT=ONESr, rhs=Wr("Gle"), start=True, stop=True)
                        return e.matmul(Q(ab_, 1), lhsT=Wr("Gmat"), rhs=Mr["INC"], start=True, stop=True)
                    c.op("pe", [k("Gmat"), k("Gle"), "r_cstr"], [K_(ab_, 0), K_(ab_, 1), K_(ab_, 2)], mmA)
                    c.op("act", [K_(ab_, 0)], [k("eD")], lambda e: e.activation(out=W["eD"][:], in_=Q(ab_, 0), func=AF.Exp))
                    c.op("act", [K_(ab_, 1)], [k("eDT")], lambda e: e.activation(out=W["eDT"][:], in_=Q(ab_, 1), func=AF.Exp))
                    c.op("act", [K_(ab_, 2)], [k("egb")], lambda e: e.activation(out=W["egb"][:], in_=Q(ab_, 2), func=AF.Exp))
                    yield
                    c.op("pool", [k("eD")], [k("t1")], lambda e: e.tensor_tensor(out=W["t1"][:], in0=W["eD"][:], in1=M["STRICT"], op=ALU.mult))
                    c.op("pool", [k("eDT")], [k("t2")], lambda e: e.tensor_tensor(out=W["t2"][:], in0=W["eDT"][:], in1=M["INCLji"], op=ALU.mult))
                    yield
                    c.op("pool", [k("egb"), "r_qTb"], [k("qd")], lambda e: e.tensor_tensor(out=W["qd"][:], in0=qTb[:, ts], in1=W["egb"][:], op=ALU.mult))
                    yield

                    def mmB(e):
                        e.matmul(Q(vb_, 1), lhsT=kTb[:, ts], rhs=kTb[:, ts], start=True, stop=True)
                        return e.matmul(Q(vb_, 0), lhsT=kTb[:, ts], rhs=qTb[:, ts], start=True, stop=True)
                    c.op("pe", ["r_kTb", "r_qTb"], [K_(vb_, 1), K_(vb_, 0)], mmB)
                    c.op("dve", [K_(vb_, 1), k("t1"), "ab_nBt"], [k("P0")], lambda e: e.scalar_tensor_tensor(out=Wr("P0"), in0=Q(vb_, 1), scalar=nBt[:, t, col:col + 1], in1=W["t1"][:], op0=ALU.mult, op1=ALU.mult))
                    c.op("dve", [K_(vb_, 0), k("t2")], [k("attnT")], lambda e: e.tensor_tensor(out=W["attnT"][:], in0=Q(vb_, 0), in1=W["t2"][:], op=ALU.mult))
                    yield
                    if "P0" in bfn:
                        qv = Q(vb_, 2).bitcast(BF16)[:, 0:128]
                        c.op("pe", [k("P0"), "idb"], [K_(vb_, 2)], lambda e: e.transpose(out=qv, in_=W["P0"][:], identity=self.idb[:]))
                        c.op("dve", [K_(vb_, 2)], [k("PT0")], lambda e: e.tensor_copy(out=W["PT0"][:], in_=qv))
                    else:
                        c.op("pe", [k("P0")], [K_(vb_, 2)], lambda e: e.transpose(out=Q(vb_, 2), in_=W["P0"][:], identity=I))
                        c.op("dve", [K_(vb_, 2)], [k("PT0")], lambda e: e.tensor_copy(out=Wr("PT0"), in_=Q(vb_, 2)))
                    c.op("dve", [k("PT0")], [k("R0")], lambda e: e.tensor_tensor(out=Wr("R0"), in0=W["PT0"][:], in1=I, op=ALU.add))
                    yield
                    rc = "R0"
                    for kk in range(1, 7):
                        q2 = kk % 2
                        c.op("pe", [k(f"P{kk-1}"), k(f"PT{kk-1}")], [K_(ab_, q2)], lambda e, kk=kk, q2=q2: e.matmul(Q(ab_, q2), lhsT=Wr(f"PT{kk-1}"), rhs=Wr(f"P{kk-1}"), start=True, stop=True))
                        c.op("act", [K_(ab_, q2)], [k(f"P{kk}")], lambda e, kk=kk, q2=q2: e.copy(out=Wr(f"P{kk}"), in_=Q(ab_, q2)))
                        yield
                        if kk < 6:
                            c.op("pe", [k(f"P{kk-1}"), k(f"PT{kk-1}")], [K_(vb_, 2 + q2)], lambda e, kk=kk, q2=q2: e.matmul(Q(vb_, 2 + q2), lhsT=Wr(f"P{kk-1}"), rhs=Wr(f"PT{kk-1}"), start=True, stop=True))
                            c.op("dve", [K_(vb_, 2 + q2)], [k(f"PT{kk}")], lambda e, kk=kk, q2=q2: e.tensor_copy(out=Wr(f"PT{kk}"), in_=Q(vb_, 2 + q2)))
                            yield
                        rn = "R1" if rc == "R0" else "R0"
                        c.op("pe", [k(f"P{kk}"), k(rc)], [K_(vb_, q2)], lambda e, kk=kk, rc=rc, q2=q2: e.matmul(Q(vb_, q2), lhsT=Wr(f"P{kk}"), rhs=Wr(rc), start=True, stop=True))
                        c.op("dve", [K_(vb_, q2), k(rc)], [k(rn)], lambda e, rc=rc, rn=rn, q2=q2: e.tensor_tensor(out=Wr(rn), in0=W[rc][:], in1=Q(vb_, q2), op=ALU.add))
                        yield
                        rc = rn
                    c.op("pe", [k(rc), k("Xv")], [K_(ab_, 0)], lambda e: e.matmul(Q(ab_, 0), lhsT=Wr(rc), rhs=Wr("Xv"), start=True, stop=True))
                    c.op("act", [K_(ab_, 0)], [k("u")], lambda e: e.copy(out=W["u"][:], in_=Q(ab_, 0)))
                    yield
                    c.op("pe", [k(rc), k("Xk")], [K_(ab_, 1)], lambda e: e.matmul(Q(ab_, 1), lhsT=Wr("Xk"), rhs=Wr(rc), start=True, stop=True))
                    c.op("act", [K_(ab_, 1)], [k("wT")], lambda e: e.copy(out=W["wT"][:], in_=Q(ab_, 1)))
                    yield
                    while turn[dr] != si:
                        yield
                    Sbc, Sbn = Ssh[dr][si % 2], Ssh[dr][(si + 1) % 2]
                    Sbck, Sbnk = ("r_Ssh", dr, si % 2), ("r_Ssh", dr, (si + 1) % 2)
                    c.op("pe", [k("wT"), Sbck], [K_(sb_, 0)], lambda e: e.matmul(Q(sb_, 0), lhsT=W["wT"][:], rhs=Sbc[:, 0:128], start=True, stop=True))
                    c.op("dve", [K_(sb_, 0), k("u")], [k("vn")], lambda e: e.tensor_tensor(out=W["vn"][:], in0=W["u"][:], in1=Q(sb_, 0), op=ALU.subtract))

                    def mmo(e):
                        e.matmul(Q(sb_, 2), lhsT=W["qd"][:], rhs=Sbc[:, 0:128], start=True, stop=False)
                        e.matmul(Q(sb_, 2), lhsT=W["attnT"][:], rhs=W["vn"][:], start=False, stop=True)
                        return e.matmul(Q(sb_, 1), lhsT=W["kd"][:], rhs=W["vn"][:], start=True, stop=True)
                    c.op("pe", [k("qd"), k("attnT"), k("vn"), k("kd"), Sbck], [K_(sb_, 2), K_(sb_, 1)], mmo)
                    c.op("dve", [K_(sb_, 1), Sck, "ab_Edn"], [Snk], lambda e: e.scalar_tensor_tensor(out=Sn[:, 0:128], in0=Sc[:, 0:128], scalar=Edn[:, t, 16 + col:16 + col + 1], in1=Q(sb_, 1), op0=ALU.mult, op1=ALU.add))
                    c.op("act", [Snk], [Sbnk], lambda e: e.copy(out=Sbn[:, 0:128], in_=Sn[:, 0:128]))
                    c.op("dve", [K_(sb_, 2), ("r_ost", t)], [("r_ost", t)], lambda e: e.tensor_tensor(out=ost[:, t, :], in0=ost[:, t, :], in1=Q(sb_, 2), op=ALU.add))
                    turn[dr] += 1
                    free.append(sl_)

                MI, MS, B1, MNEW, A1, MT, NEGM, INTER, EMT, DEC, SRC, TMP = range(12)

                def ml_prep(h, dr, si, t):
                    M = dirs[dr]
                    col = dr * 4 + h
                    sl_ = free.pop()
                    X = wt[alias["X"]][sl_]
                    xk = ("r_" + alias["X"], sl_)
                    vb_ = 2 * (sl_ % 2) + 1
                    lf, li = Lf[:, t, col:col + 1], Li[:, t, col:col + 1]
                    Xr = X[:].bitcast(F32R)
                    c.op("dve", ["ab_Lf"], [xk], lambda e: e.tensor_scalar(out=Xr, in0=M["AFT"], scalar1=lf, scalar2=None, op0=ALU.mult))
                    c.op("dve", ["ab_Li", xk], [xk], lambda e: e.scalar_tensor_tensor(out=Xr, in0=I, scalar=li, in1=X[:], op0=ALU.mult, op1=ALU.add))
                    yield

                    def mmA(e):
                        Mr = dirs_r[dr]
                        e.matmul(Q(vb_, 0), lhsT=Mr["INC"], rhs=Xr, start=True, stop=True)
                        return e.matmul(Q(vb_, 1), lhsT=ONESr, rhs=Xr, start=True, stop=True)
                    c.op("pe", [xk, "r_cstr"], [K_(vb_, 0), K_(vb_, 1)], mmA)
                    c.op("dve", [K_(vb_, 0)], [("r_dmall", dr, t)], lambda e: e.tensor_tensor(out=dmall[:, dr, t, :], in0=Q(vb_, 0), in1=M["NEGM"], op=ALU.add))
                    c.op("dve", [K_(vb_, 1)], [("r_mlc", MS, dr)], lambda e: e.tensor_reduce(out=mlc[:, MS, dr, t:t + 1], in_=Q(vb_, 1), axis=AX.X, op=ALU.max))
                    yield
                    c.op("dve", [("r_dmall", dr, t)], [("r_mlc", MI, dr)], lambda e: e.tensor_reduce(out=mlc[:, MI, dr, t:t + 1], in_=dmall[:, dr, t, :], axis=AX.X, op=ALU.max))
                    free.append(sl_)

                def ml_main(h, dr, si, t):
                    col = dr * 4 + h
                    sl_ = free.pop()
                    ts = slice(t * 128, (t + 1) * 128)
                    e_ = wt[alias["e"]][sl_]
                    ek = ("r_" + alias["e"], sl_)
                    p_, pT_, ks_ = pm[sl_], pTm[sl_], ksm[sl_]
                    pk_, pTk, ksk = ("r_pm", sl_), ("r_pTm", sl_), ("r_ksm", sl_)
                    Cbc, Cbn = Ssh[dr][si % 2], Ssh[dr][(si + 1) % 2]
                    Cbck, Cbnk = ("r_Ssh", dr, si % 2), ("r_Ssh", dr, (si + 1) % 2)
                    Cc, Cn = Sst[dr][si % 2], Sst[dr][(si + 1) % 2]
                    Cck, Cnk = ("r_S", dr, si % 2), ("r_S", dr, (si + 1) % 2)
                    ab_, vb_, sb_ = 2 * (sl_ % 2), 2 * (sl_ % 2) + 1, 4 + dr

                    def col_(i_):
                        return mlc[:, i_, dr, t:t + 1]
                    c.op("act", [("r_dmall", dr, t), ("r_mlc", NEGM, dr)], [ek], lambda e: e.activation(out=e_[:], in_=dmall[:, dr, t, :], func=AF.Exp, bias=col_(NEGM)))
                    c.op("act", ["r_ktok", ("r_mlc", SRC, dr)], [ksk], lambda e: e.activation(out=ks_[:], in_=ktok[:, t, :], func=AF.Copy, scale=col_(SRC)))
                    yield
                    c.op("pe", ["r_qTb", "r_kTb"], [K_(vb_, 2)], lambda e: e.matmul(Q(vb_, 2), lhsT=qTb[:, ts], rhs=kTb[:, ts], start=True, stop=True))
                    c.op("dve", [ek, K_(vb_, 2)], [pk_], lambda e: e.tensor_tensor(out=p_[:], in0=e_[:], in1=Q(vb_, 2), op=ALU.mult))
                    yield
                    qv = Q(ab_, 0).bitcast(BF16)[:, 0:128]
                    c.op("pe", [pk_, "idb"], [K_(ab_, 0)], lambda e: e.transpose(out=qv, in_=p_[:], identity=self.idb[:]))
                    c.op("act", [K_(ab_, 0)], [pTk], lambda e: e.copy(out=pT_[:], in_=qv))
                    yield
                    c.op("pe", [pTk, "r_vtokb"], [K_(ab_, 2)], lambda e: e.matmul(pb[ab_][:, 256:385], lhsT=pT_[:], rhs=vtokb[:, t, :], start=True, stop=True))
                    c.op("dve", [K_(ab_, 2)], [("r_nd", sl_)], lambda e: e.tensor_copy(out=nd[sl_][:], in_=pb[ab_][:, 256:385]))
                    yield
                    while turn[dr] != si:
                        yield

                    def mms(e):
                        e.matmul(pb[sb_][:, 0:129], lhsT=qTb[:, ts], rhs=Cbc[:, 0:129], start=True, stop=True)
                        return e.matmul(pb[sb_][:, 256:385], lhsT=ks_[:], rhs=vtokb[:, t, :], start=True, stop=True)
                    c.op("pe", ["r_qTb", Cbck, ksk, "r_vtokb"], [K_(sb_, 0), K_(sb_, 2)], mms)
                    c.op("dve", [K_(sb_, 2), Cck, ("r_mlc", DEC, dr)], [Cnk], lambda e: e.scalar_tensor_tensor(out=Cn[:], in0=Cc[:], scalar=col_(DEC), in1=pb[sb_][:, 256:385], op0=ALU.mult, op1=ALU.add))
                    c.op("act", [Cnk], [Cbnk], lambda e: e.copy(out=Cbn[:], in_=Cn[:]))
                    c.op("dve", [K_(sb_, 0), ("r_mlc", INTER, dr), ("r_nd", sl_)], [("r_nd", sl_)], lambda e: e.scalar_tensor_tensor(out=nd[sl_][:], in0=pb[sb_][:, 0:129], scalar=col_(INTER), in1=nd[sl_][:], op0=ALU.mult, op1=ALU.add))
                    turn[dr] += 1
                    yield
                    dc = dcol[sl_]
                    dk = ("r_dcol", sl_)
                    c.op("dve", [("r_nd", sl_)], [dk], lambda e: e.tensor_scalar(out=dc[:, 0:1], in0=nd[sl_][:, 128:129], scalar1=-1.0, scalar2=None, op0=ALU.mult))
                    yield
                    c.op("dve", [("r_nd", sl_), dk], [dk], lambda e: e.tensor_tensor(out=dc[:, 1:2], in0=nd[sl_][:, 128:129], in1=dc[:, 0:1], op=ALU.max))
                    yield
                    c.op("dve", [dk, ("r_mlc", EMT, dr)], [dk], lambda e: e.tensor_tensor(out=dc[:, 2:3], in0=dc[:, 1:2], in1=col_(EMT), op=ALU.max))
                    yield
                    c.op("dve", [dk], [dk], lambda e: e.reciprocal(out=dc[:, 3:4], in_=dc[:, 2:3]))
                    yield
                    c.op("dve", [("r_nd", sl_), dk, ("r_ost", t)], [("r_ost", t)], lambda e: e.scalar_tensor_tensor(out=ost[:, t, :], in0=nd[sl_][:, 0:128], scalar=dc[:, 3:4], in1=ost[:, t, :], op0=ALU.mult, op1=ALU.add))
                    free.append(sl_)

                for alg in _algs:
                    for h in range(int(_os.environ.get("AB_NH", "4"))):
                        if alg == "dn":
                            fq, fk, tk, tv, tg = h, 4 + h, "dnk", "dnv", "dnz"
                        else:
                            fq, fk, tk, tv, tg = 8 + h, 12 + h, "mlk", "mlv", "mlo"
                        c.op("sp", [("ab_fm", fk)], ["r_qT"], lambda e: e.dma_start(out=qT[:], in_=FM[fk]))
                        c.op("act", ["r_qT"], ["r_kTb"], lambda e: e.copy(out=kTb[:], in_=qT[:]))
                        c.op("sp", [("ab_fm", fq)], ["r_qT"], lambda e: e.dma_start(out=qT[:], in_=FM[fq]))
                        c.op("pool", ["r_qT"], ["r_qTb"], lambda e: e.tensor_copy(out=qTb[:], in_=qT[:]))
                        for t0 in range(0, NT, 4):
                            t1_ = min(NT, t0 + 4)
                            for (dst_, nm_, ky_) in ((ktok, tk, "r_ktok"), (vtok, tv, "r_vtok")):
                                c.op("sp", [("ab_tm", nm_, h)], [ky_], lambda e, dst_=dst_, nm_=nm_, t0=t0, t1_=t1_: e.dma_start(out=dst_[:, t0:t1_, 0:128], in_=TM[nm_][t0 * 128:t1_ * 128, h * 128:(h + 1) * 128].rearrange("(t p) d -> p t d", p=128)))
                        c.op("pool", ["r_vtok"], ["r_vtok1"], lambda e: e.memset(vtok[:, :, 128:129], 1.0))
                        c.op("dve", ["r_vtok", "r_vtok1"], ["r_vtokb"], lambda e: e.tensor_copy(out=vtokb[:], in_=vtok[:]))
                        c.op("pool", [("r_ost", t) for t in range(NT)], [("r_ost", t) for t in range(NT)], lambda e: e.memset(ost[:], 0.0))
                        for dr in range(2):
                            c.op("pool", [("r_S", dr, 0)], [("r_S", dr, 0)], lambda e, dr=dr: e.memset(Sst[dr][0][:], 0.0))
                            c.op("pool", [("r_Ssh", dr, 0)], [("r_Ssh", dr, 0)], lambda e, dr=dr: e.memset(Ssh[dr][0][:], 0.0))
                        orders = [list(range(NT)), list(range(NT - 1, -1, -1))]
                        if alg == "dn":
                            turn[0] = turn[1] = 0
                            gens = []
                            for si in range(NT):
                                for dr in range(2):
                                    gens.append(dn_unit(h, dr, si, orders[dr][si]))
                            run_units(gens)
                        else:
                            gens = []
                            for si in range(NT):
                                for dr in range(2):
                                    gens.append(ml_prep(h, dr, si, orders[dr][si]))
                            run_units(gens)
                            for dr in range(2):
                                col = dr * 4 + h
                                bt_ = Mlt[:, :, 16 + col]
                                mn_, ms_, b1_ = mlc[:, MNEW, dr, :], mlc[:, MS, dr, :], mlc[:, B1, dr, :]
                                if dr == 0:
                                    c.op("dve", ["ab_Mlt", ("r_mlc", MS, dr)], [("r_mlc", MNEW, dr)], lambda e, bt_=bt_, mn_=mn_, ms_=ms_: e.tensor_tensor_scan(out=mn_, data0=bt_, data1=ms_, initial=0.0, op0=ALU.add, op1=ALU.max))
                                    c.op("dve", ["ab_Mlt", ("r_mlc", MNEW, dr)], [("r_mlc", B1, dr)], lambda e, bt_=bt_, mn_=mn_, b1_=b1_: e.tensor_tensor(out=b1_[:, 1:NT], in0=bt_[:, 1:NT], in1=mn_[:, 0:NT - 1], op=ALU.add))
                                    c.op("dve", ["ab_Mlt", ("r_mlc", B1, dr)], [("r_mlc", B1, dr)], lambda e, bt_=bt_, b1_=b1_: e.tensor_copy(out=b1_[:, 0:1], in_=bt_[:, 0:1]))
                                else:
                                    c.op("dve", ["ab_Mlt", ("r_mlc", MS, dr)], [("r_mlc", MNEW, dr)], lambda e, bt_=bt_, mn_=mn_, ms_=ms_: e.tensor_tensor_scan(out=mn_[:, ::-1], data0=bt_[:, ::-1], data1=ms_[:, ::-1], initial=0.0, op0=ALU.add, op1=ALU.max))
                                    c.op("dve", ["ab_Mlt", ("r_mlc", MNEW, dr)], [("r_mlc", B1, dr)], lambda e, bt_=bt_, mn_=mn_, b1_=b1_: e.tensor_tensor(out=b1_[:, 0:NT - 1], in0=bt_[:, 0:NT - 1], in1=mn_[:, 1:NT], op=ALU.add))
                                    c.op("dve", ["ab_Mlt", ("r_mlc", B1, dr)], [("r_mlc", B1, dr)], lambda e, bt_=bt_, b1_=b1_: e.tensor_copy(out=b1_[:, NT - 1:NT], in_=bt_[:, NT - 1:NT]))
                            for dr in range(2):
                                col = dr * 4 + h

                                def A_(i_, dr=dr):
                                    return mlc[:, i_, dr, :]

                                def kk_(i_, dr=dr):
                                    return ("r_mlc", i_, dr)
                                bcum_, btot_, wsrc_ = Mlt[:, :, col], Mlt[:, :, 16 + col], Ws[:, :, col]
                                c.op("dve", ["ab_Mlt"], [kk_(A1)], lambda e, dr=dr: e.tensor_tensor(out=A_(A1), in0=bcum_, in1=btot_, op=ALU.subtract))
                                c.op("dve", [kk_(A1), kk_(B1)], [kk_(A1)], lambda e, dr=dr: e.tensor_tensor(out=A_(A1), in0=A_(A1), in1=A_(B1), op=ALU.add))
                                c.op("dve", [kk_(A1), kk_(MI)], [kk_(MT)], lambda e, dr=dr: e.tensor_tensor(out=A_(MT), in0=A_(A1), in1=A_(MI), op=ALU.max))
                                c.op("dve", [kk_(MT)], [kk_(NEGM)], lambda e, dr=dr: e.tensor_scalar(out=A_(NEGM), in0=A_(MT), scalar1=-1.0, scalar2=None, op0=ALU.mult))
                                c.op("dve", [kk_(A1), kk_(MT)], [kk_(TMP)], lambda e, dr=dr: e.tensor_tensor(out=A_(TMP), in0=A_(A1), in1=A_(MT), op=ALU.subtract))
                                c.op("act", [kk_(TMP)], [kk_(INTER)], lambda e, dr=dr: e.activation(out=A_(INTER), in_=A_(TMP), func=AF.Exp))
                                c.op("act", [kk_(NEGM)], [kk_(EMT)], lambda e, dr=dr: e.activation(out=A_(EMT), in_=A_(NEGM), func=AF.Exp))
                                c.op("dve", [kk_(B1), kk_(MNEW), kk_(INTER)], [kk_(TMP)], lambda e, dr=dr: e.tensor_tensor(out=A_(TMP), in0=A_(B1), in1=A_(MNEW), op=ALU.subtract))
                                c.op("act", [kk_(TMP)], [kk_(DEC)], lambda e, dr=dr: e.activation(out=A_(DEC), in_=A_(TMP), func=AF.Exp))
                                c.op("dve", ["ab_Ws", kk_(MNEW), kk_(DEC)], [kk_(TMP)], lambda e, dr=dr: e.tensor_tensor(out=A_(TMP), in0=wsrc_, in1=A_(MNEW), op=ALU.subtract))
                                c.op("act", [kk_(TMP)], [kk_(SRC)], lambda e, dr=dr: e.activation(out=A_(SRC), in_=A_(TMP), func=AF.Exp))
                            turn[0] = turn[1] = 0
                            gens = []
                            for si in range(NT):
                                for dr in range(2):
                                    gens.append(ml_main(h, dr, si, orders[dr][si]))
                            run_units(gens)
                        mchunk = h if alg == "dn" else 4 + h
                        for t0 in range(0, NT, 4):
                            t1_ = min(NT, t0 + 4)
                            c.op("sp", [("ab_tm", tg, h)], ["r_qT"], lambda e, t0=t0, t1_=t1_: e.dma_start(out=gtok[:, t0:t1_, :], in_=TM[tg][t0 * 128:t1_ * 128, h * 128:(h + 1) * 128].rearrange("(t p) d -> p t d", p=128)))
                        for t in range(NT):
                            u2 = t % 2
                            if alg == "dn":
                                c.op("act", [("r_ost", t)], [("r_ot", u2), ("r_oss", u2)], lambda e, t=t, u2=u2: e.activation(out=ot[u2][:], in_=ost[:, t, :], func=AF.Square, accum_out=oss[u2][:, 0:1]))
                                c.op("dve", [("r_oss", u2)], [("r_oss", u2)], lambda e, u2=u2: e.tensor_scalar(out=oss[u2][:, 0:1], in0=oss[u2][:, 0:1], scalar1=1.0 / 128, scalar2=EPS, op0=ALU.mult, op1=ALU.add))
                                c.op("act", [("r_oss", u2)], [("r_oss", u2)], lambda e, u2=u2: e.activation(out=oss[u2][:, 0:1], in_=oss[u2][:, 0:1], func=AF.Sqrt))
                                c.op("dve", [("r_oss", u2)], [("r_oss", u2)], lambda e, u2=u2: e.reciprocal(out=oss[u2][:, 0:1], in_=oss[u2][:, 0:1]))
                                c.op("dve", [("r_ost", t), ("r_oss", u2), "ab_dnw"], [("r_ot", u2)], lambda e, t=t, u2=u2: e.scalar_tensor_tensor(out=ot[u2][:], in0=ost[:, t, :], scalar=oss[u2][:, 0:1], in1=dnw[:], op0=ALU.mult, op1=ALU.mult))
                                c.op("dve", [("r_ot", u2), "r_qT"], [("r_ob", u2)], lambda e, t=t, u2=u2: e.tensor_tensor(out=ob[u2][:], in0=ot[u2][:], in1=gtok[:, t, :], op=ALU.mult))
                            else:
                                c.op("dve", [("r_ost", t), "r_qT"], [("r_ob", u2)], lambda e, t=t, u2=u2: e.tensor_tensor(out=ob[u2][:], in0=ost[:, t, :], in1=gtok[:, t, :], op=ALU.mult))
                            c.op("pe", [("r_ob", u2), "idb"], [("r_pTb", t % 4)], lambda e, t=t, u2=u2: e.transpose(out=pTb[:, (t % 4) * 128:(t % 4 + 1) * 128], in_=ob[u2][:], identity=self.idb[:]))
                            if t % 4 == 3:
                                tb = t // 4
                                mx = mst4[tb % 2]
                                c.op("act", [("r_pTb", q_) for q_ in range(4)], [("r_mx", tb % 2)], lambda e, mx=mx: e.copy(out=mx[:], in_=pTb[:, 0:512]))
                                c.op("gq", [("r_mx", tb % 2)], [("ab_mix", tb)], lambda e, mx=mx, tb=tb, mchunk=mchunk: e.dma_start(out=mixscr[mchunk, :, tb * 512:(tb + 1) * 512], in_=mx[:]))
                c.barrier()
        self.stage_c_scr(c, L, mixscr, 8, Dr["ab_w_out"][j], Dr["mix_norm_post"][L:L + 1, :], src, dst, "abc")


W_NAMES = ['mix_norm_pre', 'mix_norm_post', 'ab_w_in', 'ab_w_out', 'dn_conv_w', 'dn_a_log', 'dn_dt_bias',
           'dn_out_norm', 'ml_conv_w', 'ml_i_bias', 'ml_f_bias', 'c_w_in', 'c_w_out', 'c_conv_w', 'c_conv_b',
           'c_w_a', 'c_b_a', 'c_w_x', 'c_b_x', 'c_lambda', 'xa_norm_pre', 'xa_norm_post', 'xa_mem_norm',
           'xa_w_q', 'xa_w_kv', 'xa_w_o', 'ffn_norm_pre', 'ffn_norm_post', 'ffn_w_up', 'ffn_w_down']


def const_masks():
    p = np.arange(128)[:, None]
    f = np.arange(128)[None, :]
    ms = [p == f, p <= f, p >= f, p < f, p > f, np.ones((128, 128), bool)]
    arr = [m.astype(np.float32) for m in ms]
    arr.append(NEG * (p < f).astype(np.float32))
    arr.append(NEG * (p > f).astype(np.float32))
    return np.ascontiguousarray(np.concatenate(arr, axis=1)).astype(np.float32)


def build(S, shapes, stages, Mm=256):
    B = Builder(S, Mm)
    P = B.P
    nc = B.nc
    for n, shp in shapes.items():
        P.din(n, shp)
    P.din("cmask", [128, 8 * 128])
    out = P.dout("out", [S, D])
    with ExitStack() as es:
        c = Ctx(nc, es)
        B.load_consts(c, es)
        src = P.dram["x"]
        for (name, L) in stages:
            getattr(B, name)(c, L, src, out)
            src = out
        c.barrier()
        c.finish()
    B.nops = c.nops
    return B


STAGES = [("mixer_ab", 0), ("xa", 0), ("ffn", 0), ("mixer_c", 1), ("xa", 1), ("ffn", 1)]


def kernel(**inputs):
    x = np.ascontiguousarray(np.asarray(inputs["x"], dtype=np.float32))
    mem = np.ascontiguousarray(np.asarray(inputs["mem"], dtype=np.float32))
    nb, S, _ = x.shape
    shapes = {"x": (S, D), "mem": tuple(mem.shape[1:])}
    ws = {}
    for n in W_NAMES:
        ws[n] = np.ascontiguousarray(np.asarray(inputs[n], dtype=np.float32))
        shapes[n] = ws[n].shape
    B = build(S, shapes, STAGES, Mm=mem.shape[1])
    cm = const_masks()
    in_maps = []
    for b in range(nb):
        m = {"x": x[b], "mem": mem[b], "cmask": cm}
        m.update(ws)
        in_maps.append(m)
    res = run_bass_kernel_spmd(B.nc, in_maps, core_ids=list(range(nb)))
    return np.stack([np.asarray(r["out"], dtype=np.float32) for r in res.results], axis=0)
```

```python
import numpy as np
from contextlib import ExitStack
import concourse.bass as bass
import concourse.mybir as mybir
from concourse.bass_utils import run_bass_kernel_spmd

F32 = mybir.dt.float32
BF16 = mybir.dt.bfloat16
AF = mybir.ActivationFunctionType
ALU = mybir.AluOpType
AX = mybir.AxisListType
D = 1024
EPS = 1e-6
NEG = -30000.0


class _Eng:
    def __init__(self, ctx, name, be, is_dma, nslots=14):
        self.name, self.be, self.is_dma = name, be, is_dma
        self.waited = {}
        if is_dma:
            self.slots = [ctx.new_sem(f"{name}_d{i}") for i in range(nslots)]
            self.n = 0
        else:
            self.sem = ctx.new_sem(f"{name}_s")
            self.count = 0


class _Buf:
    __slots__ = ("w", "r", "rd")

    def __init__(self):
        self.w = None
        self.r = {}
        self.rd = []


class Ctx:
    def __init__(self, nc, es):
        self.nc, self.es = nc, es
        self.sems = []
        self.bufs = {}
        self.engs = {}
        for name, be, dma in (("pe", nc.tensor, False), ("act", nc.scalar, False),
                              ("dve", nc.vector, False), ("pool", nc.gpsimd, False),
                              ("sp", nc.sync, True), ("gq", nc.gpsimd, True)):
            self.engs[name] = _Eng(self, name, be, dma)
        self.nops = 0
        import os as _os
        self.limit = int(_os.environ["OP_LIMIT"]) if "OP_LIMIT" in _os.environ else None
        self.trace = tuple(int(v) for v in _os.environ["OP_TRACE"].split(",")) if "OP_TRACE" in _os.environ else None

    def new_sem(self, name):
        s = self.es.enter_context(self.nc.semaphore(name))
        self.sems.append(s)
        return len(self.sems) - 1

    def buf(self, k):
        b = self.bufs.get(k)
        if b is None:
            b = self.bufs[k] = _Buf()
        return b

    def op(self, eng, reads, writes, fn):
        E = self.engs[eng]
        need = {}
        if self.trace is not None and self.trace[0] <= self.nops < self.trace[1]:
            print("OP", self.nops, eng, "R", reads, "W", writes)
        if self.limit is not None and self.nops >= self.limit:
            self.nops += 1
            return None

        def add(ev, raw):
            de, si, val = ev
            if de is E and not E.is_dma:
                if E.name == "pe" or not raw:
                    return
            if need.get(si, 0) < val:
                need[si] = val

        for k in reads:
            b = self.buf(k)
            if b.w is not None:
                add(b.w, True)
        for k in writes:
            b = self.buf(k)
            if b.w is not None:
                add(b.w, True)
            for ev in b.r.values():
                add(ev, False)
            for ev in b.rd:
                add(ev, False)
        banks = set()
        for k in list(reads) + list(writes):
            bk = self.bank(k)
            if bk is not None:
                banks.add(bk)
        for bk in banks:
            b = self.buf(bk)
            if b.w is not None:
                add(b.w, False)
        if E.is_dma:
            slot = E.n % len(E.slots)
            gen = E.n // len(E.slots)
            si_own = E.slots[slot]
            if gen > 0 and need.get(si_own, 0) < 16 * gen:
                need[si_own] = 16 * gen
        for si, val in need.items():
            if E.waited.get(si, 0) >= val:
                continue
            E.be.wait_ge(self.sems[si], val)
            E.waited[si] = val
        ins = fn(E.be)
        if E.is_dma:
            ins.then_inc(self.sems[si_own], 16)
            ev = (E, si_own, 16 * (gen + 1))
            E.n += 1
        else:
            E.count += 1
            ins.then_inc(self.sems[E.sem], 1)
            ev = (E, E.sem, E.count)
        for k in reads:
            b = self.buf(k)
            if E.is_dma:
                b.rd.append(ev)
            else:
                b.r[E.name] = ev
        for k in writes:
            b = self.buf(k)
            b.w = ev
            b.r = {}
            b.rd = []
        for bk in banks:
            self.buf(bk).w = ev
        self.nops += 1
        return ev

    _BANKED = ("r_pb", "ab_pp", "x_pa", "f_pu", "c_pp", "pT", "py")

    def bank(self, k):
        if isinstance(k, tuple):
            if k[0] in self._BANKED:
                return ("BANK", k[0], k[1])
            if k[0] == "r_pTb":
                return ("BANK", "r_pTb")
        elif k == "x_pss":
            return ("BANK", "x_pss")
        return None

    def barrier(self):
        evs = []
        for E in self.engs.values():
            if E.is_dma:
                for i, si in enumerate(E.slots):
                    cnt = (E.n - i + len(E.slots) - 1) // len(E.slots) if E.n > i else 0
                    if cnt > 0:
                        evs.append((si, 16 * cnt))
            elif E.count > 0:
                evs.append((E.sem, E.count))
        for E in self.engs.values():
            for si, val in evs:
                if (not E.is_dma) and si == E.sem:
                    continue
                if E.waited.get(si, 0) >= val:
                    continue
                E.be.wait_ge(self.sems[si], val)
                E.waited[si] = val

    def finish(self):
        for E in self.engs.values():
            if E.is_dma:
                for i, si in enumerate(E.slots):
                    cnt = (E.n - i + len(E.slots) - 1) // len(E.slots) if E.n > i else 0
                    if cnt > 0 and E.waited.get(si, 0) < 16 * cnt:
                        E.be.wait_ge(self.sems[si], 16 * cnt)
                        E.waited[si] = 16 * cnt


class Prog:
    def __init__(self, S):
        self.S = S
        self.NT = S // 128
        self.nc = bass.Bass("TRN2", target_bir_lowering=False)
        self.dram = {}

    def din(self, name, shape, dt=F32):
        self.dram[name] = self.nc.dram_tensor(name, list(shape), dt, kind="ExternalInput").ap()
        return self.dram[name]

    def dout(self, name, shape, dt=F32):
        self.dram[name] = self.nc.dram_tensor(name, list(shape), dt, kind="ExternalOutput").ap()
        return self.dram[name]

    def dscr(self, name, shape, dt=F32):
        self.dram[name] = self.nc.dram_tensor(name, list(shape), dt, kind="Internal").ap()
        return self.dram[name]


_UID = [0]


def _sb(es, nc, name, shape, dt):
    _UID[0] += 1
    return es.enter_context(nc.sbuf_tensor(f"{name}_{_UID[0]}", list(shape), dt))


def _ps(es, nc, name, shape, dt):
    _UID[0] += 1
    return es.enter_context(nc.psum_tensor(f"{name}_{_UID[0]}", list(shape), dt))


class Builder:
    def __init__(self, S, Mm=256):
        self.S, self.NT, self.Mm = S, S // 128, Mm
        self.P = Prog(S)
        self.nc = self.P.nc

    def load_consts(self, c, es):
        nc = self.nc
        cm = self.P.dram["cmask"]
        self.cst = _sb(es, nc, "cst", [128, 8, 128], F32)
        c.op("sp", [], ["cst"], lambda e: e.dma_start(out=self.cst[:], in_=cm.rearrange("p (k f) -> p k f", k=8)))
        self.idb = _sb(es, nc, "idb", [128, 128], BF16)
        c.op("dve", ["cst"], ["idb"], lambda e: e.tensor_copy(out=self.idb[:], in_=self.cst[:, 0, :]))
        self.onesb = _sb(es, nc, "onesb", [128, 128], BF16)
        c.op("dve", ["cst"], ["onesb"], lambda e: e.tensor_copy(out=self.onesb[:], in_=self.cst[:, 5, :]))
        self.I = self.cst[:, 0, :]
        self.LE = self.cst[:, 1, :]
        self.GE = self.cst[:, 2, :]
        self.LT = self.cst[:, 3, :]
        self.GT = self.cst[:, 4, :]
        self.ONES = self.cst[:, 5, :]
        self.NLT = self.cst[:, 6, :]
        self.NGT = self.cst[:, 7, :]

    def load_bc(self, c, tile, key, src_row):
        c.op("sp", [], [key], lambda e: e.dma_start(out=tile, in_=src_row.partition_broadcast(128)))

    def rstd_from_ss(self, c, ss, key, n):
        c.op("dve", [key], [key], lambda e: e.tensor_scalar(out=ss, in0=ss, scalar1=1.0 / n, scalar2=EPS, op0=ALU.mult, op1=ALU.add))
        c.op("act", [key], [key], lambda e: e.activation(out=ss, in_=ss, func=AF.Sqrt))
        c.op("dve", [key], [key], lambda e: e.reciprocal(out=ss, in_=ss))

    def prenorm_tile(self, c, W, src_ap, wkey, wbc, dstT, dkey, slot, skey=None):
        h, sq, ss, xn, pT = W["h"][slot], W["sq"], W["ss"][slot], W["xn"][slot], W["pT"][slot]
        hk, ssk, xnk, pk = ("h", slot), ("ss", slot), ("xn", slot), ("pT", slot)
        c.op("sp", [skey] if skey else [], [hk], lambda e: e.dma_start(out=h[:], in_=src_ap))
        c.op("act", [hk], ["sq", ssk], lambda e: e.activation(out=sq[:], in_=h[:], func=AF.Square, accum_out=ss[:]))
        self.rstd_from_ss(c, ss[:], ssk, D)
        c.op("dve", [hk, ssk, wkey], [xnk], lambda e: e.scalar_tensor_tensor(out=xn[:], in0=h[:], scalar=ss[:], in1=wbc, op0=ALU.mult, op1=ALU.mult))

        def tr(e):
            for k in range(8):
                ins = e.transpose(out=pT[:, k * 128:(k + 1) * 128], in_=xn[:, k * 128:(k + 1) * 128], identity=self.idb[:])
            return ins
        c.op("pe", [xnk, "idb"], [pk], tr)
        c.op("act", [pk], [dkey], lambda e: e.copy(out=dstT, in_=pT[:].rearrange("p (k t) -> p k t", k=8)))

    def alloc_prenorm(self, es, tag="", sq=None):
        nc = self.nc
        W = {"h": [_sb(es, nc, f"pn_h{i}{tag}", [128, D], F32) for i in range(2)],
             "sq": sq if sq is not None else _sb(es, nc, f"pn_sq{tag}", [128, D], F32),
             "ss": [_sb(es, nc, f"pn_ss{i}{tag}", [128, 1], F32) for i in range(2)],
             "xn": [_sb(es, nc, f"pn_xn{i}{tag}", [128, D], BF16) for i in range(2)],
             "pT": [_ps(es, nc, f"pn_pT{i}{tag}", [128, D], BF16) for i in range(2)]}
        return W

    def outproj_tile(self, c, W, zT_fn, zkeys, KC, wo, wokey, wpost, wpkey, res_ap, out_ap, slot, rkey=None, okey=None):
        py = W["py"][slot % len(W["py"])]
        pk = ("py", slot % len(W["py"]))
        hr, ss2, t1 = W["hr"][slot], W["ss2"][slot], W["t1"][slot]
        hrk, s2k, t1k = ("hr", slot), ("ss2", slot), ("t1", slot)

        def mm(e):
            for nb in range(2):
                for kc in range(KC):
                    ins = e.matmul(py[nb][:], lhsT=zT_fn(kc), rhs=wo[:, kc, nb * 512:(nb + 1) * 512], start=(kc == 0), stop=(kc == KC - 1))
            return ins
        c.op("pe", list(zkeys) + [wokey], [pk], mm)
        c.op("sp", [rkey] if rkey else [], [hrk], lambda e: e.dma_start(out=hr[:], in_=res_ap))
        c.op("act", [pk], ["sq2", (s2k, 0)], lambda e: e.activation(out=W["sq2"][:, 0:512], in_=py[0][:], func=AF.Square, accum_out=ss2[:, 0:1]))
        c.op("act", [pk], ["sq2", (s2k, 1)], lambda e: e.activation(out=W["sq2"][:, 512:1024], in_=py[1][:], func=AF.Square, accum_out=ss2[:, 1:2]))
        c.op("dve", [(s2k, 0), (s2k, 1)], [s2k], lambda e: e.tensor_tensor(out=ss2[:, 2:3], in0=ss2[:, 0:1], in1=ss2[:, 1:2], op=ALU.add))
        self.rstd_from_ss(c, ss2[:, 2:3], s2k, D)
        for nb in range(2):
            c.op("dve", [pk, s2k, wpkey], [(t1k, nb)], lambda e, nb=nb: e.scalar_tensor_tensor(out=t1[:, nb * 512:(nb + 1) * 512], in0=py[nb][:], scalar=ss2[:, 2:3], in1=wpost[:, nb * 512:(nb + 1) * 512], op0=ALU.mult, op1=ALU.mult))
        c.op("pool", [(t1k, 0), (t1k, 1), hrk], [hrk], lambda e: e.tensor_tensor(out=hr[:], in0=hr[:], in1=t1[:], op=ALU.add))
        c.op("gq", [hrk], [okey] if okey else [], lambda e: e.dma_start(out=out_ap, in_=hr[:]))

    def alloc_outproj(self, es, tag="", npy=2):
        nc = self.nc
        return {"py": [[_ps(es, nc, f"op_py{i}{j}{tag}", [128, 512], F32) for j in range(2)] for i in range(npy)],
                "hr": [_sb(es, nc, f"op_hr{i}{tag}", [128, D], F32) for i in range(2)],
                "ss2": [_sb(es, nc, f"op_ss{i}{tag}", [128, 4], F32) for i in range(2)],
                "t1": [_sb(es, nc, f"op_t1{i}{tag}", [128, D], F32) for i in range(2)],
                "sq2": _sb(es, nc, f"op_sq2{tag}", [128, D], F32)}

    def load_w_bf16(self, c, dst, key, src3):
        KC = dst.shape[1]
        N = dst.shape[2]
        step = max(1, 2048 // N)
        step = min(KC, 4)
        for k0 in range(0, KC, step):
            k1 = min(KC, k0 + step)
            for n0 in range(0, N, 2048):
                n1 = min(N, n0 + 2048)
                c.op("gq", [], [key], lambda e, k0=k0, k1=k1, n0=n0, n1=n1: e.dma_start(out=dst[:, k0:k1, n0:n1], in_=src3[:, k0:k1, n0:n1]))

    def ffn(self, c, L, src, dst):
        nc, S, NT = self.nc, self.S, self.NT
        Dr = self.P.dram
        with ExitStack() as es:
            wup = _sb(es, nc, "f_wup", [128, 8, 4096], BF16)
            wdn = _sb(es, nc, "f_wdn", [128, 32, 1024], BF16)
            wpre = _sb(es, nc, "f_wpre", [128, D], F32)
            wpost = _sb(es, nc, "f_wpost", [128, D], F32)
            self.load_bc(c, wpre[:], "f_wpre", Dr["ffn_norm_pre"][L:L + 1, :])
            self.load_bc(c, wpost[:], "f_wpost", Dr["ffn_norm_post"][L:L + 1, :])
            self.load_w_bf16(c, wup, "f_wup", Dr["ffn_w_up"][L].rearrange("(k p) n -> p k n", p=128))
            self.load_w_bf16(c, wdn, "f_wdn", Dr["ffn_w_down"][L].rearrange("(k p) n -> p k n", p=128))
            OP = self.alloc_outproj(es, "f")
            PN = self.alloc_prenorm(es, "f", sq=OP["sq2"])
            TB = 256
            hnT = [_sb(es, nc, f"f_hnT{i}", [128, 8, TB], BF16) for i in range(2)]
            aT = _sb(es, nc, "f_aT", [128, 32, TB], BF16)
            rl = [_sb(es, nc, f"f_rl{i}", [128, TB], F32) for i in range(2)]
            pu = [_ps(es, nc, f"f_pu{i}", [128, 512], F32) for i in range(2)]
            nblk = S // TB
            tpb = TB // 128
            for b in range(nblk):
                hb = hnT[b % 2]
                for t in range(tpb):
                    tt = b * tpb + t
                    self.prenorm_tile(c, PN, src[tt * 128:(tt + 1) * 128, :], "f_wpre", wpre[:], hb[:, :, t * 128:(t + 1) * 128], ("f_hnT", b % 2, t), tt % 2, skey=(src.tensor.name, tt))
                hkeys = [("f_hnT", b % 2, t) for t in range(tpb)]
                for fc in range(32):
                    p = pu[fc % 2]

                    def mm(e, fc=fc, p=p):
                        for kc in range(8):
                            ins = e.matmul(p[:, 0:TB], lhsT=wup[:, kc, fc * 128:(fc + 1) * 128], rhs=hb[:, kc, :], start=(kc == 0), stop=(kc == 7))
                        return ins
                    c.op("pe", hkeys + ["f_wup"], [("f_pu", fc % 2)], mm)
                    r = rl[fc % 2]
                    c.op("act", [("f_pu", fc % 2)], [("f_rl", fc % 2)], lambda e, p=p, r=r: e.activation(out=r[:], in_=p[:, 0:TB], func=AF.Relu))
                    c.op("dve", [("f_rl", fc % 2)], [("f_aT", fc)], lambda e, r=r, fc=fc: e.tensor_tensor(out=aT[:, fc, :], in0=r[:], in1=r[:], op=ALU.mult))
                akeys = [("f_aT", fc) for fc in range(32)]
                for t in range(tpb):
                    tt = b * tpb + t
                    self.outproj_tile(c, OP, lambda kc, t=t: aT[:, kc, t * 128:(t + 1) * 128], akeys, 32, wdn, "f_wdn", wpost, "f_wpost",
                                      src[tt * 128:(tt + 1) * 128, :], dst[tt * 128:(tt + 1) * 128, :], tt % 2,
                                      rkey=(src.tensor.name, tt), okey=(dst.tensor.name, tt))
            c.barrier()

    def xa(self, c, L, src, dst):
        nc, S, NT, Mm = self.nc, self.S, self.NT, self.Mm
        Dr = self.P.dram
        MC = Mm // 128
        with ExitStack() as es:
            wq = _sb(es, nc, "x_wq", [128, 8, 1024], BF16)
            wo = _sb(es, nc, "x_wo", [128, 8, 1024], BF16)
            wpre = _sb(es, nc, "x_wpre", [128, D], F32)
            wpost = _sb(es, nc, "x_wpost", [128, D], F32)
            wmem = _sb(es, nc, "x_wmem", [128, D], F32)
            self.load_bc(c, wpre[:], "x_wpre", Dr["xa_norm_pre"][L:L + 1, :])
            self.load_bc(c, wpost[:], "x_wpost", Dr["xa_norm_post"][L:L + 1, :])
            self.load_bc(c, wmem[:], "x_wmem", Dr["xa_mem_norm"][L:L + 1, :])
            self.load_w_bf16(c, wq, "x_wq", Dr["xa_w_q"][L].rearrange("(k p) n -> p k n", p=128))
            self.load_w_bf16(c, wo, "x_wo", Dr["xa_w_o"][L].rearrange("(k p) n -> p k n", p=128))
            PN = self.alloc_prenorm(es, "x")
            OP = self.alloc_outproj(es, "x", npy=1)
            kT = _sb(es, nc, "x_kT", [128, 8, Mm], BF16)
            V = _sb(es, nc, "x_V", [128, MC, 1024], BF16)
            pa = [_ps(es, nc, f"x_pa{i}", [128, 512], F32) for i in range(2)]
            with ExitStack() as es2:
                wkv = _sb(es2, nc, "x_wkv", [128, 8, 2048], BF16)
                self.load_w_bf16(c, wkv, "x_wkv", Dr["xa_w_kv"][L].rearrange("(k p) n -> p k n", p=128))
                mnT = _sb(es2, nc, "x_mnT", [128, 8, Mm], BF16)
                for mc in range(MC):
                    self.prenorm_tile(c, PN, Dr["mem"][mc * 128:(mc + 1) * 128, :], "x_wmem", wmem[:], mnT[:, :, mc * 128:(mc + 1) * 128], ("x_mnT", mc), mc % 2)
                mkeys = [("x_mnT", mc) for mc in range(MC)]
                for ch in range(8):
                    p = pa[ch % 2]

                    def mm(e, ch=ch, p=p):
                        for kc in range(8):
                            ins = e.matmul(p[:, 0:Mm], lhsT=wkv[:, kc, ch * 128:(ch + 1) * 128], rhs=mnT[:, kc, :], start=(kc == 0), stop=(kc == 7))
                        return ins
                    c.op("pe", mkeys + ["x_wkv"], [("x_pa", ch % 2)], mm)
                    c.op("act", [("x_pa", ch % 2)], ["x_kT"], lambda e, ch=ch, p=p: e.copy(out=kT[:, ch, :], in_=p[:, 0:Mm]))
                for mc in range(MC):
                    for nb in range(2):
                        p = pa[nb]

                        def mm(e, mc=mc, nb=nb, p=p):
                            for kc in range(8):
                                ins = e.matmul(p[:], lhsT=mnT[:, kc, mc * 128:(mc + 1) * 128], rhs=wkv[:, kc, 1024 + nb * 512:1024 + (nb + 1) * 512], start=(kc == 0), stop=(kc == 7))
                            return ins
                        c.op("pe", mkeys + ["x_wkv"], [("x_pa", nb)], mm)
                        c.op("act", [("x_pa", nb)], ["x_V"], lambda e, mc=mc, nb=nb, p=p: e.copy(out=V[:, mc, nb * 512:(nb + 1) * 512], in_=p[:]))
                c.barrier()
            hnT = [_sb(es, nc, f"x_hnT{i}", [128, 8, 512], BF16) for i in range(2)]
            qT = _sb(es, nc, "x_qT", [128, 8, 512], BF16)
            oT = [_sb(es, nc, f"x_oT{i}", [128, 8, 512], BF16) for i in range(2)]
            sc = [_sb(es, nc, f"x_sc{i}", [128, Mm], F32) for i in range(2)]
            pb = [_sb(es, nc, f"x_pb{i}", [128, Mm], BF16) for i in range(2)]
            pTs = [_sb(es, nc, f"x_pTs{i}", [128, MC, 512], BF16) for i in range(2)]
            st = [_sb(es, nc, f"x_st{i}", [128, 4], F32) for i in range(2)]
            ps_s = [_ps(es, nc, f"x_pss{i}", [128, 512], F32) for i in range(1)]
            scale = 256.0 ** -0.5
            it = 0
            for blk in range(NT // 4):
                sl = blk % 2
                for t in range(4):
                    tt = blk * 4 + t
                    self.prenorm_tile(c, PN, src[tt * 128:(tt + 1) * 128, :], "x_wpre", wpre[:], hnT[sl][:, :, t * 128:(t + 1) * 128], ("x_hnT", sl, t), tt % 2, skey=(src.tensor.name, tt))
                hk = [("x_hnT", sl, t) for t in range(4)]
                for ch in range(8):
                    p = pa[ch % 2]

                    def mm(e, ch=ch, p=p, sl=sl):
                        for kc in range(8):
                            ins = e.matmul(p[:], lhsT=wq[:, kc, ch * 128:(ch + 1) * 128], rhs=hnT[sl][:, kc, :], start=(kc == 0), stop=(kc == 7))
                        return ins
                    c.op("pe", hk + ["x_wq"], [("x_pa", ch % 2)], mm)
                    c.op("act", [("x_pa", ch % 2)], [("x_qT", ch)], lambda e, ch=ch, p=p: e.copy(out=qT[:, ch, :], in_=p[:]))
                for hd in range(4):
                    h2 = hd % 2
                    pT4 = pTs[h2]
                    for t in range(4):
                        i2 = it % 2
                        it += 1
                        pss = ps_s[0]
                        tsl = slice(t * 128, (t + 1) * 128)

                        def mm(e, hd=hd, pss=pss, tsl=tsl):
                            for k2 in range(2):
                                ins = e.matmul(pss[:, 0:Mm], lhsT=qT[:, hd * 2 + k2, tsl], rhs=kT[:, hd * 2 + k2, :], start=(k2 == 0), stop=(k2 == 1))
                            return ins
                        c.op("pe", [("x_qT", hd * 2), ("x_qT", hd * 2 + 1), "x_kT"], ["x_pss"], mm)
                        s_, sc_, pb_ = st[i2], sc[i2], pb[i2]
                        sk, sck, pbk = ("x_st", i2), ("x_sc", i2), ("x_pb", i2)
                        c.op("dve", ["x_pss"], [sk], lambda e, s_=s_, pss=pss: e.tensor_reduce(out=s_[:, 0:1], in_=pss[:, 0:Mm], axis=AX.X, op=ALU.max))
                        c.op("dve", [sk], [sk], lambda e, s_=s_: e.tensor_scalar(out=s_[:, 1:2], in0=s_[:, 0:1], scalar1=-scale, scalar2=None, op0=ALU.mult))
                        c.op("act", ["x_pss", sk], [sck, (sk, "sum")], lambda e, s_=s_, sc_=sc_, pss=pss: e.activation(out=sc_[:], in_=pss[:, 0:Mm], func=AF.Exp, bias=s_[:, 1:2], scale=scale, accum_out=s_[:, 2:3]))
                        c.op("dve", [(sk, "sum")], [(sk, "sum")], lambda e, s_=s_: e.reciprocal(out=s_[:, 3:4], in_=s_[:, 2:3]))
                        c.op("dve", [sck, (sk, "sum")], [pbk], lambda e, s_=s_, sc_=sc_, pb_=pb_: e.tensor_scalar(out=pb_[:], in0=sc_[:], scalar1=s_[:, 3:4], scalar2=None, op0=ALU.mult))
                        ptp = PN["pT"][i2]

                        def tr(e, pb_=pb_, ptp=ptp):
                            for mc in range(MC):
                                ins = e.transpose(out=ptp[:, mc * 128:(mc + 1) * 128], in_=pb_[:, mc * 128:(mc + 1) * 128], identity=self.idb[:])
                            return ins
                        c.op("pe", [pbk, "idb"], [("pT", i2)], tr)
                        c.op("act", [("pT", i2)], [("x_pTs", h2, t)], lambda e, pT4=pT4, ptp=ptp, tsl=tsl: e.copy(out=pT4[:, :, tsl], in_=ptp[:, 0:Mm].rearrange("p (k t) -> p k t", k=MC)))
                    pk4 = [("x_pTs", h2, t) for t in range(4)]
                    for d2 in range(2):
                        po = pa[d2]

                        def mm2(e, hd=hd, d2=d2, pT4=pT4, po=po):
                            for mc in range(MC):
                                ins = e.matmul(po[:], lhsT=V[:, mc, hd * 256 + d2 * 128: hd * 256 + (d2 + 1) * 128], rhs=pT4[:, mc, :], start=(mc == 0), stop=(mc == MC - 1))
                            return ins
                        c.op("pe", pk4 + ["x_V"], [("x_pa", d2)], mm2)
                        c.op("act", [("x_pa", d2)], [("x_oT", sl, hd * 2 + d2)], lambda e, hd=hd, d2=d2, po=po, sl=sl: e.copy(out=oT[sl][:, hd * 2 + d2, :], in_=po[:]))
                okeys = [("x_oT", sl, q_) for q_ in range(8)]
                for t in range(4):
                    tt = blk * 4 + t
                    self.outproj_tile(c, OP, lambda kc, sl=sl, t=t: oT[sl][:, kc, t * 128:(t + 1) * 128], okeys, 8, wo, "x_wo", wpost, "x_wpost",
                                      src[tt * 128:(tt + 1) * 128, :], dst[tt * 128:(tt + 1) * 128, :], tt % 2,
                                      rkey=(src.tensor.name, tt), okey=(dst.tensor.name, tt))
            c.barrier()

    def load_cols(self, c, dst, key, vec, n):
        for k in range(n):
            c.op("sp", [], [key], lambda e, k=k: e.dma_start(out=dst[:, k:k + 1], in_=vec[k * 128:(k + 1) * 128].rearrange("(p o) -> p o", o=1)))

    def stage_a_full(self, c, es, hnT, hkey, wrow, src, tag):
        nc = self.nc
        with ExitStack() as es2:
            PN = self.alloc_prenorm(es2, tag)
            wpre = _sb(es2, nc, f"{tag}_wpre", [128, D], F32)
            self.load_bc(c, wpre[:], f"{tag}_wpre", wrow)
            for tt in range(self.NT):
                self.prenorm_tile(c, PN, src[tt * 128:(tt + 1) * 128, :], f"{tag}_wpre", wpre[:], hnT[:, :, tt * 128:(tt + 1) * 128], (hkey, tt), tt % 2, skey=(src.tensor.name, tt))
            c.barrier()

    def stage_c_scr(self, c, L, zscr, KC, wout_ap, wpost_row, src, dst, tag):
        nc = self.nc
        with ExitStack() as es:
            wo = _sb(es, nc, f"{tag}_wo", [128, KC, 1024], BF16)
            wpost = _sb(es, nc, f"{tag}_wpost", [128, D], F32)
            self.load_bc(c, wpost[:], f"{tag}_wpost", wpost_row)
            self.load_w_bf16(c, wo, f"{tag}_wo", wout_ap.rearrange("(k p) n -> p k n", p=128))
            OP = self.alloc_outproj(es, tag)
            zt = [_sb(es, nc, f"{tag}_zt{i}", [128, KC, 512], BF16) for i in range(2)]
            for b in range(self.S // 512):
                z = zt[b % 2]
                c.op("sp", [(zscr.tensor.name, b)], [(f"{tag}_zt", b % 2)], lambda e, z=z, b=b: e.dma_start(out=z[:], in_=zscr[:, :, b * 512:(b + 1) * 512].rearrange("k p t -> p k t")))
                for t in range(4):
                    tt = b * 4 + t
                    self.outproj_tile(c, OP, lambda kc, z=z, t=t: z[:, kc, t * 128:(t + 1) * 128], [(f"{tag}_zt", b % 2)], KC, wo, f"{tag}_wo", wpost, f"{tag}_wpost",
                                      src[tt * 128:(tt + 1) * 128, :], dst[tt * 128:(tt + 1) * 128, :], tt % 2,
                                      rkey=(src.tensor.name, tt), okey=(dst.tensor.name, tt))
            c.barrier()

    def mixer_c(self, c, L, src, dst):
        nc, S, NT = self.nc, self.S, self.NT
        Dr = self.P.dram
        j = L // 2
        NB = S // 512
        yscr = self.P.dram.get("c_yscr")
        if yscr is None:
            yscr = self.P.dscr("c_yscr", [8, 128, S], BF16)
        with ExitStack() as es:
            hnT = _sb(es, nc, "c_hnT", [128, 8, S], BF16)
            self.stage_a_full(c, es, hnT, "c_hnT", Dr["mix_norm_pre"][L:L + 1, :], src, "c")
            hkeys = [("c_hnT", tt) for tt in range(NT)]
            cw = _sb(es, nc, "c_cw", [128, 4, 8], F32)
            for tap in range(4):
                self.load_cols(c, cw[:, tap, :], "c_cw", Dr["c_conv_w"][j, tap], 8)
            cb = _sb(es, nc, "c_cb", [128, 8], F32)
            self.load_cols(c, cb, "c_cb", Dr["c_conv_b"][j], 8)
            ba = _sb(es, nc, "c_ba", [128, 2, 8], F32)
            bx = _sb(es, nc, "c_bx", [128, 2, 8], F32)
            c1 = _sb(es, nc, "c_c1", [128, 2, 8], F32)
            for d_ in range(2):
                self.load_cols(c, ba[:, d_, :], "c_ba", Dr["c_b_a"][j, d_], 8)
                self.load_cols(c, bx[:, d_, :], "c_bx", Dr["c_b_x"][j, d_], 8)
                self.load_cols(c, c1[:, d_, :], "c_c1", Dr["c_lambda"][j, d_], 8)
            c.op("act", ["c_c1"], ["c_c1"], lambda e: e.activation(out=c1[:], in_=c1[:], func=AF.Exp, scale=-1.0))
            c.op("act", ["c_c1"], ["c_c1"], lambda e: e.activation(out=c1[:], in_=c1[:], func=AF.Ln, bias=1.0))
            c.op("dve", ["c_c1"], ["c_c1"], lambda e: e.tensor_scalar(out=c1[:], in0=c1[:], scalar1=-8.0, scalar2=None, op0=ALU.mult))
            xbf = _sb(es, nc, "c_xbf", [128, S + 3], F32)
            u = [_sb(es, nc, f"c_u{i}", [128, S], F32) for i in range(2)]
            ub = [_sb(es, nc, f"c_ub{i}", [128, S], BF16) for i in range(2)]
            af = _sb(es, nc, "c_af", [128, S], F32)
            inp = _sb(es, nc, "c_inp", [128, S], F32)
            acc = _sb(es, nc, "c_acc", [128, S], F32)
            win = [_sb(es, nc, f"c_win{i}", [128, 8, 128], BF16) for i in range(2)]
            wa = [_sb(es, nc, f"c_wa{i}", [128, 2, 128], BF16) for i in range(2)]
            wx = [_sb(es, nc, f"c_wx{i}", [128, 2, 128], BF16) for i in range(2)]
            tmp = {n: [_sb(es, nc, f"c_{n}{i}", [128, 512], F32) for i in range(2)] for n in ("r", "i", "mu")}
            yst = [_sb(es, nc, f"c_yst{i}", [128, 512], BF16) for i in range(2)]
            pp = [_ps(es, nc, f"c_pp{i}", [128, 512], F32) for i in range(4)]
            win_n = 0
            wg_n = 0
            pn = 0

            def inproj(col0, dst_fn, dkeys_fn):
                nonlocal win_n, pn
                w = win[win_n % 2]
                wk = ("c_win", win_n % 2)
                win_n += 1
                src3 = Dr["c_w_in"][j].rearrange("(k p) n -> p k n", p=128)[:, :, col0:col0 + 128]
                self.load_w_bf16(c, w, wk, src3)
                for tb in range(NB):
                    p = pp[pn % 2]
                    pk = ("c_pp", pn % 2)
                    pn += 1

                    def mm(e, p=p, w=w, tb=tb):
                        for kc in range(8):
                            ins = e.matmul(p[:], lhsT=w[:, kc, :], rhs=hnT[:, kc, tb * 512:(tb + 1) * 512], start=(kc == 0), stop=(kc == 7))
                        return ins
                    c.op("pe", hkeys[tb * 4:(tb + 1) * 4] + [wk], [pk], mm)
                    dst_fn(tb, p, pk)

            for blk in range(4):
                for c2 in range(2):
                    ch = blk * 2 + c2
                    c.op("pool", [], ["c_xbf_h"], lambda e: e.memset(xbf[:, 0:2], 0.0))
                    c.op("pool", [], ["c_xbf_h"], lambda e: e.memset(xbf[:, S + 2:S + 3], 0.0))

                    def ev(tb, p, pk):
                        c.op("act", [pk], [("c_xbf", tb)], lambda e: e.copy(out=xbf[:, 2 + tb * 512:2 + (tb + 1) * 512], in_=p[:]))
                    c.op("pool", ["c_hs"], ["c_hs"] + [("c_xbf", tb) for tb in range(NB)], lambda e: e.memset(xbf[:, 0:1], 0.0))
                    inproj(1024 + ch * 128, ev, None)
                    xk = [("c_xbf", tb) for tb in range(NB)] + ["c_xbf_h"]
                    uu = u[c2]
                    uk = ("c_u", c2)
                    c.op("dve", xk + ["c_cw", "c_cb"], [uk], lambda e, uu=uu, ch=ch: e.tensor_scalar(out=uu[:], in0=xbf[:, 0:S], scalar1=cw[:, 0, ch:ch + 1], scalar2=cb[:, ch:ch + 1], op0=ALU.mult, op1=ALU.add))
                    for tap in range(1, 4):
                        c.op("dve", xk + [uk], [uk], lambda e, uu=uu, ch=ch, tap=tap: e.scalar_tensor_tensor(out=uu[:], in0=xbf[:, tap:tap + S], scalar=cw[:, tap, ch:ch + 1], in1=uu[:], op0=ALU.mult, op1=ALU.add))
                    c.op("pool", [uk], [("c_ub", c2)], lambda e, uu=uu, c2=c2: e.tensor_copy(out=ub[c2][:], in_=uu[:]))
                    c.op("pool", xk, ["c_hs"], lambda e: e.memset(xbf[:, 0:1], 0.0))
                ubk = [("c_ub", 0), ("c_ub", 1)]
                for jc in range(2):
                    ch = blk * 2 + jc
                    for dr in range(2):
                        wa_, wx_ = wa[wg_n % 2], wx[wg_n % 2]
                        wak, wxk = ("c_wa", wg_n % 2), ("c_wx", wg_n % 2)
                        wg_n += 1
                        self.load_w_bf16(c, wa_, wak, Dr["c_w_a"][j, dr, blk].rearrange("(k p) n -> p k n", p=128)[:, :, jc * 128:(jc + 1) * 128])
                        self.load_w_bf16(c, wx_, wxk, Dr["c_w_x"][j, dr, blk].rearrange("(k p) n -> p k n", p=128)[:, :, jc * 128:(jc + 1) * 128])
                        for tb in range(NB):
                            sl = tb % 2
                            pa_, px_ = pp[2], pp[3]
                            ts = slice(tb * 512, (tb + 1) * 512)

                            def mm(e, w_=wa_, p_=pa_, ts=ts):
                                for kc in range(2):
                                    ins = e.matmul(p_[:], lhsT=w_[:, kc, :], rhs=ub[kc][:, ts], start=(kc == 0), stop=(kc == 1))
                                return ins
                            c.op("pe", ubk + [wak], [("c_pp", 2)], mm)

                            def mm2(e, w_=wx_, p_=px_, ts=ts):
                                for kc in range(2):
                                    ins = e.matmul(p_[:], lhsT=w_[:, kc, :], rhs=ub[kc][:, ts], start=(kc == 0), stop=(kc == 1))
                                return ins
                            c.op("pe", ubk + [wxk], [("c_pp", 3)], mm2)
                            r_, i_, mu_ = tmp["r"][sl], tmp["i"][sl], tmp["mu"][sl]
                            c.op("act", [("c_pp", 2), "c_ba"], [("c_r", sl)], lambda e, r_=r_, pa_=pa_, dr=dr, ch=ch: e.activation(out=r_[:], in_=pa_[:], func=AF.Sigmoid, bias=ba[:, dr, ch:ch + 1]))
                            c.op("act", [("c_pp", 3), "c_bx"], [("c_i", sl)], lambda e, i_=i_, px_=px_, dr=dr, ch=ch: e.activation(out=i_[:], in_=px_[:], func=AF.Sigmoid, bias=bx[:, dr, ch:ch + 1]))
                            c.op("act", [("c_r", sl), "c_c1"], [("c_af", tb)], lambda e, r_=r_, ts=ts, dr=dr, ch=ch: e.activation(out=af[:, ts], in_=r_[:], func=AF.Exp, scale=c1[:, dr, ch:ch + 1]))
                            c.op("dve", [("c_af", tb)], [("c_mu", sl)], lambda e, mu_=mu_, ts=ts: e.tensor_tensor(out=mu_[:], in0=af[:, ts], in1=af[:, ts], op=ALU.mult))
                            c.op("act", [("c_mu", sl)], [("c_mu", sl)], lambda e, mu_=mu_: e.activation(out=mu_[:], in_=mu_[:], func=AF.Sqrt, scale=-1.0, bias=1.0))
                            c.op("dve", [("c_mu", sl), ("c_i", sl)], [("c_mu", sl)], lambda e, mu_=mu_, i_=i_: e.tensor_tensor(out=mu_[:], in0=mu_[:], in1=i_[:], op=ALU.mult))
                            c.op("dve", [("c_mu", sl), ("c_u", jc)], [("c_inp", tb)], lambda e, mu_=mu_, ts=ts, jc=jc: e.tensor_tensor(out=inp[:, ts], in0=mu_[:], in1=u[jc][:, ts], op=ALU.mult))
                        afk = [("c_af", tb) for tb in range(NB)]
                        ink = [("c_inp", tb) for tb in range(NB)]
                        if dr == 0:
                            c.op("dve", afk + ink, ["c_acc"], lambda e: e.tensor_tensor_scan(out=acc[:], data0=af[:], data1=inp[:], initial=0.0, op0=ALU.mult, op1=ALU.add))
                        else:
                            c.op("dve", afk + ink, ["c_hs"], lambda e: e.tensor_tensor_scan(out=xbf[:, 0:S][:, ::-1], data0=af[:, ::-1], data1=inp[:, ::-1], initial=0.0, op0=ALU.mult, op1=ALU.add))
                            c.op("pool", ["c_hs", "c_acc"], ["c_acc"], lambda e: e.tensor_tensor(out=acc[:], in0=acc[:], in1=xbf[:, 0:S], op=ALU.add))

                    def evg(tb, p, pk, ch=ch):
                        sl = tb % 2
                        g1, g2, ys = tmp["r"][sl], tmp["i"][sl], yst[sl]
                        ts = slice(tb * 512, (tb + 1) * 512)
                        c.op("act", [pk], [("c_r", sl)], lambda e: e.activation(out=g1[:], in_=p[:], func=AF.Square))
                        c.op("dve", [("c_r", sl)], [("c_r", sl)], lambda e: e.tensor_scalar(out=g1[:], in0=g1[:], scalar1=0.044715, scalar2=1.0, op0=ALU.mult, op1=ALU.add))
                        c.op("dve", [("c_r", sl), pk], [("c_r", sl)], lambda e: e.tensor_tensor(out=g1[:], in0=g1[:], in1=p[:], op=ALU.mult))
                        c.op("act", [("c_r", sl)], [("c_i", sl)], lambda e: e.activation(out=g2[:], in_=g1[:], func=AF.Sigmoid, scale=1.5957691216057308))
                        c.op("dve", [("c_i", sl), pk], [("c_i", sl)], lambda e: e.tensor_tensor(out=g2[:], in0=g2[:], in1=p[:], op=ALU.mult))
                        c.op("dve", [("c_i", sl), "c_acc"], [("c_yst", sl)], lambda e: e.tensor_tensor(out=ys[:], in0=g2[:], in1=acc[:, ts], op=ALU.mult))
                        c.op("gq", [("c_yst", sl)], [("c_yscr", tb)], lambda e: e.dma_start(out=yscr[ch, :, ts], in_=ys[:]))
                    inproj(ch * 128, evg, None)
            c.barrier()
        self.stage_c_scr(c, L, yscr, 8, Dr["c_w_out"][j], Dr["mix_norm_post"][L:L + 1, :], src, dst, "cc")

    def mixer_ab(self, c, L, src, dst):
        nc, S, NT = self.nc, self.S, self.NT
        Dr = self.P.dram
        j = L // 2
        NB = S // 512
        P = self.P
        FM = P.dram.get("ab_fm") or P.dscr("ab_fm", [16, 128, S], F32)
        TM = {n: (P.dram.get("ab_" + n) or P.dscr("ab_" + n, [S, 512], F32)) for n in ("dnk", "dnv", "dnz", "mlk", "mlv", "mlo")}
        mixscr = P.dram.get("ab_mix") or P.dscr("ab_mix", [8, 128, S], BF16)
        I, ONES = self.I, self.ONES
        dirs = [dict(INC=self.LE, AFT=self.GT, STRICT=self.GT, INCLji=self.LE, NEGM=self.NLT),
                dict(INC=self.GE, AFT=self.LT, STRICT=self.LT, INCLji=self.GE, NEGM=self.NGT)]
        with ExitStack() as eo:
            gates = _sb(eo, nc, "ab_gates", [128, NT, 32], F32)
            Gg = _sb(eo, nc, "ab_Gg", [128, NT, 8], F32)
            Bt = _sb(eo, nc, "ab_Bt", [128, NT, 8], F32)
            nBt = _sb(eo, nc, "ab_nBt", [128, NT, 8], F32)
            Li = _sb(eo, nc, "ab_Li", [128, NT, 8], F32)
            Lf = _sb(eo, nc, "ab_Lf", [128, NT, 8], F32)
            Edn = _sb(eo, nc, "ab_Edn", [128, NT, 24], F32)
            Mlt = _sb(eo, nc, "ab_Mlt", [128, NT, 24], F32)
            Bk = _sb(eo, nc, "ab_Bk", [128, NT, 8], F32)
            Ws = _sb(eo, nc, "ab_Ws", [128, NT, 8], F32)
            prm = _sb(eo, nc, "ab_prm", [128, 4, 8], F32)
            dnw = _sb(eo, nc, "ab_dnw", [128, 128], F32)
            for i_, nm in enumerate(("dn_a_log", "dn_dt_bias", "ml_i_bias", "ml_f_bias")):
                self.load_bc(c, prm[:, i_, :], "ab_prm", Dr[nm][j:j + 1].rearrange("o d h -> o (d h)"))
            self.load_bc(c, dnw[:], "ab_dnw", Dr["dn_out_norm"][j:j + 1, :])
            with ExitStack() as es:
                hnT = _sb(es, nc, "ab_hnT", [128, 8, S], BF16)
                self.stage_a_full(c, es, hnT, "ab_hnT", Dr["mix_norm_pre"][L:L + 1, :], src, "ab")
                hkeys = [("ab_hnT", tt) for tt in range(NT)]
                win3 = Dr["ab_w_in"][j].rearrange("(k p) n -> p k n", p=128)
                wg = _sb(es, nc, "ab_wg", [128, 8, 32], BF16)
                c.op("gq", [], ["ab_wg"], lambda e: e.dma_start(out=wg[:, :, 0:16], in_=win3[:, :, 2048:2064]))
                c.op("gq", [], ["ab_wg"], lambda e: e.dma_start(out=wg[:, :, 16:32], in_=win3[:, :, 4112:4128]))
                pp = [_ps(es, nc, f"ab_pp{i}", [128, 512], F32) for i in range(4)]
                for tt in range(NT):
                    p = pp[tt % 2]

                    def mm(e, p=p, tt=tt):
                        for kc in range(8):
                            ins = e.matmul(p[:, 0:32], lhsT=hnT[:, kc, tt * 128:(tt + 1) * 128], rhs=wg[:, kc, :], start=(kc == 0), stop=(kc == 7))
                        return ins
                    c.op("pe", [hkeys[tt], "ab_wg"], [("ab_pp", tt % 2)], mm)
                    c.op("act", [("ab_pp", tt % 2)], ["ab_gates"], lambda e, p=p, tt=tt: e.copy(out=gates[:, tt, :], in_=p[:, 0:32]))
                def bc(i_):
                    return prm[:, i_, :].unsqueeze(1).broadcast_to([128, NT, 8])
                c.op("dve", ["ab_gates", "ab_prm"], ["ab_Gg"], lambda e: e.tensor_tensor(out=Gg[:], in0=gates[:, :, 0:8], in1=bc(1), op=ALU.add))
                c.op("act", ["ab_Gg"], ["ab_Gg"], lambda e: e.activation(out=Gg[:], in_=Gg[:], func=AF.Exp))
                c.op("act", ["ab_Gg"], ["ab_Gg"], lambda e: e.activation(out=Gg[:], in_=Gg[:], func=AF.Ln, bias=1.0))
                c.op("act", ["ab_prm"], ["ab_prm0"], lambda e: e.activation(out=prm[:, 0, :], in_=prm[:, 0, :], func=AF.Exp))
                c.op("dve", ["ab_Gg", "ab_prm0"], ["ab_Gg"], lambda e: e.scalar_tensor_tensor(out=Gg[:], in0=Gg[:], scalar=-1.0, in1=bc(0), op0=ALU.mult, op1=ALU.mult))
                c.op("act", ["ab_gates"], ["ab_Bt"], lambda e: e.activation(out=Bt[:], in_=gates[:, :, 8:16], func=AF.Sigmoid))
                c.op("dve", ["ab_Bt"], ["ab_nBt"], lambda e: e.tensor_scalar(out=nBt[:], in0=Bt[:], scalar1=-1.0, scalar2=None, op0=ALU.mult))
                c.op("dve", ["ab_gates", "ab_prm"], ["ab_Li"], lambda e: e.tensor_tensor(out=Li[:], in0=gates[:, :, 16:24], in1=bc(2), op=ALU.add))
                c.op("dve", ["ab_gates", "ab_prm"], ["ab_Lf"], lambda e: e.tensor_tensor(out=Lf[:], in0=gates[:, :, 24:32], in1=bc(3), op=ALU.add))
                c.op("act", ["ab_Lf"], ["ab_Lf"], lambda e: e.activation(out=Lf[:], in_=Lf[:], func=AF.Exp, scale=-1.0))
                c.op("act", ["ab_Lf"], ["ab_Lf"], lambda e: e.activation(out=Lf[:], in_=Lf[:], func=AF.Ln, bias=1.0))
                c.op("dve", ["ab_Lf"], ["ab_Lf"], lambda e: e.tensor_scalar(out=Lf[:], in0=Lf[:], scalar1=-1.0, scalar2=None, op0=ALU.mult))
                LE, GE, LT, GT = self.LE, self.GE, self.LT, self.GT
                for tt in range(NT):
                    p = pp[2 + tt % 2]

                    def mm(e, p=p, tt=tt):
                        for o_, T_ in ((0, Gg), (24, Lf)):
                            e.matmul(p[:, o_ + 0:o_ + 4], lhsT=LE, rhs=T_[:, tt, 0:4], start=True, stop=True)
                            e.matmul(p[:, o_ + 4:o_ + 8], lhsT=GE, rhs=T_[:, tt, 4:8], start=True, stop=True)
                            e.matmul(p[:, o_ + 8:o_ + 12], lhsT=GT, rhs=T_[:, tt, 0:4], start=True, stop=True)
                            e.matmul(p[:, o_ + 12:o_ + 16], lhsT=LT, rhs=T_[:, tt, 4:8], start=True, stop=True)
                            ins = e.matmul(p[:, o_ + 16:o_ + 24], lhsT=ONES, rhs=T_[:, tt, 0:8], start=True, stop=True)
                        return ins
                    c.op("pe", ["ab_Gg", "ab_Lf", "cst"], [("ab_pp", 2 + tt % 2)], mm)
                    c.op("act", [("ab_pp", 2 + tt % 2)], ["ab_Edn"], lambda e, p=p, tt=tt: e.activation(out=Edn[:, tt, :], in_=p[:, 0:24], func=AF.Exp))
                    c.op("act", [("ab_pp", 2 + tt % 2)], ["ab_Mlt"], lambda e, p=p, tt=tt: e.copy(out=Mlt[:, tt, :], in_=p[:, 24:48]))
                c.op("dve", ["ab_Bt", "ab_Edn"], ["ab_Bk"], lambda e: e.tensor_tensor(out=Bk[:], in0=Bt[:], in1=Edn[:, :, 0:8], op=ALU.mult))
                c.op("dve", ["ab_Li", "ab_Mlt"], ["ab_Ws"], lambda e: e.tensor_tensor(out=Ws[:], in0=Li[:], in1=Mlt[:, :, 8:16], op=ALU.add))
                xbf = _sb(es, nc, "ab_xbf", [128, S + 3], F32)
                uu = _sb(es, nc, "ab_u", [128, S], F32)
                cwt = [_sb(es, nc, f"ab_cwt{i}", [128, 4], F32) for i in range(2)]
                win = [_sb(es, nc, f"ab_win{i}", [128, 8, 128], BF16) for i in range(2)]
                sqb = [_sb(es, nc, f"ab_sqb{i}", [128, 512], F32) for i in range(2)]
                rsb = [_sb(es, nc, f"ab_rsb{i}", [128, 512], F32) for i in range(2)]
                stg = [_sb(es, nc, f"ab_stg{i}", [128, 4, 128], F32) for i in range(2)]
                c.op("pool", [], ["ab_xbf_h"], lambda e: e.memset(xbf[:, 0:2], 0.0))
                c.op("pool", [], ["ab_xbf_h"], lambda e: e.memset(xbf[:, S + 2:S + 3], 0.0))
                specs = []
                for h in range(4):
                    specs.append(dict(col=h * 128, conv=("dn_conv_w", h * 128), act=AF.Silu, l2=True, scale=128.0 ** -0.5, fm=h, tm=None))
                for h in range(4):
                    specs.append(dict(col=512 + h * 128, conv=("dn_conv_w", 512 + h * 128), act=AF.Silu, l2=True, scale=1.0, fm=4 + h, tm=("dnk", h)))
                for h in range(4):
                    specs.append(dict(col=1024 + h * 128, conv=("dn_conv_w", 1024 + h * 128), act=AF.Silu, l2=False, scale=None, fm=None, tm=("dnv", h)))
                for h in range(4):
                    specs.append(dict(col=1536 + h * 128, conv=None, act=AF.Silu, l2=False, scale=None, fm=None, tm=("dnz", h)))
                for h in range(4):
                    specs.append(dict(col=2064 + h * 128, conv=("ml_conv_w", h * 128), act=AF.Silu, l2=False, scale=None, fm=8 + h, tm=None))
                for h in range(4):
                    specs.append(dict(col=2576 + h * 128, conv=("ml_conv_w", 512 + h * 128), act=AF.Silu, l2=False, scale=128.0 ** -0.5, fm=12 + h, tm=("mlk", h)))
                for h in range(4):
                    specs.append(dict(col=3088 + h * 128, conv=None, act=None, l2=False, scale=None, fm=None, tm=("mlv", h)))
                for h in range(4):
                    specs.append(dict(col=3600 + h * 128, conv=None, act=AF.Sigmoid, l2=False, scale=None, fm=None, tm=("mlo", h)))
                pn = 0
                for si, sp in enumerate(specs):
                    w = win[si % 2]
                    wk = ("ab_win", si % 2)
                    self.load_w_bf16(c, w, wk, win3[:, :, sp["col"]:sp["col"] + 128])
                    xk = [("ab_xbf", tb) for tb in range(NB)]
                    for tb in range(NB):
                        p = pp[pn % 2]
                        pk = ("ab_pp", pn % 2)
                        pn += 1

                        def mm(e, p=p, w=w, tb=tb):
                            for kc in range(8):
                                ins = e.matmul(p[:], lhsT=w[:, kc, :], rhs=hnT[:, kc, tb * 512:(tb + 1) * 512], start=(kc == 0), stop=(kc == 7))
                            return ins
                        c.op("pe", hkeys[tb * 4:(tb + 1) * 4] + [wk], [pk], mm)
                        c.op("act", [pk], [("ab_xbf", tb)], lambda e, p=p, tb=tb: e.copy(out=xbf[:, 2 + tb * 512:2 + (tb + 1) * 512], in_=p[:]))
                    if sp["conv"] is not None:
                        cw_ = cwt[si % 2]
                        cwk = ("ab_cwt", si % 2)
                        nm, c0 = sp["conv"]
                        for tap in range(4):
                            c.op("sp", [], [cwk], lambda e, cw_=cw_, tap=tap, nm=nm, c0=c0: e.dma_start(out=cw_[:, tap:tap + 1], in_=Dr[nm][j, tap, c0:c0 + 128].rearrange("(p o) -> p o", o=1)))
                        c.op("dve", xk + ["ab_xbf_h", cwk], ["ab_u"], lambda e, cw_=cw_: e.tensor_scalar(out=uu[:], in0=xbf[:, 0:S], scalar1=cw_[:, 0:1], scalar2=None, op0=ALU.mult))
                        for tap in range(1, 4):
                            c.op("dve", xk + ["ab_xbf_h", cwk, "ab_u"], ["ab_u"], lambda e, cw_=cw_, tap=tap: e.scalar_tensor_tensor(out=uu[:], in0=xbf[:, tap:tap + S], scalar=cw_[:, tap:tap + 1], in1=uu[:], op0=ALU.mult, op1=ALU.add))
                        if sp["act"] is not None:
                            c.op("act", ["ab_u"], ["ab_u"], lambda e, f=sp["act"]: e.activation(out=uu[:], in_=uu[:], func=f))
                    else:
                        if sp["act"] is not None:
                            c.op("act", xk, ["ab_u"], lambda e, f=sp["act"]: e.activation(out=uu[:], in_=xbf[:, 2:S + 2], func=f))
                        else:
                            c.op("pool", xk, ["ab_u"], lambda e: e.tensor_copy(out=uu[:], in_=xbf[:, 2:S + 2]))
                    if sp["l2"]:
                        for tb in range(NB):
                            sl = tb % 2
                            ts = slice(tb * 512, (tb + 1) * 512)
                            c.op("act", ["ab_u"], [("ab_sqb", sl)], lambda e, sl=sl, ts=ts: e.activation(out=sqb[sl][:], in_=uu[:, ts], func=AF.Square))
                            p = pp[2 + sl]
                            c.op("pe", [("ab_sqb", sl), "cst"], [("ab_pp", 2 + sl)], lambda e, p=p, sl=sl: e.matmul(p[:], lhsT=ONES, rhs=sqb[sl][:], start=True, stop=True))
                            c.op("act", [("ab_pp", 2 + sl)], [("ab_rsb", sl)], lambda e, p=p, sl=sl: e.activation(out=rsb[sl][:], in_=p[:], func=AF.Sqrt, bias=1e-6))
                            c.op("dve", [("ab_rsb", sl)], [("ab_rsb", sl)], lambda e, sl=sl: e.reciprocal(out=rsb[sl][:], in_=rsb[sl][:]))
                            c.op("dve", [("ab_rsb", sl), "ab_u"], ["ab_u"], lambda e, sl=sl, ts=ts, sc_=sp["scale"]: e.scalar_tensor_tensor(out=uu[:, ts], in0=uu[:, ts], scalar=sc_, in1=rsb[sl][:], op0=ALU.mult, op1=ALU.mult))
                    elif sp["scale"] is not None:
                        c.op("dve", ["ab_u"], ["ab_u"], lambda e, sc_=sp["scale"]: e.tensor_scalar(out=uu[:], in0=uu[:], scalar1=sc_, scalar2=None, op0=ALU.mult))
                    if sp["fm"] is not None:
                        c.op("sp", ["ab_u"], [("ab_fm", sp["fm"])], lambda e, f=sp["fm"]: e.dma_start(out=FM[f], in_=uu[:]))
                    if sp["tm"] is not None:
                        nm, h = sp["tm"]
                        for tb in range(NB):
                            sl = tb % 2
                            p = pp[2 + sl]

                            def tr(e, p=p, tb=tb):
                                for t4 in range(4):
                                    ins = e.transpose(out=p[:, t4 * 128:(t4 + 1) * 128], in_=uu[:, tb * 512 + t4 * 128: tb * 512 + (t4 + 1) * 128], identity=I)
                                return ins
                            c.op("pe", ["ab_u", "cst"], [("ab_pp", 2 + sl)], tr)
                            c.op("act", [("ab_pp", 2 + sl)], [("ab_stg", sl)], lambda e, p=p, sl=sl: e.copy(out=stg[sl][:], in_=p[:].rearrange("p (t d) -> p t d", t=4)))
                            c.op("gq", [("ab_stg", sl)], [("ab_tm", nm, h)], lambda e, sl=sl, tb=tb, nm=nm, h=h: e.dma_start(out=TM[nm][tb * 512:(tb + 1) * 512, h * 128:(h + 1) * 128].rearrange("(t p) d -> p t d", p=128), in_=stg[sl][:]))
                c.barrier()
            import os as _os
            _algs = tuple(a for a in _os.environ.get("AB_ALGS", "dn,ml").split(",") if a)
            WIN = int(_os.environ.get("AB_WIN", "4"))
            with ExitStack() as es:
                qT = _sb(es, nc, "r_qT", [128, S], F32)
                ktok = _sb(es, nc, "r_ktok", [128, NT, 128], F32)
                vtok = _sb(es, nc, "r_vtok", [128, NT, 129], F32)
                ost = _sb(es, nc, "r_ost", [128, NT, 128], F32)
                gtok = qT[:].rearrange("p (t d) -> p t d", d=128)
                dn_names = ["Gmat", "Gle", "eD", "eDT", "egb", "t1", "t2", "attnT", "qd", "Xv", "Xk", "kd", "u", "wT", "vn"] + \
                           [f"P{k}" for k in range(2)] + [f"PT{k}" for k in range(2)] + [f"R{k}" for k in range(2)]
                F32R = mybir.dt.float32r
                cstr = _sb(es, nc, "r_cstr", [128, 8, 128], F32)
                c.op("dve", ["cst"], ["r_cstr"], lambda e: e.tensor_copy(out=cstr[:].bitcast(F32R), in_=self.cst[:]))
                mr = {"LE": cstr[:, 1, :].bitcast(F32R), "GE": cstr[:, 2, :].bitcast(F32R), "LT": cstr[:, 3, :].bitcast(F32R), "GT": cstr[:, 4, :].bitcast(F32R)}
                ONESr = cstr[:, 5, :].bitcast(F32R)
                dirs_r = [dict(INC=mr["LE"], AFT=mr["GT"]), dict(INC=mr["GE"], AFT=mr["LT"])]
                rset = set([f"P{k}" for k in range(7)] + [f"PT{k}" for k in range(6)] + ["R0", "R1", "Xv", "Xk"])
                bfn = set(["wT", "qd", "attnT", "kd", "vn"])
                wt = {n: [_sb(es, nc, f"r_{n}{i}", [128, 128], BF16 if n in bfn else F32) for i in range(WIN)] for n in dn_names}
                qTb = _sb(es, nc, "r_qTb", [128, S], BF16)
                kTb = _sb(es, nc, "r_kTb", [128, S], BF16)
                vtokb = _sb(es, nc, "r_vtokb", [128, NT, 129], BF16)
                Ssh = [[_sb(es, nc, f"r_Ssh{d_}{i}", [128, 129], BF16) for i in range(2)] for d_ in range(2)]
                pTm = [_sb(es, nc, f"r_pTm{i}", [128, 128], BF16) for i in range(WIN)]
                pm = [_sb(es, nc, f"r_pm{i}", [128, 128], BF16) for i in range(WIN)]
                ksm = [_sb(es, nc, f"r_ksm{i}", [128, 128], BF16) for i in range(WIN)]
                alias = {"X": "Gmat", "e": "eD", "p": "eDT", "pT": "egb", "ks": "t1"}
                dmall = _sb(es, nc, "r_dmall", [128, 2, NT, 128], F32)
                mlc = _sb(es, nc, "r_mlc", [128, 12, 2, NT], F32)
                zc = _sb(es, nc, "r_zc", [128, 1], F32)
                c.op("pool", [], ["r_zc"], lambda e: e.memset(zc[:], 0.0))
                nd = [_sb(es, nc, f"r_nd{i}", [128, 129], F32) for i in range(WIN)]
                dcol = [_sb(es, nc, f"r_dcol{i}", [128, 4], F32) for i in range(WIN)]
                Sst = [[_sb(es, nc, f"r_S{d_}{i}", [128, 129], F32) for i in range(2)] for d_ in range(2)]
                ob = [_sb(es, nc, f"r_ob{i}", [128, 128], BF16) for i in range(2)]
                ot = [_sb(es, nc, f"r_ot{i}", [128, 128], F32) for i in range(2)]
                oss = [_sb(es, nc, f"r_oss{i}", [128, 2], F32) for i in range(2)]
                mst4 = [_sb(es, nc, f"r_mx{i}", [128, 512], BF16) for i in range(2)]
                pb = [_ps(es, nc, f"r_pb{i}", [128, 512], F32) for i in range(7)]
                pTb = _ps(es, nc, "r_pTb", [128, 1024], BF16)

                def Q(b, q, n=128):
                    return pb[b][:, q * 128:q * 128 + n]

                def K_(b, q):
                    return ("r_pb", b, q)

                STAG = int(_os.environ.get("AB_STAG", "0"))

                def run_units(gens, stag=None):
                    stag = STAG if stag is None else stag
                    active = []
                    it = iter(gens)
                    done = False
                    since = stag
                    while True:
                        if (not done) and len(active) < WIN and (since >= stag or not active):
                            g = next(it, None)
                            if g is None:
                                done = True
                            else:
                                active.append(g)
                                since = 0
                        if not active:
                            if done:
                                break
                            continue
                        since += 1
                        for g in list(active):
                            try:
                                next(g)
                            except StopIteration:
                                active.remove(g)

                free = list(range(WIN))
                turn = [0, 0]

                def dn_unit(h, dr, si, t):
                    M = dirs[dr]
                    col = dr * 4 + h
                    sl_ = free.pop()
                    W = {n: wt[n][sl_] for n in dn_names}
                    for kq in range(7):
                        W[f"P{kq}"] = wt[f"P{kq % 2}"][sl_]
                    for kq in range(6):
                        W[f"PT{kq}"] = wt[f"PT{kq % 2}"][sl_]
                    ts = slice(t * 128, (t + 1) * 128)
                    ab_, vb_, sb_ = 2 * (sl_ % 2), 2 * (sl_ % 2) + 1, 4 + dr

                    def Wr(n):
                        return W[n][:].bitcast(F32R)

                    def k(n):
                        if n[0] == "P" and n[-1].isdigit():
                            n = n[:-1] + str(int(n[-1]) % 2)
                        return ("r_" + n, sl_)
                    Sc, Sn = Sst[dr][si % 2], Sst[dr][(si + 1) % 2]
                    Sck, Snk = ("r_S", dr, si % 2), ("r_S", dr, (si + 1) % 2)
                    gcol = Gg[:, t, col:col + 1]
                    c.op("dve", ["ab_Gg"], [k("Gmat")], lambda e: e.tensor_scalar(out=Wr("Gmat"), in0=M["AFT"], scalar1=gcol, scalar2=None, op0=ALU.mult))
                    c.op("act", ["ab_Gg"], [k("Gle")], lambda e: e.activation(out=Wr("Gle"), in_=M["INC"], func=AF.Copy, scale=gcol))
                    yield
                    c.op("act", ["r_vtok", "ab_Bt"], [k("Xv")], lambda e: e.activation(out=Wr("Xv"), in_=vtok[:, t, 0:128], func=AF.Copy, scale=Bt[:, t, col:col + 1]))
                    c.op("act", ["r_ktok", "ab_Bk"], [k("Xk")], lambda e: e.activation(out=Wr("Xk"), in_=ktok[:, t, :], func=AF.Copy, scale=Bk[:, t, col:col + 1]))
                    c.op("act", ["r_ktok", "ab_Edn"], [k("kd")], lambda e: e.activation(out=W["kd"][:], in_=ktok[:, t, :], func=AF.Copy, scale=Edn[:, t, 8 + col:8 + col + 1]))
                    yield

                    def mmA(e):
                        Mr = dirs_r[dr]
                        e.matmul(Q(ab_, 0), lhsT=Mr["INC"], rhs=Wr("Gmat"), start=True, stop=True)
                        e.matmul(Q(ab_, 2), lhsT=ONESr, rhs=Wr("Gle"), start=True, stop=True)
                        return e.matmul(Q(ab_, 1), lhsT=Wr("Gmat"), rhs=Mr["INC"], start=True, stop=True)
                    c.op("pe", [k("Gmat"), k("Gle"), "r_cstr"], [K_(ab_, 0), K_(ab_, 1), K_(ab_, 2)], mmA)
                    c.op("act", [K_(ab_, 0)], [k("eD")], lambda e: e.activation(out=W["eD"][:], in_=Q(ab_, 0), func=AF.Exp))
                    c.op("act", [K_(ab_, 1)], [k("eDT")], lambda e: e.activation(out=W["eDT"][:], in_=Q(ab_, 1), func=AF.Exp))
                    c.op("act", [K_(ab_, 2)], [k("egb")], lambda e: e.activation(out=W["egb"][:], in_=Q(ab_, 2), func=AF.Exp))
                    yield
                    c.op("pool", [k("eD")], [k("t1")], lambda e: e.tensor_tensor(out=W["t1"][:], in0=W["eD"][:], in1=M["STRICT"], op=ALU.mult))
                    c.op("pool", [k("eDT")], [k("t2")], lambda e: e.tensor_tensor(out=W["t2"][:], in0=W["eDT"][:], in1=M["INCLji"], op=ALU.mult))
                    yield
                    c.op("pool", [k("egb"), "r_qTb"], [k("qd")], lambda e: e.tensor_tensor(out=W["qd"][:], in0=qTb[:, ts], in1=W["egb"][:], op=ALU.mult))
                    yield

                    def mmB(e):
                        e.matmul(Q(vb_, 1), lhsT=kTb[:, ts], rhs=kTb[:, ts], start=True, stop=True)
                        return e.matmul(Q(vb_, 0), lhsT=kTb[:, ts], rhs=qTb[:, ts], start=True, stop=True)
                    c.op("pe", ["r_kTb", "r_qTb"], [K_(vb_, 1), K_(vb_, 0)], mmB)
                    c.op("dve", [K_(vb_, 1), k("t1"), "ab_nBt"], [k("P0")], lambda e: e.scalar_tensor_tensor(out=Wr("P0"), in0=Q(vb_, 1), scalar=nBt[:, t, col:col + 1], in1=W["t1"][:], op0=ALU.mult, op1=ALU.mult))
                    c.op("dve", [K_(vb_, 0), k("t2")], [k("attnT")], lambda e: e.tensor_tensor(out=W["attnT"][:], in0=Q(vb_, 0), in1=W["t2"][:], op=ALU.mult))
                    yield
                    if "P0" in bfn:
                        qv = Q(vb_, 2).bitcast(BF16)[:, 0:128]
                        c.op("pe", [k("P0"), "idb"], [K_(vb_, 2)], lambda e: e.transpose(out=qv, in_=W["P0"][:], identity=self.idb[:]))
                        c.op("dve", [K_(vb_, 2)], [k("PT0")], lambda e: e.tensor_copy(out=W["PT0"][:], in_=qv))
                    else:
                        c.op("pe", [k("P0")], [K_(vb_, 2)], lambda e: e.transpose(out=Q(vb_, 2), in_=W["P0"][:], identity=I))
                        c.op("dve", [K_(vb_, 2)], [k("PT0")], lambda e: e.tensor_copy(out=Wr("PT0"), in_=Q(vb_, 2)))
                    c.op("dve", [k("PT0")], [k("R0")], lambda e: e.tensor_tensor(out=Wr("R0"), in0=W["PT0"][:], in1=I, op=ALU.add))
                    yield
                    rc = "R0"
                    for kk in range(1, 7):
                        q2 = kk % 2
                        c.op("pe", [k(f"P{kk-1}"), k(f"PT{kk-1}")], [K_(ab_, q2)], lambda e, kk=kk, q2=q2: e.matmul(Q(ab_, q2), lhsT=Wr(f"PT{kk-1}"), rhs=Wr(f"P{kk-1}"), start=True, stop=True))
                        c.op("act", [K_(ab_, q2)], [k(f"P{kk}")], lambda e, kk=kk, q2=q2: e.copy(out=Wr(f"P{kk}"), in_=Q(ab_, q2)))
                        yield
                        if kk < 6:
                            c.op("pe", [k(f"P{kk-1}"), k(f"PT{kk-1}")], [K_(vb_, 2 + q2)], lambda e, kk=kk, q2=q2: e.matmul(Q(vb_, 2 + q2), lhsT=Wr(f"P{kk-1}"), rhs=Wr(f"PT{kk-1}"), start=True, stop=True))
                            c.op("dve", [K_(vb_, 2 + q2)], [k(f"PT{kk}")], lambda e, kk=kk, q2=q2: e.tensor_copy(out=Wr(f"PT{kk}"), in_=Q(vb_, 2 + q2)))
                            yield
                        rn = "R1" if rc == "R0" else "R0"
                        c.op("pe", [k(f"P{kk}"), k(rc)], [K_(vb_, q2)], lambda e, kk=kk, rc=rc, q2=q2: e.matmul(Q(vb_, q2), lhsT=Wr(f"P{kk}"), rhs=Wr(rc), start=True, stop=True))
                        c.op("dve", [K_(vb_, q2), k(rc)], [k(rn)], lambda e, rc=rc, rn=rn, q2=q2: e.tensor_tensor(out=Wr(rn), in0=W[rc][:], in1=Q(vb_, q2), op=ALU.add))
                        yield
                        rc = rn
                    c.op("pe", [k(rc), k("Xv")], [K_(ab_, 0)], lambda e: e.matmul(Q(ab_, 0), lhsT=Wr(rc), rhs=Wr("Xv"), start=True, stop=True))
                    c.op("act", [K_(ab_, 0)], [k("u")], lambda e: e.copy(out=W["u"][:], in_=Q(ab_, 0)))
                    yield
                    c.op("pe", [k(rc), k("Xk")], [K_(ab_, 1)], lambda e: e.matmul(Q(ab_, 1), lhsT=Wr("Xk"), rhs=Wr(rc), start=True, stop=True))
                    c.op("act", [K_(ab_, 1)], [k("wT")], lambda e: e.copy(out=W["wT"][:], in_=Q(ab_, 1)))
                    yield
                    while turn[dr] != si:
                        yield
                    Sbc, Sbn = Ssh[dr][si % 2], Ssh[dr][(si + 1) % 2]
                    Sbck, Sbnk = ("r_Ssh", dr, si % 2), ("r_Ssh", dr, (si + 1) % 2)
                    c.op("pe", [k("wT"), Sbck], [K_(sb_, 0)], lambda e: e.matmul(Q(sb_, 0), lhsT=W["wT"][:], rhs=Sbc[:, 0:128], start=True, stop=True))
                    c.op("dve", [K_(sb_, 0), k("u")], [k("vn")], lambda e: e.tensor_tensor(out=W["vn"][:], in0=W["u"][:], in1=Q(sb_, 0), op=ALU.subtract))

                    def mmo(e):
                        e.matmul(Q(sb_, 2), lhsT=W["qd"][:], rhs=Sbc[:, 0:128], start=True, stop=False)
                        e.matmul(Q(sb_, 2), lhsT=W["attnT"][:], rhs=W["vn"][:], start=False, stop=True)
                        return e.matmul(Q(sb_, 1), lhsT=W["kd"][:], rhs=W["vn"][:], start=True, stop=True)
                    c.op("pe", [k("qd"), k("attnT"), k("vn"), k("kd"), Sbck], [K_(sb_, 2), K_(sb_, 1)], mmo)
                    c.op("dve", [K_(sb_, 1), Sck, "ab_Edn"], [Snk], lambda e: e.scalar_tensor_tensor(out=Sn[:, 0:128], in0=Sc[:, 0:128], scalar=Edn[:, t, 16 + col:16 + col + 1], in1=Q(sb_, 1), op0=ALU.mult, op1=ALU.add))
                    c.op("act", [Snk], [Sbnk], lambda e: e.copy(out=Sbn[:, 0:128], in_=Sn[:, 0:128]))
                    c.op("dve", [K_(sb_, 2), ("r_ost", t)], [("r_ost", t)], lambda e: e.tensor_tensor(out=ost[:, t, :], in0=ost[:, t, :], in1=Q(sb_, 2), op=ALU.add))
                    turn[dr] += 1
                    free.append(sl_)

                MI, MS, B1, MNEW, A1, MT, NEGM, INTER, EMT, DEC, SRC, TMP = range(12)

                def ml_prep(h, dr, si, t):
                    M = dirs[dr]
                    col = dr * 4 + h
                    sl_ = free.pop()
                    X = wt[alias["X"]][sl_]
                    xk = ("r_" + alias["X"], sl_)
                    vb_ = 2 * (sl_ % 2) + 1
                    lf, li = Lf[:, t, col:col + 1], Li[:, t, col:col + 1]
                    Xr = X[:].bitcast(F32R)
                    c.op("dve", ["ab_Lf"], [xk], lambda e: e.tensor_scalar(out=Xr, in0=M["AFT"], scalar1=lf, scalar2=None, op0=ALU.mult))
                    c.op("dve", ["ab_Li", xk], [xk], lambda e: e.scalar_tensor_tensor(out=Xr, in0=I, scalar=li, in1=X[:], op0=ALU.mult, op1=ALU.add))
                    yield

                    def mmA(e):
                        Mr = dirs_r[dr]
                        e.matmul(Q(vb_, 0), lhsT=Mr["INC"], rhs=Xr, start=True, stop=True)
                        return e.matmul(Q(vb_, 1), lhsT=ONESr, rhs=Xr, start=True, stop=True)
                    c.op("pe", [xk, "r_cstr"], [K_(vb_, 0), K_(vb_, 1)], mmA)
                    c.op("dve", [K_(vb_, 0)], [("r_dmall", dr, t)], lambda e: e.tensor_tensor(out=dmall[:, dr, t, :], in0=Q(vb_, 0), in1=M["NEGM"], op=ALU.add))
                    c.op("dve", [K_(vb_, 1)], [("r_mlc", MS, dr)], lambda e: e.tensor_reduce(out=mlc[:, MS, dr, t:t + 1], in_=Q(vb_, 1), axis=AX.X, op=ALU.max))
                    yield
                    c.op("dve", [("r_dmall", dr, t)], [("r_mlc", MI, dr)], lambda e: e.tensor_reduce(out=mlc[:, MI, dr, t:t + 1], in_=dmall[:, dr, t, :], axis=AX.X, op=ALU.max))
                    free.append(sl_)

                def ml_main(h, dr, si, t):
                    col = dr * 4 + h
                    sl_ = free.pop()
                    ts = slice(t * 128, (t + 1) * 128)
                    e_ = wt[alias["e"]][sl_]
                    ek = ("r_" + alias["e"], sl_)
                    p_, pT_, ks_ = pm[sl_], pTm[sl_], ksm[sl_]
                    pk_, pTk, ksk = ("r_pm", sl_), ("r_pTm", sl_), ("r_ksm", sl_)
                    Cbc, Cbn = Ssh[dr][si % 2], Ssh[dr][(si + 1) % 2]
                    Cbck, Cbnk = ("r_Ssh", dr, si % 2), ("r_Ssh", dr, (si + 1) % 2)
                    Cc, Cn = Sst[dr][si % 2], Sst[dr][(si + 1) % 2]
                    Cck, Cnk = ("r_S", dr, si % 2), ("r_S", dr, (si + 1) % 2)
                    ab_, vb_, sb_ = 2 * (sl_ % 2), 2 * (sl_ % 2) + 1, 4 + dr

                    def col_(i_):
                        return mlc[:, i_, dr, t:t + 1]
                    c.op("act", [("r_dmall", dr, t), ("r_mlc", NEGM, dr)], [ek], lambda e: e.activation(out=e_[:], in_=dmall[:, dr, t, :], func=AF.Exp, bias=col_(NEGM)))
                    c.op("act", ["r_ktok", ("r_mlc", SRC, dr)], [ksk], lambda e: e.activation(out=ks_[:], in_=ktok[:, t, :], func=AF.Copy, scale=col_(SRC)))
                    yield
                    c.op("pe", ["r_qTb", "r_kTb"], [K_(vb_, 2)], lambda e: e.matmul(Q(vb_, 2), lhsT=qTb[:, ts], rhs=kTb[:, ts], start=True, stop=True))
                    c.op("dve", [ek, K_(vb_, 2)], [pk_], lambda e: e.tensor_tensor(out=p_[:], in0=e_[:], in1=Q(vb_, 2), op=ALU.mult))
                    yield
                    qv = Q(ab_, 0).bitcast(BF16)[:, 0:128]
                    c.op("pe", [pk_, "idb"], [K_(ab_, 0)], lambda e: e.transpose(out=qv, in_=p_[:], identity=self.idb[:]))
                    c.op("act", [K_(ab_, 0)], [pTk], lambda e: e.copy(out=pT_[:], in_=qv))
                    yield
                    c.op("pe", [pTk, "r_vtokb"], [K_(ab_, 2)], lambda e: e.matmul(pb[ab_][:, 256:385], lhsT=pT_[:], rhs=vtokb[:, t, :], start=True, stop=True))
                    c.op("dve", [K_(ab_, 2)], [("r_nd", sl_)], lambda e: e.tensor_copy(out=nd[sl_][:], in_=pb[ab_][:, 256:385]))
                    yield
                    while turn[dr] != si:
                        yield

                    def mms(e):
                        e.matmul(pb[sb_][:, 0:129], lhsT=qTb[:, ts], rhs=Cbc[:, 0:129], start=True, stop=True)
                        return e.matmul(pb[sb_][:, 256:385], lhsT=ks_[:], rhs=vtokb[:, t, :], start=True, stop=True)
                    c.op("pe", ["r_qTb", Cbck, ksk, "r_vtokb"], [K_(sb_, 0), K_(sb_, 2)], mms)
                    c.op("dve", [K_(sb_, 2), Cck, ("r_mlc", DEC, dr)], [Cnk], lambda e: e.scalar_tensor_tensor(out=Cn[:], in0=Cc[:], scalar=col_(DEC), in1=pb[sb_][:, 256:385], op0=ALU.mult, op1=ALU.add))
                    c.op("act", [Cnk], [Cbnk], lambda e: e.copy(out=Cbn[:], in_=Cn[:]))
                    c.op("dve", [K_(sb_, 0), ("r_mlc", INTER, dr), ("r_nd", sl_)], [("r_nd", sl_)], lambda e: e.scalar_tensor_tensor(out=nd[sl_][:], in0=pb[sb_][:, 0:129], scalar=col_(INTER), in1=nd[sl_][:], op0=ALU.mult, op1=ALU.add))
                    turn[dr] += 1
                    yield
                    dc = dcol[sl_]
                    dk = ("r_dcol", sl_)
                    c.op("dve", [("r_nd", sl_)], [dk], lambda e: e.tensor_scalar(out=dc[:, 0:1], in0=nd[sl_][:, 128:129], scalar1=-1.0, scalar2=None, op0=ALU.mult))
                    yield
                    c.op("dve", [("r_nd", sl_), dk], [dk], lambda e: e.tensor_tensor(out=dc[:, 1:2], in0=nd[sl_][:, 128:129], in1=dc[:, 0:1], op=ALU.max))
                    yield
                    c.op("dve", [dk, ("r_mlc", EMT, dr)], [dk], lambda e: e.tensor_tensor(out=dc[:, 2:3], in0=dc[:, 1:2], in1=col_(EMT), op=ALU.max))
                    yield
                    c.op("dve", [dk], [dk], lambda e: e.reciprocal(out=dc[:, 3:4], in_=dc[:, 2:3]))
                    yield
                    c.op("dve", [("r_nd", sl_), dk, ("r_ost", t)], [("r_ost", t)], lambda e: e.scalar_tensor_tensor(out=ost[:, t, :], in0=nd[sl_][:, 0:128], scalar=dc[:, 3:4], in1=ost[:, t, :], op0=ALU.mult, op1=ALU.add))
                    free.append(sl_)

                for alg in _algs:
                    for h in range(int(_os.environ.get("AB_NH", "4"))):
                        if alg == "dn":
                            fq, fk, tk, tv, tg = h, 4 + h, "dnk", "dnv", "dnz"
                        else:
                            fq, fk, tk, tv, tg = 8 + h, 12 + h, "mlk", "mlv", "mlo"
                        c.op("sp", [("ab_fm", fk)], ["r_qT"], lambda e: e.dma_start(out=qT[:], in_=FM[fk]))
                        c.op("act", ["r_qT"], ["r_kTb"], lambda e: e.copy(out=kTb[:], in_=qT[:]))
                        c.op("sp", [("ab_fm", fq)], ["r_qT"], lambda e: e.dma_start(out=qT[:], in_=FM[fq]))
                        c.op("pool", ["r_qT"], ["r_qTb"], lambda e: e.tensor_copy(out=qTb[:], in_=qT[:]))
                        for t0 in range(0, NT, 4):
                            t1_ = min(NT, t0 + 4)
                            for (dst_, nm_, ky_) in ((ktok, tk, "r_ktok"), (vtok, tv, "r_vtok")):
                                c.op("sp", [("ab_tm", nm_, h)], [ky_], lambda e, dst_=dst_, nm_=nm_, t0=t0, t1_=t1_: e.dma_start(out=dst_[:, t0:t1_, 0:128], in_=TM[nm_][t0 * 128:t1_ * 128, h * 128:(h + 1) * 128].rearrange("(t p) d -> p t d", p=128)))
                        c.op("pool", ["r_vtok"], ["r_vtok1"], lambda e: e.memset(vtok[:, :, 128:129], 1.0))
                        c.op("dve", ["r_vtok", "r_vtok1"], ["r_vtokb"], lambda e: e.tensor_copy(out=vtokb[:], in_=vtok[:]))
                        c.op("pool", [("r_ost", t) for t in range(NT)], [("r_ost", t) for t in range(NT)], lambda e: e.memset(ost[:], 0.0))
                        for dr in range(2):
                            c.op("pool", [("r_S", dr, 0)], [("r_S", dr, 0)], lambda e, dr=dr: e.memset(Sst[dr][0][:], 0.0))
                            c.op("pool", [("r_Ssh", dr, 0)], [("r_Ssh", dr, 0)], lambda e, dr=dr: e.memset(Ssh[dr][0][:], 0.0))
                        orders = [list(range(NT)), list(range(NT - 1, -1, -1))]
                        if alg == "dn":
                            turn[0] = turn[1] = 0
                            gens = []
                            for si in range(NT):
                                for dr in range(2):
                                    gens.append(dn_unit(h, dr, si, orders[dr][si]))
                            run_units(gens)
                        else:
                            gens = []
                            for si in range(NT):
                                for dr in range(2):
                                    gens.append(ml_prep(h, dr, si, orders[dr][si]))
                            run_units(gens)
                            for dr in range(2):
                                col = dr * 4 + h
                                bt_ = Mlt[:, :, 16 + col]
                                mn_, ms_, b1_ = mlc[:, MNEW, dr, :], mlc[:, MS, dr, :], mlc[:, B1, dr, :]
                                if dr == 0:
                                    c.op("dve", ["ab_Mlt", ("r_mlc", MS, dr)], [("r_mlc", MNEW, dr)], lambda e, bt_=bt_, mn_=mn_, ms_=ms_: e.tensor_tensor_scan(out=mn_, data0=bt_, data1=ms_, initial=0.0, op0=ALU.add, op1=ALU.max))
                                    c.op("dve", ["ab_Mlt", ("r_mlc", MNEW, dr)], [("r_mlc", B1, dr)], lambda e, bt_=bt_, mn_=mn_, b1_=b1_: e.tensor_tensor(out=b1_[:, 1:NT], in0=bt_[:, 1:NT], in1=mn_[:, 0:NT - 1], op=ALU.add))
                                    c.op("dve", ["ab_Mlt", ("r_mlc", B1, dr)], [("r_mlc", B1, dr)], lambda e, bt_=bt_, b1_=b1_: e.tensor_copy(out=b1_[:, 0:1], in_=bt_[:, 0:1]))
                                else:
                                    c.op("dve", ["ab_Mlt", ("r_mlc", MS, dr)], [("r_mlc", MNEW, dr)], lambda e, bt_=bt_, mn_=mn_, ms_=ms_: e.tensor_tensor_scan(out=mn_[:, ::-1], data0=bt_[:, ::-1], data1=ms_[:, ::-1], initial=0.0, op0=ALU.add, op1=ALU.max))
                                    c.op("dve", ["ab_Mlt", ("r_mlc", MNEW, dr)], [("r_mlc", B1, dr)], lambda e, bt_=bt_, mn_=mn_, b1_=b1_: e.tensor_tensor(out=b1_[:, 0:NT - 1], in0=bt_[:, 0:NT - 1], in1=mn_[:, 1:NT], op=ALU.add))
                                    c.op("dve", ["ab_Mlt", ("r_mlc", B1, dr)], [("r_mlc", B1, dr)], lambda e, bt_=bt_, b1_=b1_: e.tensor_copy(out=b1_[:, NT - 1:NT], in_=bt_[:, NT - 1:NT]))
                            for dr in range(2):
                                col = dr * 4 + h

                                def A_(i_, dr=dr):
                                    return mlc[:, i_, dr, :]

                                def kk_(i_, dr=dr):
                                    return ("r_mlc", i_, dr)
                                bcum_, btot_, wsrc_ = Mlt[:, :, col], Mlt[:, :, 16 + col], Ws[:, :, col]
                                c.op("dve", ["ab_Mlt"], [kk_(A1)], lambda e, dr=dr: e.tensor_tensor(out=A_(A1), in0=bcum_, in1=btot_, op=ALU.subtract))
                                c.op("dve", [kk_(A1), kk_(B1)], [kk_(A1)], lambda e, dr=dr: e.tensor_tensor(out=A_(A1), in0=A_(A1), in1=A_(B1), op=ALU.add))
                                c.op("dve", [kk_(A1), kk_(MI)], [kk_(MT)], lambda e, dr=dr: e.tensor_tensor(out=A_(MT), in0=A_(A1), in1=A_(MI), op=ALU.max))
                                c.op("dve", [kk_(MT)], [kk_(NEGM)], lambda e, dr=dr: e.tensor_scalar(out=A_(NEGM), in0=A_(MT), scalar1=-1.0, scalar2=None, op0=ALU.mult))
                                c.op("dve", [kk_(A1), kk_(MT)], [kk_(TMP)], lambda e, dr=dr: e.tensor_tensor(out=A_(TMP), in0=A_(A1), in1=A_(MT), op=ALU.subtract))
                                c.op("act", [kk_(TMP)], [kk_(INTER)], lambda e, dr=dr: e.activation(out=A_(INTER), in_=A_(TMP), func=AF.Exp))
                                c.op("act", [kk_(NEGM)], [kk_(EMT)], lambda e, dr=dr: e.activation(out=A_(EMT), in_=A_(NEGM), func=AF.Exp))
                                c.op("dve", [kk_(B1), kk_(MNEW), kk_(INTER)], [kk_(TMP)], lambda e, dr=dr: e.tensor_tensor(out=A_(TMP), in0=A_(B1), in1=A_(MNEW), op=ALU.subtract))
                                c.op("act", [kk_(TMP)], [kk_(DEC)], lambda e, dr=dr: e.activation(out=A_(DEC), in_=A_(TMP), func=AF.Exp))
                                c.op("dve", ["ab_Ws", kk_(MNEW), kk_(DEC)], [kk_(TMP)], lambda e, dr=dr: e.tensor_tensor(out=A_(TMP), in0=wsrc_, in1=A_(MNEW), op=ALU.subtract))
                                c.op("act", [kk_(TMP)], [kk_(SRC)], lambda e, dr=dr: e.activation(out=A_(SRC), in_=A_(TMP), func=AF.Exp))
                            turn[0] = turn[1] = 0
                            gens = []
                            for si in range(NT):
                                for dr in range(2):
                                    gens.append(ml_main(h, dr, si, orders[dr][si]))
                            run_units(gens)
                        mchunk = h if alg == "dn" else 4 + h
                        for t0 in range(0, NT, 4):
                            t1_ = min(NT, t0 + 4)
                            c.op("sp", [("ab_tm", tg, h)], ["r_qT"], lambda e, t0=t0, t1_=t1_: e.dma_start(out=gtok[:, t0:t1_, :], in_=TM[tg][t0 * 128:t1_ * 128, h * 128:(h + 1) * 128].rearrange("(t p) d -> p t d", p=128)))
                        for t in range(NT):
                            u2 = t % 2
                            if alg == "dn":
                                c.op("act", [("r_ost", t)], [("r_ot", u2), ("r_oss", u2)], lambda e, t=t, u2=u2: e.activation(out=ot[u2][:], in_=ost[:, t, :], func=AF.Square, accum_out=oss[u2][:, 0:1]))
                                c.op("dve", [("r_oss", u2)], [("r_oss", u2)], lambda e, u2=u2: e.tensor_scalar(out=oss[u2][:, 0:1], in0=oss[u2][:, 0:1], scalar1=1.0 / 128, scalar2=EPS, op0=ALU.mult, op1=ALU.add))
                                c.op("act", [("r_oss", u2)], [("r_oss", u2)], lambda e, u2=u2: e.activation(out=oss[u2][:, 0:1], in_=oss[u2][:, 0:1], func=AF.Sqrt))
                                c.op("dve", [("r_oss", u2)], [("r_oss", u2)], lambda e, u2=u2: e.reciprocal(out=oss[u2][:, 0:1], in_=oss[u2][:, 0:1]))
                                c.op("dve", [("r_ost", t), ("r_oss", u2), "ab_dnw"], [("r_ot", u2)], lambda e, t=t, u2=u2: e.scalar_tensor_tensor(out=ot[u2][:], in0=ost[:, t, :], scalar=oss[u2][:, 0:1], in1=dnw[:], op0=ALU.mult, op1=ALU.mult))
                                c.op("dve", [("r_ot", u2), "r_qT"], [("r_ob", u2)], lambda e, t=t, u2=u2: e.tensor_tensor(out=ob[u2][:], in0=ot[u2][:], in1=gtok[:, t, :], op=ALU.mult))
                            else:
                                c.op("dve", [("r_ost", t), "r_qT"], [("r_ob", u2)], lambda e, t=t, u2=u2: e.tensor_tensor(out=ob[u2][:], in0=ost[:, t, :], in1=gtok[:, t, :], op=ALU.mult))
                            c.op("pe", [("r_ob", u2), "idb"], [("r_pTb", t % 4)], lambda e, t=t, u2=u2: e.transpose(out=pTb[:, (t % 4) * 128:(t % 4 + 1) * 128], in_=ob[u2][:], identity=self.idb[:]))
                            if t % 4 == 3:
                                tb = t // 4
                                mx = mst4[tb % 2]
                                c.op("act", [("r_pTb", q_) for q_ in range(4)], [("r_mx", tb % 2)], lambda e, mx=mx: e.copy(out=mx[:], in_=pTb[:, 0:512]))
                                c.op("gq", [("r_mx", tb % 2)], [("ab_mix", tb)], lambda e, mx=mx, tb=tb, mchunk=mchunk: e.dma_start(out=mixscr[mchunk, :, tb * 512:(tb + 1) * 512], in_=mx[:]))
                c.barrier()
        self.stage_c_scr(c, L, mixscr, 8, Dr["ab_w_out"][j], Dr["mix_norm_post"][L:L + 1, :], src, dst, "abc")


W_NAMES = ['mix_norm_pre', 'mix_norm_post', 'ab_w_in', 'ab_w_out', 'dn_conv_w', 'dn_a_log', 'dn_dt_bias',
           'dn_out_norm', 'ml_conv_w', 'ml_i_bias', 'ml_f_bias', 'c_w_in', 'c_w_out', 'c_conv_w', 'c_conv_b',
           'c_w_a', 'c_b_a', 'c_w_x', 'c_b_x', 'c_lambda', 'xa_norm_pre', 'xa_norm_post', 'xa_mem_norm',
           'xa_w_q', 'xa_w_kv', 'xa_w_o', 'ffn_norm_pre', 'ffn_norm_post', 'ffn_w_up', 'ffn_w_down']


def const_masks():
    p = np.arange(128)[:, None]
    f = np.arange(128)[None, :]
    ms = [p == f, p <= f, p >= f, p < f, p > f, np.ones((128, 128), bool)]
    arr = [m.astype(np.float32) for m in ms]
    arr.append(NEG * (p < f).astype(np.float32))
    arr.append(NEG * (p > f).astype(np.float32))
    return np.ascontiguousarray(np.concatenate(arr, axis=1)).astype(np.float32)


def build(S, shapes, stages, Mm=256):
    B = Builder(S, Mm)
    P = B.P
    nc = B.nc
    for n, shp in shapes.items():
        P.din(n, shp)
    P.din("cmask", [128, 8 * 128])
    out = P.dout("out", [S, D])
    with ExitStack() as es:
        c = Ctx(nc, es)
        B.load_consts(c, es)
        src = P.dram["x"]
        for (name, L) in stages:
            getattr(B, name)(c, L, src, out)
            src = out
        c.barrier()
        c.finish()
    B.nops = c.nops
    return B


STAGES = [("mixer_ab", 0), ("xa", 0), ("ffn", 0), ("mixer_c", 1), ("xa", 1), ("ffn", 1)]


def kernel(**inputs):
    x = np.ascontiguousarray(np.asarray(inputs["x"], dtype=np.float32))
    mem = np.ascontiguousarray(np.asarray(inputs["mem"], dtype=np.float32))
    nb, S, _ = x.shape
    shapes = {"x": (S, D), "mem": tuple(mem.shape[1:])}
    ws = {}
    for n in W_NAMES:
        ws[n] = np.ascontiguousarray(np.asarray(inputs[n], dtype=np.float32))
        shapes[n] = ws[n].shape
    B = build(S, shapes, STAGES, Mm=mem.shape[1])
    cm = const_masks()
    in_maps = []
    for b in range(nb):
        m = {"x": x[b], "mem": mem[b], "cmask": cm}
        m.update(ws)
        in_maps.append(m)
    res = run_bass_kernel_spmd(B.nc, in_maps, core_ids=list(range(nb)))
    return np.stack([np.asarray(r["out"], dtype=np.float32) for r in res.results], axis=0)
```

```python
import numpy as np
from contextlib import ExitStack
import concourse.bass as bass
import concourse.mybir as mybir
from concourse.bass_utils import run_bass_kernel_spmd

F32 = mybir.dt.float32
BF16 = mybir.dt.bfloat16
AF = mybir.ActivationFunctionType
ALU = mybir.AluOpType
AX = mybir.AxisListType
D = 1024
EPS = 1e-6
NEG = -30000.0


class _Eng:
    def __init__(self, ctx, name, be, is_dma, nslots=14):
        self.name, self.be, self.is_dma = name, be, is_dma
        self.waited = {}
        if is_dma:
            self.slots = [ctx.new_sem(f"{name}_d{i}") for i in range(nslots)]
            self.n = 0
        else:
            self.sem = ctx.new_sem(f"{name}_s")
            self.count = 0


class _Buf:
    __slots__ = ("w", "r", "rd")

    def __init__(self):
        self.w = None
        self.r = {}
        self.rd = []


class Ctx:
    def __init__(self, nc, es):
        self.nc, self.es = nc, es
        self.sems = []
        self.bufs = {}
        self.engs = {}
        for name, be, dma in (("pe", nc.tensor, False), ("act", nc.scalar, False),
                              ("dve", nc.vector, False), ("pool", nc.gpsimd, False),
                              ("sp", nc.sync, True), ("gq", nc.gpsimd, True)):
            self.engs[name] = _Eng(self, name, be, dma)
        self.nops = 0
        import os as _os
        self.limit = int(_os.environ["OP_LIMIT"]) if "OP_LIMIT" in _os.environ else None
        self.trace = tuple(int(v) for v in _os.environ["OP_TRACE"].split(",")) if "OP_TRACE" in _os.environ else None

    def new_sem(self, name):
        s = self.es.enter_context(self.nc.semaphore(name))
        self.sems.append(s)
        return len(self.sems) - 1

    def buf(self, k):
        b = self.bufs.get(k)
        if b is None:
            b = self.bufs[k] = _Buf()
        return b

    def op(self, eng, reads, writes, fn):
        E = self.engs[eng]
        need = {}
        if self.trace is not None and self.trace[0] <= self.nops < self.trace[1]:
            print("OP", self.nops, eng, "R", reads, "W", writes)
        if self.limit is not None and self.nops >= self.limit:
            self.nops += 1
            return None

        def add(ev, raw):
            de, si, val = ev
            if de is E and not E.is_dma:
                if E.name == "pe" or not raw:
                    return
            if need.get(si, 0) < val:
                need[si] = val

        for k in reads:
            b = self.buf(k)
            if b.w is not None:
                add(b.w, True)
        for k in writes:
            b = self.buf(k)
            if b.w is not None:
                add(b.w, True)
            for ev in b.r.values():
                add(ev, False)
            for ev in b.rd:
                add(ev, False)
        banks = set()
        for k in list(reads) + list(writes):
            bk = self.bank(k)
            if bk is not None:
                banks.add(bk)
        for bk in banks:
            b = self.buf(bk)
            if b.w is not None:
                add(b.w, False)
        if E.is_dma:
            slot = E.n % len(E.slots)
            gen = E.n // len(E.slots)
            si_own = E.slots[slot]
            if gen > 0 and need.get(si_own, 0) < 16 * gen:
                need[si_own] = 16 * gen
        for si, val in need.items():
            if E.waited.get(si, 0) >= val:
                continue
            E.be.wait_ge(self.sems[si], val)
            E.waited[si] = val
        ins = fn(E.be)
        if E.is_dma:
            ins.then_inc(self.sems[si_own], 16)
            ev = (E, si_own, 16 * (gen + 1))
            E.n += 1
        else:
            E.count += 1
            ins.then_inc(self.sems[E.sem], 1)
            ev = (E, E.sem, E.count)
        for k in reads:
            b = self.buf(k)
            if E.is_dma:
                b.rd.append(ev)
            else:
                b.r[E.name] = ev
        for k in writes:
            b = self.buf(k)
            b.w = ev
            b.r = {}
            b.rd = []
        for bk in banks:
            self.buf(bk).w = ev
        self.nops += 1
        return ev

    _BANKED = ("r_pb", "ab_pp", "x_pa", "f_pu", "c_pp", "pT", "py")

    def bank(self, k):
        if isinstance(k, tuple):
            if k[0] in self._BANKED:
                return ("BANK", k[0], k[1])
            if k[0] == "r_pTb":
                return ("BANK", "r_pTb")
        elif k == "x_pss":
            return ("BANK", "x_pss")
        return None

    def barrier(self):
        evs = []
        for E in self.engs.values():
            if E.is_dma:
                for i, si in enumerate(E.slots):
                    cnt = (E.n - i + len(E.slots) - 1) // len(E.slots) if E.n > i else 0
                    if cnt > 0:
                        evs.append((si, 16 * cnt))
            elif E.count > 0:
                evs.append((E.sem, E.count))
        for E in self.engs.values():
            for si, val in evs:
                if (not E.is_dma) and si == E.sem:
                    continue
                if E.waited.get(si, 0) >= val:
                    continue
                E.be.wait_ge(self.sems[si], val)
                E.waited[si] = val

    def finish(self):
        for E in self.engs.values():
            if E.is_dma:
                for i, si in enumerate(E.slots):
                    cnt = (E.n - i + len(E.slots) - 1) // len(E.slots) if E.n > i else 0
                    if cnt > 0 and E.waited.get(si, 0) < 16 * cnt:
                        E.be.wait_ge(self.sems[si], 16 * cnt)
                        E.waited[si] = 16 * cnt


class Prog:
    def __init__(self, S):
        self.S = S
        self.NT = S // 128
        self.nc = bass.Bass("TRN2", target_bir_lowering=False)
        self.dram = {}

    def din(self, name, shape, dt=F32):
        self.dram[name] = self.nc.dram_tensor(name, list(shape), dt, kind="ExternalInput").ap()
        return self.dram[name]

    def dout(self, name, shape, dt=F32):
        self.dram[name] = self.nc.dram_tensor(name, list(shape), dt, kind="ExternalOutput").ap()
        return self.dram[name]

    def dscr(self, name, shape, dt=F32):
        self.dram[name] = self.nc.dram_tensor(name, list(shape), dt, kind="Internal").ap()
        return self.dram[name]


_UID = [0]


def _sb(es, nc, name, shape, dt):
    _UID[0] += 1
    return es.enter_context(nc.sbuf_tensor(f"{name}_{_UID[0]}", list(shape), dt))


def _ps(es, nc, name, shape, dt):
    _UID[0] += 1
    return es.enter_context(nc.psum_tensor(f"{name}_{_UID[0]}", list(shape), dt))


class Builder:
    def __init__(self, S, Mm=256):
        self.S, self.NT, self.Mm = S, S // 128, Mm
        self.P = Prog(S)
        self.nc = self.P.nc

    def load_consts(self, c, es):
        nc = self.nc
        cm = self.P.dram["cmask"]
        self.cst = _sb(es, nc, "cst", [128, 8, 128], F32)
        c.op("sp", [], ["cst"], lambda e: e.dma_start(out=self.cst[:], in_=cm.rearrange("p (k f) -> p k f", k=8)))
        self.idb = _sb(es, nc, "idb", [128, 128], BF16)
        c.op("dve", ["cst"], ["idb"], lambda e: e.tensor_copy(out=self.idb[:], in_=self.cst[:, 0, :]))
        self.onesb = _sb(es, nc, "onesb", [128, 128], BF16)
        c.op("dve", ["cst"], ["onesb"], lambda e: e.tensor_copy(out=self.onesb[:], in_=self.cst[:, 5, :]))
        self.I = self.cst[:, 0, :]
        self.LE = self.cst[:, 1, :]
        self.GE = self.cst[:, 2, :]
        self.LT = self.cst[:, 3, :]
        self.GT = self.cst[:, 4, :]
        self.ONES = self.cst[:, 5, :]
        self.NLT = self.cst[:, 6, :]
        self.NGT = self.cst[:, 7, :]

    def load_bc(self, c, tile, key, src_row):
        c.op("sp", [], [key], lambda e: e.dma_start(out=tile, in_=src_row.partition_broadcast(128)))

    def rstd_from_ss(self, c, ss, key, n):
        c.op("dve", [key], [key], lambda e: e.tensor_scalar(out=ss, in0=ss, scalar1=1.0 / n, scalar2=EPS, op0=ALU.mult, op1=ALU.add))
        c.op("act", [key], [key], lambda e: e.activation(out=ss, in_=ss, func=AF.Sqrt))
        c.op("dve", [key], [key], lambda e: e.reciprocal(out=ss, in_=ss))

    def prenorm_tile(self, c, W, src_ap, wkey, wbc, dstT, dkey, slot, skey=None):
        h, sq, ss, xn, pT = W["h"][slot], W["sq"], W["ss"][slot], W["xn"][slot], W["pT"][slot]
        hk, ssk, xnk, pk = ("h", slot), ("ss", slot), ("xn", slot), ("pT", slot)
        c.op("sp", [skey] if skey else [], [hk], lambda e: e.dma_start(out=h[:], in_=src_ap))
        c.op("act", [hk], ["sq", ssk], lambda e: e.activation(out=sq[:], in_=h[:], func=AF.Square, accum_out=ss[:]))
        self.rstd_from_ss(c, ss[:], ssk, D)
        c.op("dve", [hk, ssk, wkey], [xnk], lambda e: e.scalar_tensor_tensor(out=xn[:], in0=h[:], scalar=ss[:], in1=wbc, op0=ALU.mult, op1=ALU.mult))

        def tr(e):
            for k in range(8):
                ins = e.transpose(out=pT[:, k * 128:(k + 1) * 128], in_=xn[:, k * 128:(k + 1) * 128], identity=self.idb[:])
            return ins
        c.op("pe", [xnk, "idb"], [pk], tr)
        c.op("act", [pk], [dkey], lambda e: e.copy(out=dstT, in_=pT[:].rearrange("p (k t) -> p k t", k=8)))

    def alloc_prenorm(self, es, tag="", sq=None):
        nc = self.nc
        W = {"h": [_sb(es, nc, f"pn_h{i}{tag}", [128, D], F32) for i in range(2)],
             "sq": sq if sq is not None else _sb(es, nc, f"pn_sq{tag}", [128, D], F32),
             "ss": [_sb(es, nc, f"pn_ss{i}{tag}", [128, 1], F32) for i in range(2)],
             "xn": [_sb(es, nc, f"pn_xn{i}{tag}", [128, D], BF16) for i in range(2)],
             "pT": [_ps(es, nc, f"pn_pT{i}{tag}", [128, D], BF16) for i in range(2)]}
        return W

    def outproj_tile(self, c, W, zT_fn, zkeys, KC, wo, wokey, wpost, wpkey, res_ap, out_ap, slot, rkey=None, okey=None):
        py = W["py"][slot % len(W["py"])]
        pk = ("py", slot % len(W["py"]))
        hr, ss2, t1 = W["hr"][slot], W["ss2"][slot], W["t1"][slot]
        hrk, s2k, t1k = ("hr", slot), ("ss2", slot), ("t1", slot)

        def mm(e):
            for nb in range(2):
                for kc in range(KC):
                    ins = e.matmul(py[nb][:], lhsT=zT_fn(kc), rhs=wo[:, kc, nb * 512:(nb + 1) * 512], start=(kc == 0), stop=(kc == KC - 1))
            return ins
        c.op("pe", list(zkeys) + [wokey], [pk], mm)
        c.op("sp", [rkey] if rkey else [], [hrk], lambda e: e.dma_start(out=hr[:], in_=res_ap))
        c.op("act", [pk], ["sq2", (s2k, 0)], lambda e: e.activation(out=W["sq2"][:, 0:512], in_=py[0][:], func=AF.Square, accum_out=ss2[:, 0:1]))
        c.op("act", [pk], ["sq2", (s2k, 1)], lambda e: e.activation(out=W["sq2"][:, 512:1024], in_=py[1][:], func=AF.Square, accum_out=ss2[:, 1:2]))
        c.op("dve", [(s2k, 0), (s2k, 1)], [s2k], lambda e: e.tensor_tensor(out=ss2[:, 2:3], in0=ss2[:, 0:1], in1=ss2[:, 1:2], op=ALU.add))
        self.rstd_from_ss(c, ss2[:, 2:3], s2k, D)
        for nb in range(2):
            c.op("dve", [pk, s2k, wpkey], [(t1k, nb)], lambda e, nb=nb: e.scalar_tensor_tensor(out=t1[:, nb * 512:(nb + 1) * 512], in0=py[nb][:], scalar=ss2[:, 2:3], in1=wpost[:, nb * 512:(nb + 1) * 512], op0=ALU.mult, op1=ALU.mult))
        c.op("pool", [(t1k, 0), (t1k, 1), hrk], [hrk], lambda e: e.tensor_tensor(out=hr[:], in0=hr[:], in1=t1[:], op=ALU.add))
        c.op("gq", [hrk], [okey] if okey else [], lambda e: e.dma_start(out=out_ap, in_=hr[:]))

    def alloc_outproj(self, es, tag="", npy=2):
        nc = self.nc
        return {"py": [[_ps(es, nc, f"op_py{i}{j}{tag}", [128, 512], F32) for j in range(2)] for i in range(npy)],
                "hr": [_sb(es, nc, f"op_hr{i}{tag}", [128, D], F32) for i in range(2)],
                "ss2": [_sb(es, nc, f"op_ss{i}{tag}", [128, 4], F32) for i in range(2)],
                "t1": [_sb(es, nc, f"op_t1{i}{tag}", [128, D], F32) for i in range(2)],
                "sq2": _sb(es, nc, f"op_sq2{tag}", [128, D], F32)}

    def load_w_bf16(self, c, dst, key, src3):
        KC = dst.shape[1]
        N = dst.shape[2]
        step = max(1, 2048 // N)
        step = min(KC, 4)
        for k0 in range(0, KC, step):
            k1 = min(KC, k0 + step)
            for n0 in range(0, N, 2048):
                n1 = min(N, n0 + 2048)
                c.op("gq", [], [key], lambda e, k0=k0, k1=k1, n0=n0, n1=n1: e.dma_start(out=dst[:, k0:k1, n0:n1], in_=src3[:, k0:k1, n0:n1]))

    def ffn(self, c, L, src, dst):
        nc, S, NT = self.nc, self.S, self.NT
        Dr = self.P.dram
        with ExitStack() as es:
            wup = _sb(es, nc, "f_wup", [128, 8, 4096], BF16)
            wdn = _sb(es, nc, "f_wdn", [128, 32, 1024], BF16)
            wpre = _sb(es, nc, "f_wpre", [128, D], F32)
            wpost = _sb(es, nc, "f_wpost", [128, D], F32)
            self.load_bc(c, wpre[:], "f_wpre", Dr["ffn_norm_pre"][L:L + 1, :])
            self.load_bc(c, wpost[:], "f_wpost", Dr["ffn_norm_post"][L:L + 1, :])
            self.load_w_bf16(c, wup, "f_wup", Dr["ffn_w_up"][L].rearrange("(k p) n -> p k n", p=128))
            self.load_w_bf16(c, wdn, "f_wdn", Dr["ffn_w_down"][L].rearrange("(k p) n -> p k n", p=128))
            OP = self.alloc_outproj(es, "f")
            PN = self.alloc_prenorm(es, "f", sq=OP["sq2"])
            TB = 256
            hnT = [_sb(es, nc, f"f_hnT{i}", [128, 8, TB], BF16) for i in range(2)]
            aT = _sb(es, nc, "f_aT", [128, 32, TB], BF16)
            rl = [_sb(es, nc, f"f_rl{i}", [128, TB], F32) for i in range(2)]
            pu = [_ps(es, nc, f"f_pu{i}", [128, 512], F32) for i in range(2)]
            nblk = S // TB
            tpb = TB // 128
            for b in range(nblk):
                hb = hnT[b % 2]
                for t in range(tpb):
                    tt = b * tpb + t
                    self.prenorm_tile(c, PN, src[tt * 128:(tt + 1) * 128, :], "f_wpre", wpre[:], hb[:, :, t * 128:(t + 1) * 128], ("f_hnT", b % 2, t), tt % 2, skey=(src.tensor.name, tt))
                hkeys = [("f_hnT", b % 2, t) for t in range(tpb)]
                for fc in range(32):
                    p = pu[fc % 2]

                    def mm(e, fc=fc, p=p):
                        for kc in range(8):
                            ins = e.matmul(p[:, 0:TB], lhsT=wup[:, kc, fc * 128:(fc + 1) * 128], rhs=hb[:, kc, :], start=(kc == 0), stop=(kc == 7))
                        return ins
                    c.op("pe", hkeys + ["f_wup"], [("f_pu", fc % 2)], mm)
                    r = rl[fc % 2]
                    c.op("act", [("f_pu", fc % 2)], [("f_rl", fc % 2)], lambda e, p=p, r=r: e.activation(out=r[:], in_=p[:, 0:TB], func=AF.Relu))
                    c.op("dve", [("f_rl", fc % 2)], [("f_aT", fc)], lambda e, r=r, fc=fc: e.tensor_tensor(out=aT[:, fc, :], in0=r[:], in1=r[:], op=ALU.mult))
                akeys = [("f_aT", fc) for fc in range(32)]
                for t in range(tpb):
                    tt = b * tpb + t
                    self.outproj_tile(c, OP, lambda kc, t=t: aT[:, kc, t * 128:(t + 1) * 128], akeys, 32, wdn, "f_wdn", wpost, "f_wpost",
                                      src[tt * 128:(tt + 1) * 128, :], dst[tt * 128:(tt + 1) * 128, :], tt % 2,
                                      rkey=(src.tensor.name, tt), okey=(dst.tensor.name, tt))
            c.barrier()

    def xa(self, c, L, src, dst):
        nc, S, NT, Mm = self.nc, self.S, self.NT, self.Mm
        Dr = self.P.dram
        MC = Mm // 128
        with ExitStack() as es:
            wq = _sb(es, nc, "x_wq", [128, 8, 1024], BF16)
            wo = _sb(es, nc, "x_wo", [128, 8, 1024], BF16)
            wpre = _sb(es, nc, "x_wpre", [128, D], F32)
            wpost = _sb(es, nc, "x_wpost", [128, D], F32)
            wmem = _sb(es, nc, "x_wmem", [128, D], F32)
            self.load_bc(c, wpre[:], "x_wpre", Dr["xa_norm_pre"][L:L + 1, :])
            self.load_bc(c, wpost[:], "x_wpost", Dr["xa_norm_post"][L:L + 1, :])
            self.load_bc(c, wmem[:], "x_wmem", Dr["xa_mem_norm"][L:L + 1, :])
            self.load_w_bf16(c, wq, "x_wq", Dr["xa_w_q"][L].rearrange("(k p) n -> p k n", p=128))
            self.load_w_bf16(c, wo, "x_wo", Dr["xa_w_o"][L].rearrange("(k p) n -> p k n", p=128))
            PN = self.alloc_prenorm(es, "x")
            OP = self.alloc_outproj(es, "x", npy=1)
            kT = _sb(es, nc, "x_kT", [128, 8, Mm], BF16)
            V = _sb(es, nc, "x_V", [128, MC, 1024], BF16)
            pa = [_ps(es, nc, f"x_pa{i}", [128, 512], F32) for i in range(2)]
            with ExitStack() as es2:
                wkv = _sb(es2, nc, "x_wkv", [128, 8, 2048], BF16)
                self.load_w_bf16(c, wkv, "x_wkv", Dr["xa_w_kv"][L].rearrange("(k p) n -> p k n", p=128))
                mnT = _sb(es2, nc, "x_mnT", [128, 8, Mm], BF16)
                for mc in range(MC):
                    self.prenorm_tile(c, PN, Dr["mem"][mc * 128:(mc + 1) * 128, :], "x_wmem", wmem[:], mnT[:, :, mc * 128:(mc + 1) * 128], ("x_mnT", mc), mc % 2)
                mkeys = [("x_mnT", mc) for mc in range(MC)]
                for ch in range(8):
                    p = pa[ch % 2]

                    def mm(e, ch=ch, p=p):
                        for kc in range(8):
                            ins = e.matmul(p[:, 0:Mm], lhsT=wkv[:, kc, ch * 128:(ch + 1) * 128], rhs=mnT[:, kc, :], start=(kc == 0), stop=(kc == 7))
                        return ins
                    c.op("pe", mkeys + ["x_wkv"], [("x_pa", ch % 2)], mm)
                    c.op("act", [("x_pa", ch % 2)], ["x_kT"], lambda e, ch=ch, p=p: e.copy(out=kT[:, ch, :], in_=p[:, 0:Mm]))
                for mc in range(MC):
                    for nb in range(2):
                        p = pa[nb]

                        def mm(e, mc=mc, nb=nb, p=p):
                            for kc in range(8):
                                ins = e.matmul(p[:], lhsT=mnT[:, kc, mc * 128:(mc + 1) * 128], rhs=wkv[:, kc, 1024 + nb * 512:1024 + (nb + 1) * 512], start=(kc == 0), stop=(kc == 7))
                            return ins
                        c.op("pe", mkeys + ["x_wkv"], [("x_pa", nb)], mm)
                        c.op("act", [("x_pa", nb)], ["x_V"], lambda e, mc=mc, nb=nb, p=p: e.copy(out=V[:, mc, nb * 512:(nb + 1) * 512], in_=p[:]))
                c.barrier()
            hnT = [_sb(es, nc, f"x_hnT{i}", [128, 8, 512], BF16) for i in range(2)]
            qT = _sb(es, nc, "x_qT", [128, 8, 512], BF16)
            oT = [_sb(es, nc, f"x_oT{i}", [128, 8, 512], BF16) for i in range(2)]
            sc = [_sb(es, nc, f"x_sc{i}", [128, Mm], F32) for i in range(2)]
            pb = [_sb(es, nc, f"x_pb{i}", [128, Mm], BF16) for i in range(2)]
            pTs = [_sb(es, nc, f"x_pTs{i}", [128, MC, 512], BF16) for i in range(2)]
            st = [_sb(es, nc, f"x_st{i}", [128, 4], F32) for i in range(2)]
            ps_s = [_ps(es, nc, f"x_pss{i}", [128, 512], F32) for i in range(1)]
            scale = 256.0 ** -0.5
            it = 0
            for blk in range(NT // 4):
                sl = blk % 2
                for t in range(4):
                    tt = blk * 4 + t
                    self.prenorm_tile(c, PN, src[tt * 128:(tt + 1) * 128, :], "x_wpre", wpre[:], hnT[sl][:, :, t * 128:(t + 1) * 128], ("x_hnT", sl, t), tt % 2, skey=(src.tensor.name, tt))
                hk = [("x_hnT", sl, t) for t in range(4)]
                for ch in range(8):
                    p = pa[ch % 2]

                    def mm(e, ch=ch, p=p, sl=sl):
                        for kc in range(8):
                            ins = e.matmul(p[:], lhsT=wq[:, kc, ch * 128:(ch + 1) * 128], rhs=hnT[sl][:, kc, :], start=(kc == 0), stop=(kc == 7))
                        return ins
                    c.op("pe", hk + ["x_wq"], [("x_pa", ch % 2)], mm)
                    c.op("act", [("x_pa", ch % 2)], [("x_qT", ch)], lambda e, ch=ch, p=p: e.copy(out=qT[:, ch, :], in_=p[:]))
                for hd in range(4):
                    h2 = hd % 2
                    pT4 = pTs[h2]
                    for t in range(4):
                        i2 = it % 2
                        it += 1
                        pss = ps_s[0]
                        tsl = slice(t * 128, (t + 1) * 128)

                        def mm(e, hd=hd, pss=pss, tsl=tsl):
                            for k2 in range(2):
                                ins = e.matmul(pss[:, 0:Mm], lhsT=qT[:, hd * 2 + k2, tsl], rhs=kT[:, hd * 2 + k2, :], start=(k2 == 0), stop=(k2 == 1))
                            return ins
                        c.op("pe", [("x_qT", hd * 2), ("x_qT", hd * 2 + 1), "x_kT"], ["x_pss"], mm)
                        s_, sc_, pb_ = st[i2], sc[i2], pb[i2]
                        sk, sck, pbk = ("x_st", i2), ("x_sc", i2), ("x_pb", i2)
                        c.op("dve", ["x_pss"], [sk], lambda e, s_=s_, pss=pss: e.tensor_reduce(out=s_[:, 0:1], in_=pss[:, 0:Mm], axis=AX.X, op=ALU.max))
                        c.op("dve", [sk], [sk], lambda e, s_=s_: e.tensor_scalar(out=s_[:, 1:2], in0=s_[:, 0:1], scalar1=-scale, scalar2=None, op0=ALU.mult))
                        c.op("act", ["x_pss", sk], [sck, (sk, "sum")], lambda e, s_=s_, sc_=sc_, pss=pss: e.activation(out=sc_[:], in_=pss[:, 0:Mm], func=AF.Exp, bias=s_[:, 1:2], scale=scale, accum_out=s_[:, 2:3]))
                        c.op("dve", [(sk, "sum")], [(sk, "sum")], lambda e, s_=s_: e.reciprocal(out=s_[:, 3:4], in_=s_[:, 2:3]))
                        c.op("dve", [sck, (sk, "sum")], [pbk], lambda e, s_=s_, sc_=sc_, pb_=pb_: e.tensor_scalar(out=pb_[:], in0=sc_[:], scalar1=s_[:, 3:4], scalar2=None, op0=ALU.mult))
                        ptp = PN["pT"][i2]

                        def tr(e, pb_=pb_, ptp=ptp):
                            for mc in range(MC):
                                ins = e.transpose(out=ptp[:, mc * 128:(mc + 1) * 128], in_=pb_[:, mc * 128:(mc + 1) * 128], identity=self.idb[:])
                            return ins
                        c.op("pe", [pbk, "idb"], [("pT", i2)], tr)
                        c.op("act", [("pT", i2)], [("x_pTs", h2, t)], lambda e, pT4=pT4, ptp=ptp, tsl=tsl: e.copy(out=pT4[:, :, tsl], in_=ptp[:, 0:Mm].rearrange("p (k t) -> p k t", k=MC)))
                    pk4 = [("x_pTs", h2, t) for t in range(4)]
                    for d2 in range(2):
                        po = pa[d2]

                        def mm2(e, hd=hd, d2=d2, pT4=pT4, po=po):
                            for mc in range(MC):
                                ins = e.matmul(po[:], lhsT=V[:, mc, hd * 256 + d2 * 128: hd * 256 + (d2 + 1) * 128], rhs=pT4[:, mc, :], start=(mc == 0), stop=(mc == MC - 1))
                            return ins
                        c.op("pe", pk4 + ["x_V"], [("x_pa", d2)], mm2)
                        c.op("act", [("x_pa", d2)], [("x_oT", sl, hd * 2 + d2)], lambda e, hd=hd, d2=d2, po=po, sl=sl: e.copy(out=oT[sl][:, hd * 2 + d2, :], in_=po[:]))
                okeys = [("x_oT", sl, q_) for q_ in range(8)]
                for t in range(4):
                    tt = blk * 4 + t
                    self.outproj_tile(c, OP, lambda kc, sl=sl, t=t: oT[sl][:, kc, t * 128:(t + 1) * 128], okeys, 8, wo, "x_wo", wpost, "x_wpost",
                                      src[tt * 128:(tt + 1) * 128, :], dst[tt * 128:(tt + 1) * 128, :], tt % 2,
                                      rkey=(src.tensor.name, tt), okey=(dst.tensor.name, tt))
            c.barrier()

    def load_cols(self, c, dst, key, vec, n):
        for k in range(n):
            c.op("sp", [], [key], lambda e, k=k: e.dma_start(out=dst[:, k:k + 1], in_=vec[k * 128:(k + 1) * 128].rearrange("(p o) -> p o", o=1)))

    def stage_a_full(self, c, es, hnT, hkey, wrow, src, tag):
        nc = self.nc
        with ExitStack() as es2:
            PN = self.alloc_prenorm(es2, tag)
            wpre = _sb(es2, nc, f"{tag}_wpre", [128, D], F32)
            self.load_bc(c, wpre[:], f"{tag}_wpre", wrow)
            for tt in range(self.NT):
                self.prenorm_tile(c, PN, src[tt * 128:(tt + 1) * 128, :], f"{tag}_wpre", wpre[:], hnT[:, :, tt * 128:(tt + 1) * 128], (hkey, tt), tt % 2, skey=(src.tensor.name, tt))
            c.barrier()

    def stage_c_scr(self, c, L, zscr, KC, wout_ap, wpost_row, src, dst, tag):
        nc = self.nc
        with ExitStack() as es:
            wo = _sb(es, nc, f"{tag}_wo", [128, KC, 1024], BF16)
            wpost = _sb(es, nc, f"{tag}_wpost", [128, D], F32)
            self.load_bc(c, wpost[:], f"{tag}_wpost", wpost_row)
            self.load_w_bf16(c, wo, f"{tag}_wo", wout_ap.rearrange("(k p) n -> p k n", p=128))
            OP = self.alloc_outproj(es, tag)
            zt = [_sb(es, nc, f"{tag}_zt{i}", [128, KC, 512], BF16) for i in range(2)]
            for b in range(self.S // 512):
                z = zt[b % 2]
                c.op("sp", [(zscr.tensor.name, b)], [(f"{tag}_zt", b % 2)], lambda e, z=z, b=b: e.dma_start(out=z[:], in_=zscr[:, :, b * 512:(b + 1) * 512].rearrange("k p t -> p k t")))
                for t in range(4):
                    tt = b * 4 + t
                    self.outproj_tile(c, OP, lambda kc, z=z, t=t: z[:, kc, t * 128:(t + 1) * 128], [(f"{tag}_zt", b % 2)], KC, wo, f"{tag}_wo", wpost, f"{tag}_wpost",
                                      src[tt * 128:(tt + 1) * 128, :], dst[tt * 128:(tt + 1) * 128, :], tt % 2,
                                      rkey=(src.tensor.name, tt), okey=(dst.tensor.name, tt))
            c.barrier()

    def mixer_c(self, c, L, src, dst):
        nc, S, NT = self.nc, self.S, self.NT
        Dr = self.P.dram
        j = L // 2
        NB = S // 512
        yscr = self.P.dram.get("c_yscr")
        if yscr is None:
            yscr = self.P.dscr("c_yscr", [8, 128, S], BF16)
        with ExitStack() as es:
            hnT = _sb(es, nc, "c_hnT", [128, 8, S], BF16)
            self.stage_a_full(c, es, hnT, "c_hnT", Dr["mix_norm_pre"][L:L + 1, :], src, "c")
            hkeys = [("c_hnT", tt) for tt in range(NT)]
            cw = _sb(es, nc, "c_cw", [128, 4, 8], F32)
            for tap in range(4):
                self.load_cols(c, cw[:, tap, :], "c_cw", Dr["c_conv_w"][j, tap], 8)
            cb = _sb(es, nc, "c_cb", [128, 8], F32)
            self.load_cols(c, cb, "c_cb", Dr["c_conv_b"][j], 8)
            ba = _sb(es, nc, "c_ba", [128, 2, 8], F32)
            bx = _sb(es, nc, "c_bx", [128, 2, 8], F32)
            c1 = _sb(es, nc, "c_c1", [128, 2, 8], F32)
            for d_ in range(2):
                self.load_cols(c, ba[:, d_, :], "c_ba", Dr["c_b_a"][j, d_], 8)
                self.load_cols(c, bx[:, d_, :], "c_bx", Dr["c_b_x"][j, d_], 8)
                self.load_cols(c, c1[:, d_, :], "c_c1", Dr["c_lambda"][j, d_], 8)
            c.op("act", ["c_c1"], ["c_c1"], lambda e: e.activation(out=c1[:], in_=c1[:], func=AF.Exp, scale=-1.0))
            c.op("act", ["c_c1"], ["c_c1"], lambda e: e.activation(out=c1[:], in_=c1[:], func=AF.Ln, bias=1.0))
            c.op("dve", ["c_c1"], ["c_c1"], lambda e: e.tensor_scalar(out=c1[:], in0=c1[:], scalar1=-8.0, scalar2=None, op0=ALU.mult))
            xbf = _sb(es, nc, "c_xbf", [128, S + 3], F32)
            u = [_sb(es, nc, f"c_u{i}", [128, S], F32) for i in range(2)]
            ub = [_sb(es, nc, f"c_ub{i}", [128, S], BF16) for i in range(2)]
            af = _sb(es, nc, "c_af", [128, S], F32)
            inp = _sb(es, nc, "c_inp", [128, S], F32)
            acc = _sb(es, nc, "c_acc", [128, S], F32)
            win = [_sb(es, nc, f"c_win{i}", [128, 8, 128], BF16) for i in range(2)]
            wa = [_sb(es, nc, f"c_wa{i}", [128, 2, 128], BF16) for i in range(2)]
            wx = [_sb(es, nc, f"c_wx{i}", [128, 2, 128], BF16) for i in range(2)]
            tmp = {n: [_sb(es, nc, f"c_{n}{i}", [128, 512], F32) for i in range(2)] for n in ("r", "i", "mu")}
            yst = [_sb(es, nc, f"c_yst{i}", [128, 512], BF16) for i in range(2)]
            pp = [_ps(es, nc, f"c_pp{i}", [128, 512], F32) for i in range(4)]
            win_n = 0
            wg_n = 0
            pn = 0

            def inproj(col0, dst_fn, dkeys_fn):
                nonlocal win_n, pn
                w = win[win_n % 2]
                wk = ("c_win", win_n % 2)
                win_n += 1
                src3 = Dr["c_w_in"][j].rearrange("(k p) n -> p k n", p=128)[:, :, col0:col0 + 128]
                self.load_w_bf16(c, w, wk, src3)
                for tb in range(NB):
                    p = pp[pn % 2]
                    pk = ("c_pp", pn % 2)
                    pn += 1

                    def mm(e, p=p, w=w, tb=tb):
                        for kc in range(8):
                            ins = e.matmul(p[:], lhsT=w[:, kc, :], rhs=hnT[:, kc, tb * 512:(tb + 1) * 512], start=(kc == 0), stop=(kc == 7))
                        return ins
                    c.op("pe", hkeys[tb * 4:(tb + 1) * 4] + [wk], [pk], mm)
                    dst_fn(tb, p, pk)

            for blk in range(4):
                for c2 in range(2):
                    ch = blk * 2 + c2
                    c.op("pool", [], ["c_xbf_h"], lambda e: e.memset(xbf[:, 0:2], 0.0))
                    c.op("pool", [], ["c_xbf_h"], lambda e: e.memset(xbf[:, S + 2:S + 3], 0.0))

                    def ev(tb, p, pk):
                        c.op("act", [pk], [("c_xbf", tb)], lambda e: e.copy(out=xbf[:, 2 + tb * 512:2 + (tb + 1) * 512], in_=p[:]))
                    c.op("pool", ["c_hs"], ["c_hs"] + [("c_xbf", tb) for tb in range(NB)], lambda e: e.memset(xbf[:, 0:1], 0.0))
                    inproj(1024 + ch * 128, ev, None)
                    xk = [("c_xbf", tb) for tb in range(NB)] + ["c_xbf_h"]
                    uu = u[c2]
                    uk = ("c_u", c2)
                    c.op("dve", xk + ["c_cw", "c_cb"], [uk], lambda e, uu=uu, ch=ch: e.tensor_scalar(out=uu[:], in0=xbf[:, 0:S], scalar1=cw[:, 0, ch:ch + 1], scalar2=cb[:, ch:ch + 1], op0=ALU.mult, op1=ALU.add))
                    for tap in range(1, 4):
                        c.op("dve", xk + [uk], [uk], lambda e, uu=uu, ch=ch, tap=tap: e.scalar_tensor_tensor(out=uu[:], in0=xbf[:, tap:tap + S], scalar=cw[:, tap, ch:ch + 1], in1=uu[:], op0=ALU.mult, op1=ALU.add))
                    c.op("pool", [uk], [("c_ub", c2)], lambda e, uu=uu, c2=c2: e.tensor_copy(out=ub[c2][:], in_=uu[:]))
                    c.op("pool", xk, ["c_hs"], lambda e: e.memset(xbf[:, 0:1], 0.0))
                ubk = [("c_ub", 0), ("c_ub", 1)]
                for jc in range(2):
                    ch = blk * 2 + jc
                    for dr in range(2):
                        wa_, wx_ = wa[wg_n % 2], wx[wg_n % 2]
                        wak, wxk = ("c_wa", wg_n % 2), ("c_wx", wg_n % 2)
                        wg_n += 1
                        self.load_w_bf16(c, wa_, wak, Dr["c_w_a"][j, dr, blk].rearrange("(k p) n -> p k n", p=128)[:, :, jc * 128:(jc + 1) * 128])
                        self.load_w_bf16(c, wx_, wxk, Dr["c_w_x"][j, dr, blk].rearrange("(k p) n -> p k n", p=128)[:, :, jc * 128:(jc + 1) * 128])
                        for tb in range(NB):
                            sl = tb % 2
                            ba_, bx_ = 2 * (tb % 2), 2 * (tb % 2) + 1
                            pa_, px_ = pp[ba_], pp[bx_]
                            ts = slice(tb * 512, (tb + 1) * 512)

                            def mm(e, w_=wa_, p_=pa_, ts=ts):
                                for kc in range(2):
                                    ins = e.matmul(p_[:], lhsT=w_[:, kc, :], rhs=ub[kc][:, ts], start=(kc == 0), stop=(kc == 1))
                                return ins
                            c.op("pe", ubk + [wak], [("c_pp", ba_)], mm)

                            def mm2(e, w_=wx_, p_=px_, ts=ts):
                                for kc in range(2):
                                    ins = e.matmul(p_[:], lhsT=w_[:, kc, :], rhs=ub[kc][:, ts], start=(kc == 0), stop=(kc == 1))
                                return ins
                            c.op("pe", ubk + [wxk], [("c_pp", bx_)], mm2)
                            r_, i_, mu_ = tmp["r"][sl], tmp["i"][sl], tmp["mu"][sl]
                            c.op("act", [("c_pp", ba_), "c_ba"], [("c_r", sl)], lambda e, r_=r_, pa_=pa_, dr=dr, ch=ch: e.activation(out=r_[:], in_=pa_[:], func=AF.Sigmoid, bias=ba[:, dr, ch:ch + 1]))
                            c.op("act", [("c_pp", bx_), "c_bx"], [("c_i", sl)], lambda e, i_=i_, px_=px_, dr=dr, ch=ch: e.activation(out=i_[:], in_=px_[:], func=AF.Sigmoid, bias=bx[:, dr, ch:ch + 1]))
                            c.op("act", [("c_r", sl), "c_c1"], [("c_af", tb)], lambda e, r_=r_, ts=ts, dr=dr, ch=ch: e.activation(out=af[:, ts], in_=r_[:], func=AF.Exp, scale=c1[:, dr, ch:ch + 1]))
                            c.op("dve", [("c_af", tb)], [("c_mu", sl)], lambda e, mu_=mu_, ts=ts: e.tensor_tensor(out=mu_[:], in0=af[:, ts], in1=af[:, ts], op=ALU.mult))
                            c.op("act", [("c_mu", sl)], [("c_mu", sl)], lambda e, mu_=mu_: e.activation(out=mu_[:], in_=mu_[:], func=AF.Sqrt, scale=-1.0, bias=1.0))
                            c.op("dve", [("c_mu", sl), ("c_i", sl)], [("c_mu", sl)], lambda e, mu_=mu_, i_=i_: e.tensor_tensor(out=mu_[:], in0=mu_[:], in1=i_[:], op=ALU.mult))
                            c.op("dve", [("c_mu", sl), ("c_u", jc)], [("c_inp", tb)], lambda e, mu_=mu_, ts=ts, jc=jc: e.tensor_tensor(out=inp[:, ts], in0=mu_[:], in1=u[jc][:, ts], op=ALU.mult))
                        afk = [("c_af", tb) for tb in range(NB)]
                        ink = [("c_inp", tb) for tb in range(NB)]
                        if dr == 0:
                            c.op("dve", afk + ink, ["c_acc"], lambda e: e.tensor_tensor_scan(out=acc[:], data0=af[:], data1=inp[:], initial=0.0, op0=ALU.mult, op1=ALU.add))
                        else:
                            c.op("dve", afk + ink, ["c_hs"], lambda e: e.tensor_tensor_scan(out=xbf[:, 0:S][:, ::-1], data0=af[:, ::-1], data1=inp[:, ::-1], initial=0.0, op0=ALU.mult, op1=ALU.add))
                            c.op("pool", ["c_hs", "c_acc"], ["c_acc"], lambda e: e.tensor_tensor(out=acc[:], in0=acc[:], in1=xbf[:, 0:S], op=ALU.add))

                    def evg(tb, p, pk, ch=ch):
                        sl = tb % 2
                        g1, g2, ys = tmp["r"][sl], tmp["i"][sl], yst[sl]
                        ts = slice(tb * 512, (tb + 1) * 512)
                        c.op("act", [pk], [("c_r", sl)], lambda e: e.activation(out=g1[:], in_=p[:], func=AF.Square))
                        c.op("dve", [("c_r", sl)], [("c_r", sl)], lambda e: e.tensor_scalar(out=g1[:], in0=g1[:], scalar1=0.044715, scalar2=1.0, op0=ALU.mult, op1=ALU.add))
                        c.op("dve", [("c_r", sl), pk], [("c_r", sl)], lambda e: e.tensor_tensor(out=g1[:], in0=g1[:], in1=p[:], op=ALU.mult))
                        c.op("act", [("c_r", sl)], [("c_i", sl)], lambda e: e.activation(out=g2[:], in_=g1[:], func=AF.Sigmoid, scale=1.5957691216057308))
                        c.op("dve", [("c_i", sl), pk], [("c_i", sl)], lambda e: e.tensor_tensor(out=g2[:], in0=g2[:], in1=p[:], op=ALU.mult))
                        c.op("dve", [("c_i", sl), "c_acc"], [("c_yst", sl)], lambda e: e.tensor_tensor(out=ys[:], in0=g2[:], in1=acc[:, ts], op=ALU.mult))
                        c.op("gq", [("c_yst", sl)], [("c_yscr", tb)], lambda e: e.dma_start(out=yscr[ch, :, ts], in_=ys[:]))
                    inproj(ch * 128, evg, None)
            c.barrier()
        self.stage_c_scr(c, L, yscr, 8, Dr["c_w_out"][j], Dr["mix_norm_post"][L:L + 1, :], src, dst, "cc")

    def mixer_ab(self, c, L, src, dst):
        nc, S, NT = self.nc, self.S, self.NT
        Dr = self.P.dram
        j = L // 2
        NB = S // 512
        P = self.P
        FM = P.dram.get("ab_fm") or P.dscr("ab_fm", [16, 128, S], F32)
        TM = {n: (P.dram.get("ab_" + n) or P.dscr("ab_" + n, [S, 512], F32)) for n in ("dnk", "dnv", "dnz", "mlk", "mlv", "mlo")}
        mixscr = P.dram.get("ab_mix") or P.dscr("ab_mix", [8, 128, S], BF16)
        I, ONES = self.I, self.ONES
        dirs = [dict(INC=self.LE, AFT=self.GT, STRICT=self.GT, INCLji=self.LE, NEGM=self.NLT),
                dict(INC=self.GE, AFT=self.LT, STRICT=self.LT, INCLji=self.GE, NEGM=self.NGT)]
        with ExitStack() as eo:
            gates = _sb(eo, nc, "ab_gates", [128, NT, 32], F32)
            Gg = _sb(eo, nc, "ab_Gg", [128, NT, 8], F32)
            Bt = _sb(eo, nc, "ab_Bt", [128, NT, 8], F32)
            nBt = _sb(eo, nc, "ab_nBt", [128, NT, 8], F32)
            Li = _sb(eo, nc, "ab_Li", [128, NT, 8], F32)
            Lf = _sb(eo, nc, "ab_Lf", [128, NT, 8], F32)
            Edn = _sb(eo, nc, "ab_Edn", [128, NT, 24], F32)
            Mlt = _sb(eo, nc, "ab_Mlt", [128, NT, 24], F32)
            Bk = _sb(eo, nc, "ab_Bk", [128, NT, 8], F32)
            Ws = _sb(eo, nc, "ab_Ws", [128, NT, 8], F32)
            prm = _sb(eo, nc, "ab_prm", [128, 4, 8], F32)
            dnw = _sb(eo, nc, "ab_dnw", [128, 128], F32)
            for i_, nm in enumerate(("dn_a_log", "dn_dt_bias", "ml_i_bias", "ml_f_bias")):
                self.load_bc(c, prm[:, i_, :], "ab_prm", Dr[nm][j:j + 1].rearrange("o d h -> o (d h)"))
            self.load_bc(c, dnw[:], "ab_dnw", Dr["dn_out_norm"][j:j + 1, :])
            with ExitStack() as es:
                hnT = _sb(es, nc, "ab_hnT", [128, 8, S], BF16)
                self.stage_a_full(c, es, hnT, "ab_hnT", Dr["mix_norm_pre"][L:L + 1, :], src, "ab")
                hkeys = [("ab_hnT", tt) for tt in range(NT)]
                win3 = Dr["ab_w_in"][j].rearrange("(k p) n -> p k n", p=128)
                wg = _sb(es, nc, "ab_wg", [128, 8, 32], BF16)
                c.op("gq", [], ["ab_wg"], lambda e: e.dma_start(out=wg[:, :, 0:16], in_=win3[:, :, 2048:2064]))
                c.op("gq", [], ["ab_wg"], lambda e: e.dma_start(out=wg[:, :, 16:32], in_=win3[:, :, 4112:4128]))
                pp = [_ps(es, nc, f"ab_pp{i}", [128, 512], F32) for i in range(4)]
                for tt in range(NT):
                    p = pp[tt % 2]

                    def mm(e, p=p, tt=tt):
                        for kc in range(8):
                            ins = e.matmul(p[:, 0:32], lhsT=hnT[:, kc, tt * 128:(tt + 1) * 128], rhs=wg[:, kc, :], start=(kc == 0), stop=(kc == 7))
                        return ins
                    c.op("pe", [hkeys[tt], "ab_wg"], [("ab_pp", tt % 2)], mm)
                    c.op("act", [("ab_pp", tt % 2)], ["ab_gates"], lambda e, p=p, tt=tt: e.copy(out=gates[:, tt, :], in_=p[:, 0:32]))
                def bc(i_):
                    return prm[:, i_, :].unsqueeze(1).broadcast_to([128, NT, 8])
                c.op("dve", ["ab_gates", "ab_prm"], ["ab_Gg"], lambda e: e.tensor_tensor(out=Gg[:], in0=gates[:, :, 0:8], in1=bc(1), op=ALU.add))
                c.op("act", ["ab_Gg"], ["ab_Gg"], lambda e: e.activation(out=Gg[:], in_=Gg[:], func=AF.Exp))
                c.op("act", ["ab_Gg"], ["ab_Gg"], lambda e: e.activation(out=Gg[:], in_=Gg[:], func=AF.Ln, bias=1.0))
                c.op("act", ["ab_prm"], ["ab_prm0"], lambda e: e.activation(out=prm[:, 0, :], in_=prm[:, 0, :], func=AF.Exp))
                c.op("dve", ["ab_Gg", "ab_prm0"], ["ab_Gg"], lambda e: e.scalar_tensor_tensor(out=Gg[:], in0=Gg[:], scalar=-1.0, in1=bc(0), op0=ALU.mult, op1=ALU.mult))
                c.op("act", ["ab_gates"], ["ab_Bt"], lambda e: e.activation(out=Bt[:], in_=gates[:, :, 8:16], func=AF.Sigmoid))
                c.op("dve", ["ab_Bt"], ["ab_nBt"], lambda e: e.tensor_scalar(out=nBt[:], in0=Bt[:], scalar1=-1.0, scalar2=None, op0=ALU.mult))
                c.op("dve", ["ab_gates", "ab_prm"], ["ab_Li"], lambda e: e.tensor_tensor(out=Li[:], in0=gates[:, :, 16:24], in1=bc(2), op=ALU.add))
                c.op("dve", ["ab_gates", "ab_prm"], ["ab_Lf"], lambda e: e.tensor_tensor(out=Lf[:], in0=gates[:, :, 24:32], in1=bc(3), op=ALU.add))
                c.op("act", ["ab_Lf"], ["ab_Lf"], lambda e: e.activation(out=Lf[:], in_=Lf[:], func=AF.Exp, scale=-1.0))
                c.op("act", ["ab_Lf"], ["ab_Lf"], lambda e: e.activation(out=Lf[:], in_=Lf[:], func=AF.Ln, bias=1.0))
                c.op("dve", ["ab_Lf"], ["ab_Lf"], lambda e: e.tensor_scalar(out=Lf[:], in0=Lf[:], scalar1=-1.0, scalar2=None, op0=ALU.mult))
                LE, GE, LT, GT = self.LE, self.GE, self.LT, self.GT
                for tt in range(NT):
                    p = pp[2 + tt % 2]

                    def mm(e, p=p, tt=tt):
                        for o_, T_ in ((0, Gg), (24, Lf)):
                            e.matmul(p[:, o_ + 0:o_ + 4], lhsT=LE, rhs=T_[:, tt, 0:4], start=True, stop=True)
                            e.matmul(p[:, o_ + 4:o_ + 8], lhsT=GE, rhs=T_[:, tt, 4:8], start=True, stop=True)
                            e.matmul(p[:, o_ + 8:o_ + 12], lhsT=GT, rhs=T_[:, tt, 0:4], start=True, stop=True)
                            e.matmul(p[:, o_ + 12:o_ + 16], lhsT=LT, rhs=T_[:, tt, 4:8], start=True, stop=True)
                            ins = e.matmul(p[:, o_ + 16:o_ + 24], lhsT=ONES, rhs=T_[:, tt, 0:8], start=True, stop=True)
                        return ins
                    c.op("pe", ["ab_Gg", "ab_Lf", "cst"], [("ab_pp", 2 + tt % 2)], mm)
                    c.op("act", [("ab_pp", 2 + tt % 2)], ["ab_Edn"], lambda e, p=p, tt=tt: e.activation(out=Edn[:, tt, :], in_=p[:, 0:24], func=AF.Exp))
                    c.op("act", [("ab_pp", 2 + tt % 2)], ["ab_Mlt"], lambda e, p=p, tt=tt: e.copy(out=Mlt[:, tt, :], in_=p[:, 24:48]))
                c.op("dve", ["ab_Bt", "ab_Edn"], ["ab_Bk"], lambda e: e.tensor_tensor(out=Bk[:], in0=Bt[:], in1=Edn[:, :, 0:8], op=ALU.mult))
                c.op("dve", ["ab_Li", "ab_Mlt"], ["ab_Ws"], lambda e: e.tensor_tensor(out=Ws[:], in0=Li[:], in1=Mlt[:, :, 8:16], op=ALU.add))
                xbf = _sb(es, nc, "ab_xbf", [128, S + 3], F32)
                uu = _sb(es, nc, "ab_u", [128, S], F32)
                cwt = [_sb(es, nc, f"ab_cwt{i}", [128, 4], F32) for i in range(2)]
                win = [_sb(es, nc, f"ab_win{i}", [128, 8, 128], BF16) for i in range(2)]
                sqb = [_sb(es, nc, f"ab_sqb{i}", [128, 512], F32) for i in range(2)]
                rsb = [_sb(es, nc, f"ab_rsb{i}", [128, 512], F32) for i in range(2)]
                stg = [_sb(es, nc, f"ab_stg{i}", [128, 4, 128], F32) for i in range(2)]
                c.op("pool", [], ["ab_xbf_h"], lambda e: e.memset(xbf[:, 0:2], 0.0))
                c.op("pool", [], ["ab_xbf_h"], lambda e: e.memset(xbf[:, S + 2:S + 3], 0.0))
                specs = []
                for h in range(4):
                    specs.append(dict(col=h * 128, conv=("dn_conv_w", h * 128), act=AF.Silu, l2=True, scale=128.0 ** -0.5, fm=h, tm=None))
                for h in range(4):
                    specs.append(dict(col=512 + h * 128, conv=("dn_conv_w", 512 + h * 128), act=AF.Silu, l2=True, scale=1.0, fm=4 + h, tm=("dnk", h)))
                for h in range(4):
                    specs.append(dict(col=1024 + h * 128, conv=("dn_conv_w", 1024 + h * 128), act=AF.Silu, l2=False, scale=None, fm=None, tm=("dnv", h)))
                for h in range(4):
                    specs.append(dict(col=1536 + h * 128, conv=None, act=AF.Silu, l2=False, scale=None, fm=None, tm=("dnz", h)))
                for h in range(4):
                    specs.append(dict(col=2064 + h * 128, conv=("ml_conv_w", h * 128), act=AF.Silu, l2=False, scale=None, fm=8 + h, tm=None))
                for h in range(4):
                    specs.append(dict(col=2576 + h * 128, conv=("ml_conv_w", 512 + h * 128), act=AF.Silu, l2=False, scale=128.0 ** -0.5, fm=12 + h, tm=("mlk", h)))
                for h in range(4):
                    specs.append(dict(col=3088 + h * 128, conv=None, act=None, l2=False, scale=None, fm=None, tm=("mlv", h)))
                for h in range(4):
                    specs.append(dict(col=3600 + h * 128, conv=None, act=AF.Sigmoid, l2=False, scale=None, fm=None, tm=("mlo", h)))
                xbf2 = _sb(es, nc, "ab_xbf2", [128, S + 3], F32)
                xbfs = [xbf, xbf2]
                c.op("pool", [], [("ab_xbf_h", 0)], lambda e: e.memset(xbf[:, 0:2], 0.0))
                c.op("pool", [], [("ab_xbf_h", 0)], lambda e: e.memset(xbf[:, S + 2:S + 3], 0.0))
                c.op("pool", [], [("ab_xbf_h", 1)], lambda e: e.memset(xbf2[:, 0:2], 0.0))
                c.op("pool", [], [("ab_xbf_h", 1)], lambda e: e.memset(xbf2[:, S + 2:S + 3], 0.0))
                pnc = [0]

                def stA(si, sp):
                        pn = pnc[0]
                        w = win[si % 2]
                        wk = ("ab_win", si % 2)
                        self.load_w_bf16(c, w, wk, win3[:, :, sp["col"]:sp["col"] + 128])
                        xk = [("ab_xbf", si % 2, tb) for tb in range(NB)]
                        for tb in range(NB):
                            p = pp[pn % 2]
                            pk = ("ab_pp", pn % 2)
                            pn += 1

                            def mm(e, p=p, w=w, tb=tb):
                                for kc in range(8):
                                    ins = e.matmul(p[:], lhsT=w[:, kc, :], rhs=hnT[:, kc, tb * 512:(tb + 1) * 512], start=(kc == 0), stop=(kc == 7))
                                return ins
                            c.op("pe", hkeys[tb * 4:(tb + 1) * 4] + [wk], [pk], mm)
                            c.op("act", [pk], [("ab_xbf", si % 2, tb)], lambda e, p=p, tb=tb: e.copy(out=xbfs[si % 2][:, 2 + tb * 512:2 + (tb + 1) * 512], in_=p[:]))
                        pnc[0] = pn

                def stB(si, sp):
                        xk = [("ab_xbf", si % 2, tb) for tb in range(NB)]
                        if sp["conv"] is not None:
                            cw_ = cwt[si % 2]
                            cwk = ("ab_cwt", si % 2)
                            nm, c0 = sp["conv"]
                            for tap in range(4):
                                c.op("sp", [], [cwk], lambda e, cw_=cw_, tap=tap, nm=nm, c0=c0: e.dma_start(out=cw_[:, tap:tap + 1], in_=Dr[nm][j, tap, c0:c0 + 128].rearrange("(p o) -> p o", o=1)))
                            c.op("dve", xk + [("ab_xbf_h", si % 2), cwk], ["ab_u"], lambda e, cw_=cw_: e.tensor_scalar(out=uu[:], in0=xbfs[si % 2][:, 0:S], scalar1=cw_[:, 0:1], scalar2=None, op0=ALU.mult))
                            for tap in range(1, 4):
                                c.op("dve", xk + [("ab_xbf_h", si % 2), cwk, "ab_u"], ["ab_u"], lambda e, cw_=cw_, tap=tap: e.scalar_tensor_tensor(out=uu[:], in0=xbfs[si % 2][:, tap:tap + S], scalar=cw_[:, tap:tap + 1], in1=uu[:], op0=ALU.mult, op1=ALU.add))
                            if sp["act"] is not None:
                                c.op("act", ["ab_u"], ["ab_u"], lambda e, f=sp["act"]: e.activation(out=uu[:], in_=uu[:], func=f))
                        else:
                            if sp["act"] is not None:
                                c.op("act", xk, ["ab_u"], lambda e, f=sp["act"]: e.activation(out=uu[:], in_=xbfs[si % 2][:, 2:S + 2], func=f))
                            else:
                                c.op("pool", xk, ["ab_u"], lambda e: e.tensor_copy(out=uu[:], in_=xbfs[si % 2][:, 2:S + 2]))
                        if sp["l2"]:
                            for tb in range(NB):
                                sl = tb % 2
                                ts = slice(tb * 512, (tb + 1) * 512)
                                c.op("act", ["ab_u"], [("ab_sqb", sl)], lambda e, sl=sl, ts=ts: e.activation(out=sqb[sl][:], in_=uu[:, ts], func=AF.Square))
                                p = pp[2 + sl]
                                c.op("pe", [("ab_sqb", sl), "cst"], [("ab_pp", 2 + sl)], lambda e, p=p, sl=sl: e.matmul(p[:], lhsT=ONES, rhs=sqb[sl][:], start=True, stop=True))
                                c.op("act", [("ab_pp", 2 + sl)], [("ab_rsb", sl)], lambda e, p=p, sl=sl: e.activation(out=rsb[sl][:], in_=p[:], func=AF.Ln, bias=1e-6))
                                c.op("act", [("ab_rsb", sl)], [("ab_rsb", sl)], lambda e, sl=sl: e.activation(out=rsb[sl][:], in_=rsb[sl][:], func=AF.Exp, scale=-0.5))
                                c.op("dve", [("ab_rsb", sl), "ab_u"], ["ab_u"], lambda e, sl=sl, ts=ts, sc_=sp["scale"]: e.scalar_tensor_tensor(out=uu[:, ts], in0=uu[:, ts], scalar=sc_, in1=rsb[sl][:], op0=ALU.mult, op1=ALU.mult))
                        elif sp["scale"] is not None:
                            c.op("dve", ["ab_u"], ["ab_u"], lambda e, sc_=sp["scale"]: e.tensor_scalar(out=uu[:], in0=uu[:], scalar1=sc_, scalar2=None, op0=ALU.mult))
                        if sp["fm"] is not None:
                            c.op("sp", ["ab_u"], [("ab_fm", sp["fm"])], lambda e, f=sp["fm"]: e.dma_start(out=FM[f], in_=uu[:]))
                        if sp["tm"] is not None:
                            nm, h = sp["tm"]
                            for tb in range(NB):
                                sl = tb % 2
                                p = pp[2 + sl]

                                def tr(e, p=p, tb=tb):
                                    for t4 in range(4):
                                        ins = e.transpose(out=p[:, t4 * 128:(t4 + 1) * 128], in_=uu[:, tb * 512 + t4 * 128: tb * 512 + (t4 + 1) * 128], identity=I)
                                    return ins
                                c.op("pe", ["ab_u", "cst"], [("ab_pp", 2 + sl)], tr)
                                c.op("act", [("ab_pp", 2 + sl)], [("ab_stg", sl)], lambda e, p=p, sl=sl: e.copy(out=stg[sl][:], in_=p[:].rearrange("p (t d) -> p t d", t=4)))
                                c.op("gq", [("ab_stg", sl)], [("ab_tm", nm, h)], lambda e, sl=sl, tb=tb, nm=nm, h=h: e.dma_start(out=TM[nm][tb * 512:(tb + 1) * 512, h * 128:(h + 1) * 128].rearrange("(t p) d -> p t d", p=128), in_=stg[sl][:]))
                stA(0, specs[0])
                for si, sp in enumerate(specs):
                    if si + 1 < len(specs):
                        stA(si + 1, specs[si + 1])
                    stB(si, sp)
                c.barrier()
            import os as _os
            _algs = tuple(a for a in _os.environ.get("AB_ALGS", "dn,ml").split(",") if a)
            WIN = int(_os.environ.get("AB_WIN", "4"))
            with ExitStack() as es:
                qT = _sb(es, nc, "r_qT", [128, S], F32)
                ktok = _sb(es, nc, "r_ktok", [128, NT, 128], F32)
                vtok = _sb(es, nc, "r_vtok", [128, NT, 129], F32)
                ost = _sb(es, nc, "r_ost", [128, NT, 128], F32)
                gtok = qT[:].rearrange("p (t d) -> p t d", d=128)
                dn_names = ["Gmat", "Gle", "eD", "eDT", "egb", "t1", "t2", "attnT", "qd", "Xv", "Xk", "kd", "u", "wT", "vn"] + \
                           [f"P{k}" for k in range(2)] + [f"PT{k}" for k in range(2)] + [f"R{k}" for k in range(2)]
                F32R = mybir.dt.float32r
                cstr = _sb(es, nc, "r_cstr", [128, 8, 128], F32)
                c.op("dve", ["cst"], ["r_cstr"], lambda e: e.tensor_copy(out=cstr[:].bitcast(F32R), in_=self.cst[:]))
                mr = {"LE": cstr[:, 1, :].bitcast(F32R), "GE": cstr[:, 2, :].bitcast(F32R), "LT": cstr[:, 3, :].bitcast(F32R), "GT": cstr[:, 4, :].bitcast(F32R)}
                ONESr = cstr[:, 5, :].bitcast(F32R)
                dirs_r = [dict(INC=mr["LE"], AFT=mr["GT"]), dict(INC=mr["GE"], AFT=mr["LT"])]
                rset = set([f"P{k}" for k in range(7)] + [f"PT{k}" for k in range(6)] + ["R0", "R1", "Xv", "Xk"])
                bfn = set(["wT", "qd", "attnT", "kd", "vn"])
                wt = {n: [_sb(es, nc, f"r_{n}{i}", [128, 128], BF16 if n in bfn else F32) for i in range(WIN)] for n in dn_names}
                qTb = _sb(es, nc, "r_qTb", [128, S], BF16)
                kTb = _sb(es, nc, "r_kTb", [128, S], BF16)
                vtokb = _sb(es, nc, "r_vtokb", [128, NT, 129], BF16)
                Ssh = [[_sb(es, nc, f"r_Ssh{d_}{i}", [128, 129], BF16) for i in range(2)] for d_ in range(2)]
                pTm = [_sb(es, nc, f"r_pTm{i}", [128, 128], BF16) for i in range(WIN)]
                pm = [_sb(es, nc, f"r_pm{i}", [128, 128], BF16) for i in range(WIN)]
                ksm = [_sb(es, nc, f"r_ksm{i}", [128, 128], BF16) for i in range(WIN)]
                alias = {"X": "Gmat", "e": "eD", "p": "eDT", "pT": "egb", "ks": "t1"}
                dmall = _sb(es, nc, "r_dmall", [128, 2, NT, 128], F32)
                mlc = _sb(es, nc, "r_mlc", [128, 12, 2, NT], F32)
                zc = _sb(es, nc, "r_zc", [128, 1], F32)
                c.op("pool", [], ["r_zc"], lambda e: e.memset(zc[:], 0.0))
                nd = [_sb(es, nc, f"r_nd{i}", [128, 129], F32) for i in range(WIN)]
                dcol = [_sb(es, nc, f"r_dcol{i}", [128, 4], F32) for i in range(WIN)]
                Sst = [[_sb(es, nc, f"r_S{d_}{i}", [128, 129], F32) for i in range(2)] for d_ in range(2)]
                ob = [_sb(es, nc, f"r_ob{i}", [128, 128], BF16) for i in range(2)]
                ot = [_sb(es, nc, f"r_ot{i}", [128, 128], F32) for i in range(2)]
                oss = [_sb(es, nc, f"r_oss{i}", [128, 2], F32) for i in range(2)]
                mst4 = [_sb(es, nc, f"r_mx{i}", [128, 512], BF16) for i in range(2)]
                pb = [_ps(es, nc, f"r_pb{i}", [128, 512], F32) for i in range(7)]
                pTb = _ps(es, nc, "r_pTb", [128, 1024], BF16)

                def Q(b, q, n=128):
                    return pb[b][:, q * 128:q * 128 + n]

                def K_(b, q):
                    return ("r_pb", b, q)

                STAG = int(_os.environ.get("AB_STAG", "0"))

                def run_units(gens, stag=None):
                    stag = STAG if stag is None else stag
                    active = []
                    it = iter(gens)
                    done = False
                    since = stag
                    while True:
                        if (not done) and len(active) < WIN and (since >= stag or not active):
                            g = next(it, None)
                            if g is None:
                                done = True
                            else:
                                active.append(g)
                                since = 0
                        if not active:
                            if done:
                                break
                            continue
                        since += 1
                        for g in list(active):
                            try:
                                next(g)
                            except StopIteration:
                                active.remove(g)

                free = list(range(WIN))
                turn = [0, 0]

                def dn_unit(h, dr, si, t):
                    M = dirs[dr]
                    col = dr * 4 + h
                    sl_ = free.pop()
                    W = {n: wt[n][sl_] for n in dn_names}
                    for kq in range(7):
                        W[f"P{kq}"] = wt[f"P{kq % 2}"][sl_]
                    for kq in range(6):
                        W[f"PT{kq}"] = wt[f"PT{kq % 2}"][sl_]
                    ts = slice(t * 128, (t + 1) * 128)
                    ab_, vb_, sb_ = 2 * (sl_ % 2), 2 * (sl_ % 2) + 1, 4 + dr

                    def Wr(n):
                        return W[n][:].bitcast(F32R)

                    def k(n):
                        if n[0] == "P" and n[-1].isdigit():
                            n = n[:-1] + str(int(n[-1]) % 2)
                        return ("r_" + n, sl_)
                    Sc, Sn = Sst[dr][si % 2], Sst[dr][(si + 1) % 2]
                    Sck, Snk = ("r_S", dr, si % 2), ("r_S", dr, (si + 1) % 2)
                    gcol = Gg[:, t, col:col + 1]
                    c.op("dve", ["ab_Gg"], [k("Gmat")], lambda e: e.tensor_scalar(out=Wr("Gmat"), in0=M["AFT"], scalar1=gcol, scalar2=None, op0=ALU.mult))
                    c.op("act", ["ab_Gg"], [k("Gle")], lambda e: e.activation(out=Wr("Gle"), in_=M["INC"], func=AF.Copy, scale=gcol))
                    yield
                    c.op("act", ["r_vtok", "ab_Bt"], [k("Xv")], lambda e: e.activation(out=Wr("Xv"), in_=vtok[:, t, 0:128], func=AF.Copy, scale=Bt[:, t, col:col + 1]))
                    c.op("act", ["r_ktok", "ab_Bk"], [k("Xk")], lambda e: e.activation(out=Wr("Xk"), in_=ktok[:, t, :], func=AF.Copy, scale=Bk[:, t, col:col + 1]))
                    c.op("act", ["r_ktok", "ab_Edn"], [k("kd")], lambda e: e.activation(out=W["kd"][:], in_=ktok[:, t, :], func=AF.Copy, scale=Edn[:, t, 8 + col:8 + col + 1]))
                    yield

                    def mmA(e):
                        Mr = dirs_r[dr]
                        e.matmul(Q(ab_, 0), lhsT=Mr["INC"], rhs=Wr("Gmat"), start=True, stop=True)
                        e.matmul(Q(ab_, 2), lhsT=ONESr, rhs=Wr("Gle"), start=True, stop=True)
                        return e.matmul(Q(ab_, 1), lhsT=Wr("Gmat"), rhs=Mr["INC"], start=True, stop=True)
                    c.op("pe", [k("Gmat"), k("Gle"), "r_cstr"], [K_(ab_, 0), K_(ab_, 1), K_(ab_, 2)], mmA)
                    c.op("act", [K_(ab_, 0)], [k("eD")], lambda e: e.activation(out=W["eD"][:], in_=Q(ab_, 0), func=AF.Exp))
                    c.op("act", [K_(ab_, 1)], [k("eDT")], lambda e: e.activation(out=W["eDT"][:], in_=Q(ab_, 1), func=AF.Exp))
                    c.op("act", [K_(ab_, 2)], [k("egb")], lambda e: e.activation(out=W["egb"][:], in_=Q(ab_, 2), func=AF.Exp))
                    yield
                    c.op("pool", [k("eD")], [k("t1")], lambda e: e.tensor_tensor(out=W["t1"][:], in0=W["eD"][:], in1=M["STRICT"], op=ALU.mult))
                    c.op("pool", [k("eDT")], [k("t2")], lambda e: e.tensor_tensor(out=W["t2"][:], in0=W["eDT"][:], in1=M["INCLji"], op=ALU.mult))
                    yield
                    c.op("pool", [k("egb"), "r_qTb"], [k("qd")], lambda e: e.tensor_tensor(out=W["qd"][:], in0=qTb[:, ts], in1=W["egb"][:], op=ALU.mult))
                    yield

                    def mmB(e):
                        e.matmul(Q(vb_, 1), lhsT=kTb[:, ts], rhs=kTb[:, ts], start=True, stop=True)
                        return e.matmul(Q(vb_, 0), lhsT=kTb[:, ts], rhs=qTb[:, ts], start=True, stop=True)
                    c.op("pe", ["r_kTb", "r_qTb"], [K_(vb_, 1), K_(vb_, 0)], mmB)
                    c.op("dve", [K_(vb_, 1), k("t1"), "ab_nBt"], [k("P0")], lambda e: e.scalar_tensor_tensor(out=Wr("P0"), in0=Q(vb_, 1), scalar=nBt[:, t, col:col + 1], in1=W["t1"][:], op0=ALU.mult, op1=ALU.mult))
                    c.op("dve", [K_(vb_, 0), k("t2")], [k("attnT")], lambda e: e.tensor_tensor(out=W["attnT"][:], in0=Q(vb_, 0), in1=W["t2"][:], op=ALU.mult))
                    yield
                    if "P0" in bfn:
                        qv = Q(vb_, 2).bitcast(BF16)[:, 0:128]
                        c.op("pe", [k("P0"), "idb"], [K_(vb_, 2)], lambda e: e.transpose(out=qv, in_=W["P0"][:], identity=self.idb[:]))
                        c.op("dve", [K_(vb_, 2)], [k("PT0")], lambda e: e.tensor_copy(out=W["PT0"][:], in_=qv))
                    else:
                        c.op("pe", [k("P0")], [K_(vb_, 2)], lambda e: e.transpose(out=Q(vb_, 2), in_=W["P0"][:], identity=I))
                        c.op("dve", [K_(vb_, 2)], [k("PT0")], lambda e: e.tensor_copy(out=Wr("PT0"), in_=Q(vb_, 2)))
                    c.op("dve", [k("PT0")], [k("R0")], lambda e: e.tensor_tensor(out=Wr("R0"), in0=W["PT0"][:], in1=I, op=ALU.add))
                    yield
                    rc = "R0"
                    for kk in range(1, 7):
                        q2 = kk % 2
                        c.op("pe", [k(f"P{kk-1}"), k(f"PT{kk-1}")], [K_(ab_, q2)], lambda e, kk=kk, q2=q2: e.matmul(Q(ab_, q2), lhsT=Wr(f"PT{kk-1}"), rhs=Wr(f"P{kk-1}"), start=True, stop=True))
                        c.op("act", [K_(ab_, q2)], [k(f"P{kk}")], lambda e, kk=kk, q2=q2: e.copy(out=Wr(f"P{kk}"), in_=Q(ab_, q2)))
                        yield
                        if kk < 6:
                            c.op("pe", [k(f"P{kk-1}"), k(f"PT{kk-1}")], [K_(vb_, 2 + q2)], lambda e, kk=kk, q2=q2: e.matmul(Q(vb_, 2 + q2), lhsT=Wr(f"P{kk-1}"), rhs=Wr(f"PT{kk-1}"), start=True, stop=True))
                            c.op("dve", [K_(vb_, 2 + q2)], [k(f"PT{kk}")], lambda e, kk=kk, q2=q2: e.tensor_copy(out=Wr(f"PT{kk}"), in_=Q(vb_, 2 + q2)))
                            yield
                        rn = "R1" if rc == "R0" else "R0"
                        c.op("pe", [k(f"P{kk}"), k(rc)], [K_(vb_, q2)], lambda e, kk=kk, rc=rc, q2=q2: e.matmul(Q(vb_, q2), lhsT=Wr(f"P{kk}"), rhs=Wr(rc), start=True, stop=True))
                        c.op("dve", [K_(vb_, q2), k(rc)], [k(rn)], lambda e, rc=rc, rn=rn, q2=q2: e.tensor_tensor(out=Wr(rn), in0=W[rc][:], in1=Q(vb_, q2), op=ALU.add))
                        yield
                        rc = rn
                    c.op("pe", [k(rc), k("Xv")], [K_(ab_, 0)], lambda e: e.matmul(Q(ab_, 0), lhsT=Wr(rc), rhs=Wr("Xv"), start=True, stop=True))
                    c.op("act", [K_(ab_, 0)], [k("u")], lambda e: e.copy(out=W["u"][:], in_=Q(ab_, 0)))
                    yield
                    c.op("pe", [k(rc), k("Xk")], [K_(ab_, 1)], lambda e: e.matmul(Q(ab_, 1), lhsT=Wr("Xk"), rhs=Wr(rc), start=True, stop=True))
                    c.op("act", [K_(ab_, 1)], [k("wT")], lambda e: e.copy(out=W["wT"][:], in_=Q(ab_, 1)))
                    yield
                    while turn[dr] != si:
                        yield
                    Sbc, Sbn = Ssh[dr][si % 2], Ssh[dr][(si + 1) % 2]
                    Sbck, Sbnk = ("r_Ssh", dr, si % 2), ("r_Ssh", dr, (si + 1) % 2)
                    c.op("pe", [k("wT"), Sbck], [K_(sb_, 0)], lambda e: e.matmul(Q(sb_, 0), lhsT=W["wT"][:], rhs=Sbc[:, 0:128], start=True, stop=True))
                    c.op("dve", [K_(sb_, 0), k("u")], [k("vn")], lambda e: e.tensor_tensor(out=W["vn"][:], in0=W["u"][:], in1=Q(sb_, 0), op=ALU.subtract))

                    def mmo(e):
                        e.matmul(Q(sb_, 2), lhsT=W["qd"][:], rhs=Sbc[:, 0:128], start=True, stop=False)
                        e.matmul(Q(sb_, 2), lhsT=W["attnT"][:], rhs=W["vn"][:], start=False, stop=True)
                        return e.matmul(Q(sb_, 1), lhsT=W["kd"][:], rhs=W["vn"][:], start=True, stop=True)
                    c.op("pe", [k("qd"), k("attnT"), k("vn"), k("kd"), Sbck], [K_(sb_, 2), K_(sb_, 1)], mmo)
                    c.op("dve", [K_(sb_, 1), Sck, "ab_Edn"], [Snk], lambda e: e.scalar_tensor_tensor(out=Sn[:, 0:128], in0=Sc[:, 0:128], scalar=Edn[:, t, 16 + col:16 + col + 1], in1=Q(sb_, 1), op0=ALU.mult, op1=ALU.add))
                    c.op("act", [Snk], [Sbnk], lambda e: e.copy(out=Sbn[:, 0:128], in_=Sn[:, 0:128]))
                    c.op("dve", [K_(sb_, 2), ("r_ost", t)], [("r_ost", t)], lambda e: e.tensor_tensor(out=ost[:, t, :], in0=ost[:, t, :], in1=Q(sb_, 2), op=ALU.add))
                    turn[dr] += 1
                    free.append(sl_)

                MI, MS, B1, MNEW, A1, MT, NEGM, INTER, EMT, DEC, SRC, TMP = range(12)

                def ml_prep(h, dr, si, t):
                    M = dirs[dr]
                    col = dr * 4 + h
                    sl_ = free.pop()
                    X = wt[alias["X"]][sl_]
                    xk = ("r_" + alias["X"], sl_)
                    vb_ = 2 * (sl_ % 2) + 1
                    lf, li = Lf[:, t, col:col + 1], Li[:, t, col:col + 1]
                    Xr = X[:].bitcast(F32R)
                    c.op("dve", ["ab_Lf"], [xk], lambda e: e.tensor_scalar(out=Xr, in0=M["AFT"], scalar1=lf, scalar2=None, op0=ALU.mult))
                    c.op("dve", ["ab_Li", xk], [xk], lambda e: e.scalar_tensor_tensor(out=Xr, in0=I, scalar=li, in1=X[:], op0=ALU.mult, op1=ALU.add))
                    yield

                    def mmA(e):
                        Mr = dirs_r[dr]
                        e.matmul(Q(vb_, 0), lhsT=Mr["INC"], rhs=Xr, start=True, stop=True)
                        return e.matmul(Q(vb_, 1), lhsT=ONESr, rhs=Xr, start=True, stop=True)
                    c.op("pe", [xk, "r_cstr"], [K_(vb_, 0), K_(vb_, 1)], mmA)
                    c.op("dve", [K_(vb_, 0)], [("r_dmall", dr, t)], lambda e: e.tensor_tensor(out=dmall[:, dr, t, :], in0=Q(vb_, 0), in1=M["NEGM"], op=ALU.add))
                    c.op("dve", [K_(vb_, 1)], [("r_mlc", MS, dr)], lambda e: e.tensor_reduce(out=mlc[:, MS, dr, t:t + 1], in_=Q(vb_, 1), axis=AX.X, op=ALU.max))
                    yield
                    c.op("dve", [("r_dmall", dr, t)], [("r_mlc", MI, dr)], lambda e: e.tensor_reduce(out=mlc[:, MI, dr, t:t + 1], in_=dmall[:, dr, t, :], axis=AX.X, op=ALU.max))
                    free.append(sl_)

                def ml_main(h, dr, si, t):
                    col = dr * 4 + h
                    sl_ = free.pop()
                    ts = slice(t * 128, (t + 1) * 128)
                    e_ = wt[alias["e"]][sl_]
                    ek = ("r_" + alias["e"], sl_)
                    p_, pT_, ks_ = pm[sl_], pTm[sl_], ksm[sl_]
                    pk_, pTk, ksk = ("r_pm", sl_), ("r_pTm", sl_), ("r_ksm", sl_)
                    Cbc, Cbn = Ssh[dr][si % 2], Ssh[dr][(si + 1) % 2]
                    Cbck, Cbnk = ("r_Ssh", dr, si % 2), ("r_Ssh", dr, (si + 1) % 2)
                    Cc, Cn = Sst[dr][si % 2], Sst[dr][(si + 1) % 2]
                    Cck, Cnk = ("r_S", dr, si % 2), ("r_S", dr, (si + 1) % 2)
                    ab_, vb_, sb_ = 2 * (sl_ % 2), 2 * (sl_ % 2) + 1, 4 + dr

                    def col_(i_):
                        return mlc[:, i_, dr, t:t + 1]
                    c.op("act", [("r_dmall", dr, t), ("r_mlc", NEGM, dr)], [ek], lambda e: e.activation(out=e_[:], in_=dmall[:, dr, t, :], func=AF.Exp, bias=col_(NEGM)))
                    c.op("act", ["r_ktok", ("r_mlc", SRC, dr)], [ksk], lambda e: e.activation(out=ks_[:], in_=ktok[:, t, :], func=AF.Copy, scale=col_(SRC)))
                    yield
                    c.op("pe", ["r_qTb", "r_kTb"], [K_(vb_, 2)], lambda e: e.matmul(Q(vb_, 2), lhsT=qTb[:, ts], rhs=kTb[:, ts], start=True, stop=True))
                    c.op("dve", [ek, K_(vb_, 2)], [pk_], lambda e: e.tensor_tensor(out=p_[:], in0=e_[:], in1=Q(vb_, 2), op=ALU.mult))
                    yield
                    qv = Q(ab_, 0).bitcast(BF16)[:, 0:128]
                    c.op("pe", [pk_, "idb"], [K_(ab_, 0)], lambda e: e.transpose(out=qv, in_=p_[:], identity=self.idb[:]))
                    c.op("act", [K_(ab_, 0)], [pTk], lambda e: e.copy(out=pT_[:], in_=qv))
                    yield
                    c.op("pe", [pTk, "r_vtokb"], [K_(ab_, 2)], lambda e: e.matmul(pb[ab_][:, 256:385], lhsT=pT_[:], rhs=vtokb[:, t, :], start=True, stop=True))
                    c.op("dve", [K_(ab_, 2)], [("r_nd", sl_)], lambda e: e.tensor_copy(out=nd[sl_][:], in_=pb[ab_][:, 256:385]))
                    yield
                    while turn[dr] != si:
                        yield

                    def mms(e):
                        e.matmul(pb[sb_][:, 0:129], lhsT=qTb[:, ts], rhs=Cbc[:, 0:129], start=True, stop=True)
                        return e.matmul(pb[sb_][:, 256:385], lhsT=ks_[:], rhs=vtokb[:, t, :], start=True, stop=True)
                    c.op("pe", ["r_qTb", Cbck, ksk, "r_vtokb"], [K_(sb_, 0), K_(sb_, 2)], mms)
                    c.op("dve", [K_(sb_, 2), Cck, ("r_mlc", DEC, dr)], [Cnk], lambda e: e.scalar_tensor_tensor(out=Cn[:], in0=Cc[:], scalar=col_(DEC), in1=pb[sb_][:, 256:385], op0=ALU.mult, op1=ALU.add))
                    c.op("act", [Cnk], [Cbnk], lambda e: e.copy(out=Cbn[:], in_=Cn[:]))
                    c.op("dve", [K_(sb_, 0), ("r_mlc", INTER, dr), ("r_nd", sl_)], [("r_nd", sl_)], lambda e: e.scalar_tensor_tensor(out=nd[sl_][:], in0=pb[sb_][:, 0:129], scalar=col_(INTER), in1=nd[sl_][:], op0=ALU.mult, op1=ALU.add))
                    turn[dr] += 1
                    yield
                    dc = dcol[sl_]
                    dk = ("r_dcol", sl_)
                    c.op("dve", [("r_nd", sl_)], [dk], lambda e: e.tensor_scalar(out=dc[:, 0:1], in0=nd[sl_][:, 128:129], scalar1=-1.0, scalar2=None, op0=ALU.mult))
                    yield
                    c.op("dve", [("r_nd", sl_), dk], [dk], lambda e: e.tensor_tensor(out=dc[:, 1:2], in0=nd[sl_][:, 128:129], in1=dc[:, 0:1], op=ALU.max))
                    yield
                    c.op("dve", [dk, ("r_mlc", EMT, dr)], [dk], lambda e: e.tensor_tensor(out=dc[:, 2:3], in0=dc[:, 1:2], in1=col_(EMT), op=ALU.max))
                    yield
                    c.op("dve", [dk], [dk], lambda e: e.reciprocal(out=dc[:, 3:4], in_=dc[:, 2:3]))
                    yield
                    c.op("dve", [("r_nd", sl_), dk, ("r_ost", t)], [("r_ost", t)], lambda e: e.scalar_tensor_tensor(out=ost[:, t, :], in0=nd[sl_][:, 0:128], scalar=dc[:, 3:4], in1=ost[:, t, :], op0=ALU.mult, op1=ALU.add))
                    free.append(sl_)

                for alg in _algs:
                    for h in range(int(_os.environ.get("AB_NH", "4"))):
                        if alg == "dn":
                            fq, fk, tk, tv, tg = h, 4 + h, "dnk", "dnv", "dnz"
                        else:
                            fq, fk, tk, tv, tg = 8 + h, 12 + h, "mlk", "mlv", "mlo"
                        c.op("sp", [("ab_fm", fk)], ["r_qT"], lambda e: e.dma_start(out=qT[:], in_=FM[fk]))
                        c.op("act", ["r_qT"], ["r_kTb"], lambda e: e.copy(out=kTb[:], in_=qT[:]))
                        c.op("sp", [("ab_fm", fq)], ["r_qT"], lambda e: e.dma_start(out=qT[:], in_=FM[fq]))
                        c.op("pool", ["r_qT"], ["r_qTb"], lambda e: e.tensor_copy(out=qTb[:], in_=qT[:]))
                        for t0 in range(0, NT, 4):
                            t1_ = min(NT, t0 + 4)
                            for (dst_, nm_, ky_) in ((ktok, tk, "r_ktok"), (vtok, tv, "r_vtok")):
                                c.op("sp", [("ab_tm", nm_, h)], [ky_], lambda e, dst_=dst_, nm_=nm_, t0=t0, t1_=t1_: e.dma_start(out=dst_[:, t0:t1_, 0:128], in_=TM[nm_][t0 * 128:t1_ * 128, h * 128:(h + 1) * 128].rearrange("(t p) d -> p t d", p=128)))
                        c.op("pool", ["r_vtok"], ["r_vtok1"], lambda e: e.memset(vtok[:, :, 128:129], 1.0))
                        c.op("dve", ["r_vtok", "r_vtok1"], ["r_vtokb"], lambda e: e.tensor_copy(out=vtokb[:], in_=vtok[:]))
                        c.op("pool", [("r_ost", t) for t in range(NT)], [("r_ost", t) for t in range(NT)], lambda e: e.memset(ost[:], 0.0))
                        for dr in range(2):
                            c.op("pool", [("r_S", dr, 0)], [("r_S", dr, 0)], lambda e, dr=dr: e.memset(Sst[dr][0][:], 0.0))
                            c.op("pool", [("r_Ssh", dr, 0)], [("r_Ssh", dr, 0)], lambda e, dr=dr: e.memset(Ssh[dr][0][:], 0.0))
                        orders = [list(range(NT)), list(range(NT - 1, -1, -1))]
                        if alg == "dn":
                            turn[0] = turn[1] = 0
                            gens = []
                            for si in range(NT):
                                for dr in range(2):
                                    gens.append(dn_unit(h, dr, si, orders[dr][si]))
                            run_units(gens)
                        else:
                            gens = []
                            for si in range(NT):
                                for dr in range(2):
                                    gens.append(ml_prep(h, dr, si, orders[dr][si]))
                            run_units(gens)
                            for dr in range(2):
                                col = dr * 4 + h
                                bt_ = Mlt[:, :, 16 + col]
                                mn_, ms_, b1_ = mlc[:, MNEW, dr, :], mlc[:, MS, dr, :], mlc[:, B1, dr, :]
                                if dr == 0:
                                    c.op("dve", ["ab_Mlt", ("r_mlc", MS, dr)], [("r_mlc", MNEW, dr)], lambda e, bt_=bt_, mn_=mn_, ms_=ms_: e.tensor_tensor_scan(out=mn_, data0=bt_, data1=ms_, initial=0.0, op0=ALU.add, op1=ALU.max))
                                    c.op("dve", ["ab_Mlt", ("r_mlc", MNEW, dr)], [("r_mlc", B1, dr)], lambda e, bt_=bt_, mn_=mn_, b1_=b1_: e.tensor_tensor(out=b1_[:, 1:NT], in0=bt_[:, 1:NT], in1=mn_[:, 0:NT - 1], op=ALU.add))
                                    c.op("dve", ["ab_Mlt", ("r_mlc", B1, dr)], [("r_mlc", B1, dr)], lambda e, bt_=bt_, b1_=b1_: e.tensor_copy(out=b1_[:, 0:1], in_=bt_[:, 0:1]))
                                else:
                                    c.op("dve", ["ab_Mlt", ("r_mlc", MS, dr)], [("r_mlc", MNEW, dr)], lambda e, bt_=bt_, mn_=mn_, ms_=ms_: e.tensor_tensor_scan(out=mn_[:, ::-1], data0=bt_[:, ::-1], data1=ms_[:, ::-1], initial=0.0, op0=ALU.add, op1=ALU.max))
                                    c.op("dve", ["ab_Mlt", ("r_mlc", MNEW, dr)], [("r_mlc", B1, dr)], lambda e, bt_=bt_, mn_=mn_, b1_=b1_: e.tensor_tensor(out=b1_[:, 0:NT - 1], in0=bt_[:, 0:NT - 1], in1=mn_[:, 1:NT], op=ALU.add))
                                    c.op("dve", ["ab_Mlt", ("r_mlc", B1, dr)], [("r_mlc", B1, dr)], lambda e, bt_=bt_, b1_=b1_: e.tensor_copy(out=b1_[:, NT - 1:NT], in_=bt_[:, NT - 1:NT]))
                            for dr in range(2):
                                col = dr * 4 + h

                                def A_(i_, dr=dr):
                                    return mlc[:, i_, dr, :]

                                def kk_(i_, dr=dr):
                                    return ("r_mlc", i_, dr)
                                bcum_, btot_, wsrc_ = Mlt[:, :, col], Mlt[:, :, 16 + col], Ws[:, :, col]
                                c.op("dve", ["ab_Mlt"], [kk_(A1)], lambda e, dr=dr: e.tensor_tensor(out=A_(A1), in0=bcum_, in1=btot_, op=ALU.subtract))
                                c.op("dve", [kk_(A1), kk_(B1)], [kk_(A1)], lambda e, dr=dr: e.tensor_tensor(out=A_(A1), in0=A_(A1), in1=A_(B1), op=ALU.add))
                                c.op("dve", [kk_(A1), kk_(MI)], [kk_(MT)], lambda e, dr=dr: e.tensor_tensor(out=A_(MT), in0=A_(A1), in1=A_(MI), op=ALU.max))
                                c.op("dve", [kk_(MT)], [kk_(NEGM)], lambda e, dr=dr: e.tensor_scalar(out=A_(NEGM), in0=A_(MT), scalar1=-1.0, scalar2=None, op0=ALU.mult))
                                c.op("dve", [kk_(A1), kk_(MT)], [kk_(TMP)], lambda e, dr=dr: e.tensor_tensor(out=A_(TMP), in0=A_(A1), in1=A_(MT), op=ALU.subtract))
                                c.op("act", [kk_(TMP)], [kk_(INTER)], lambda e, dr=dr: e.activation(out=A_(INTER), in_=A_(TMP), func=AF.Exp))
                                c.op("act", [kk_(NEGM)], [kk_(EMT)], lambda e, dr=dr: e.activation(out=A_(EMT), in_=A_(NEGM), func=AF.Exp))
                                c.op("dve", [kk_(B1), kk_(MNEW), kk_(INTER)], [kk_(TMP)], lambda e, dr=dr: e.tensor_tensor(out=A_(TMP), in0=A_(B1), in1=A_(MNEW), op=ALU.subtract))
                                c.op("act", [kk_(TMP)], [kk_(DEC)], lambda e, dr=dr: e.activation(out=A_(DEC), in_=A_(TMP), func=AF.Exp))
                                c.op("dve", ["ab_Ws", kk_(MNEW), kk_(DEC)], [kk_(TMP)], lambda e, dr=dr: e.tensor_tensor(out=A_(TMP), in0=wsrc_, in1=A_(MNEW), op=ALU.subtract))
                                c.op("act", [kk_(TMP)], [kk_(SRC)], lambda e, dr=dr: e.activation(out=A_(SRC), in_=A_(TMP), func=AF.Exp))
                            turn[0] = turn[1] = 0
                            gens = []
                            for si in range(NT):
                                for dr in range(2):
                                    gens.append(ml_main(h, dr, si, orders[dr][si]))
                            run_units(gens)
                        mchunk = h if alg == "dn" else 4 + h
                        for t0 in range(0, NT, 4):
                            t1_ = min(NT, t0 + 4)
                            c.op("sp", [("ab_tm", tg, h)], ["r_qT"], lambda e, t0=t0, t1_=t1_: e.dma_start(out=gtok[:, t0:t1_, :], in_=TM[tg][t0 * 128:t1_ * 128, h * 128:(h + 1) * 128].rearrange("(t p) d -> p t d", p=128)))
                        for t in range(NT):
                            u2 = t % 2
                            if alg == "dn":
                                c.op("act", [("r_ost", t)], [("r_ot", u2), ("r_oss", u2)], lambda e, t=t, u2=u2: e.activation(out=ot[u2][:], in_=ost[:, t, :], func=AF.Square, accum_out=oss[u2][:, 0:1]))
                                c.op("dve", [("r_oss", u2)], [("r_oss", u2)], lambda e, u2=u2: e.tensor_scalar(out=oss[u2][:, 0:1], in0=oss[u2][:, 0:1], scalar1=1.0 / 128, scalar2=EPS, op0=ALU.mult, op1=ALU.add))
                                c.op("act", [("r_oss", u2)], [("r_oss", u2)], lambda e, u2=u2: e.activation(out=oss[u2][:, 0:1], in_=oss[u2][:, 0:1], func=AF.Sqrt))
                                c.op("dve", [("r_oss", u2)], [("r_oss", u2)], lambda e, u2=u2: e.reciprocal(out=oss[u2][:, 0:1], in_=oss[u2][:, 0:1]))
                                c.op("dve", [("r_ost", t), ("r_oss", u2), "ab_dnw"], [("r_ot", u2)], lambda e, t=t, u2=u2: e.scalar_tensor_tensor(out=ot[u2][:], in0=ost[:, t, :], scalar=oss[u2][:, 0:1], in1=dnw[:], op0=ALU.mult, op1=ALU.mult))
                                c.op("dve", [("r_ot", u2), "r_qT"], [("r_ob", u2)], lambda e, t=t, u2=u2: e.tensor_tensor(out=ob[u2][:], in0=ot[u2][:], in1=gtok[:, t, :], op=ALU.mult))
                            else:
                                c.op("dve", [("r_ost", t), "r_qT"], [("r_ob", u2)], lambda e, t=t, u2=u2: e.tensor_tensor(out=ob[u2][:], in0=ost[:, t, :], in1=gtok[:, t, :], op=ALU.mult))
                            c.op("pe", [("r_ob", u2), "idb"], [("r_pTb", t % 4)], lambda e, t=t, u2=u2: e.transpose(out=pTb[:, (t % 4) * 128:(t % 4 + 1) * 128], in_=ob[u2][:], identity=self.idb[:]))
                            if t % 4 == 3:
                                tb = t // 4
                                mx = mst4[tb % 2]
                                c.op("act", [("r_pTb", q_) for q_ in range(4)], [("r_mx", tb % 2)], lambda e, mx=mx: e.copy(out=mx[:], in_=pTb[:, 0:512]))
                                c.op("gq", [("r_mx", tb % 2)], [("ab_mix", tb)], lambda e, mx=mx, tb=tb, mchunk=mchunk: e.dma_start(out=mixscr[mchunk, :, tb * 512:(tb + 1) * 512], in_=mx[:]))
                c.barrier()
        self.stage_c_scr(c, L, mixscr, 8, Dr["ab_w_out"][j], Dr["mix_norm_post"][L:L + 1, :], src, dst, "abc")


W_NAMES = ['mix_norm_pre', 'mix_norm_post', 'ab_w_in', 'ab_w_out', 'dn_conv_w', 'dn_a_log', 'dn_dt_bias',
           'dn_out_norm', 'ml_conv_w', 'ml_i_bias', 'ml_f_bias', 'c_w_in', 'c_w_out', 'c_conv_w', 'c_conv_b',
           'c_w_a', 'c_b_a', 'c_w_x', 'c_b_x', 'c_lambda', 'xa_norm_pre', 'xa_norm_post', 'xa_mem_norm',
           'xa_w_q', 'xa_w_kv', 'xa_w_o', 'ffn_norm_pre', 'ffn_norm_post', 'ffn_w_up', 'ffn_w_down']


def const_masks():
    p = np.arange(128)[:, None]
    f = np.arange(128)[None, :]
    ms = [p == f, p <= f, p >= f, p < f, p > f, np.ones((128, 128), bool)]
    arr = [m.astype(np.float32) for m in ms]
    arr.append(NEG * (p < f).astype(np.float32))
    arr.append(NEG * (p > f).astype(np.float32))
    return np.ascontiguousarray(np.concatenate(arr, axis=1)).astype(np.float32)


def build(S, shapes, stages, Mm=256):
    B = Builder(S, Mm)
    P = B.P
    nc = B.nc
    for n, shp in shapes.items():
        P.din(n, shp)
    P.din("cmask", [128, 8 * 128])
    out = P.dout("out", [S, D])
    with ExitStack() as es:
        c = Ctx(nc, es)
        B.load_consts(c, es)
        src = P.dram["x"]
        for (name, L) in stages:
            getattr(B, name)(c, L, src, out)
            src = out
        c.barrier()
        c.finish()
    B.nops = c.nops
    return B


STAGES = [("mixer_ab", 0), ("xa", 0), ("ffn", 0), ("mixer_c", 1), ("xa", 1), ("ffn", 1)]


def kernel(**inputs):
    x = np.ascontiguousarray(np.asarray(inputs["x"], dtype=np.float32))
    mem = np.ascontiguousarray(np.asarray(inputs["mem"], dtype=np.float32))
    nb, S, _ = x.shape
    shapes = {"x": (S, D), "mem": tuple(mem.shape[1:])}
    ws = {}
    for n in W_NAMES:
        ws[n] = np.ascontiguousarray(np.asarray(inputs[n], dtype=np.float32))
        shapes[n] = ws[n].shape
    B = build(S, shapes, STAGES, Mm=mem.shape[1])
    cm = const_masks()
    in_maps = []
    for b in range(nb):
        m = {"x": x[b], "mem": mem[b], "cmask": cm}
        m.update(ws)
        in_maps.append(m)
    res = run_bass_kernel_spmd(B.nc, in_maps, core_ids=list(range(nb)))
    return np.stack([np.asarray(r["out"], dtype=np.float32) for r in res.results], axis=0)
```

```python
import numpy as np
from contextlib import ExitStack
import concourse.bass as bass
import concourse.mybir as mybir
from concourse.bass_utils import run_bass_kernel_spmd

F32 = mybir.dt.float32
BF16 = mybir.dt.bfloat16
AF = mybir.ActivationFunctionType
ALU = mybir.AluOpType
AX = mybir.AxisListType
D = 1024
EPS = 1e-6
NEG = -30000.0


class _Eng:
    def __init__(self, ctx, name, be, is_dma, nslots=14):
        self.name, self.be, self.is_dma = name, be, is_dma
        self.waited = {}
        if is_dma:
            self.slots = [ctx.new_sem(f"{name}_d{i}") for i in range(nslots)]
            self.n = 0
        else:
            self.sem = ctx.new_sem(f"{name}_s")
            self.count = 0


class _Buf:
    __slots__ = ("w", "r", "rd")

    def __init__(self):
        self.w = None
        self.r = {}
        self.rd = []


class Ctx:
    def __init__(self, nc, es):
        self.nc, self.es = nc, es
        self.sems = []
        self.bufs = {}
        self.engs = {}
        for name, be, dma in (("pe", nc.tensor, False), ("act", nc.scalar, False),
                              ("dve", nc.vector, False), ("pool", nc.gpsimd, False),
                              ("sp", nc.sync, True), ("gq", nc.gpsimd, True)):
            self.engs[name] = _Eng(self, name, be, dma)
        self.nops = 0
        import os as _os
        self.limit = int(_os.environ["OP_LIMIT"]) if "OP_LIMIT" in _os.environ else None
        self.trace = tuple(int(v) for v in _os.environ["OP_TRACE"].split(",")) if "OP_TRACE" in _os.environ else None

    def new_sem(self, name):
        s = self.es.enter_context(self.nc.semaphore(name))
        self.sems.append(s)
        return len(self.sems) - 1

    def buf(self, k):
        b = self.bufs.get(k)
        if b is None:
            b = self.bufs[k] = _Buf()
        return b

    def op(self, eng, reads, writes, fn):
        E = self.engs[eng]
        need = {}
        if self.trace is not None and self.trace[0] <= self.nops < self.trace[1]:
            print("OP", self.nops, eng, "R", reads, "W", writes)
        if self.limit is not None and self.nops >= self.limit:
            self.nops += 1
            return None

        def add(ev, raw):
            de, si, val = ev
            if de is E and not E.is_dma:
                if E.name == "pe" or not raw:
                    return
            if need.get(si, 0) < val:
                need[si] = val

        for k in reads:
            b = self.buf(k)
            if b.w is not None:
                add(b.w, True)
        for k in writes:
            b = self.buf(k)
            if b.w is not None:
                add(b.w, True)
            for ev in b.r.values():
                add(ev, False)
            for ev in b.rd:
                add(ev, False)
        banks = set()
        for k in list(reads) + list(writes):
            bk = self.bank(k)
            if bk is not None:
                banks.add(bk)
        for bk in banks:
            b = self.buf(bk)
            if b.w is not None:
                add(b.w, False)
        if E.is_dma:
            slot = E.n % len(E.slots)
            gen = E.n // len(E.slots)
            si_own = E.slots[slot]
            if gen > 0 and need.get(si_own, 0) < 16 * gen:
                need[si_own] = 16 * gen
        for si, val in need.items():
            if E.waited.get(si, 0) >= val:
                continue
            E.be.wait_ge(self.sems[si], val)
            E.waited[si] = val
        ins = fn(E.be)
        if E.is_dma:
            ins.then_inc(self.sems[si_own], 16)
            ev = (E, si_own, 16 * (gen + 1))
            E.n += 1
        else:
            E.count += 1
            ins.then_inc(self.sems[E.sem], 1)
            ev = (E, E.sem, E.count)
        for k in reads:
            b = self.buf(k)
            if E.is_dma:
                b.rd.append(ev)
            else:
                b.r[E.name] = ev
        for k in writes:
            b = self.buf(k)
            b.w = ev
            b.r = {}
            b.rd = []
        for bk in banks:
            self.buf(bk).w = ev
        self.nops += 1
        return ev

    _BANKED = ("r_pb", "ab_pp", "x_pa", "f_pu", "c_pp", "pT", "py")

    def bank(self, k):
        if isinstance(k, tuple):
            if k[0] in self._BANKED:
                return ("BANK", k[0], k[1])
            if k[0] == "r_pTb":
                return ("BANK", "r_pTb")
        elif k == "x_pss":
            return ("BANK", "x_pss")
        return None

    def barrier(self):
        evs = []
        for E in self.engs.values():
            if E.is_dma:
                for i, si in enumerate(E.slots):
                    cnt = (E.n - i + len(E.slots) - 1) // len(E.slots) if E.n > i else 0
                    if cnt > 0:
                        evs.append((si, 16 * cnt))
            elif E.count > 0:
                evs.append((E.sem, E.count))
        for E in self.engs.values():
            for si, val in evs:
                if (not E.is_dma) and si == E.sem:
                    continue
                if E.waited.get(si, 0) >= val:
                    continue
                E.be.wait_ge(self.sems[si], val)
                E.waited[si] = val

    def finish(self):
        for E in self.engs.values():
            if E.is_dma:
                for i, si in enumerate(E.slots):
                    cnt = (E.n - i + len(E.slots) - 1) // len(E.slots) if E.n > i else 0
                    if cnt > 0 and E.waited.get(si, 0) < 16 * cnt:
                        E.be.wait_ge(self.sems[si], 16 * cnt)
                        E.waited[si] = 16 * cnt


class Prog:
    def __init__(self, S):
        self.S = S
        self.NT = S // 128
        self.nc = bass.Bass("TRN2", target_bir_lowering=False)
        self.dram = {}

    def din(self, name, shape, dt=F32):
        self.dram[name] = self.nc.dram_tensor(name, list(shape), dt, kind="ExternalInput").ap()
        return self.dram[name]

    def dout(self, name, shape, dt=F32):
        self.dram[name] = self.nc.dram_tensor(name, list(shape), dt, kind="ExternalOutput").ap()
        return self.dram[name]

    def dscr(self, name, shape, dt=F32):
        self.dram[name] = self.nc.dram_tensor(name, list(shape), dt, kind="Internal").ap()
        return self.dram[name]


_UID = [0]


def _sb(es, nc, name, shape, dt):
    _UID[0] += 1
    return es.enter_context(nc.sbuf_tensor(f"{name}_{_UID[0]}", list(shape), dt))


def _ps(es, nc, name, shape, dt):
    _UID[0] += 1
    return es.enter_context(nc.psum_tensor(f"{name}_{_UID[0]}", list(shape), dt))


class Builder:
    def __init__(self, S, Mm=256):
        self.S, self.NT, self.Mm = S, S // 128, Mm
        self.P = Prog(S)
        self.nc = self.P.nc

    def load_consts(self, c, es):
        nc = self.nc
        cm = self.P.dram["cmask"]
        self.cst = _sb(es, nc, "cst", [128, 8, 128], F32)
        c.op("sp", [], ["cst"], lambda e: e.dma_start(out=self.cst[:], in_=cm.rearrange("p (k f) -> p k f", k=8)))
        self.idb = _sb(es, nc, "idb", [128, 128], BF16)
        c.op("dve", ["cst"], ["idb"], lambda e: e.tensor_copy(out=self.idb[:], in_=self.cst[:, 0, :]))
        self.onesb = _sb(es, nc, "onesb", [128, 128], BF16)
        c.op("dve", ["cst"], ["onesb"], lambda e: e.tensor_copy(out=self.onesb[:], in_=self.cst[:, 5, :]))
        self.I = self.cst[:, 0, :]
        self.LE = self.cst[:, 1, :]
        self.GE = self.cst[:, 2, :]
        self.LT = self.cst[:, 3, :]
        self.GT = self.cst[:, 4, :]
        self.ONES = self.cst[:, 5, :]
        self.NLT = self.cst[:, 6, :]
        self.NGT = self.cst[:, 7, :]

    def load_bc(self, c, tile, key, src_row):
        c.op("sp", [], [key], lambda e: e.dma_start(out=tile, in_=src_row.partition_broadcast(128)))

    def rstd_from_ss(self, c, ss, key, n):
        c.op("dve", [key], [key], lambda e: e.tensor_scalar(out=ss, in0=ss, scalar1=1.0 / n, scalar2=EPS, op0=ALU.mult, op1=ALU.add))
        c.op("act", [key], [key], lambda e: e.activation(out=ss, in_=ss, func=AF.Sqrt))
        c.op("dve", [key], [key], lambda e: e.reciprocal(out=ss, in_=ss))

    def prenorm_tile(self, c, W, src_ap, wkey, wbc, dstT, dkey, slot, skey=None):
        h, sq, ss, xn, pT = W["h"][slot], W["sq"], W["ss"][slot], W["xn"][slot], W["pT"][slot]
        hk, ssk, xnk, pk = ("h", slot), ("ss", slot), ("xn", slot), ("pT", slot)
        c.op("sp", [skey] if skey else [], [hk], lambda e: e.dma_start(out=h[:], in_=src_ap))
        c.op("act", [hk], ["sq", ssk], lambda e: e.activation(out=sq[:], in_=h[:], func=AF.Square, accum_out=ss[:]))
        self.rstd_from_ss(c, ss[:], ssk, D)
        c.op("dve", [hk, ssk, wkey], [xnk], lambda e: e.scalar_tensor_tensor(out=xn[:], in0=h[:], scalar=ss[:], in1=wbc, op0=ALU.mult, op1=ALU.mult))

        def tr(e):
            for k in range(8):
                ins = e.transpose(out=pT[:, k * 128:(k + 1) * 128], in_=xn[:, k * 128:(k + 1) * 128], identity=self.idb[:])
            return ins
        c.op("pe", [xnk, "idb"], [pk], tr)
        c.op("act", [pk], [dkey], lambda e: e.copy(out=dstT, in_=pT[:].rearrange("p (k t) -> p k t", k=8)))

    def alloc_prenorm(self, es, tag="", sq=None):
        nc = self.nc
        W = {"h": [_sb(es, nc, f"pn_h{i}{tag}", [128, D], F32) for i in range(2)],
             "sq": sq if sq is not None else _sb(es, nc, f"pn_sq{tag}", [128, D], F32),
             "ss": [_sb(es, nc, f"pn_ss{i}{tag}", [128, 1], F32) for i in range(2)],
             "xn": [_sb(es, nc, f"pn_xn{i}{tag}", [128, D], BF16) for i in range(2)],
             "pT": [_ps(es, nc, f"pn_pT{i}{tag}", [128, D], BF16) for i in range(2)]}
        return W

    def outproj_tile(self, c, W, zT_fn, zkeys, KC, wo, wokey, wpost, wpkey, res_ap, out_ap, slot, rkey=None, okey=None):
        py = W["py"][slot % len(W["py"])]
        pk = ("py", slot % len(W["py"]))
        hr, ss2, t1 = W["hr"][slot], W["ss2"][slot], W["t1"][slot]
        hrk, s2k, t1k = ("hr", slot), ("ss2", slot), ("t1", slot)

        def mm(e):
            for nb in range(2):
                for kc in range(KC):
                    ins = e.matmul(py[nb][:], lhsT=zT_fn(kc), rhs=wo[:, kc, nb * 512:(nb + 1) * 512], start=(kc == 0), stop=(kc == KC - 1))
            return ins
        c.op("pe", list(zkeys) + (list(wokey) if isinstance(wokey, (list, tuple)) and wokey and isinstance(wokey[0], tuple) else [wokey]), [pk], mm)
        c.op("sp", [rkey] if rkey else [], [hrk], lambda e: e.dma_start(out=hr[:], in_=res_ap))
        c.op("act", [pk], ["sq2", (s2k, 0)], lambda e: e.activation(out=W["sq2"][:, 0:512], in_=py[0][:], func=AF.Square, accum_out=ss2[:, 0:1]))
        c.op("act", [pk], ["sq2", (s2k, 1)], lambda e: e.activation(out=W["sq2"][:, 512:1024], in_=py[1][:], func=AF.Square, accum_out=ss2[:, 1:2]))
        c.op("dve", [(s2k, 0), (s2k, 1)], [s2k], lambda e: e.tensor_tensor(out=ss2[:, 2:3], in0=ss2[:, 0:1], in1=ss2[:, 1:2], op=ALU.add))
        self.rstd_from_ss(c, ss2[:, 2:3], s2k, D)
        for nb in range(2):
            c.op("dve", [pk, s2k, wpkey], [(t1k, nb)], lambda e, nb=nb: e.scalar_tensor_tensor(out=t1[:, nb * 512:(nb + 1) * 512], in0=py[nb][:], scalar=ss2[:, 2:3], in1=wpost[:, nb * 512:(nb + 1) * 512], op0=ALU.mult, op1=ALU.mult))
        c.op("pool", [(t1k, 0), (t1k, 1), hrk], [hrk], lambda e: e.tensor_tensor(out=hr[:], in0=hr[:], in1=t1[:], op=ALU.add))
        c.op("gq", [hrk], [okey] if okey else [], lambda e: e.dma_start(out=out_ap, in_=hr[:]))

    def alloc_outproj(self, es, tag="", npy=2):
        nc = self.nc
        return {"py": [[_ps(es, nc, f"op_py{i}{j}{tag}", [128, 512], F32) for j in range(2)] for i in range(npy)],
                "hr": [_sb(es, nc, f"op_hr{i}{tag}", [128, D], F32) for i in range(2)],
                "ss2": [_sb(es, nc, f"op_ss{i}{tag}", [128, 4], F32) for i in range(2)],
                "t1": [_sb(es, nc, f"op_t1{i}{tag}", [128, D], F32) for i in range(2)],
                "sq2": _sb(es, nc, f"op_sq2{tag}", [128, D], F32)}

    def load_w_bf16(self, c, dst, key, src3):
        KC = dst.shape[1]
        N = dst.shape[2]
        step = max(1, 2048 // N)
        step = min(KC, 4)
        for k0 in range(0, KC, step):
            k1 = min(KC, k0 + step)
            for n0 in range(0, N, 2048):
                n1 = min(N, n0 + 2048)
                c.op("gq", [], [key], lambda e, k0=k0, k1=k1, n0=n0, n1=n1: e.dma_start(out=dst[:, k0:k1, n0:n1], in_=src3[:, k0:k1, n0:n1]))

    def ffn(self, c, L, src, dst):
        nc, S, NT = self.nc, self.S, self.NT
        Dr = self.P.dram
        with ExitStack() as es:
            wup = _sb(es, nc, "f_wup", [128, 8, 4096], BF16)
            wdn = _sb(es, nc, "f_wdn", [128, 32, 1024], BF16)
            wpre = _sb(es, nc, "f_wpre", [128, D], F32)
            wpost = _sb(es, nc, "f_wpost", [128, D], F32)
            self.load_bc(c, wpre[:], "f_wpre", Dr["ffn_norm_pre"][L:L + 1, :])
            self.load_bc(c, wpost[:], "f_wpost", Dr["ffn_norm_post"][L:L + 1, :])
            wup3 = Dr["ffn_w_up"][L].rearrange("(k p) n -> p k n", p=128)
            wdn3 = Dr["ffn_w_down"][L].rearrange("(k p) n -> p k n", p=128)
            for cb in range(8):
                c.op("gq", [], [("f_wup", cb)], lambda e, cb=cb: e.dma_start(out=wup[:, :, cb * 512:(cb + 1) * 512], in_=wup3[:, :, cb * 512:(cb + 1) * 512]))
            for g in range(8):
                c.op("gq", [], [("f_wdn", g)], lambda e, g=g: e.dma_start(out=wdn[:, g * 4:(g + 1) * 4, :], in_=wdn3[:, g * 4:(g + 1) * 4, :]))
            wdn_keys = [("f_wdn", g) for g in range(8)]
            OP = self.alloc_outproj(es, "f")
            PN = self.alloc_prenorm(es, "f", sq=OP["sq2"])
            TB = 256
            hnT = [_sb(es, nc, f"f_hnT{i}", [128, 8, TB], BF16) for i in range(2)]
            aT = _sb(es, nc, "f_aT", [128, 32, TB], BF16)
            rl = [_sb(es, nc, f"f_rl{i}", [128, TB], F32) for i in range(2)]
            pu = [_ps(es, nc, f"f_pu{i}", [128, 512], F32) for i in range(2)]
            nblk = S // TB
            tpb = TB // 128
            for b in range(nblk):
                hb = hnT[b % 2]
                for t in range(tpb):
                    tt = b * tpb + t
                    self.prenorm_tile(c, PN, src[tt * 128:(tt + 1) * 128, :], "f_wpre", wpre[:], hb[:, :, t * 128:(t + 1) * 128], ("f_hnT", b % 2, t), tt % 2, skey=(src.tensor.name, tt))
                hkeys = [("f_hnT", b % 2, t) for t in range(tpb)]
                for fc in range(32):
                    p = pu[fc % 2]

                    def mm(e, fc=fc, p=p):
                        for kc in range(8):
                            ins = e.matmul(p[:, 0:TB], lhsT=wup[:, kc, fc * 128:(fc + 1) * 128], rhs=hb[:, kc, :], start=(kc == 0), stop=(kc == 7))
                        return ins
                    c.op("pe", hkeys + [("f_wup", fc // 4)], [("f_pu", fc % 2)], mm)
                    r = rl[fc % 2]
                    c.op("act", [("f_pu", fc % 2)], [("f_rl", fc % 2)], lambda e, p=p, r=r: e.activation(out=r[:], in_=p[:, 0:TB], func=AF.Relu))
                    c.op("dve", [("f_rl", fc % 2)], [("f_aT", fc)], lambda e, r=r, fc=fc: e.tensor_tensor(out=aT[:, fc, :], in0=r[:], in1=r[:], op=ALU.mult))
                akeys = [("f_aT", fc) for fc in range(32)]
                for t in range(tpb):
                    tt = b * tpb + t
                    self.outproj_tile(c, OP, lambda kc, t=t: aT[:, kc, t * 128:(t + 1) * 128], akeys, 32, wdn, wdn_keys, wpost, "f_wpost",
                                      src[tt * 128:(tt + 1) * 128, :], dst[tt * 128:(tt + 1) * 128, :], tt % 2,
                                      rkey=(src.tensor.name, tt), okey=(dst.tensor.name, tt))
            c.barrier()

    def xa(self, c, L, src, dst):
        nc, S, NT, Mm = self.nc, self.S, self.NT, self.Mm
        Dr = self.P.dram
        MC = Mm // 128
        with ExitStack() as es:
            wq = _sb(es, nc, "x_wq", [128, 8, 1024], BF16)
            wo = _sb(es, nc, "x_wo", [128, 8, 1024], BF16)
            wpre = _sb(es, nc, "x_wpre", [128, D], F32)
            wpost = _sb(es, nc, "x_wpost", [128, D], F32)
            wmem = _sb(es, nc, "x_wmem", [128, D], F32)
            self.load_bc(c, wpre[:], "x_wpre", Dr["xa_norm_pre"][L:L + 1, :])
            self.load_bc(c, wpost[:], "x_wpost", Dr["xa_norm_post"][L:L + 1, :])
            self.load_bc(c, wmem[:], "x_wmem", Dr["xa_mem_norm"][L:L + 1, :])
            self.load_w_bf16(c, wq, "x_wq", Dr["xa_w_q"][L].rearrange("(k p) n -> p k n", p=128))
            self.load_w_bf16(c, wo, "x_wo", Dr["xa_w_o"][L].rearrange("(k p) n -> p k n", p=128))
            PN = self.alloc_prenorm(es, "x")
            OP = self.alloc_outproj(es, "x", npy=1)
            kT = _sb(es, nc, "x_kT", [128, 8, Mm], BF16)
            V = _sb(es, nc, "x_V", [128, MC, 1024], BF16)
            pa = [_ps(es, nc, f"x_pa{i}", [128, 512], F32) for i in range(2)]
            with ExitStack() as es2:
                wkv = _sb(es2, nc, "x_wkv", [128, 8, 2048], BF16)
                self.load_w_bf16(c, wkv, "x_wkv", Dr["xa_w_kv"][L].rearrange("(k p) n -> p k n", p=128))
                mnT = _sb(es2, nc, "x_mnT", [128, 8, Mm], BF16)
                for mc in range(MC):
                    self.prenorm_tile(c, PN, Dr["mem"][mc * 128:(mc + 1) * 128, :], "x_wmem", wmem[:], mnT[:, :, mc * 128:(mc + 1) * 128], ("x_mnT", mc), mc % 2)
                mkeys = [("x_mnT", mc) for mc in range(MC)]
                for ch in range(8):
                    p = pa[ch % 2]

                    def mm(e, ch=ch, p=p):
                        for kc in range(8):
                            ins = e.matmul(p[:, 0:Mm], lhsT=wkv[:, kc, ch * 128:(ch + 1) * 128], rhs=mnT[:, kc, :], start=(kc == 0), stop=(kc == 7))
                        return ins
                    c.op("pe", mkeys + ["x_wkv"], [("x_pa", ch % 2)], mm)
                    c.op("act", [("x_pa", ch % 2)], ["x_kT"], lambda e, ch=ch, p=p: e.copy(out=kT[:, ch, :], in_=p[:, 0:Mm]))
                for mc in range(MC):
                    for nb in range(2):
                        p = pa[nb]

                        def mm(e, mc=mc, nb=nb, p=p):
                            for kc in range(8):
                                ins = e.matmul(p[:], lhsT=mnT[:, kc, mc * 128:(mc + 1) * 128], rhs=wkv[:, kc, 1024 + nb * 512:1024 + (nb + 1) * 512], start=(kc == 0), stop=(kc == 7))
                            return ins
                        c.op("pe", mkeys + ["x_wkv"], [("x_pa", nb)], mm)
                        c.op("act", [("x_pa", nb)], ["x_V"], lambda e, mc=mc, nb=nb, p=p: e.copy(out=V[:, mc, nb * 512:(nb + 1) * 512], in_=p[:]))
                c.barrier()
            hnT = [_sb(es, nc, f"x_hnT{i}", [128, 8, 512], BF16) for i in range(2)]
            qT = _sb(es, nc, "x_qT", [128, 8, 512], BF16)
            oT = [_sb(es, nc, f"x_oT{i}", [128, 8, 512], BF16) for i in range(2)]
            sc = [_sb(es, nc, f"x_sc{i}", [128, Mm], F32) for i in range(2)]
            pb = [_sb(es, nc, f"x_pb{i}", [128, Mm], BF16) for i in range(2)]
            pTs = [_sb(es, nc, f"x_pTs{i}", [128, MC, 512], BF16) for i in range(2)]
            st = [_sb(es, nc, f"x_st{i}", [128, 4], F32) for i in range(2)]
            ps_s = [_ps(es, nc, f"x_pss{i}", [128, 512], F32) for i in range(1)]
            scale = 256.0 ** -0.5
            it = 0
            for blk in range(NT // 4):
                sl = blk % 2
                for t in range(4):
                    tt = blk * 4 + t
                    self.prenorm_tile(c, PN, src[tt * 128:(tt + 1) * 128, :], "x_wpre", wpre[:], hnT[sl][:, :, t * 128:(t + 1) * 128], ("x_hnT", sl, t), tt % 2, skey=(src.tensor.name, tt))
                hk = [("x_hnT", sl, t) for t in range(4)]
                for ch in range(8):
                    p = pa[ch % 2]

                    def mm(e, ch=ch, p=p, sl=sl):
                        for kc in range(8):
                            ins = e.matmul(p[:], lhsT=wq[:, kc, ch * 128:(ch + 1) * 128], rhs=hnT[sl][:, kc, :], start=(kc == 0), stop=(kc == 7))
                        return ins
                    c.op("pe", hk + ["x_wq"], [("x_pa", ch % 2)], mm)
                    c.op("act", [("x_pa", ch % 2)], [("x_qT", ch)], lambda e, ch=ch, p=p: e.copy(out=qT[:, ch, :], in_=p[:]))
                for hd in range(4):
                    h2 = hd % 2
                    pT4 = pTs[h2]
                    for t in range(4):
                        i2 = it % 2
                        it += 1
                        pss = ps_s[0]
                        tsl = slice(t * 128, (t + 1) * 128)

                        def mm(e, hd=hd, pss=pss, tsl=tsl):
                            for k2 in range(2):
                                ins = e.matmul(pss[:, 0:Mm], lhsT=qT[:, hd * 2 + k2, tsl], rhs=kT[:, hd * 2 + k2, :], start=(k2 == 0), stop=(k2 == 1))
                            return ins
                        c.op("pe", [("x_qT", hd * 2), ("x_qT", hd * 2 + 1), "x_kT"], ["x_pss"], mm)
                        s_, sc_, pb_ = st[i2], sc[i2], pb[i2]
                        sk, sck, pbk = ("x_st", i2), ("x_sc", i2), ("x_pb", i2)
                        c.op("dve", ["x_pss"], [sk], lambda e, s_=s_, pss=pss: e.tensor_reduce(out=s_[:, 0:1], in_=pss[:, 0:Mm], axis=AX.X, op=ALU.max))
                        c.op("dve", [sk], [sk], lambda e, s_=s_: e.tensor_scalar(out=s_[:, 1:2], in0=s_[:, 0:1], scalar1=-scale, scalar2=None, op0=ALU.mult))
                        c.op("act", ["x_pss", sk], [sck, (sk, "sum")], lambda e, s_=s_, sc_=sc_, pss=pss: e.activation(out=sc_[:], in_=pss[:, 0:Mm], func=AF.Exp, bias=s_[:, 1:2], scale=scale, accum_out=s_[:, 2:3]))
                        c.op("dve", [(sk, "sum")], [(sk, "sum")], lambda e, s_=s_: e.reciprocal(out=s_[:, 3:4], in_=s_[:, 2:3]))
                        c.op("dve", [sck, (sk, "sum")], [pbk], lambda e, s_=s_, sc_=sc_, pb_=pb_: e.tensor_scalar(out=pb_[:], in0=sc_[:], scalar1=s_[:, 3:4], scalar2=None, op0=ALU.mult))
                        ptp = PN["pT"][i2]

                        def tr(e, pb_=pb_, ptp=ptp):
                            for mc in range(MC):
                                ins = e.transpose(out=ptp[:, mc * 128:(mc + 1) * 128], in_=pb_[:, mc * 128:(mc + 1) * 128], identity=self.idb[:])
                            return ins
                        c.op("pe", [pbk, "idb"], [("pT", i2)], tr)
                        c.op("act", [("pT", i2)], [("x_pTs", h2, t)], lambda e, pT4=pT4, ptp=ptp, tsl=tsl: e.copy(out=pT4[:, :, tsl], in_=ptp[:, 0:Mm].rearrange("p (k t) -> p k t", k=MC)))
                    pk4 = [("x_pTs", h2, t) for t in range(4)]
                    for d2 in range(2):
                        po = pa[d2]

                        def mm2(e, hd=hd, d2=d2, pT4=pT4, po=po):
                            for mc in range(MC):
                                ins = e.matmul(po[:], lhsT=V[:, mc, hd * 256 + d2 * 128: hd * 256 + (d2 + 1) * 128], rhs=pT4[:, mc, :], start=(mc == 0), stop=(mc == MC - 1))
                            return ins
                        c.op("pe", pk4 + ["x_V"], [("x_pa", d2)], mm2)
                        c.op("act", [("x_pa", d2)], [("x_oT", sl, hd * 2 + d2)], lambda e, hd=hd, d2=d2, po=po, sl=sl: e.copy(out=oT[sl][:, hd * 2 + d2, :], in_=po[:]))
                okeys = [("x_oT", sl, q_) for q_ in range(8)]
                for t in range(4):
                    tt = blk * 4 + t
                    self.outproj_tile(c, OP, lambda kc, sl=sl, t=t: oT[sl][:, kc, t * 128:(t + 1) * 128], okeys, 8, wo, "x_wo", wpost, "x_wpost",
                                      src[tt * 128:(tt + 1) * 128, :], dst[tt * 128:(tt + 1) * 128, :], tt % 2,
                                      rkey=(src.tensor.name, tt), okey=(dst.tensor.name, tt))
            c.barrier()

    def load_cols(self, c, dst, key, vec, n):
        for k in range(n):
            c.op("sp", [], [key], lambda e, k=k: e.dma_start(out=dst[:, k:k + 1], in_=vec[k * 128:(k + 1) * 128].rearrange("(p o) -> p o", o=1)))

    def stage_a_full(self, c, es, hnT, hkey, wrow, src, tag):
        nc = self.nc
        with ExitStack() as es2:
            PN = self.alloc_prenorm(es2, tag)
            wpre = _sb(es2, nc, f"{tag}_wpre", [128, D], F32)
            self.load_bc(c, wpre[:], f"{tag}_wpre", wrow)
            for tt in range(self.NT):
                self.prenorm_tile(c, PN, src[tt * 128:(tt + 1) * 128, :], f"{tag}_wpre", wpre[:], hnT[:, :, tt * 128:(tt + 1) * 128], (hkey, tt), tt % 2, skey=(src.tensor.name, tt))
            c.barrier()

    def stage_c_scr(self, c, L, zscr, KC, wout_ap, wpost_row, src, dst, tag):
        nc = self.nc
        with ExitStack() as es:
            wo = _sb(es, nc, f"{tag}_wo", [128, KC, 1024], BF16)
            wpost = _sb(es, nc, f"{tag}_wpost", [128, D], F32)
            self.load_bc(c, wpost[:], f"{tag}_wpost", wpost_row)
            self.load_w_bf16(c, wo, f"{tag}_wo", wout_ap.rearrange("(k p) n -> p k n", p=128))
            OP = self.alloc_outproj(es, tag)
            zt = [_sb(es, nc, f"{tag}_zt{i}", [128, KC, 512], BF16) for i in range(2)]
            for b in range(self.S // 512):
                z = zt[b % 2]
                c.op("sp", [(zscr.tensor.name, b)], [(f"{tag}_zt", b % 2)], lambda e, z=z, b=b: e.dma_start(out=z[:], in_=zscr[:, :, b * 512:(b + 1) * 512].rearrange("k p t -> p k t")))
                for t in range(4):
                    tt = b * 4 + t
                    self.outproj_tile(c, OP, lambda kc, z=z, t=t: z[:, kc, t * 128:(t + 1) * 128], [(f"{tag}_zt", b % 2)], KC, wo, f"{tag}_wo", wpost, f"{tag}_wpost",
                                      src[tt * 128:(tt + 1) * 128, :], dst[tt * 128:(tt + 1) * 128, :], tt % 2,
                                      rkey=(src.tensor.name, tt), okey=(dst.tensor.name, tt))
            c.barrier()

    def mixer_c(self, c, L, src, dst):
        nc, S, NT = self.nc, self.S, self.NT
        Dr = self.P.dram
        j = L // 2
        NB = S // 512
        yscr = self.P.dram.get("c_yscr")
        if yscr is None:
            yscr = self.P.dscr("c_yscr", [8, 128, S], BF16)
        with ExitStack() as es:
            hnT = _sb(es, nc, "c_hnT", [128, 8, S], BF16)
            self.stage_a_full(c, es, hnT, "c_hnT", Dr["mix_norm_pre"][L:L + 1, :], src, "c")
            hkeys = [("c_hnT", tt) for tt in range(NT)]
            cw = _sb(es, nc, "c_cw", [128, 4, 8], F32)
            for tap in range(4):
                self.load_cols(c, cw[:, tap, :], "c_cw", Dr["c_conv_w"][j, tap], 8)
            cb = _sb(es, nc, "c_cb", [128, 8], F32)
            self.load_cols(c, cb, "c_cb", Dr["c_conv_b"][j], 8)
            ba = _sb(es, nc, "c_ba", [128, 2, 8], F32)
            bx = _sb(es, nc, "c_bx", [128, 2, 8], F32)
            c1 = _sb(es, nc, "c_c1", [128, 2, 8], F32)
            for d_ in range(2):
                self.load_cols(c, ba[:, d_, :], "c_ba", Dr["c_b_a"][j, d_], 8)
                self.load_cols(c, bx[:, d_, :], "c_bx", Dr["c_b_x"][j, d_], 8)
                self.load_cols(c, c1[:, d_, :], "c_c1", Dr["c_lambda"][j, d_], 8)
            c.op("act", ["c_c1"], ["c_c1"], lambda e: e.activation(out=c1[:], in_=c1[:], func=AF.Exp, scale=-1.0))
            c.op("act", ["c_c1"], ["c_c1"], lambda e: e.activation(out=c1[:], in_=c1[:], func=AF.Ln, bias=1.0))
            c.op("dve", ["c_c1"], ["c_c1"], lambda e: e.tensor_scalar(out=c1[:], in0=c1[:], scalar1=-8.0, scalar2=None, op0=ALU.mult))
            xbf = _sb(es, nc, "c_xbf", [128, S + 3], F32)
            u = [_sb(es, nc, f"c_u{i}", [128, S], F32) for i in range(2)]
            ub = [_sb(es, nc, f"c_ub{i}", [128, S], BF16) for i in range(2)]
            af = _sb(es, nc, "c_af", [128, S], F32)
            inp = _sb(es, nc, "c_inp", [128, S], F32)
            acc = _sb(es, nc, "c_acc", [128, S], F32)
            win = [_sb(es, nc, f"c_win{i}", [128, 8, 128], BF16) for i in range(2)]
            wa = [_sb(es, nc, f"c_wa{i}", [128, 2, 128], BF16) for i in range(2)]
            wx = [_sb(es, nc, f"c_wx{i}", [128, 2, 128], BF16) for i in range(2)]
            tmp = {n: [_sb(es, nc, f"c_{n}{i}", [128, 512], F32) for i in range(2)] for n in ("r", "i", "mu")}
            yst = [_sb(es, nc, f"c_yst{i}", [128, 512], BF16) for i in range(2)]
            pp = [_ps(es, nc, f"c_pp{i}", [128, 512], F32) for i in range(4)]
            win_n = 0
            wg_n = 0
            pn = 0

            def inproj(col0, dst_fn, dkeys_fn):
                nonlocal win_n, pn
                w = win[win_n % 2]
                wk = ("c_win", win_n % 2)
                win_n += 1
                src3 = Dr["c_w_in"][j].rearrange("(k p) n -> p k n", p=128)[:, :, col0:col0 + 128]
                self.load_w_bf16(c, w, wk, src3)
                for tb in range(NB):
                    p = pp[pn % 2]
                    pk = ("c_pp", pn % 2)
                    pn += 1

                    def mm(e, p=p, w=w, tb=tb):
                        for kc in range(8):
                            ins = e.matmul(p[:], lhsT=w[:, kc, :], rhs=hnT[:, kc, tb * 512:(tb + 1) * 512], start=(kc == 0), stop=(kc == 7))
                        return ins
                    c.op("pe", hkeys[tb * 4:(tb + 1) * 4] + [wk], [pk], mm)
                    dst_fn(tb, p, pk)

            for blk in range(4):
                for c2 in range(2):
                    ch = blk * 2 + c2
                    c.op("pool", [], ["c_xbf_h"], lambda e: e.memset(xbf[:, 0:2], 0.0))
                    c.op("pool", [], ["c_xbf_h"], lambda e: e.memset(xbf[:, S + 2:S + 3], 0.0))

                    def ev(tb, p, pk):
                        c.op("act", [pk], [("c_xbf", tb)], lambda e: e.copy(out=xbf[:, 2 + tb * 512:2 + (tb + 1) * 512], in_=p[:]))
                    c.op("pool", ["c_hs"], ["c_hs"] + [("c_xbf", tb) for tb in range(NB)], lambda e: e.memset(xbf[:, 0:1], 0.0))
                    inproj(1024 + ch * 128, ev, None)
                    xk = [("c_xbf", tb) for tb in range(NB)] + ["c_xbf_h"]
                    uu = u[c2]
                    uk = ("c_u", c2)
                    c.op("dve", xk + ["c_cw", "c_cb"], [uk], lambda e, uu=uu, ch=ch: e.tensor_scalar(out=uu[:], in0=xbf[:, 0:S], scalar1=cw[:, 0, ch:ch + 1], scalar2=cb[:, ch:ch + 1], op0=ALU.mult, op1=ALU.add))
                    for tap in range(1, 4):
                        c.op("dve", xk + [uk], [uk], lambda e, uu=uu, ch=ch, tap=tap: e.scalar_tensor_tensor(out=uu[:], in0=xbf[:, tap:tap + S], scalar=cw[:, tap, ch:ch + 1], in1=uu[:], op0=ALU.mult, op1=ALU.add))
                    c.op("pool", [uk], [("c_ub", c2)], lambda e, uu=uu, c2=c2: e.tensor_copy(out=ub[c2][:], in_=uu[:]))
                    c.op("pool", xk, ["c_hs"], lambda e: e.memset(xbf[:, 0:1], 0.0))
                ubk = [("c_ub", 0), ("c_ub", 1)]
                for jc in range(2):
                    ch = blk * 2 + jc
                    for dr in range(2):
                        wa_, wx_ = wa[wg_n % 2], wx[wg_n % 2]
                        wak, wxk = ("c_wa", wg_n % 2), ("c_wx", wg_n % 2)
                        wg_n += 1
                        self.load_w_bf16(c, wa_, wak, Dr["c_w_a"][j, dr, blk].rearrange("(k p) n -> p k n", p=128)[:, :, jc * 128:(jc + 1) * 128])
                        self.load_w_bf16(c, wx_, wxk, Dr["c_w_x"][j, dr, blk].rearrange("(k p) n -> p k n", p=128)[:, :, jc * 128:(jc + 1) * 128])
                        for tb in range(NB):
                            sl = tb % 2
                            ba_, bx_ = 2 * (tb % 2), 2 * (tb % 2) + 1
                            pa_, px_ = pp[ba_], pp[bx_]
                            ts = slice(tb * 512, (tb + 1) * 512)

                            def mm(e, w_=wa_, p_=pa_, ts=ts):
                                for kc in range(2):
                                    ins = e.matmul(p_[:], lhsT=w_[:, kc, :], rhs=ub[kc][:, ts], start=(kc == 0), stop=(kc == 1))
                                return ins
                            c.op("pe", ubk + [wak], [("c_pp", ba_)], mm)

                            def mm2(e, w_=wx_, p_=px_, ts=ts):
                                for kc in range(2):
                                    ins = e.matmul(p_[:], lhsT=w_[:, kc, :], rhs=ub[kc][:, ts], start=(kc == 0), stop=(kc == 1))
                                return ins
                            c.op("pe", ubk + [wxk], [("c_pp", bx_)], mm2)
                            r_, i_, mu_ = tmp["r"][sl], tmp["i"][sl], tmp["mu"][sl]
                            c.op("act", [("c_pp", ba_), "c_ba"], [("c_r", sl)], lambda e, r_=r_, pa_=pa_, dr=dr, ch=ch: e.activation(out=r_[:], in_=pa_[:], func=AF.Sigmoid, bias=ba[:, dr, ch:ch + 1]))
                            c.op("act", [("c_pp", bx_), "c_bx"], [("c_i", sl)], lambda e, i_=i_, px_=px_, dr=dr, ch=ch: e.activation(out=i_[:], in_=px_[:], func=AF.Sigmoid, bias=bx[:, dr, ch:ch + 1]))
                            c.op("act", [("c_r", sl), "c_c1"], [("c_af", tb)], lambda e, r_=r_, ts=ts, dr=dr, ch=ch: e.activation(out=af[:, ts], in_=r_[:], func=AF.Exp, scale=c1[:, dr, ch:ch + 1]))
                            c.op("dve", [("c_af", tb)], [("c_mu", sl)], lambda e, mu_=mu_, ts=ts: e.tensor_tensor(out=mu_[:], in0=af[:, ts], in1=af[:, ts], op=ALU.mult))
                            c.op("act", [("c_mu", sl)], [("c_mu", sl)], lambda e, mu_=mu_: e.activation(out=mu_[:], in_=mu_[:], func=AF.Sqrt, scale=-1.0, bias=1.0))
                            c.op("dve", [("c_mu", sl), ("c_i", sl)], [("c_mu", sl)], lambda e, mu_=mu_, i_=i_: e.tensor_tensor(out=mu_[:], in0=mu_[:], in1=i_[:], op=ALU.mult))
                            c.op("dve", [("c_mu", sl), ("c_u", jc)], [("c_inp", tb)], lambda e, mu_=mu_, ts=ts, jc=jc: e.tensor_tensor(out=inp[:, ts], in0=mu_[:], in1=u[jc][:, ts], op=ALU.mult))
                        afk = [("c_af", tb) for tb in range(NB)]
                        ink = [("c_inp", tb) for tb in range(NB)]
                        if dr == 0:
                            c.op("dve", afk + ink, ["c_acc"], lambda e: e.tensor_tensor_scan(out=acc[:], data0=af[:], data1=inp[:], initial=0.0, op0=ALU.mult, op1=ALU.add))
                        else:
                            c.op("dve", afk + ink, ["c_hs"], lambda e: e.tensor_tensor_scan(out=xbf[:, 0:S][:, ::-1], data0=af[:, ::-1], data1=inp[:, ::-1], initial=0.0, op0=ALU.mult, op1=ALU.add))
                            c.op("pool", ["c_hs", "c_acc"], ["c_acc"], lambda e: e.tensor_tensor(out=acc[:], in0=acc[:], in1=xbf[:, 0:S], op=ALU.add))

                    def evg(tb, p, pk, ch=ch):
                        sl = tb % 2
                        g1, g2, ys = tmp["r"][sl], tmp["i"][sl], yst[sl]
                        ts = slice(tb * 512, (tb + 1) * 512)
                        c.op("act", [pk], [("c_r", sl)], lambda e: e.activation(out=g1[:], in_=p[:], func=AF.Square))
                        c.op("dve", [("c_r", sl)], [("c_r", sl)], lambda e: e.tensor_scalar(out=g1[:], in0=g1[:], scalar1=0.044715, scalar2=1.0, op0=ALU.mult, op1=ALU.add))
                        c.op("dve", [("c_r", sl), pk], [("c_r", sl)], lambda e: e.tensor_tensor(out=g1[:], in0=g1[:], in1=p[:], op=ALU.mult))
                        c.op("act", [("c_r", sl)], [("c_i", sl)], lambda e: e.activation(out=g2[:], in_=g1[:], func=AF.Sigmoid, scale=1.5957691216057308))
                        c.op("dve", [("c_i", sl), pk], [("c_i", sl)], lambda e: e.tensor_tensor(out=g2[:], in0=g2[:], in1=p[:], op=ALU.mult))
                        c.op("dve", [("c_i", sl), "c_acc"], [("c_yst", sl)], lambda e: e.tensor_tensor(out=ys[:], in0=g2[:], in1=acc[:, ts], op=ALU.mult))
                        c.op("gq", [("c_yst", sl)], [("c_yscr", tb)], lambda e: e.dma_start(out=yscr[ch, :, ts], in_=ys[:]))
                    inproj(ch * 128, evg, None)
            c.barrier()
        self.stage_c_scr(c, L, yscr, 8, Dr["c_w_out"][j], Dr["mix_norm_post"][L:L + 1, :], src, dst, "cc")

    def mixer_ab(self, c, L, src, dst):
        nc, S, NT = self.nc, self.S, self.NT
        Dr = self.P.dram
        j = L // 2
        NB = S // 512
        P = self.P
        FM = P.dram.get("ab_fm") or P.dscr("ab_fm", [16, 128, S], F32)
        TM = {n: (P.dram.get("ab_" + n) or P.dscr("ab_" + n, [S, 512], F32)) for n in ("dnk", "dnv", "dnz", "mlk", "mlv", "mlo")}
        mixscr = P.dram.get("ab_mix") or P.dscr("ab_mix", [8, 128, S], BF16)
        I, ONES = self.I, self.ONES
        dirs = [dict(INC=self.LE, AFT=self.GT, STRICT=self.GT, INCLji=self.LE, NEGM=self.NLT),
                dict(INC=self.GE, AFT=self.LT, STRICT=self.LT, INCLji=self.GE, NEGM=self.NGT)]
        with ExitStack() as eo:
            gates = _sb(eo, nc, "ab_gates", [128, NT, 32], F32)
            Gg = _sb(eo, nc, "ab_Gg", [128, NT, 8], F32)
            Bt = _sb(eo, nc, "ab_Bt", [128, NT, 8], F32)
            nBt = _sb(eo, nc, "ab_nBt", [128, NT, 8], F32)
            Li = _sb(eo, nc, "ab_Li", [128, NT, 8], F32)
            Lf = _sb(eo, nc, "ab_Lf", [128, NT, 8], F32)
            Edn = _sb(eo, nc, "ab_Edn", [128, NT, 24], F32)
            Mlt = _sb(eo, nc, "ab_Mlt", [128, NT, 24], F32)
            Bk = _sb(eo, nc, "ab_Bk", [128, NT, 8], F32)
            Ws = _sb(eo, nc, "ab_Ws", [128, NT, 8], F32)
            prm = _sb(eo, nc, "ab_prm", [128, 4, 8], F32)
            dnw = _sb(eo, nc, "ab_dnw", [128, 128], F32)
            for i_, nm in enumerate(("dn_a_log", "dn_dt_bias", "ml_i_bias", "ml_f_bias")):
                self.load_bc(c, prm[:, i_, :], "ab_prm", Dr[nm][j:j + 1].rearrange("o d h -> o (d h)"))
            self.load_bc(c, dnw[:], "ab_dnw", Dr["dn_out_norm"][j:j + 1, :])
            with ExitStack() as es:
                hnT = _sb(es, nc, "ab_hnT", [128, 8, S], BF16)
                self.stage_a_full(c, es, hnT, "ab_hnT", Dr["mix_norm_pre"][L:L + 1, :], src, "ab")
                hkeys = [("ab_hnT", tt) for tt in range(NT)]
                win3 = Dr["ab_w_in"][j].rearrange("(k p) n -> p k n", p=128)
                wg = _sb(es, nc, "ab_wg", [128, 8, 32], BF16)
                c.op("gq", [], ["ab_wg"], lambda e: e.dma_start(out=wg[:, :, 0:16], in_=win3[:, :, 2048:2064]))
                c.op("gq", [], ["ab_wg"], lambda e: e.dma_start(out=wg[:, :, 16:32], in_=win3[:, :, 4112:4128]))
                pp = [_ps(es, nc, f"ab_pp{i}", [128, 512], F32) for i in range(4)]
                for tt in range(NT):
                    p = pp[tt % 2]

                    def mm(e, p=p, tt=tt):
                        for kc in range(8):
                            ins = e.matmul(p[:, 0:32], lhsT=hnT[:, kc, tt * 128:(tt + 1) * 128], rhs=wg[:, kc, :], start=(kc == 0), stop=(kc == 7))
                        return ins
                    c.op("pe", [hkeys[tt], "ab_wg"], [("ab_pp", tt % 2)], mm)
                    c.op("act", [("ab_pp", tt % 2)], ["ab_gates"], lambda e, p=p, tt=tt: e.copy(out=gates[:, tt, :], in_=p[:, 0:32]))
                def bc(i_):
                    return prm[:, i_, :].unsqueeze(1).broadcast_to([128, NT, 8])
                c.op("dve", ["ab_gates", "ab_prm"], ["ab_Gg"], lambda e: e.tensor_tensor(out=Gg[:], in0=gates[:, :, 0:8], in1=bc(1), op=ALU.add))
                c.op("act", ["ab_Gg"], ["ab_Gg"], lambda e: e.activation(out=Gg[:], in_=Gg[:], func=AF.Exp))
                c.op("act", ["ab_Gg"], ["ab_Gg"], lambda e: e.activation(out=Gg[:], in_=Gg[:], func=AF.Ln, bias=1.0))
                c.op("act", ["ab_prm"], ["ab_prm0"], lambda e: e.activation(out=prm[:, 0, :], in_=prm[:, 0, :], func=AF.Exp))
                c.op("dve", ["ab_Gg", "ab_prm0"], ["ab_Gg"], lambda e: e.scalar_tensor_tensor(out=Gg[:], in0=Gg[:], scalar=-1.0, in1=bc(0), op0=ALU.mult, op1=ALU.mult))
                c.op("act", ["ab_gates"], ["ab_Bt"], lambda e: e.activation(out=Bt[:], in_=gates[:, :, 8:16], func=AF.Sigmoid))
                c.op("dve", ["ab_Bt"], ["ab_nBt"], lambda e: e.tensor_scalar(out=nBt[:], in0=Bt[:], scalar1=-1.0, scalar2=None, op0=ALU.mult))
                c.op("dve", ["ab_gates", "ab_prm"], ["ab_Li"], lambda e: e.tensor_tensor(out=Li[:], in0=gates[:, :, 16:24], in1=bc(2), op=ALU.add))
                c.op("dve", ["ab_gates", "ab_prm"], ["ab_Lf"], lambda e: e.tensor_tensor(out=Lf[:], in0=gates[:, :, 24:32], in1=bc(3), op=ALU.add))
                c.op("act", ["ab_Lf"], ["ab_Lf"], lambda e: e.activation(out=Lf[:], in_=Lf[:], func=AF.Exp, scale=-1.0))
                c.op("act", ["ab_Lf"], ["ab_Lf"], lambda e: e.activation(out=Lf[:], in_=Lf[:], func=AF.Ln, bias=1.0))
                c.op("dve", ["ab_Lf"], ["ab_Lf"], lambda e: e.tensor_scalar(out=Lf[:], in0=Lf[:], scalar1=-1.0, scalar2=None, op0=ALU.mult))
                LE, GE, LT, GT = self.LE, self.GE, self.LT, self.GT
                for tt in range(NT):
                    p = pp[2 + tt % 2]

                    def mm(e, p=p, tt=tt):
                        for o_, T_ in ((0, Gg), (24, Lf)):
                            e.matmul(p[:, o_ + 0:o_ + 4], lhsT=LE, rhs=T_[:, tt, 0:4], start=True, stop=True)
                            e.matmul(p[:, o_ + 4:o_ + 8], lhsT=GE, rhs=T_[:, tt, 4:8], start=True, stop=True)
                            e.matmul(p[:, o_ + 8:o_ + 12], lhsT=GT, rhs=T_[:, tt, 0:4], start=True, stop=True)
                            e.matmul(p[:, o_ + 12:o_ + 16], lhsT=LT, rhs=T_[:, tt, 4:8], start=True, stop=True)
                            ins = e.matmul(p[:, o_ + 16:o_ + 24], lhsT=ONES, rhs=T_[:, tt, 0:8], start=True, stop=True)
                        return ins
                    c.op("pe", ["ab_Gg", "ab_Lf", "cst"], [("ab_pp", 2 + tt % 2)], mm)
                    c.op("act", [("ab_pp", 2 + tt % 2)], ["ab_Edn"], lambda e, p=p, tt=tt: e.activation(out=Edn[:, tt, :], in_=p[:, 0:24], func=AF.Exp))
                    c.op("act", [("ab_pp", 2 + tt % 2)], ["ab_Mlt"], lambda e, p=p, tt=tt: e.copy(out=Mlt[:, tt, :], in_=p[:, 24:48]))
                c.op("dve", ["ab_Bt", "ab_Edn"], ["ab_Bk"], lambda e: e.tensor_tensor(out=Bk[:], in0=Bt[:], in1=Edn[:, :, 0:8], op=ALU.mult))
                c.op("dve", ["ab_Li", "ab_Mlt"], ["ab_Ws"], lambda e: e.tensor_tensor(out=Ws[:], in0=Li[:], in1=Mlt[:, :, 8:16], op=ALU.add))
                xbf = _sb(es, nc, "ab_xbf", [128, S + 3], F32)
                uu = _sb(es, nc, "ab_u", [128, S], F32)
                cwt = [_sb(es, nc, f"ab_cwt{i}", [128, 4], F32) for i in range(2)]
                win = [_sb(es, nc, f"ab_win{i}", [128, 8, 128], BF16) for i in range(2)]
                sqb = [_sb(es, nc, f"ab_sqb{i}", [128, 512], F32) for i in range(2)]
                rsb = [_sb(es, nc, f"ab_rsb{i}", [128, 512], F32) for i in range(2)]
                stg = [_sb(es, nc, f"ab_stg{i}", [128, 4, 128], F32) for i in range(2)]
                c.op("pool", [], ["ab_xbf_h"], lambda e: e.memset(xbf[:, 0:2], 0.0))
                c.op("pool", [], ["ab_xbf_h"], lambda e: e.memset(xbf[:, S + 2:S + 3], 0.0))
                specs = []
                for h in range(4):
                    specs.append(dict(col=h * 128, conv=("dn_conv_w", h * 128), act=AF.Silu, l2=True, scale=128.0 ** -0.5, fm=h, tm=None))
                for h in range(4):
                    specs.append(dict(col=512 + h * 128, conv=("dn_conv_w", 512 + h * 128), act=AF.Silu, l2=True, scale=1.0, fm=4 + h, tm=("dnk", h)))
                for h in range(4):
                    specs.append(dict(col=1024 + h * 128, conv=("dn_conv_w", 1024 + h * 128), act=AF.Silu, l2=False, scale=None, fm=None, tm=("dnv", h)))
                for h in range(4):
                    specs.append(dict(col=1536 + h * 128, conv=None, act=AF.Silu, l2=False, scale=None, fm=None, tm=("dnz", h)))
                for h in range(4):
                    specs.append(dict(col=2064 + h * 128, conv=("ml_conv_w", h * 128), act=AF.Silu, l2=False, scale=None, fm=8 + h, tm=None))
                for h in range(4):
                    specs.append(dict(col=2576 + h * 128, conv=("ml_conv_w", 512 + h * 128), act=AF.Silu, l2=False, scale=128.0 ** -0.5, fm=12 + h, tm=("mlk", h)))
                for h in range(4):
                    specs.append(dict(col=3088 + h * 128, conv=None, act=None, l2=False, scale=None, fm=None, tm=("mlv", h)))
                for h in range(4):
                    specs.append(dict(col=3600 + h * 128, conv=None, act=AF.Sigmoid, l2=False, scale=None, fm=None, tm=("mlo", h)))
                xbf2 = _sb(es, nc, "ab_xbf2", [128, S + 3], F32)
                xbfs = [xbf, xbf2]
                c.op("pool", [], [("ab_xbf_h", 0)], lambda e: e.memset(xbf[:, 0:2], 0.0))
                c.op("pool", [], [("ab_xbf_h", 0)], lambda e: e.memset(xbf[:, S + 2:S + 3], 0.0))
                c.op("pool", [], [("ab_xbf_h", 1)], lambda e: e.memset(xbf2[:, 0:2], 0.0))
                c.op("pool", [], [("ab_xbf_h", 1)], lambda e: e.memset(xbf2[:, S + 2:S + 3], 0.0))
                pnc = [0]

                def stA(si, sp):
                        pn = pnc[0]
                        w = win[si % 2]
                        wk = ("ab_win", si % 2)
                        self.load_w_bf16(c, w, wk, win3[:, :, sp["col"]:sp["col"] + 128])
                        xk = [("ab_xbf", si % 2, tb) for tb in range(NB)]
                        for tb in range(NB):
                            p = pp[pn % 2]
                            pk = ("ab_pp", pn % 2)
                            pn += 1

                            def mm(e, p=p, w=w, tb=tb):
                                for kc in range(8):
                                    ins = e.matmul(p[:], lhsT=w[:, kc, :], rhs=hnT[:, kc, tb * 512:(tb + 1) * 512], start=(kc == 0), stop=(kc == 7))
                                return ins
                            c.op("pe", hkeys[tb * 4:(tb + 1) * 4] + [wk], [pk], mm)
                            c.op("act", [pk], [("ab_xbf", si % 2, tb)], lambda e, p=p, tb=tb: e.copy(out=xbfs[si % 2][:, 2 + tb * 512:2 + (tb + 1) * 512], in_=p[:]))
                        pnc[0] = pn

                def stB(si, sp):
                        xk = [("ab_xbf", si % 2, tb) for tb in range(NB)]
                        if sp["conv"] is not None:
                            cw_ = cwt[si % 2]
                            cwk = ("ab_cwt", si % 2)
                            nm, c0 = sp["conv"]
                            for tap in range(4):
                                c.op("sp", [], [cwk], lambda e, cw_=cw_, tap=tap, nm=nm, c0=c0: e.dma_start(out=cw_[:, tap:tap + 1], in_=Dr[nm][j, tap, c0:c0 + 128].rearrange("(p o) -> p o", o=1)))
                            c.op("dve", xk + [("ab_xbf_h", si % 2), cwk], ["ab_u"], lambda e, cw_=cw_: e.tensor_scalar(out=uu[:], in0=xbfs[si % 2][:, 0:S], scalar1=cw_[:, 0:1], scalar2=None, op0=ALU.mult))
                            for tap in range(1, 4):
                                c.op("dve", xk + [("ab_xbf_h", si % 2), cwk, "ab_u"], ["ab_u"], lambda e, cw_=cw_, tap=tap: e.scalar_tensor_tensor(out=uu[:], in0=xbfs[si % 2][:, tap:tap + S], scalar=cw_[:, tap:tap + 1], in1=uu[:], op0=ALU.mult, op1=ALU.add))
                            if sp["act"] is not None:
                                c.op("act", ["ab_u"], ["ab_u"], lambda e, f=sp["act"]: e.activation(out=uu[:], in_=uu[:], func=f))
                        else:
                            if sp["act"] is not None:
                                c.op("act", xk, ["ab_u"], lambda e, f=sp["act"]: e.activation(out=uu[:], in_=xbfs[si % 2][:, 2:S + 2], func=f))
                            else:
                                c.op("pool", xk, ["ab_u"], lambda e: e.tensor_copy(out=uu[:], in_=xbfs[si % 2][:, 2:S + 2]))
                        if sp["l2"]:
                            for tb in range(NB):
                                sl = tb % 2
                                ts = slice(tb * 512, (tb + 1) * 512)
                                c.op("act", ["ab_u"], [("ab_sqb", sl)], lambda e, sl=sl, ts=ts: e.activation(out=sqb[sl][:], in_=uu[:, ts], func=AF.Square))
                                p = pp[2 + sl]
                                c.op("pe", [("ab_sqb", sl), "cst"], [("ab_pp", 2 + sl)], lambda e, p=p, sl=sl: e.matmul(p[:], lhsT=ONES, rhs=sqb[sl][:], start=True, stop=True))
                                c.op("act", [("ab_pp", 2 + sl)], [("ab_rsb", sl)], lambda e, p=p, sl=sl: e.activation(out=rsb[sl][:], in_=p[:], func=AF.Ln, bias=1e-6))
                                c.op("act", [("ab_rsb", sl)], [("ab_rsb", sl)], lambda e, sl=sl: e.activation(out=rsb[sl][:], in_=rsb[sl][:], func=AF.Exp, scale=-0.5))
                                c.op("dve", [("ab_rsb", sl), "ab_u"], ["ab_u"], lambda e, sl=sl, ts=ts, sc_=sp["scale"]: e.scalar_tensor_tensor(out=uu[:, ts], in0=uu[:, ts], scalar=sc_, in1=rsb[sl][:], op0=ALU.mult, op1=ALU.mult))
                        elif sp["scale"] is not None:
                            c.op("dve", ["ab_u"], ["ab_u"], lambda e, sc_=sp["scale"]: e.tensor_scalar(out=uu[:], in0=uu[:], scalar1=sc_, scalar2=None, op0=ALU.mult))
                        if sp["fm"] is not None:
                            c.op("sp", ["ab_u"], [("ab_fm", sp["fm"])], lambda e, f=sp["fm"]: e.dma_start(out=FM[f], in_=uu[:]))
                        if sp["tm"] is not None:
                            nm, h = sp["tm"]
                            for tb in range(NB):
                                sl = tb % 2
                                p = pp[2 + sl]

                                def tr(e, p=p, tb=tb):
                                    for t4 in range(4):
                                        ins = e.transpose(out=p[:, t4 * 128:(t4 + 1) * 128], in_=uu[:, tb * 512 + t4 * 128: tb * 512 + (t4 + 1) * 128], identity=I)
                                    return ins
                                c.op("pe", ["ab_u", "cst"], [("ab_pp", 2 + sl)], tr)
                                c.op("act", [("ab_pp", 2 + sl)], [("ab_stg", sl)], lambda e, p=p, sl=sl: e.copy(out=stg[sl][:], in_=p[:].rearrange("p (t d) -> p t d", t=4)))
                                c.op("gq", [("ab_stg", sl)], [("ab_tm", nm, h)], lambda e, sl=sl, tb=tb, nm=nm, h=h: e.dma_start(out=TM[nm][tb * 512:(tb + 1) * 512, h * 128:(h + 1) * 128].rearrange("(t p) d -> p t d", p=128), in_=stg[sl][:]))
                stA(0, specs[0])
                for si, sp in enumerate(specs):
                    if si + 1 < len(specs):
                        stA(si + 1, specs[si + 1])
                    stB(si, sp)
                c.barrier()
            import os as _os
            _algs = tuple(a for a in _os.environ.get("AB_ALGS", "dn,ml").split(",") if a)
            WIN = int(_os.environ.get("AB_WIN", "4"))
            with ExitStack() as es:
                qT = _sb(es, nc, "r_qT", [128, S], F32)
                ktok = _sb(es, nc, "r_ktok", [128, NT, 128], F32)
                vtok = _sb(es, nc, "r_vtok", [128, NT, 129], F32)
                ost = _sb(es, nc, "r_ost", [128, NT, 128], F32)
                gtok = qT[:].rearrange("p (t d) -> p t d", d=128)
                dn_names = ["Gmat", "Gle", "eD", "eDT", "egb", "t1", "t2", "attnT", "qd", "Xv", "Xk", "kd", "u", "wT", "vn"] + \
                           [f"P{k}" for k in range(2)] + [f"PT{k}" for k in range(2)] + [f"R{k}" for k in range(2)]
                F32R = mybir.dt.float32r
                cstr = _sb(es, nc, "r_cstr", [128, 8, 128], F32)
                c.op("dve", ["cst"], ["r_cstr"], lambda e: e.tensor_copy(out=cstr[:].bitcast(F32R), in_=self.cst[:]))
                mr = {"LE": cstr[:, 1, :].bitcast(F32R), "GE": cstr[:, 2, :].bitcast(F32R), "LT": cstr[:, 3, :].bitcast(F32R), "GT": cstr[:, 4, :].bitcast(F32R)}
                ONESr = cstr[:, 5, :].bitcast(F32R)
                dirs_r = [dict(INC=mr["LE"], AFT=mr["GT"]), dict(INC=mr["GE"], AFT=mr["LT"])]
                rset = set([f"P{k}" for k in range(7)] + [f"PT{k}" for k in range(6)] + ["R0", "R1", "Xv", "Xk"])
                bfn = set(["wT", "qd", "attnT", "kd", "vn"])
                wt = {n: [_sb(es, nc, f"r_{n}{i}", [128, 128], BF16 if n in bfn else F32) for i in range(WIN)] for n in dn_names}
                qTb = _sb(es, nc, "r_qTb", [128, S], BF16)
                kTb = _sb(es, nc, "r_kTb", [128, S], BF16)
                vtokb = _sb(es, nc, "r_vtokb", [128, NT, 129], BF16)
                Ssh = [[_sb(es, nc, f"r_Ssh{d_}{i}", [128, 129], BF16) for i in range(2)] for d_ in range(2)]
                pTm = [_sb(es, nc, f"r_pTm{i}", [128, 128], BF16) for i in range(WIN)]
                pm = [_sb(es, nc, f"r_pm{i}", [128, 128], BF16) for i in range(WIN)]
                ksm = [_sb(es, nc, f"r_ksm{i}", [128, 128], BF16) for i in range(WIN)]
                alias = {"X": "Gmat", "e": "eD", "p": "eDT", "pT": "egb", "ks": "t1"}
                dmall = _sb(es, nc, "r_dmall", [128, 2, NT, 128], F32)
                mlc = _sb(es, nc, "r_mlc", [128, 12, 2, NT], F32)
                zc = _sb(es, nc, "r_zc", [128, 1], F32)
                c.op("pool", [], ["r_zc"], lambda e: e.memset(zc[:], 0.0))
                nd = [_sb(es, nc, f"r_nd{i}", [128, 129], F32) for i in range(WIN)]
                dcol = [_sb(es, nc, f"r_dcol{i}", [128, 4], F32) for i in range(WIN)]
                Sst = [[_sb(es, nc, f"r_S{d_}{i}", [128, 129], F32) for i in range(2)] for d_ in range(2)]
                ob = [_sb(es, nc, f"r_ob{i}", [128, 128], BF16) for i in range(2)]
                ot = [_sb(es, nc, f"r_ot{i}", [128, 128], F32) for i in range(2)]
                oss = [_sb(es, nc, f"r_oss{i}", [128, 2], F32) for i in range(2)]
                mst4 = [_sb(es, nc, f"r_mx{i}", [128, 512], BF16) for i in range(2)]
                pb = [_ps(es, nc, f"r_pb{i}", [128, 512], F32) for i in range(7)]
                pTb = _ps(es, nc, "r_pTb", [128, 1024], BF16)

                def Q(b, q, n=128):
                    return pb[b][:, q * 128:q * 128 + n]

                def K_(b, q):
                    return ("r_pb", b, q)

                STAG = int(_os.environ.get("AB_STAG", "0"))

                def run_units(gens, stag=None):
                    stag = STAG if stag is None else stag
                    active = []
                    it = iter(gens)
                    done = False
                    since = stag
                    while True:
                        if (not done) and len(active) < WIN and (since >= stag or not active):
                            g = next(it, None)
                            if g is None:
                                done = True
                            else:
                                active.append(g)
                                since = 0
                        if not active:
                            if done:
                                break
                            continue
                        since += 1
                        for g in list(active):
                            try:
                                next(g)
                            except StopIteration:
                                active.remove(g)

                free = list(range(WIN))
                turn = [0, 0]

                def dn_unit(h, dr, si, t):
                    M = dirs[dr]
                    col = dr * 4 + h
                    sl_ = free.pop()
                    W = {n: wt[n][sl_] for n in dn_names}
                    for kq in range(7):
                        W[f"P{kq}"] = wt[f"P{kq % 2}"][sl_]
                    for kq in range(6):
                        W[f"PT{kq}"] = wt[f"PT{kq % 2}"][sl_]
                    ts = slice(t * 128, (t + 1) * 128)
                    ab_, vb_, sb_ = 2 * (sl_ % 2), 2 * (sl_ % 2) + 1, 4 + dr

                    def Wr(n):
                        return W[n][:].bitcast(F32R)

                    def k(n):
                        if n[0] == "P" and n[-1].isdigit():
                            n = n[:-1] + str(int(n[-1]) % 2)
                        return ("r_" + n, sl_)
                    Sc, Sn = Sst[dr][si % 2], Sst[dr][(si + 1) % 2]
                    Sck, Snk = ("r_S", dr, si % 2), ("r_S", dr, (si + 1) % 2)
                    gcol = Gg[:, t, col:col + 1]
                    c.op("dve", ["ab_Gg"], [k("Gmat")], lambda e: e.tensor_scalar(out=Wr("Gmat"), in0=M["AFT"], scalar1=gcol, scalar2=None, op0=ALU.mult))
                    c.op("act", ["ab_Gg"], [k("Gle")], lambda e: e.activation(out=Wr("Gle"), in_=M["INC"], func=AF.Copy, scale=gcol))
                    yield
                    c.op("act", ["r_vtok", "ab_Bt"], [k("Xv")], lambda e: e.activation(out=Wr("Xv"), in_=vtok[:, t, 0:128], func=AF.Copy, scale=Bt[:, t, col:col + 1]))
                    c.op("act", ["r_ktok", "ab_Bk"], [k("Xk")], lambda e: e.activation(out=Wr("Xk"), in_=ktok[:, t, :], func=AF.Copy, scale=Bk[:, t, col:col + 1]))
                    c.op("act", ["r_ktok", "ab_Edn"], [k("kd")], lambda e: e.activation(out=W["kd"][:], in_=ktok[:, t, :], func=AF.Copy, scale=Edn[:, t, 8 + col:8 + col + 1]))
                    yield

                    def mmA(e):
                        Mr = dirs_r[dr]
                        e.matmul(Q(ab_, 0), lhsT=Mr["INC"], rhs=Wr("Gmat"), start=True, stop=True)
                        e.matmul(Q(ab_, 2), lhsT=ONESr, rhs=Wr("Gle"), start=True, stop=True)
                        return e.matmul(Q(ab_, 1), lhsT=Wr("Gmat"), rhs=Mr["INC"], start=True, stop=True)
                    c.op("pe", [k("Gmat"), k("Gle"), "r_cstr"], [K_(ab_, 0), K_(ab_, 1), K_(ab_, 2)], mmA)
                    c.op("act", [K_(ab_, 0)], [k("eD")], lambda e: e.activation(out=W["eD"][:], in_=Q(ab_, 0), func=AF.Exp))
                    c.op("act", [K_(ab_, 1)], [k("eDT")], lambda e: e.activation(out=W["eDT"][:], in_=Q(ab_, 1), func=AF.Exp))
                    c.op("act", [K_(ab_, 2)], [k("egb")], lambda e: e.activation(out=W["egb"][:], in_=Q(ab_, 2), func=AF.Exp))
                    yield
                    c.op("pool", [k("eD")], [k("t1")], lambda e: e.tensor_tensor(out=W["t1"][:], in0=W["eD"][:], in1=M["STRICT"], op=ALU.mult))
                    c.op("pool", [k("eDT")], [k("t2")], lambda e: e.tensor_tensor(out=W["t2"][:], in0=W["eDT"][:], in1=M["INCLji"], op=ALU.mult))
                    yield
                    c.op("pool", [k("egb"), "r_qTb"], [k("qd")], lambda e: e.tensor_tensor(out=W["qd"][:], in0=qTb[:, ts], in1=W["egb"][:], op=ALU.mult))
                    yield

                    def mmB(e):
                        e.matmul(Q(vb_, 1), lhsT=kTb[:, ts], rhs=kTb[:, ts], start=True, stop=True)
                        return e.matmul(Q(vb_, 0), lhsT=kTb[:, ts], rhs=qTb[:, ts], start=True, stop=True)
                    c.op("pe", ["r_kTb", "r_qTb"], [K_(vb_, 1), K_(vb_, 0)], mmB)
                    c.op("dve", [K_(vb_, 1), k("t1"), "ab_nBt"], [k("P0")], lambda e: e.scalar_tensor_tensor(out=Wr("P0"), in0=Q(vb_, 1), scalar=nBt[:, t, col:col + 1], in1=W["t1"][:], op0=ALU.mult, op1=ALU.mult))
                    c.op("dve", [K_(vb_, 0), k("t2")], [k("attnT")], lambda e: e.tensor_tensor(out=W["attnT"][:], in0=Q(vb_, 0), in1=W["t2"][:], op=ALU.mult))
                    yield
                    if "P0" in bfn:
                        qv = Q(vb_, 2).bitcast(BF16)[:, 0:128]
                        c.op("pe", [k("P0"), "idb"], [K_(vb_, 2)], lambda e: e.transpose(out=qv, in_=W["P0"][:], identity=self.idb[:]))
                        c.op("dve", [K_(vb_, 2)], [k("PT0")], lambda e: e.tensor_copy(out=W["PT0"][:], in_=qv))
                    else:
                        c.op("pe", [k("P0")], [K_(vb_, 2)], lambda e: e.transpose(out=Q(vb_, 2), in_=W["P0"][:], identity=I))
                        c.op("dve", [K_(vb_, 2)], [k("PT0")], lambda e: e.tensor_copy(out=Wr("PT0"), in_=Q(vb_, 2)))
                    c.op("dve", [k("PT0")], [k("R0")], lambda e: e.tensor_tensor(out=Wr("R0"), in0=W["PT0"][:], in1=I, op=ALU.add))
                    yield
                    rc = "R0"
                    for kk in range(1, 7):
                        q2 = kk % 2
                        c.op("pe", [k(f"P{kk-1}"), k(f"PT{kk-1}")], [K_(ab_, q2)], lambda e, kk=kk, q2=q2: e.matmul(Q(ab_, q2), lhsT=Wr(f"PT{kk-1}"), rhs=Wr(f"P{kk-1}"), start=True, stop=True))
                        c.op("act", [K_(ab_, q2)], [k(f"P{kk}")], lambda e, kk=kk, q2=q2: e.copy(out=Wr(f"P{kk}"), in_=Q(ab_, q2)))
                        yield
                        if kk < 6:
                            c.op("pe", [k(f"P{kk-1}"), k(f"PT{kk-1}")], [K_(vb_, 2 + q2)], lambda e, kk=kk, q2=q2: e.matmul(Q(vb_, 2 + q2), lhsT=Wr(f"P{kk-1}"), rhs=Wr(f"PT{kk-1}"), start=True, stop=True))
                            c.op("dve", [K_(vb_, 2 + q2)], [k(f"PT{kk}")], lambda e, kk=kk, q2=q2: e.tensor_copy(out=Wr(f"PT{kk}"), in_=Q(vb_, 2 + q2)))
                            yield
                        rn = "R1" if rc == "R0" else "R0"
                        c.op("pe", [k(f"P{kk}"), k(rc)], [K_(vb_, q2)], lambda e, kk=kk, rc=rc, q2=q2: e.matmul(Q(vb_, q2), lhsT=Wr(f"P{kk}"), rhs=Wr(rc), start=True, stop=True))
                        c.op("dve", [K_(vb_, q2), k(rc)], [k(rn)], lambda e, rc=rc, rn=rn, q2=q2: e.tensor_tensor(out=Wr(rn), in0=W[rc][:], in1=Q(vb_, q2), op=ALU.add))
                        yield
                        rc = rn
                    c.op("pe", [k(rc), k("Xv")], [K_(ab_, 0)], lambda e: e.matmul(Q(ab_, 0), lhsT=Wr(rc), rhs=Wr("Xv"), start=True, stop=True))
                    c.op("act", [K_(ab_, 0)], [k("u")], lambda e: e.copy(out=W["u"][:], in_=Q(ab_, 0)))
                    yield
                    c.op("pe", [k(rc), k("Xk")], [K_(ab_, 1)], lambda e: e.matmul(Q(ab_, 1), lhsT=Wr("Xk"), rhs=Wr(rc), start=True, stop=True))
                    c.op("act", [K_(ab_, 1)], [k("wT")], lambda e: e.copy(out=W["wT"][:], in_=Q(ab_, 1)))
                    yield
                    while turn[dr] != si:
                        yield
                    Sbc, Sbn = Ssh[dr][si % 2], Ssh[dr][(si + 1) % 2]
                    Sbck, Sbnk = ("r_Ssh", dr, si % 2), ("r_Ssh", dr, (si + 1) % 2)
                    c.op("pe", [k("wT"), Sbck], [K_(sb_, 0)], lambda e: e.matmul(Q(sb_, 0), lhsT=W["wT"][:], rhs=Sbc[:, 0:128], start=True, stop=True))
                    c.op("dve", [K_(sb_, 0), k("u")], [k("vn")], lambda e: e.tensor_tensor(out=W["vn"][:], in0=W["u"][:], in1=Q(sb_, 0), op=ALU.subtract))

                    def mmo(e):
                        e.matmul(Q(sb_, 2), lhsT=W["qd"][:], rhs=Sbc[:, 0:128], start=True, stop=False)
                        e.matmul(Q(sb_, 2), lhsT=W["attnT"][:], rhs=W["vn"][:], start=False, stop=True)
                        return e.matmul(Q(sb_, 1), lhsT=W["kd"][:], rhs=W["vn"][:], start=True, stop=True)
                    c.op("pe", [k("qd"), k("attnT"), k("vn"), k("kd"), Sbck], [K_(sb_, 2), K_(sb_, 1)], mmo)
                    c.op("dve", [K_(sb_, 1), Sck, "ab_Edn"], [Snk], lambda e: e.scalar_tensor_tensor(out=Sn[:, 0:128], in0=Sc[:, 0:128], scalar=Edn[:, t, 16 + col:16 + col + 1], in1=Q(sb_, 1), op0=ALU.mult, op1=ALU.add))
                    c.op("act", [Snk], [Sbnk], lambda e: e.copy(out=Sbn[:, 0:128], in_=Sn[:, 0:128]))
                    c.op("dve", [K_(sb_, 2), ("r_ost", t)], [("r_ost", t)], lambda e: e.tensor_tensor(out=ost[:, t, :], in0=ost[:, t, :], in1=Q(sb_, 2), op=ALU.add))
                    turn[dr] += 1
                    free.append(sl_)

                MI, MS, B1, MNEW, A1, MT, NEGM, INTER, EMT, DEC, SRC, TMP = range(12)

                def ml_prep(h, dr, si, t):
                    M = dirs[dr]
                    col = dr * 4 + h
                    sl_ = free.pop()
                    X = wt[alias["X"]][sl_]
                    xk = ("r_" + alias["X"], sl_)
                    vb_ = 2 * (sl_ % 2) + 1
                    lf, li = Lf[:, t, col:col + 1], Li[:, t, col:col + 1]
                    Xr = X[:].bitcast(F32R)
                    c.op("dve", ["ab_Lf"], [xk], lambda e: e.tensor_scalar(out=Xr, in0=M["AFT"], scalar1=lf, scalar2=None, op0=ALU.mult))
                    c.op("dve", ["ab_Li", xk], [xk], lambda e: e.scalar_tensor_tensor(out=Xr, in0=I, scalar=li, in1=X[:], op0=ALU.mult, op1=ALU.add))
                    yield

                    def mmA(e):
                        Mr = dirs_r[dr]
                        e.matmul(Q(vb_, 0), lhsT=Mr["INC"], rhs=Xr, start=True, stop=True)
                        return e.matmul(Q(vb_, 1), lhsT=ONESr, rhs=Xr, start=True, stop=True)
                    c.op("pe", [xk, "r_cstr"], [K_(vb_, 0), K_(vb_, 1)], mmA)
                    c.op("dve", [K_(vb_, 0)], [("r_dmall", dr, t)], lambda e: e.tensor_tensor(out=dmall[:, dr, t, :], in0=Q(vb_, 0), in1=M["NEGM"], op=ALU.add))
                    c.op("dve", [K_(vb_, 1)], [("r_mlc", MS, dr)], lambda e: e.tensor_reduce(out=mlc[:, MS, dr, t:t + 1], in_=Q(vb_, 1), axis=AX.X, op=ALU.max))
                    yield
                    c.op("dve", [("r_dmall", dr, t)], [("r_mlc", MI, dr)], lambda e: e.tensor_reduce(out=mlc[:, MI, dr, t:t + 1], in_=dmall[:, dr, t, :], axis=AX.X, op=ALU.max))
                    free.append(sl_)

                def ml_main(h, dr, si, t):
                    col = dr * 4 + h
                    sl_ = free.pop()
                    ts = slice(t * 128, (t + 1) * 128)
                    e_ = wt[alias["e"]][sl_]
                    ek = ("r_" + alias["e"], sl_)
                    p_, pT_, ks_ = pm[sl_], pTm[sl_], ksm[sl_]
                    pk_, pTk, ksk = ("r_pm", sl_), ("r_pTm", sl_), ("r_ksm", sl_)
                    Cbc, Cbn = Ssh[dr][si % 2], Ssh[dr][(si + 1) % 2]
                    Cbck, Cbnk = ("r_Ssh", dr, si % 2), ("r_Ssh", dr, (si + 1) % 2)
                    Cc, Cn = Sst[dr][si % 2], Sst[dr][(si + 1) % 2]
                    Cck, Cnk = ("r_S", dr, si % 2), ("r_S", dr, (si + 1) % 2)
                    ab_, vb_, sb_ = 2 * (sl_ % 2), 2 * (sl_ % 2) + 1, 4 + dr

                    def col_(i_):
                        return mlc[:, i_, dr, t:t + 1]
                    c.op("act", [("r_dmall", dr, t), ("r_mlc", NEGM, dr)], [ek], lambda e: e.activation(out=e_[:], in_=dmall[:, dr, t, :], func=AF.Exp, bias=col_(NEGM)))
                    c.op("act", ["r_ktok", ("r_mlc", SRC, dr)], [ksk], lambda e: e.activation(out=ks_[:], in_=ktok[:, t, :], func=AF.Copy, scale=col_(SRC)))
                    yield
                    c.op("pe", ["r_qTb", "r_kTb"], [K_(vb_, 2)], lambda e: e.matmul(Q(vb_, 2), lhsT=qTb[:, ts], rhs=kTb[:, ts], start=True, stop=True))
                    c.op("dve", [ek, K_(vb_, 2)], [pk_], lambda e: e.tensor_tensor(out=p_[:], in0=e_[:], in1=Q(vb_, 2), op=ALU.mult))
                    yield
                    qv = Q(ab_, 0).bitcast(BF16)[:, 0:128]
                    c.op("pe", [pk_, "idb"], [K_(ab_, 0)], lambda e: e.transpose(out=qv, in_=p_[:], identity=self.idb[:]))
                    c.op("act", [K_(ab_, 0)], [pTk], lambda e: e.copy(out=pT_[:], in_=qv))
                    yield
                    c.op("pe", [pTk, "r_vtokb"], [K_(ab_, 2)], lambda e: e.matmul(pb[ab_][:, 256:385], lhsT=pT_[:], rhs=vtokb[:, t, :], start=True, stop=True))
                    c.op("dve", [K_(ab_, 2)], [("r_nd", sl_)], lambda e: e.tensor_copy(out=nd[sl_][:], in_=pb[ab_][:, 256:385]))
                    yield
                    while turn[dr] != si:
                        yield

                    def mms(e):
                        e.matmul(pb[sb_][:, 0:129], lhsT=qTb[:, ts], rhs=Cbc[:, 0:129], start=True, stop=True)
                        return e.matmul(pb[sb_][:, 256:385], lhsT=ks_[:], rhs=vtokb[:, t, :], start=True, stop=True)
                    c.op("pe", ["r_qTb", Cbck, ksk, "r_vtokb"], [K_(sb_, 0), K_(sb_, 2)], mms)
                    c.op("dve", [K_(sb_, 2), Cck, ("r_mlc", DEC, dr)], [Cnk], lambda e: e.scalar_tensor_tensor(out=Cn[:], in0=Cc[:], scalar=col_(DEC), in1=pb[sb_][:, 256:385], op0=ALU.mult, op1=ALU.add))
                    c.op("act", [Cnk], [Cbnk], lambda e: e.copy(out=Cbn[:], in_=Cn[:]))
                    c.op("dve", [K_(sb_, 0), ("r_mlc", INTER, dr), ("r_nd", sl_)], [("r_nd", sl_)], lambda e: e.scalar_tensor_tensor(out=nd[sl_][:], in0=pb[sb_][:, 0:129], scalar=col_(INTER), in1=nd[sl_][:], op0=ALU.mult, op1=ALU.add))
                    turn[dr] += 1
                    yield
                    dc = dcol[sl_]
                    dk = ("r_dcol", sl_)
                    c.op("dve", [("r_nd", sl_)], [dk], lambda e: e.tensor_scalar(out=dc[:, 0:1], in0=nd[sl_][:, 128:129], scalar1=-1.0, scalar2=None, op0=ALU.mult))
                    yield
                    c.op("dve", [("r_nd", sl_), dk], [dk], lambda e: e.tensor_tensor(out=dc[:, 1:2], in0=nd[sl_][:, 128:129], in1=dc[:, 0:1], op=ALU.max))
                    yield
                    c.op("dve", [dk, ("r_mlc", EMT, dr)], [dk], lambda e: e.tensor_tensor(out=dc[:, 2:3], in0=dc[:, 1:2], in1=col_(EMT), op=ALU.max))
                    yield
                    c.op("dve", [dk], [dk], lambda e: e.reciprocal(out=dc[:, 3:4], in_=dc[:, 2:3]))
                    yield
                    c.op("dve", [("r_nd", sl_), dk, ("r_ost", t)], [("r_ost", t)], lambda e: e.scalar_tensor_tensor(out=ost[:, t, :], in0=nd[sl_][:, 0:128], scalar=dc[:, 3:4], in1=ost[:, t, :], op0=ALU.mult, op1=ALU.add))
                    free.append(sl_)

                for alg in _algs:
                    for h in range(int(_os.environ.get("AB_NH", "4"))):
                        if alg == "dn":
                            fq, fk, tk, tv, tg = h, 4 + h, "dnk", "dnv", "dnz"
                        else:
                            fq, fk, tk, tv, tg = 8 + h, 12 + h, "mlk", "mlv", "mlo"
                        c.op("sp", [("ab_fm", fk)], ["r_qT"], lambda e: e.dma_start(out=qT[:], in_=FM[fk]))
                        c.op("act", ["r_qT"], ["r_kTb"], lambda e: e.copy(out=kTb[:], in_=qT[:]))
                        c.op("sp", [("ab_fm", fq)], ["r_qT"], lambda e: e.dma_start(out=qT[:], in_=FM[fq]))
                        c.op("pool", ["r_qT"], ["r_qTb"], lambda e: e.tensor_copy(out=qTb[:], in_=qT[:]))
                        for t0 in range(0, NT, 4):
                            t1_ = min(NT, t0 + 4)
                            for (dst_, nm_, ky_) in ((ktok, tk, "r_ktok"), (vtok, tv, "r_vtok")):
                                c.op("sp", [("ab_tm", nm_, h)], [ky_], lambda e, dst_=dst_, nm_=nm_, t0=t0, t1_=t1_: e.dma_start(out=dst_[:, t0:t1_, 0:128], in_=TM[nm_][t0 * 128:t1_ * 128, h * 128:(h + 1) * 128].rearrange("(t p) d -> p t d", p=128)))
                        c.op("pool", ["r_vtok"], ["r_vtok1"], lambda e: e.memset(vtok[:, :, 128:129], 1.0))
                        c.op("dve", ["r_vtok", "r_vtok1"], ["r_vtokb"], lambda e: e.tensor_copy(out=vtokb[:], in_=vtok[:]))
                        c.op("pool", [("r_ost", t) for t in range(NT)], [("r_ost", t) for t in range(NT)], lambda e: e.memset(ost[:], 0.0))
                        for dr in range(2):
                            c.op("pool", [("r_S", dr, 0)], [("r_S", dr, 0)], lambda e, dr=dr: e.memset(Sst[dr][0][:], 0.0))
                            c.op("pool", [("r_Ssh", dr, 0)], [("r_Ssh", dr, 0)], lambda e, dr=dr: e.memset(Ssh[dr][0][:], 0.0))
                        orders = [list(range(NT)), list(range(NT - 1, -1, -1))]
                        if alg == "dn":
                            turn[0] = turn[1] = 0
                            gens = []
                            for si in range(NT):
                                for dr in range(2):
                                    gens.append(dn_unit(h, dr, si, orders[dr][si]))
                            run_units(gens)
                        else:
                            gens = []
                            for si in range(NT):
                                for dr in range(2):
                                    gens.append(ml_prep(h, dr, si, orders[dr][si]))
                            run_units(gens)
                            for dr in range(2):
                                col = dr * 4 + h
                                bt_ = Mlt[:, :, 16 + col]
                                mn_, ms_, b1_ = mlc[:, MNEW, dr, :], mlc[:, MS, dr, :], mlc[:, B1, dr, :]
                                if dr == 0:
                                    c.op("dve", ["ab_Mlt", ("r_mlc", MS, dr)], [("r_mlc", MNEW, dr)], lambda e, bt_=bt_, mn_=mn_, ms_=ms_: e.tensor_tensor_scan(out=mn_, data0=bt_, data1=ms_, initial=0.0, op0=ALU.add, op1=ALU.max))
                                    c.op("dve", ["ab_Mlt", ("r_mlc", MNEW, dr)], [("r_mlc", B1, dr)], lambda e, bt_=bt_, mn_=mn_, b1_=b1_: e.tensor_tensor(out=b1_[:, 1:NT], in0=bt_[:, 1:NT], in1=mn_[:, 0:NT - 1], op=ALU.add))
                                    c.op("dve", ["ab_Mlt", ("r_mlc", B1, dr)], [("r_mlc", B1, dr)], lambda e, bt_=bt_, b1_=b1_: e.tensor_copy(out=b1_[:, 0:1], in_=bt_[:, 0:1]))
                                else:
                                    c.op("dve", ["ab_Mlt", ("r_mlc", MS, dr)], [("r_mlc", MNEW, dr)], lambda e, bt_=bt_, mn_=mn_, ms_=ms_: e.tensor_tensor_scan(out=mn_[:, ::-1], data0=bt_[:, ::-1], data1=ms_[:, ::-1], initial=0.0, op0=ALU.add, op1=ALU.max))
                                    c.op("dve", ["ab_Mlt", ("r_mlc", MNEW, dr)], [("r_mlc", B1, dr)], lambda e, bt_=bt_, mn_=mn_, b1_=b1_: e.tensor_tensor(out=b1_[:, 0:NT - 1], in0=bt_[:, 0:NT - 1], in1=mn_[:, 1:NT], op=ALU.add))
                                    c.op("dve", ["ab_Mlt", ("r_mlc", B1, dr)], [("r_mlc", B1, dr)], lambda e, bt_=bt_, b1_=b1_: e.tensor_copy(out=b1_[:, NT - 1:NT], in_=bt_[:, NT - 1:NT]))
                            for dr in range(2):
                                col = dr * 4 + h

                                def A_(i_, dr=dr):
                                    return mlc[:, i_, dr, :]

                                def kk_(i_, dr=dr):
                                    return ("r_mlc", i_, dr)
                                bcum_, btot_, wsrc_ = Mlt[:, :, col], Mlt[:, :, 16 + col], Ws[:, :, col]
                                c.op("dve", ["ab_Mlt"], [kk_(A1)], lambda e, dr=dr: e.tensor_tensor(out=A_(A1), in0=bcum_, in1=btot_, op=ALU.subtract))
                                c.op("dve", [kk_(A1), kk_(B1)], [kk_(A1)], lambda e, dr=dr: e.tensor_tensor(out=A_(A1), in0=A_(A1), in1=A_(B1), op=ALU.add))
                                c.op("dve", [kk_(A1), kk_(MI)], [kk_(MT)], lambda e, dr=dr: e.tensor_tensor(out=A_(MT), in0=A_(A1), in1=A_(MI), op=ALU.max))
                                c.op("dve", [kk_(MT)], [kk_(NEGM)], lambda e, dr=dr: e.tensor_scalar(out=A_(NEGM), in0=A_(MT), scalar1=-1.0, scalar2=None, op0=ALU.mult))
                                c.op("dve", [kk_(A1), kk_(MT)], [kk_(TMP)], lambda e, dr=dr: e.tensor_tensor(out=A_(TMP), in0=A_(A1), in1=A_(MT), op=ALU.subtract))
                                c.op("act", [kk_(TMP)], [kk_(INTER)], lambda e, dr=dr: e.activation(out=A_(INTER), in_=A_(TMP), func=AF.Exp))
                                c.op("act", [kk_(NEGM)], [kk_(EMT)], lambda e, dr=dr: e.activation(out=A_(EMT), in_=A_(NEGM), func=AF.Exp))
                                c.op("dve", [kk_(B1), kk_(MNEW), kk_(INTER)], [kk_(TMP)], lambda e, dr=dr: e.tensor_tensor(out=A_(TMP), in0=A_(B1), in1=A_(MNEW), op=ALU.subtract))
                                c.op("act", [kk_(TMP)], [kk_(DEC)], lambda e, dr=dr: e.activation(out=A_(DEC), in_=A_(TMP), func=AF.Exp))
                                c.op("dve", ["ab_Ws", kk_(MNEW), kk_(DEC)], [kk_(TMP)], lambda e, dr=dr: e.tensor_tensor(out=A_(TMP), in0=wsrc_, in1=A_(MNEW), op=ALU.subtract))
                                c.op("act", [kk_(TMP)], [kk_(SRC)], lambda e, dr=dr: e.activation(out=A_(SRC), in_=A_(TMP), func=AF.Exp))
                            turn[0] = turn[1] = 0
                            gens = []
                            for si in range(NT):
                                for dr in range(2):
                                    gens.append(ml_main(h, dr, si, orders[dr][si]))
                            run_units(gens)
                        mchunk = h if alg == "dn" else 4 + h
                        for t0 in range(0, NT, 4):
                            t1_ = min(NT, t0 + 4)
                            c.op("sp", [("ab_tm", tg, h)], ["r_qT"], lambda e, t0=t0, t1_=t1_: e.dma_start(out=gtok[:, t0:t1_, :], in_=TM[tg][t0 * 128:t1_ * 128, h * 128:(h + 1) * 128].rearrange("(t p) d -> p t d", p=128)))
                        for t in range(NT):
                            u2 = t % 2
                            if alg == "dn":
                                c.op("act", [("r_ost", t)], [("r_ot", u2), ("r_oss", u2)], lambda e, t=t, u2=u2: e.activation(out=ot[u2][:], in_=ost[:, t, :], func=AF.Square, accum_out=oss[u2][:, 0:1]))
                                c.op("dve", [("r_oss", u2)], [("r_oss", u2)], lambda e, u2=u2: e.tensor_scalar(out=oss[u2][:, 0:1], in0=oss[u2][:, 0:1], scalar1=1.0 / 128, scalar2=EPS, op0=ALU.mult, op1=ALU.add))
                                c.op("act", [("r_oss", u2)], [("r_oss", u2)], lambda e, u2=u2: e.activation(out=oss[u2][:, 0:1], in_=oss[u2][:, 0:1], func=AF.Sqrt))
                                c.op("dve", [("r_oss", u2)], [("r_oss", u2)], lambda e, u2=u2: e.reciprocal(out=oss[u2][:, 0:1], in_=oss[u2][:, 0:1]))
                                c.op("dve", [("r_ost", t), ("r_oss", u2), "ab_dnw"], [("r_ot", u2)], lambda e, t=t, u2=u2: e.scalar_tensor_tensor(out=ot[u2][:], in0=ost[:, t, :], scalar=oss[u2][:, 0:1], in1=dnw[:], op0=ALU.mult, op1=ALU.mult))
                                c.op("dve", [("r_ot", u2), "r_qT"], [("r_ob", u2)], lambda e, t=t, u2=u2: e.tensor_tensor(out=ob[u2][:], in0=ot[u2][:], in1=gtok[:, t, :], op=ALU.mult))
                            else:
                                c.op("dve", [("r_ost", t), "r_qT"], [("r_ob", u2)], lambda e, t=t, u2=u2: e.tensor_tensor(out=ob[u2][:], in0=ost[:, t, :], in1=gtok[:, t, :], op=ALU.mult))
                            c.op("pe", [("r_ob", u2), "idb"], [("r_pTb", t % 4)], lambda e, t=t, u2=u2: e.transpose(out=pTb[:, (t % 4) * 128:(t % 4 + 1) * 128], in_=ob[u2][:], identity=self.idb[:]))
                            if t % 4 == 3:
                                tb = t // 4
                                mx = mst4[tb % 2]
                                c.op("act", [("r_pTb", q_) for q_ in range(4)], [("r_mx", tb % 2)], lambda e, mx=mx: e.copy(out=mx[:], in_=pTb[:, 0:512]))
                                c.op("gq", [("r_mx", tb % 2)], [("ab_mix", tb)], lambda e, mx=mx, tb=tb, mchunk=mchunk: e.dma_start(out=mixscr[mchunk, :, tb * 512:(tb + 1) * 512], in_=mx[:]))
                c.barrier()
        self.stage_c_scr(c, L, mixscr, 8, Dr["ab_w_out"][j], Dr["mix_norm_post"][L:L + 1, :], src, dst, "abc")


W_NAMES = ['mix_norm_pre', 'mix_norm_post', 'ab_w_in', 'ab_w_out', 'dn_conv_w', 'dn_a_log', 'dn_dt_bias',
           'dn_out_norm', 'ml_conv_w', 'ml_i_bias', 'ml_f_bias', 'c_w_in', 'c_w_out', 'c_conv_w', 'c_conv_b',
           'c_w_a', 'c_b_a', 'c_w_x', 'c_b_x', 'c_lambda', 'xa_norm_pre', 'xa_norm_post', 'xa_mem_norm',
           'xa_w_q', 'xa_w_kv', 'xa_w_o', 'ffn_norm_pre', 'ffn_norm_post', 'ffn_w_up', 'ffn_w_down']


def const_masks():
    p = np.arange(128)[:, None]
    f = np.arange(128)[None, :]
    ms = [p == f, p <= f, p >= f, p < f, p > f, np.ones((128, 128), bool)]
    arr = [m.astype(np.float32) for m in ms]
    arr.append(NEG * (p < f).astype(np.float32))
    arr.append(NEG * (p > f).astype(np.float32))
    return np.ascontiguousarray(np.concatenate(arr, axis=1)).astype(np.float32)


def build(S, shapes, stages, Mm=256):
    B = Builder(S, Mm)
    P = B.P
    nc = B.nc
    for n, shp in shapes.items():
        P.din(n, shp)
    P.din("cmask", [128, 8 * 128])
    out = P.dout("out", [S, D])
    with ExitStack() as es:
        c = Ctx(nc, es)
        B.load_consts(c, es)
        src = P.dram["x"]
        for (name, L) in stages:
            getattr(B, name)(c, L, src, out)
            src = out
        c.barrier()
        c.finish()
    B.nops = c.nops
    return B


STAGES = [("mixer_ab", 0), ("xa", 0), ("ffn", 0), ("mixer_c", 1), ("xa", 1), ("ffn", 1)]


def kernel(**inputs):
    x = np.ascontiguousarray(np.asarray(inputs["x"], dtype=np.float32))
    mem = np.ascontiguousarray(np.asarray(inputs["mem"], dtype=np.float32))
    nb, S, _ = x.shape
    shapes = {"x": (S, D), "mem": tuple(mem.shape[1:])}
    ws = {}
    for n in W_NAMES:
        ws[n] = np.ascontiguousarray(np.asarray(inputs[n], dtype=np.float32))
        shapes[n] = ws[n].shape
    B = build(S, shapes, STAGES, Mm=mem.shape[1])
    cm = const_masks()
    in_maps = []
    for b in range(nb):
        m = {"x": x[b], "mem": mem[b], "cmask": cm}
        m.update(ws)
        in_maps.append(m)
    res = run_bass_kernel_spmd(B.nc, in_maps, core_ids=list(range(nb)))
    return np.stack([np.asarray(r["out"], dtype=np.float32) for r in res.results], axis=0)
```
